# Optimizing a Trainium2 kernel written in Bass

```python
import math
import jax
import jax.numpy as jnp
from jax import lax
import numpy as np

D_MODEL = 1024
BATCH = 16
SEQ = 4096
DEPTH = 2

CTX_LEN = 256
GRID_W = 64

RG_WIDTH = D_MODEL
RG_HEADS = 8
RG_HEAD_DIM = RG_WIDTH // RG_HEADS
RG_C = 8.0
CONV_WIDTH = 4
CONV_PAD_LO = 1

S5_WIDTH = D_MODEL
S5_GROUP = 16
S5_GROUPS = S5_WIDTH // S5_GROUP
S5_STATE = 64
DT_MIN = 1e-3
DT_MAX = 1e-1

MIX_WIDTH = RG_WIDTH + S5_WIDTH
IN_WIDTH = 2 * RG_WIDTH + S5_WIDTH
D_FF = 4 * D_MODEL
N_MOD = 6
DEEPNORM_ALPHA = (2.0 * DEPTH) ** 0.25
DEEPNORM_BETA = (8.0 * DEPTH) ** -0.25
LN_EPS = 1e-5

kernel_name = 'hybrid_rglru_s5_deepnorm_prefix_dit'


def layer_norm(x, g, b):
    xf = x.astype(jnp.float32)
    mu = jnp.mean(xf, axis=-1, keepdims=True)
    var = jnp.mean(jnp.square(xf - mu), axis=-1, keepdims=True)
    return ((xf - mu) * lax.rsqrt(var + LN_EPS) * g + b).astype(x.dtype)


def modulate(x, shift, scale):
    return x * (1.0 + scale) + shift


def to_col_major(t, rows):
    b, n, ch = t.shape
    return t.reshape(b, rows, GRID_W, ch).swapaxes(1, 2).reshape(b, n, ch)


def from_col_major(t, rows):
    b, n, ch = t.shape
    return t.reshape(b, GRID_W, rows, ch).swapaxes(1, 2).reshape(b, n, ch)


def centred_dwconv(x, w, b):
    length = x.shape[1]
    xp = jnp.pad(x, ((0, 0), (CONV_PAD_LO, CONV_WIDTH - 1 - CONV_PAD_LO), (0, 0)))
    out = b
    for k in range(CONV_WIDTH):
        out = out + xp[:, k:k + length] * w[k]
    return out


def block_diag_linear(x, w, b):
    bsz, length, _ = x.shape
    xh = x.reshape(bsz, length, RG_HEADS, RG_HEAD_DIM)
    y = jnp.einsum('blhi,hij->blhj', xh, w)
    return y.reshape(bsz, length, RG_WIDTH) + b


def _combine(left, right):
    a_l, b_l = left
    a_r, b_r = right
    return a_l * a_r, a_r * b_l + b_r


def linear_scan(a, b, h0, reverse):
    if reverse:
        a = jnp.flip(a, axis=1)
        b = jnp.flip(b, axis=1)
    a_cum, h = lax.associative_scan(_combine, (a, b), axis=1)
    h = h + a_cum * h0[:, None]
    h_final = h[:, -1]
    if reverse:
        h = jnp.flip(h, axis=1)
    return h, h_final


def rglru_direction(xc_ctx, xc_lat, lam, wa, ba, wi, bi, reverse):
    def coeffs(xc):
        xf = xc.astype(jnp.float32)
        r = jax.nn.sigmoid(block_diag_linear(xf, wa, ba))
        i = jax.nn.sigmoid(block_diag_linear(xf, wi, bi))
        log_a = -RG_C * jax.nn.softplus(-lam) * r
        return jnp.exp(log_a), jnp.sqrt(-jnp.expm1(2.0 * log_a)) * (i * xf)
    a_c, b_c = coeffs(xc_ctx)
    h0 = jnp.zeros((xc_ctx.shape[0], RG_WIDTH), jnp.float32)
    h_ctx, h_fin = linear_scan(a_c, b_c, h0, reverse)
    a_l, b_l = coeffs(xc_lat)
    h_lat, _ = linear_scan(a_l, b_l, h_fin, reverse)
    return h_ctx, h_lat


def s5_direction(u_ctx, u_lat, a_re, a_im, log_dt, b_re, b_im, c_re, c_im, reverse):
    f32 = jnp.float32
    lam = lax.complex(a_re.astype(f32), a_im.astype(f32))
    dt = jnp.exp(log_dt.astype(f32))[:, None]
    lam_bar = jnp.exp(lam * dt)
    b_bar = ((lam_bar - 1.0) / lam)[..., None] * lax.complex(b_re.astype(f32), b_im.astype(f32))
    c_mat = lax.complex(c_re.astype(f32), c_im.astype(f32))

    def run(u, h0):
        bu = jnp.einsum('blgh,gph->blgp', u.astype(jnp.complex64), b_bar)
        a = jnp.broadcast_to(lam_bar, (1, u.shape[1]) + lam_bar.shape)
        h, h_fin = linear_scan(a, bu, h0, reverse)
        return jnp.einsum('blgp,ghp->blgh', h, c_mat).real, h_fin

    h0 = jnp.zeros((u_ctx.shape[0], S5_GROUPS, S5_STATE), jnp.complex64)
    y_ctx, h_fin = run(u_ctx, h0)
    y_lat, _ = run(u_lat, h_fin)
    return y_ctx, y_lat


def hybrid_mixer(u_lat, u_ctx, rows, w_in, conv_w, conv_b, rg_lambda, rg_wa, rg_ba, rg_wi, rg_bi,
                 s5_a_re, s5_a_im, s5_log_dt, s5_b_re, s5_b_im, s5_c_re, s5_c_im, s5_d,
                 s5_glu_w, s5_glu_b, w_out, b_out, ctx_out):
    dtype = u_lat.dtype
    rgx_lat, gate_lat, s5u_lat = jnp.split(u_lat @ w_in, [RG_WIDTH, 2 * RG_WIDTH], axis=-1)
    rgx_ctx, gate_ctx, s5u_ctx = jnp.split(u_ctx @ w_in, [RG_WIDTH, 2 * RG_WIDTH], axis=-1)

    xc_lat = centred_dwconv(rgx_lat, conv_w, conv_b)
    xc_ctx = centred_dwconv(rgx_ctx, conv_w, conv_b)
    hc_f, hl_f = rglru_direction(xc_ctx, xc_lat, rg_lambda[0], rg_wa[0], rg_ba[0], rg_wi[0], rg_bi[0], False)
    hc_b, hl_b = rglru_direction(xc_ctx, xc_lat, rg_lambda[1], rg_wa[1], rg_ba[1], rg_wi[1], rg_bi[1], True)
    rg_lat = (hl_f + hl_b).astype(dtype) * jax.nn.gelu(gate_lat)

    def grouped(t):
        return t.astype(jnp.float32).reshape(t.shape[0], t.shape[1], S5_GROUPS, S5_GROUP)
    s_lat = grouped(to_col_major(s5u_lat, rows))
    s_ctx = grouped(s5u_ctx)
    yc_f, yl_f = s5_direction(s_ctx, s_lat, s5_a_re[0], s5_a_im[0], s5_log_dt[0], s5_b_re[0], s5_b_im[0],
                              s5_c_re[0], s5_c_im[0], False)
    yc_b, yl_b = s5_direction(s_ctx, s_lat, s5_a_re[1], s5_a_im[1], s5_log_dt[1], s5_b_re[1], s5_b_im[1],
                              s5_c_re[1], s5_c_im[1], True)
    d_skip = s5_d.astype(jnp.float32).reshape(S5_GROUPS, S5_GROUP)

    def s5_glu(y):
        g = jax.nn.gelu(y.reshape(y.shape[0], y.shape[1], S5_WIDTH).astype(dtype))
        return g * jax.nn.sigmoid(g @ s5_glu_w + s5_glu_b)

    s5_lat = from_col_major(s5_glu(yl_f + yl_b + d_skip * s_lat), rows)
    out_lat = jnp.concatenate([rg_lat, s5_lat], axis=-1) @ w_out + b_out
    if not ctx_out:
        return out_lat, None
    rg_ctx = (hc_f + hc_b).astype(dtype) * jax.nn.gelu(gate_ctx)
    s5_ctx = s5_glu(yc_f + yc_b + d_skip * s_ctx)
    out_ctx = jnp.concatenate([rg_ctx, s5_ctx], axis=-1) @ w_out + b_out
    return out_lat, out_ctx


def sq_relu_mlp(x, w1, b1, w2, b2):
    return jnp.square(jax.nn.relu(x @ w1 + b1)) @ w2 + b2


def setup_inputs(seed: int = 0) -> dict:
    key = jax.random.key(seed)
    keys = iter(jax.random.split(key, 48))
    f32 = jnp.float32

    def normal(shape, scale):
        return scale * jax.random.normal(next(keys), shape, f32)

    def uniform(shape, lo, hi):
        return jax.random.uniform(next(keys), shape, f32, lo, hi)

    nl = DEPTH
    x = normal((BATCH, SEQ, D_MODEL), 1.0)
    c = normal((BATCH, D_MODEL), 1.0)
    ctx = normal((BATCH, CTX_LEN, D_MODEL), 1.0)
    c_ctx = normal((D_MODEL,), 1.0)
    ada_w = normal((nl, D_MODEL, N_MOD * D_MODEL), 0.5 * D_MODEL ** -0.5)
    ada_b = normal((nl, N_MOD * D_MODEL), 0.02)
    ln1_g = 1.0 + normal((nl, D_MODEL), 0.02)
    ln1_b = normal((nl, D_MODEL), 0.02)
    w_in = normal((nl, D_MODEL, IN_WIDTH), D_MODEL ** -0.5)
    conv_w = normal((nl, CONV_WIDTH, RG_WIDTH), CONV_WIDTH ** -0.5)
    conv_b = normal((nl, RG_WIDTH), 0.02)
    a_c = uniform((nl, 2, RG_WIDTH), 0.9, 0.999)
    a0 = a_c ** (1.0 / RG_C)
    rg_lambda = jnp.log(a0) - jnp.log1p(-a0)
    rg_wa = normal((nl, 2, RG_HEADS, RG_HEAD_DIM, RG_HEAD_DIM), RG_HEAD_DIM ** -0.5)
    rg_ba = normal((nl, 2, RG_WIDTH), 0.02)
    rg_wi = normal((nl, 2, RG_HEADS, RG_HEAD_DIM, RG_HEAD_DIM), RG_HEAD_DIM ** -0.5)
    rg_bi = normal((nl, 2, RG_WIDTH), 0.02)
    n_idx = jnp.arange(S5_STATE, dtype=f32)
    s5_a_re = -0.5 * jnp.exp(normal((nl, 2, S5_GROUPS, S5_STATE), 0.02))
    s5_a_im = np.pi * n_idx + normal((nl, 2, S5_GROUPS, S5_STATE), 0.02)
    s5_log_dt = uniform((nl, 2, S5_GROUPS), math.log(DT_MIN), math.log(DT_MAX))
    s5_b_re = normal((nl, 2, S5_GROUPS, S5_STATE, S5_GROUP), (2.0 * S5_GROUP) ** -0.5)
    s5_b_im = normal((nl, 2, S5_GROUPS, S5_STATE, S5_GROUP), (2.0 * S5_GROUP) ** -0.5)
    s5_c_re = normal((nl, 2, S5_GROUPS, S5_GROUP, S5_STATE), 0.5 ** 0.5)
    s5_c_im = normal((nl, 2, S5_GROUPS, S5_GROUP, S5_STATE), 0.5 ** 0.5)
    s5_d = normal((nl, S5_WIDTH), 1.0)
    s5_glu_w = normal((nl, S5_WIDTH, S5_WIDTH), S5_WIDTH ** -0.5)
    s5_glu_b = normal((nl, S5_WIDTH), 0.02)
    w_out = normal((nl, MIX_WIDTH, D_MODEL), DEEPNORM_BETA * MIX_WIDTH ** -0.5)
    b_out = normal((nl, D_MODEL), 0.02)
    ln2_g = 1.0 + normal((nl, D_MODEL), 0.02)
    ln2_b = normal((nl, D_MODEL), 0.02)
    mlp_w1 = normal((nl, D_MODEL, D_FF), D_MODEL ** -0.5)
    mlp_b1 = normal((nl, D_FF), 0.02)
    mlp_w2 = normal((nl, D_FF, D_MODEL), DEEPNORM_BETA * D_FF ** -0.5)
    mlp_b2 = normal((nl, D_MODEL), 0.02)
    return {'x': x, 'c': c, 'ctx': ctx, 'c_ctx': c_ctx, 'ada_w': ada_w, 'ada_b': ada_b,
            'ln1_g': ln1_g, 'ln1_b': ln1_b, 'w_in': w_in, 'conv_w': conv_w, 'conv_b': conv_b,
            'rg_lambda': rg_lambda, 'rg_wa': rg_wa, 'rg_ba': rg_ba, 'rg_wi': rg_wi, 'rg_bi': rg_bi,
            's5_a_re': s5_a_re, 's5_a_im': s5_a_im, 's5_log_dt': s5_log_dt, 's5_b_re': s5_b_re,
            's5_b_im': s5_b_im, 's5_c_re': s5_c_re, 's5_c_im': s5_c_im, 's5_d': s5_d,
            's5_glu_w': s5_glu_w, 's5_glu_b': s5_glu_b, 'w_out': w_out, 'b_out': b_out,
            'ln2_g': ln2_g, 'ln2_b': ln2_b, 'mlp_w1': mlp_w1, 'mlp_b1': mlp_b1,
            'mlp_w2': mlp_w2, 'mlp_b2': mlp_b2}


def reference(x, c, ctx, c_ctx, ada_w, ada_b, ln1_g, ln1_b, w_in, conv_w, conv_b,
              rg_lambda, rg_wa, rg_ba, rg_wi, rg_bi, s5_a_re, s5_a_im, s5_log_dt, s5_b_re,
              s5_b_im, s5_c_re, s5_c_im, s5_d, s5_glu_w, s5_glu_b, w_out, b_out,
              ln2_g, ln2_b, mlp_w1, mlp_b1, mlp_w2, mlp_b2):
    rows = x.shape[1] // GRID_W
    for l in range(DEPTH):
        last = l == DEPTH - 1
        mod = jax.nn.silu(c) @ ada_w[l] + ada_b[l]
        mod_c = jax.nn.silu(c_ctx) @ ada_w[l] + ada_b[l]
        sh1, sc1, g1, sh2, sc2, g2 = jnp.split(mod[:, None, :], N_MOD, axis=-1)
        sh1c, sc1c, g1c, sh2c, sc2c, g2c = jnp.split(mod_c, N_MOD, axis=-1)

        m_lat, m_ctx = hybrid_mixer(
            modulate(x, sh1, sc1), modulate(ctx, sh1c, sc1c), rows,
            w_in[l], conv_w[l], conv_b[l], rg_lambda[l], rg_wa[l], rg_ba[l], rg_wi[l], rg_bi[l],
            s5_a_re[l], s5_a_im[l], s5_log_dt[l], s5_b_re[l], s5_b_im[l], s5_c_re[l], s5_c_im[l],
            s5_d[l], s5_glu_w[l], s5_glu_b[l], w_out[l], b_out[l], not last)

        x = layer_norm(DEEPNORM_ALPHA * x + g1 * m_lat, ln1_g[l], ln1_b[l])
        f_lat = sq_relu_mlp(modulate(x, sh2, sc2), mlp_w1[l], mlp_b1[l], mlp_w2[l], mlp_b2[l])
        x = layer_norm(DEEPNORM_ALPHA * x + g2 * f_lat, ln2_g[l], ln2_b[l])

        if not last:
            ctx = layer_norm(DEEPNORM_ALPHA * ctx + g1c * m_ctx, ln1_g[l], ln1_b[l])
            f_ctx = sq_relu_mlp(modulate(ctx, sh2c, sc2c), mlp_w1[l], mlp_b1[l], mlp_w2[l], mlp_b2[l])
            ctx = layer_norm(DEEPNORM_ALPHA * ctx + g2c * f_ctx, ln2_g[l], ln2_b[l])
    return x
```

```python
import math
from contextlib import ExitStack
import numpy as np
import concourse.bass as bass
import concourse.mybir as mybir
from concourse.bass_utils import run_bass_kernel_spmd

F32 = mybir.dt.float32
BF16 = mybir.dt.bfloat16
AF = mybir.ActivationFunctionType
ALU = mybir.AluOpType

D = 1024
SEQ = 4096
CTX = 256
DEPTH = 2
NB = 2
ALPHA = (2.0 * DEPTH) ** 0.25
LN_EPS = 1e-5
RG_C = 8.0
NCORES = 8
OPT_PREP = False
OPT_1B = True
OPT_3 = True

VEC_NAMES = [('ln1_g', 8), ('ln1_b', 8), ('ln2_g', 8), ('ln2_b', 8), ('conv_w0', 8), ('conv_w1', 8),
             ('conv_w2', 8), ('conv_w3', 8), ('conv_b', 8), ('lam0', 8), ('lam1', 8), ('ba0', 8),
             ('ba1', 8), ('bi0', 8), ('bi1', 8), ('glu_b', 8), ('b_out', 8), ('b2', 8), ('b1', 32),
             ('s5d', 8)]
VOFF = {}
_o = 0
for _n, _c in VEC_NAMES:
    VOFF[_n] = _o
    _o += _c
NV = _o


class Op:
    __slots__ = ('eng', 'fn', 'deps', 'ddeps', 'signal', 'sem', 'val', 'is_dma')


class Prog:
    def __init__(self, nc, es):
        self.nc = nc
        self.engs = {'pe': nc.tensor, 'act': nc.scalar, 'dve': nc.vector, 'pool': nc.gpsimd, 'sp': nc.sync}
        self.sem = {e: es.enter_context(nc.semaphore('s_' + e)) for e in ('pe', 'act', 'dve', 'pool')}
        self.cnt = {e: 0 for e in self.sem}
        self.es = es
        self.dsem = {}
        self.dcum = {}
        self.nops = 0
        self.free = []
        self.free_sw = []
        self.dkind = {}
        self.nsem = 0
        self.barrier = []
        self._reset()

    def _reset(self):
        self.ops = {e: [] for e in self.engs}
        self.last_w = {}
        self.readers = {}

    def add(self, eng, fn, reads=(), writes=(), dma_key=None, signal=False):
        op = Op()
        op.eng = eng
        op.fn = fn
        op.signal = signal
        op.is_dma = dma_key is not None
        op.sem = None
        op.val = None
        deps = []
        for k in reads:
            w = self.last_w.get(k)
            if w is not None:
                deps.append(w)
        for k in writes:
            w = self.last_w.get(k)
            if w is not None:
                deps.append(w)
            deps.extend(self.readers.get(k, {}).values())
        cdeps = []
        ddeps = {}
        seen = set()
        for d in deps:
            if id(d) in seen:
                continue
            seen.add(id(d))
            if d.is_dma:
                ddeps[d.sem] = self.dcum[d.sem]
            else:
                if d.eng == 'pe' and eng == 'pe':
                    continue
                cdeps.append(d)
        op.deps = cdeps
        op.ddeps = ddeps
        if op.is_dma:
            if dma_key not in self.dsem:
                fl = self.free_sw if eng == 'pool' else self.free
                self.dkind[dma_key] = eng == 'pool'
                if fl:
                    h, c0 = fl.pop()
                else:
                    self.nsem += 1
                    h, c0 = self.es.enter_context(self.nc.semaphore('d_' + str(self.nsem))), 0
                self.dsem[dma_key] = h
                self.dcum[dma_key] = c0
            self.dcum[dma_key] += 16
            op.sem = dma_key
            op.val = self.dcum[dma_key]
            assert op.val < 60000, dma_key
        rk = ('d', dma_key) if op.is_dma else eng
        for k in reads:
            self.readers.setdefault(k, {})[rk] = op
        for k in writes:
            self.last_w[k] = op
            self.readers[k] = {}
        self.ops[eng].append(op)
        self.nops += 1
        return op

    def flush(self):
        nc = self.nc
        barrier = self.barrier
        for e, lst in self.ops.items():
            for op in lst:
                for d in op.deps:
                    d.signal = True
            for op in reversed(lst):
                if not op.is_dma and op.fn is not None:
                    op.signal = True
                    break
        for e, lst in self.ops.items():
            if e not in self.sem:
                continue
            for op in lst:
                if op.signal and not op.is_dma and op.fn is not None:
                    self.cnt[e] += 1
                    op.sem = e
                    op.val = self.cnt[e]
                    assert op.val < 60000
        ops = self.ops
        sem = self.sem
        dsem = self.dsem

        def emit(ename, eobj):
            waited = {}
            for s, v in barrier:
                eobj.wait_ge(s, v)
            for op in ops[ename]:
                for d in op.deps:
                    s = sem[d.sem]
                    if waited.get(d.sem, 0) < d.val:
                        eobj.wait_ge(s, d.val)
                        waited[d.sem] = d.val
                for k, v in op.ddeps.items():
                    if waited.get(k, 0) < v:
                        eobj.wait_ge(dsem[k], v)
                        waited[k] = v
                if op.fn is None:
                    continue
                ins = op.fn(eobj)
                if op.is_dma:
                    ins.then_inc(dsem[op.sem], 16)
                elif op.signal:
                    ins.then_inc(sem[op.sem], 1)

        with nc.Block() as block:
            @block.tensor
            def _(t):
                emit('pe', t)

            @block.scalar
            def _(s):
                emit('act', s)

            @block.vector
            def _(v):
                emit('dve', v)

            @block.gpsimd
            def _(g):
                emit('pool', g)

            @block.sync
            def _(sp):
                emit('sp', sp)
        self._reset()
        self.barrier = [(self.sem[e], self.cnt[e]) for e in self.sem if self.cnt[e] > 0]
        self.barrier += [(self.dsem[k], self.dcum[k]) for k in self.dsem if self.dcum[k] > 0]
        for k in self.dsem:
            (self.free_sw if self.dkind[k] else self.free).append((self.dsem[k], self.dcum[k]))
        self.dsem = {}
        self.dcum = {}
        self.dkind = {}

    def final_wait(self):
        nc = self.nc
        items = list(self.barrier)
        with nc.Block() as block:
            @block.sync
            def _(sp):
                for s, v in items:
                    sp.wait_ge(s, v)

    def dma(self, q, out, in_, key, reads=(), writes=(), **kw):
        return self.add(q, lambda e, o=out, i=in_, kw=kw: e.dma_start(out=o, in_=i, **kw), reads, writes, dma_key=key)

    def tr(self, out, in_, ident, reads=(), writes=()):
        return self.add('pe', lambda e, o=out, i=in_, d=ident: e.transpose(o, i, d), reads, writes)

    def mm(self, out, lhsT, rhs, start, stop, reads=(), writes=(), signal=False):
        return self.add('pe', lambda e, o=out, l=lhsT, r=rhs, s=start, t=stop: e.matmul(o, lhsT=l, rhs=r, start=s, stop=t),
                        reads, writes, signal=signal)

    def act(self, out, in_, func, bias=None, scale=1.0, reads=(), writes=()):
        if bias is None:
            f = lambda e, o=out, i=in_, fu=func, sc=scale: e.activation(out=o, in_=i, func=fu, scale=sc)
        else:
            f = lambda e, o=out, i=in_, fu=func, b=bias, sc=scale: e.activation(out=o, in_=i, func=fu, bias=b, scale=sc)
        return self.add('act', f, reads, writes)

    def ts(self, eng, out, in0, s1, s2, op0, op1=None, reads=(), writes=()):
        if op1 is None:
            f = lambda e, o=out, i=in0, a=s1, p0=op0: e.tensor_scalar(out=o, in0=i, scalar1=a, scalar2=None, op0=p0)
        else:
            f = lambda e, o=out, i=in0, a=s1, b=s2, p0=op0, p1=op1: e.tensor_scalar(out=o, in0=i, scalar1=a, scalar2=b, op0=p0, op1=p1)
        return self.add(eng, f, reads, writes)

    def tt(self, eng, out, in0, in1, op, reads=(), writes=()):
        return self.add(eng, lambda e, o=out, a=in0, b=in1, p=op: e.tensor_tensor(out=o, in0=a, in1=b, op=p), reads, writes)

    def stt(self, out, in0, scalar, in1, op0, op1, reads=(), writes=()):
        return self.add('dve', lambda e, o=out, a=in0, s=scalar, b=in1, p0=op0, p1=op1:
                        e.scalar_tensor_tensor(out=o, in0=a, scalar=s, in1=b, op0=p0, op1=p1), reads, writes)

    def copy(self, eng, out, in_, reads=(), writes=()):
        if eng == 'act':
            return self.add('act', lambda e, o=out, i=in_: e.activation(out=o, in_=i, func=AF.Copy), reads, writes)
        return self.add(eng, lambda e, o=out, i=in_: e.tensor_copy(out=o, in_=i), reads, writes)

    def scan(self, out, d0, d1, init, reads=(), writes=()):
        return self.add('dve', lambda e, o=out, a=d0, b=d1, i=init:
                        e.tensor_tensor_scan(out=o, data0=a, data1=b, initial=i, op0=ALU.mult, op1=ALU.add), reads, writes)

    def memset(self, eng, out, val, reads=(), writes=()):
        return self.add(eng, lambda e, o=out, v=val: e.memset(o, v), reads, writes)


class Ctx:
    pass


def build_program(debug=(), stages=None):
    nc = bass.Bass("TRN2", target_bir_lowering=False)
    G = Ctx()
    G.nc = nc
    G.uid = [0]

    def sbt(name, shape, dt):
        G.uid[0] += 1
        return nc.sbuf_tensor('%s_u%d' % (name, G.uid[0]), shape, dt)
    G.sbt = sbt
    G.debug = set(debug)
    ges = ExitStack()
    G.ges = ges
    P = Prog(nc, ges)
    G.P = P

    def din(name, shape, dt=F32):
        return nc.dram_tensor(name, list(shape), dt, kind="ExternalInput").ap()

    def dscr(name, shape, dt):
        kind = "ExternalOutput" if name in G.debug else "Internal"
        return nc.dram_tensor(name, list(shape), dt, kind=kind).ap()

    G.dscr = dscr
    I = {}
    I['xT'] = din('xT', [NB, D, SEQ])
    I['cxT'] = din('cxT', [NB, D, CTX])
    I['cT'] = din('cT', [128, 8, 4])
    I['ada_w'] = din('ada_w', [DEPTH, D, 6 * D])
    I['ada_bP'] = din('ada_bP', [DEPTH, 128, 48])
    I['vecs'] = din('vecs', [DEPTH, 128, NV])
    I['w_in'] = din('w_in', [DEPTH, D, 3 * D])
    I['rg_wa'] = din('rg_wa', [DEPTH, 2, 8, 128, 128])
    I['rg_wi'] = din('rg_wi', [DEPTH, 2, 8, 128, 128])
    I['glu_w'] = din('glu_w', [DEPTH, D, D])
    I['w_out'] = din('w_out', [DEPTH, 2 * D, D])
    I['w1'] = din('w1', [DEPTH, D, 4 * D])
    I['w2'] = din('w2', [DEPTH, 4 * D, D])
    I['ident'] = din('ident', [128, 128])
    I['s5a'] = din('s5a', [DEPTH, 128, 2, 64])
    I['s5ldt'] = din('s5ldt', [DEPTH, 128, 64])
    I['s5B'] = din('s5B', [DEPTH, 128, 2, 64, 16])
    I['s5C'] = din('s5C', [DEPTH, 128, 2, 64, 16])
    I['s5DP'] = din('s5DP', [DEPTH, 128, 64])
    I['cmask'] = din('cmask', [128, 2, 4, 4, 128])
    I['cvec'] = din('cvec', [128, 4])
    I['mvals'] = din('mvals', [128, 128])
    G.I = I
    G.yT = nc.dram_tensor('yT', [NB, D, SEQ], F32, kind="ExternalOutput").ap()

    G.PS = [ges.enter_context(nc.psum_tensor('ps%d' % i, [128, 512], F32)) for i in range(8)]
    G.MOD = ges.enter_context(G.sbt('MOD', [128, DEPTH, 48, 4], F32))
    G.MP1 = ges.enter_context(G.sbt('MP1', [128, DEPTH, 2, 8, 4], F32))
    G.VEC = ges.enter_context(G.sbt('VEC', [128, DEPTH, NV], F32))
    G.ONE = ges.enter_context(G.sbt('ONE', [128, 1], F32))
    G.EPS = ges.enter_context(G.sbt('EPS', [128, 1], F32))

    S = {}
    for b in range(NB):
        S['rgx', b] = dscr('rgx%d' % b, [D, CTX + SEQ], BF16)
        S['gg', b] = dscr('gg%d' % b, [D, CTX + SEQ], BF16)
        S['s5u', b] = dscr('s5u%d' % b, [D, CTX + SEQ], BF16)
        S['rg', b] = dscr('rg%d' % b, [D, CTX + SEQ], BF16)
        S['s5uc', b] = dscr('s5uc%d' % b, [D, CTX], BF16)
        S['y5', b] = dscr('y5_%d' % b, [D, CTX + SEQ], BF16)
    for kind, n in (('lat', SEQ), ('ctx', CTX)):
        S['x', kind] = dscr('xs_' + kind, [NB, D, n], F32)
        S['x1', kind] = dscr('x1_' + kind, [NB, D, n], F32)
    for l in range(DEPTH):
        S['WT', l] = dscr('WT%d' % l, [64, 128, 16, 128], BF16)
        S['WS', l] = dscr('WS%d' % l, [64, 128, 2, 8, 128], BF16)
        S['WO', l] = dscr('WO%d' % l, [64, 128, 2, 8, 128], BF16)
        S['L64', l] = dscr('L64_%d' % l, [128, 2, 64], F32)
    G.S = S

    G.stages = stages
    run_stages(G)
    P.final_wait()
    return nc, G


def run_stages(G):
    st = G.stages

    def on(name):
        return st is None or name in st
    if on('mod'):
        stage_mod(G)
    if 'moddbg' in G.debug:
        d = G.dscr('moddbg', [128, DEPTH * 48 * 4], F32)
        G.P.dma('sp', d, G.MOD[:].rearrange('p l c m -> p (l c m)'), 'moddbg', reads=['MOD'])
        G.P.flush()
    for l in range(DEPTH):
        if on('prep%d' % l):
            stage_s5prep(G, l)
        if on('1a%d' % l):
            stage_1a(G, l)
        if on('1b%d' % l):
            stage_1b(G, l)
        if on('2_%d' % l):
            stage_2(G, l)
        if on('3a%d' % l):
            stage_3a(G, l)
        if on('3b%d' % l):
            stage_3b(G, l)


def vec(G, l, name, k=0):
    c = VOFF[name] + k
    return G.VEC[:, l, c:c + 1]


def modv(G, l, j, k, m):
    return G.MOD[:, l, j * 8 + k, m:m + 1]


def stage_mod(G):
    nc, P, I = G.nc, G.P, G.I
    with ExitStack() as es:
        cin = es.enter_context(G.sbt('cin', [128, 8, 4], F32))
        sc = es.enter_context(G.sbt('sc', [128, 8, 4], F32))
        abp = es.enter_context(G.sbt('abp', [128, DEPTH, 48], F32))
        wt = [es.enter_context(G.sbt('adaw%d' % i, [128, 8, 512], F32)) for i in range(2)]
        P.dma('sp', cin[:], I['cT'], 'cin', writes=['cin'])
        P.dma('sp', abp[:], I['ada_bP'].rearrange('l p c -> p l c'), 'abp', writes=['abp'])
        P.dma('sp', G.VEC[:], I['vecs'].rearrange('l p c -> p l c'), 'VEC', writes=['VEC'])
        P.memset('dve', G.ONE[:], 1.0, writes=['ONE'])
        P.memset('dve', G.EPS[:], LN_EPS, writes=['EPS'])
        P.act(sc[:], cin[:], AF.Silu, reads=['cin'], writes=['sc'])
        it = 0
        for l in range(DEPTH):
            for c4 in range(12):
                slot = it % 2
                it += 1
                P.dma('sp', wt[slot][:], I['ada_w'][l, :, c4 * 512:(c4 + 1) * 512].rearrange('(k p) c -> p k c', p=128),
                      ('adaw', slot), writes=[('adaw', slot)])
                ps = G.PS[c4 % 2]
                for cj in range(4):
                    cc = c4 * 4 + cj
                    for k in range(8):
                        P.mm(ps[:, cj * 4:cj * 4 + 4], wt[slot][:, k, cj * 128:(cj + 1) * 128], sc[:, k, :],
                             start=(k == 0), stop=(k == 7), reads=[('adaw', slot), 'sc'], writes=[('ps', c4 % 2)])
                for cj in range(4):
                    cc = c4 * 4 + cj
                    P.ts('dve', G.MOD[:, l, cc, :], ps[:, cj * 4:cj * 4 + 4], abp[:, l, cc:cc + 1], None, ALU.add,
                         reads=[('ps', c4 % 2), 'abp'], writes=['MOD'])
            for s, j in ((0, 1), (1, 4)):
                P.ts('dve', G.MP1[:, l, s, :, :], G.MOD[:, l, j * 8:(j + 1) * 8, :], 1.0, None, ALU.add,
                     reads=['MOD'], writes=['MP1'])
        P.flush()


def seg_list():
    segs = [('ctx', 0, CTX)]
    for t in range(SEQ // 512):
        segs.append(('lat', t * 512, 512))
    return segs


def stage_1a(G, l):
    nc, P, I, S = G.nc, G.P, G.I, G.S
    with ExitStack() as es:
        W = es.enter_context(G.sbt('w_in_b', [128, 8, 3 * D], BF16))
        xin = [es.enter_context(G.sbt('xin%d' % i, [128, 8, 512], F32)) for i in range(2)]
        xm = [es.enter_context(G.sbt('xm%d' % i, [128, 8, 512], BF16)) for i in range(2)]
        st = [[es.enter_context(G.sbt('st%d_%d' % (j, i), [128, 8, 512], BF16)) for i in range(2)] for j in range(3)]
        stc = es.enter_context(G.sbt('stc', [128, 8, CTX], BF16))
        for k in range(8):
            for h in range(2):
                P.dma('pool', W[:, k, h * 1536:(h + 1) * 1536], I['w_in'][l, k * 128:(k + 1) * 128, h * 1536:(h + 1) * 1536],
                      'W1a', writes=[('W', k)])
        it = 0
        psi = 0
        for b in range(NB):
            for (kind, t0, n) in seg_list():
                slot = it % 2
                it += 1
                m = 2 if kind == 'ctx' else b
                src = I['cxT'] if kind == 'ctx' else I['xT']
                if l > 0:
                    src = G.S['x', kind]
                off = t0 if kind == 'ctx' else CTX + t0
                P.dma('sp', xin[slot][:, :, 0:n], src[b, :, t0:t0 + n].rearrange('(k p) t -> p k t', p=128),
                      ('xin', slot), writes=[('xin', slot)])
                for k in range(8):
                    P.ts('dve', xm[slot][:, k, 0:n], xin[slot][:, k, 0:n], G.MP1[:, l, 0, k, m:m + 1], modv(G, l, 0, k, m),
                         ALU.mult, ALU.add, reads=[('xin', slot), 'MP1', 'MOD'], writes=[('xm', slot)])
                for oc in range(24):
                    pb = psi % 4
                    psi += 1
                    ps = G.PS[pb]
                    for k in range(8):
                        P.mm(ps[:, 0:n], W[:, k, oc * 128:(oc + 1) * 128], xm[slot][:, k, 0:n], start=(k == 0), stop=(k == 7),
                             reads=[('W', k), ('xm', slot)], writes=[('ps', pb)])
                    j, o8 = oc // 8, oc % 8
                    dst = st[j][slot][:, o8, 0:n]
                    if j == 1:
                        P.act(dst, ps[:, 0:n], AF.Gelu_apprx_tanh, reads=[('ps', pb)], writes=[('st', j, slot)])
                    elif oc % 3 == 0:
                        P.copy('act', dst, ps[:, 0:n], reads=[('ps', pb)], writes=[('st', j, slot)])
                    else:
                        P.copy('dve', dst, ps[:, 0:n], reads=[('ps', pb)], writes=[('st', j, slot)])
                    if j == 2 and kind == 'ctx':
                        for cc in range(4):
                            P.copy('dve', stc[:, o8, :].rearrange('p (i c ip) -> p i c ip', i=8, c=4)[:, :, cc, :],
                                   ps[:, cc * 64:(cc + 1) * 64].rearrange('p (i ip) -> p i ip', i=8), reads=[('ps', pb)], writes=['stc'])
                if kind == 'ctx':
                    P.dma('sp', S['s5uc', b].rearrange('(k p) t -> p k t', p=128), stc[:], 'stc', reads=['stc'])
                for j, nm in enumerate(('rgx', 'gg', 's5u')):
                    P.dma('sp', S[nm, b][:, off:off + n].rearrange('(k p) t -> p k t', p=128), st[j][slot][:, :, 0:n],
                          ('st', j, slot), reads=[('st', j, slot)], writes=[('dram', nm, b)])
        P.flush()


MAGIC = 12582912.0
TWO_PI = 2.0 * math.pi


def stage_s5prep(G, l):
    nc, P, I, S = G.nc, G.P, G.I, G.S
    GB = 4
    with ExitStack() as es:
        def sb(name, shape, dt=F32):
            return es.enter_context(G.sbt(name, shape, dt))
        A_ = sb('s5a_t', [128, 2, 64])
        ldt = sb('ldt', [128, 64])
        Bin = sb('Bin', [128, 2, 64, 16])
        Cin = sb('Cin', [128, 2, 64, 16])
        DP = sb('DP', [128, 64])
        cmask = sb('cmask_t', [128, 2, 4, 4, 128])
        cvec = sb('cvec_t', [128, 4])
        mvals = sb('mvals_t', [128, 128])
        identf = sb('identf', [128, 128])
        Z = sb('Z', [128, 2, 64])
        ZP = sb('ZP', [128, 2, 64])
        L1 = sb('L1', [128, 2, 64])
        L64 = sb('L64t', [128, 2, 64])
        KS = sb('KS', [128, 2, 64])
        KO = sb('KO', [128, 2, 64])
        Q = sb('Q', [128, 2, 64])
        sm = [sb('sm%d' % i, [128, 64]) for i in range(6)]
        Bbar = sb('Bbar', [128, 2, 64, 16])
        BS = sb('BS', [128, 2, 64, 16])
        CO = sb('CO', [128, 2, 64, 16])
        CinN = sb('CinN', [128, 2, 64, 16])
        CON = sb('CON', [128, 2, 64, 16])
        tmpb = [sb("tmpb%d" % i, [128, 1024]) for i in range(8)]
        tab = sb('tab', [128, 2, GB, 128])
        BmR = sb('BmR', [128, 4, GB, 128], BF16)
        BmI = sb('BmI', [128, 4, GB, 128], BF16)
        AR = sb('AR', [128, 4, GB, 128], BF16)
        AIn = sb('AIn', [128, 4, GB, 128], BF16)
        WSP = sb('WSP', [128, 2, GB, 128])
        MZ1 = sb('MZ1', [128, GB, 128])
        Tout = sb('Tout', [128, GB, 16, 128], BF16)
        WSst = sb('WSst', [128, GB, 2, 8, 128], BF16)
        WOst = sb('WOst', [128, GB, 2, 8, 128], BF16)
        t1 = [sb('t1_%d' % i, [128, 512]) for i in range(2)]
        t2 = [sb('t2_%d' % i, [128, 512]) for i in range(2)]

        for t, nm in ((A_, 's5a'), (ldt, 's5ldt'), (Bin, 's5B'), (Cin, 's5C'), (DP, 's5DP')):
            P.dma('sp', t[:], I[nm][l], 'pl_' + nm, writes=[nm])
        for t, nm in ((cmask, 'cmask'), (cvec, 'cvec'), (mvals, 'mvals'), (identf, 'ident')):
            P.dma('sp', t[:], I[nm], 'pl_' + nm, writes=[nm])

        uid = [0]

        def key():
            uid[0] += 1
            return ('k', uid[0])

        def cexp(o_re, o_im, zr, zi, n, rk, wk, neg_im=False):
            tm = [t[:, 0:n] for t in tmpb]
            k0, k1, k2, k3, k4, k5 = [key() for _ in range(6)]
            P.act(tm[0], zr, AF.Exp, reads=rk, writes=[('tm', 0)])
            P.ts('dve', tm[1], zi, 1.0 / TWO_PI, MAGIC, ALU.mult, ALU.add, reads=rk, writes=[('tm', 1)])
            P.ts('dve', tm[2], tm[1], MAGIC, None, ALU.subtract, reads=[('tm', 1)], writes=[('tm', 2)])
            P.stt(tm[3], zi, 1.0 / TWO_PI, tm[2], ALU.mult, ALU.subtract, reads=rk + [('tm', 2)], writes=[('tm', 3)])
            P.act(tm[4], tm[3], AF.Sin, scale=TWO_PI, reads=[('tm', 3)], writes=[('tm', 4)])
            P.ts('pool', tm[5], zi, 1.0 / TWO_PI, 0.25, ALU.mult, ALU.add, reads=rk, writes=[('tm', 5)])
            P.ts('dve', tm[1], tm[5], MAGIC, None, ALU.add, reads=[('tm', 5)], writes=[('tm', 1)])
            P.ts('dve', tm[2], tm[1], MAGIC, None, ALU.subtract, reads=[('tm', 1)], writes=[('tm', 2)])
            P.tt('dve', tm[3], tm[5], tm[2], ALU.subtract, reads=[('tm', 5), ('tm', 2)], writes=[('tm', 3)])
            P.act(tm[6], tm[3], AF.Sin, scale=TWO_PI, reads=[('tm', 3)], writes=[('tm', 6)])
            P.tt('dve', o_re, tm[0], tm[6], ALU.mult, reads=[('tm', 0), ('tm', 6)], writes=wk)
            if neg_im:
                P.stt(o_im, tm[0], -1.0, tm[4], ALU.mult, ALU.mult, reads=[('tm', 0), ('tm', 4)], writes=wk)
            else:
                P.tt('pool', o_im, tm[0], tm[4], ALU.mult, reads=[('tm', 0), ('tm', 4)], writes=wk)

        def cmul(o_re, o_im, ar, ai, br, bi, n_shape, rk, wk, nb=None):
            sh = n_shape
            n = 1
            for v in sh[1:]:
                n *= v

            def tv(i):
                a = tmpb[i][:, 0:n]
                if len(sh) == 3:
                    return a.rearrange('p (a b) -> p a b', a=sh[1])
                if len(sh) == 4:
                    return a.rearrange('p (a b c) -> p a b c', a=sh[1], b=sh[2])
                return a
            ibr, ibi = (br, bi) if nb is None else nb
            P.tt('dve', tv(0), ar, br, ALU.mult, reads=rk, writes=[('tm', 0)])
            P.tt('dve', tv(1), ai, bi, ALU.mult, reads=rk, writes=[('tm', 1)])
            P.tt('dve', o_re, tv(0), tv(1), ALU.subtract, reads=[('tm', 0), ('tm', 1)], writes=wk)
            P.tt('dve' if OPT_PREP else 'pool', tv(2), ar, ibi, ALU.mult, reads=rk, writes=[('tm', 2)])
            P.tt('pool', tv(3), ai, ibr, ALU.mult, reads=rk, writes=[('tm', 3)])
            P.tt('pool', o_im, tv(2), tv(3), ALU.add, reads=[('tm', 2), ('tm', 3)], writes=wk)

        P.act(sm[0][:], ldt[:], AF.Exp, reads=['s5ldt'], writes=['dt'])
        for c in range(2):
            P.tt('dve', Z[:, c, :], A_[:, c, :], sm[0][:], ALU.mult, reads=['s5a', 'dt'], writes=['Z'])
            P.ts('dve', ZP[:, c, :], Z[:, c, :], cvec[:, 0:1], None, ALU.mult, reads=['Z', 'cvec'], writes=['ZP'])
        cexp(L1[:, 0, :], L1[:, 1, :], Z[:, 0, :], Z[:, 1, :], 64, ['Z'], ['L1'])
        for (dst, col, nm) in ((L64, None, 'L64'), (KS, 1, 'KS'), (KO, 2, 'KO')):
            for c in range(2):
                if col is None:
                    P.ts('dve', sm[1 + c][:], Z[:, c, :], 64.0, None, ALU.mult, reads=['Z'], writes=[('zs', c)])
                else:
                    P.ts('dve', sm[1 + c][:], Z[:, c, :], cvec[:, col:col + 1], None, ALU.mult, reads=['Z', 'cvec'], writes=[('zs', c)])
            cexp(dst[:, 0, :], dst[:, 1, :], sm[1][:], sm[2][:], 64, [('zs', 0), ('zs', 1)], [nm])
        P.dma('sp', S['L64', l], L64[:], 'L64st', reads=['L64'])
        P.ts('dve', sm[1][:], L1[:, 0, :], -1.0, None, ALU.add, reads=['L1'], writes=['numr'])
        P.tt('dve', sm[2][:], A_[:, 0, :], A_[:, 0, :], ALU.mult, reads=['s5a'], writes=['d1'])
        P.tt('dve', sm[3][:], A_[:, 1, :], A_[:, 1, :], ALU.mult, reads=['s5a'], writes=['d2'])
        P.tt('dve', sm[2][:], sm[2][:], sm[3][:], ALU.add, reads=['d1', 'd2'], writes=['den'])
        P.add('dve', lambda e: e.reciprocal(out=sm[3][:], in_=sm[2][:]), reads=['den'], writes=['rden'])
        P.tt('dve', sm[4][:], sm[1][:], A_[:, 0, :], ALU.mult, reads=['numr', 's5a'], writes=['q1'])
        P.tt('dve', sm[5][:], L1[:, 1, :], A_[:, 1, :], ALU.mult, reads=['L1', 's5a'], writes=['q2'])
        P.tt('dve', sm[4][:], sm[4][:], sm[5][:], ALU.add, reads=['q1', 'q2'], writes=['q3'])
        P.tt('dve', Q[:, 0, :], sm[4][:], sm[3][:], ALU.mult, reads=['q3', 'rden'], writes=['Qr'])
        P.tt('dve', sm[4][:], L1[:, 1, :], A_[:, 0, :], ALU.mult, reads=['L1', 's5a', 'Qr'], writes=['q4'])
        P.tt('dve', sm[5][:], sm[1][:], A_[:, 1, :], ALU.mult, reads=['numr', 's5a', 'Qr'], writes=['q5'])
        P.tt('dve', sm[4][:], sm[4][:], sm[5][:], ALU.subtract, reads=['q4', 'q5'], writes=['q6'])
        P.tt('dve', Q[:, 1, :], sm[4][:], sm[3][:], ALU.mult, reads=['q6', 'rden'], writes=['Qi'])

        def bc(ap2, n):
            return ap2.unsqueeze(2).to_broadcast([128, 64, n])
        cmul(Bbar[:, 0], Bbar[:, 1], bc(Q[:, 0, :], 16), bc(Q[:, 1, :], 16), Bin[:, 0], Bin[:, 1], [128, 64, 16],
             ['Qr', 'Qi', 's5B'], ['Bbar'])
        cmul(BS[:, 0], BS[:, 1], bc(KS[:, 0, :], 16), bc(KS[:, 1, :], 16), Bbar[:, 0], Bbar[:, 1], [128, 64, 16],
             ['KS', 'Bbar'], ['BS'])
        cmul(CO[:, 0], CO[:, 1], bc(KO[:, 0, :], 16), bc(KO[:, 1, :], 16), Cin[:, 0], Cin[:, 1], [128, 64, 16],
             ['KO', 's5C'], ['CO'])
        P.ts('dve', CinN[:].rearrange('p c g h -> p (c g h)'), Cin[:].rearrange('p c g h -> p (c g h)'), -1.0, None, ALU.mult,
             reads=['s5C'], writes=['CinN'])
        P.ts('dve', CON[:].rearrange('p c g h -> p (c g h)'), CO[:].rearrange('p c g h -> p (c g h)'), -1.0, None, ALU.mult,
             reads=['CO'], writes=['CON'])

        psi = 0
        for gb in range(64 // GB):
            g0 = gb * GB
            gs = slice(g0, g0 + GB)
            for c in range(2):
                P.tt('dve', tmpb[7][:, 0:GB * 128].rearrange('p (a b) -> p a b', a=GB) if c == 0 else
                     MZ1[:],
                     ZP[:, c, gs].unsqueeze(2).to_broadcast([128, GB, 128]),
                     mvals[:].unsqueeze(1).to_broadcast([128, GB, 128]), ALU.mult,
                     reads=['ZP', 'mvals'], writes=[('mz', c)])
            cexp(tab[:, 0].rearrange('p a b -> p (a b)'), tab[:, 1].rearrange('p a b -> p (a b)'),
                 tmpb[7][:, 0:GB * 128], MZ1[:].rearrange('p a b -> p (a b)'), GB * 128,
                 [('mz', 0), ('mz', 1)], ['tab'])

            def tabv(c, start, step, n_inner):
                if step > 0:
                    v = tab[:, c, :, start:start + 7 * step + 1:step]
                else:
                    stop = start + 7 * step - 1
                    v = tab[:, c, :, start:(stop if stop >= 0 else None):step]
                return v.unsqueeze(3).to_broadcast([128, GB, 8, n_inner])

            def vecv(t, c):
                return t[:, c, gs, :].unsqueeze(2).to_broadcast([128, GB, 8, 16])

            def o4(t):
                return t.rearrange('p g (a b) -> p g a b', a=8)
            for d1 in range(4):
                cmul(o4(BmR[:, d1]), o4(BmI[:, d1]), tabv(0, 63 + d1, -8, 16), tabv(1, 63 + d1, -8, 16), vecv(Bbar, 0), vecv(Bbar, 1),
                     [128, GB, 8, 16], ['tab', 'Bbar'], ['Bm'])
            for d2 in range(4):
                cmul(o4(AR[:, d2]), o4(AIn[:, d2]), tabv(0, 63 + 4 * (d2 - 2), 8, 16), tabv(1, 63 + 4 * (d2 - 2), 8, 16),
                     vecv(Cin, 0), vecv(Cin, 1), [128, GB, 8, 16], ['tab', 's5C', 'CinN'], ['Am'], nb=(vecv(CinN, 0), vecv(CinN, 1)))
            for gl in range(GB):
                for d1 in range(4):
                    pa, pb = psi % 8, (psi + 1) % 8
                    psi += 2
                    for (pp, h0) in ((pa, 0), (pb, 64)):
                        hs = slice(h0, h0 + 64)
                        out = G.PS[pp][:].rearrange('p (a b) -> p a b', a=4)
                        P.mm(out, BmR[hs, d1, gl, :], AR[hs, :, gl, :], start=True, stop=False, reads=['Bm', 'Am'], writes=[('ps', pp)])
                        P.mm(out, BmI[hs, d1, gl, :], AIn[hs, :, gl, :], start=False, stop=True, reads=['Bm', 'Am'], writes=[('ps', pp)])
                    ts_ = (gl * 4 + d1) % 2
                    P.tt('dve', t1[ts_][:], G.PS[pa][:], cmask[:, 0, d1].rearrange('p a b -> p (a b)'), ALU.mult,
                         reads=[('ps', pa), 'cmask'], writes=[('t1', ts_)])
                    if d1 == 0:
                        P.stt(t1[ts_][:, 256:384], identf[:], DP[:, g0 + gl:g0 + gl + 1], t1[ts_][:, 256:384], ALU.mult, ALU.add,
                              reads=[('t1', ts_), 'ident', 's5DP'], writes=[('t1', ts_)])
                    P.tt('dve', t2[ts_][:], G.PS[pb][:], cmask[:, 1, d1].rearrange('p a b -> p (a b)'), ALU.mult,
                         reads=[('ps', pb), 'cmask'], writes=[('t2', ts_)])
                    P.tt('pool', Tout[:, gl, d1 * 4:(d1 + 1) * 4, :].rearrange('p a b -> p (a b)'), t1[ts_][:], t2[ts_][:], ALU.add,
                         reads=[('t1', ts_), ('t2', ts_)], writes=['Tout'])
            P.dma('sp', S['WT', l][gs].rearrange('g p a b -> p g a b'), Tout[:], 'Toutst', reads=['Tout'])
            for Ip in range(8):
                cmul(o4(WSP[:, 0]), o4(WSP[:, 1]), tabv(0, 126 - Ip, -8, 16), tabv(1, 126 - Ip, -8, 16), vecv(BS, 0), vecv(BS, 1),
                     [128, GB, 8, 16], ['tab', 'BS'], ['WSP'])
                for c in range(2):
                    for g4 in range(GB // 4):
                        pp = psi % 8
                        psi += 1
                        for gq in range(4):
                            gl = g4 * 4 + gq
                            P.tr(G.PS[pp][:, gq * 128:(gq + 1) * 128], WSP[:, c, gl, :], identf[:], reads=['WSP', 'ident'], writes=[('ps', pp)])
                        P.copy('act', WSst[:, g4 * 4:(g4 + 1) * 4, c, Ip, :], G.PS[pp][:].rearrange('p (a b) -> p a b', a=4),
                               reads=[('ps', pp)], writes=['WSst'])
            P.dma('sp', S['WS', l][gs].rearrange('g p c i q -> p g c i q'), WSst[:], 'WSstst', reads=['WSst'])
            for Jp in range(8):
                cmul(o4(WOst[:, :, 0, Jp, :]), o4(WOst[:, :, 1, Jp, :]), tabv(0, 64 + Jp, 8, 16), tabv(1, 64 + Jp, 8, 16),
                     vecv(CO, 0), vecv(CO, 1), [128, GB, 8, 16], ['tab', 'CO', 'CON'], ['WOst'], nb=(vecv(CON, 0), vecv(CON, 1)))
            P.dma('sp', S['WO', l][gs].rearrange('g p c j q -> p g c j q'), WOst[:], 'WOstst', reads=['WOst'])
        P.flush()


def tslot(delta):
    d1 = delta % 4
    d2 = (delta - d1) // 4
    return d1 * 4 + d2 + 2


def stage_2(G, l):
    nc, P, I, S = G.nc, G.P, G.I, G.S
    last = (l == DEPTH - 1)
    NS = 68
    import os
    CUT = int(os.environ.get('S2CUT', '9'))
    with ExitStack() as es:
        def sb(name, shape, dt=F32):
            return es.enter_context(G.sbt(name, shape, dt))
        UTl = sb('UTl', [128, 64, 512], BF16)
        UTc = sb('UTc', [128, 64, 4, 8], BF16)
        X2 = sb('X2', [128, 2, 64, NS])
        Hb = sb('Hb', [128, 2, 64, NS], BF16)
        Lt = sb('Lt', [128, 2, 64])
        L1s = sb('L1s', [128, 2, 64])
        L2s = sb('L2s', [128, 2, 64])
        tA = sb('tA', [128, 2, 64])
        tB = sb('tB', [128, 2, 64])
        WSg = [sb('WSg%d' % i, [128, 2, 8, 128], BF16) for i in range(2)]
        WTg = [sb('WTg%d' % i, [128, 16, 128], BF16) for i in range(2)]
        WOg = [sb('WOg%d' % i, [128, 2, 8, 128], BF16) for i in range(2)]
        Yst = [sb('Yst%d' % i, [128, 8, 512], BF16) for i in range(2)]
        Yc = sb('Yc', [128, 64, 4, 8], BF16)

        P.dma('sp', Lt[:], S['L64', l], 'Lt', writes=['Lt'])
        P.copy('dve', L1s[:, 0, :], Lt[:, 0, :], reads=['Lt'], writes=['L1s'])
        P.copy('dve', L1s[:, 1, :], Lt[:, 0, :], reads=['Lt'], writes=['L1s'])
        P.ts('dve', L2s[:, 0, :], Lt[:, 1, :], -1.0, None, ALU.mult, reads=['Lt'], writes=['L2s'])
        P.copy('dve', L2s[:, 1, :], Lt[:, 1, :], reads=['Lt'], writes=['L2s'])
        P.memset('pool', Hb[0:64, :, :, 64:65], 0.0, writes=['Hb0'])
        P.memset('pool', Hb[64:128, :, :, 67:68], 0.0, writes=['Hb0'])
        psi = 0
        wi = 0
        for b in range(NB if CUT >= 9 else 1):
            for i in range(8):
                P.dma('sp', UTl[i * 16:(i + 1) * 16, :, :],
                      S['s5u', b][:, CTX + 512 * i:CTX + 512 * (i + 1)].rearrange('(g h) t -> h g t', h=16),
                      ('UTl', i), writes=['UTl'])
                P.dma('sp', UTc[i * 16:(i + 1) * 16, :, :, :].rearrange('p g c j -> p g (c j)'),
                      S['s5uc', b][:, 32 * i:32 * (i + 1)].rearrange('(g h) t -> h g t', h=16),
                      ('UTc', i), writes=['UTc'])
            if CUT < 2:
                continue
            G7 = 7
            for g0 in range(0, 64, G7):
                gn = min(G7, 64 - g0)
                pbank = []
                for c in range(2):
                    pbank.append(psi % 8)
                    psi += 1
                for gl in range(gn):
                    g = g0 + gl
                    ws = wi % 2
                    wi += 1
                    P.dma('sp', WSg[ws][:], S['WS', l][g], ('WSg', ws), writes=[('WSg', ws)])
                    for c in range(2):
                        ps = G.PS[pbank[c]]
                        for Ip in range(8):
                            P.mm(ps[:, gl * NS:gl * NS + 64], WSg[ws][:, c, Ip, :], UTl[:, g, Ip * 64:(Ip + 1) * 64], start=(Ip == 0), stop=(Ip == 7),
                                 reads=[('WSg', ws), 'UTl'], writes=[('ps', pbank[c])])
                        for Ip in range(8):
                            P.mm(ps[:, gl * NS + 64:gl * NS + 68], WSg[ws][:, c, Ip, :], UTc[:, g, :, Ip], start=(Ip == 0), stop=(Ip == 7),
                                 reads=[('WSg', ws), 'UTc'], writes=[('ps', pbank[c])])
                for c in range(2):
                    ps = G.PS[pbank[c]]
                    pv = ps[:, 0:gn * NS].rearrange('p (g s) -> p g s', s=NS)
                    eng = 'dve'
                    P.copy(eng, X2[0:64, c, g0:g0 + gn, 0:4], pv[0:64, :, 64:68], reads=[('ps', pbank[c])], writes=['X2'])
                    P.copy(eng, X2[0:64, c, g0:g0 + gn, 4:68], pv[0:64, :, 0:64], reads=[('ps', pbank[c])], writes=['X2'])
                    P.copy(eng, X2[64:128, c, g0:g0 + gn, 0:4], pv[64:128, :, 67:63:-1], reads=[('ps', pbank[c])], writes=['X2'])
                    P.copy(eng, X2[64:128, c, g0:g0 + gn, 4:68], pv[64:128, :, 63::-1], reads=[('ps', pbank[c])], writes=['X2'])
            if CUT < 3:
                continue
            for s_ in range(1, NS):
                prev = X2[:, :, :, s_ - 1]
                prev_sw = X2[:, ::-1, :, s_ - 1]
                cur = X2[:, :, :, s_]
                P.tt('dve', tA[:], L1s[:], prev, ALU.mult, reads=['L1s', 'X2'], writes=['tA'])
                P.tt('dve', tB[:], L2s[:], prev_sw, ALU.mult, reads=['L2s', 'X2'], writes=['tB'])
                P.tt('dve', cur, cur, tA[:], ALU.add, reads=['X2', 'tA'], writes=['X2'])
                P.tt('dve', cur, cur, tB[:], ALU.add, reads=['X2', 'tB'], writes=['X2'])
            for c in range(2):
                P.copy('dve', Hb[0:64, c, :, 0:64], X2[0:64, c, :, 3:67], reads=['X2'], writes=['Hb'])
                P.copy('dve', Hb[0:64, c, :, 65:68], X2[0:64, c, :, 0:3], reads=['X2'], writes=['Hb'])
                P.copy('dve', Hb[64:128, c, :, 0:64], X2[64:128, c, :, 66:2:-1], reads=['X2'], writes=['Hb'])
                P.copy('dve', Hb[64:128, c, :, 64:67], X2[64:128, c, :, 2::-1], reads=['X2'], writes=['Hb'])
            if CUT < 4:
                continue
            for g in range(64 if CUT >= 5 else 8):
                ws = wi % 2
                wi += 1
                P.dma('sp', WTg[ws][:], S['WT', l][g], ('WTg', ws), writes=[('WTg', ws)])
                P.dma('sp', WOg[ws][:], S['WO', l][g], ('WOg', ws), writes=[('WOg', ws)])
                pb = psi % 8
                psi += 1
                ps = G.PS[pb]
                rk = [('WTg', ws), ('WOg', ws), 'UTl', 'UTc', 'Hb', 'Hb0']
                for Jp in range(8):
                    out = ps[:, Jp * 64:(Jp + 1) * 64]
                    for Ip in range(8):
                        P.mm(out, WTg[ws][:, tslot(Jp - Ip), :], UTl[:, g, Ip * 64:(Ip + 1) * 64], start=(Ip == 0), stop=False,
                             reads=rk, writes=[('ps', pb)])
                    P.mm(out, WOg[ws][:, 0, Jp, :], Hb[:, 0, g, 0:64], start=False, stop=False, reads=rk, writes=[('ps', pb)])
                    P.mm(out, WOg[ws][:, 1, Jp, :], Hb[:, 1, g, 0:64], start=False, stop=True, reads=rk, writes=[('ps', pb)])
                ys = (g // 8) % 2
                P.act(Yst[ys][:, g % 8, :], ps[:], AF.Gelu_apprx_tanh, reads=[('ps', pb)], writes=[('Yst', ys)])
                if not last:
                    pc = psi % 8
                    psi += 1
                    psc = G.PS[pc]
                    for Jp in range(8):
                        out = psc[:, Jp * 4:(Jp + 1) * 4]
                        for Ip in range(8):
                            P.mm(out, WTg[ws][:, tslot(Jp - Ip), :], UTc[:, g, :, Ip], start=(Ip == 0), stop=False, reads=rk, writes=[('ps', pc)])
                        P.mm(out, WOg[ws][:, 0, Jp, :], Hb[:, 0, g, 64:68], start=False, stop=False, reads=rk, writes=[('ps', pc)])
                        P.mm(out, WOg[ws][:, 1, Jp, :], Hb[:, 1, g, 64:68], start=False, stop=True, reads=rk, writes=[('ps', pc)])
                    P.act(Yc[:, g, :, :], psc[:, 0:32].rearrange('p (j c) -> p c j', c=4), AF.Gelu_apprx_tanh, reads=[('ps', pc)], writes=['Yc'])
                if g % 8 == 7:
                    g8 = g - 7
                    for j in range(8):
                        P.dma('sp', S['y5', b][g8 * 16:(g8 + 8) * 16, CTX + 512 * j:CTX + 512 * (j + 1)].rearrange('(g h) t -> h g t', h=16),
                              Yst[ys][j * 16:(j + 1) * 16, :, :], ('Yst', ys), reads=[('Yst', ys)])
            if not last:
                for j in range(8):
                    P.dma('sp', S['y5', b][:, 32 * j:32 * (j + 1)].rearrange('(g h) t -> h g t', h=16),
                          Yc[j * 16:(j + 1) * 16, :, :, :].rearrange('p g c j -> p g (c j)'), 'Ycst', reads=['Yc'])
        P.flush()


def ln_block(G, l, P, T, Xres, n, nslot, tiles, gname, bname, psi_ref, rk_extra):
    Rb, SQb, MEAN, M2, VAR, RSTD, ONESB = tiles
    tk = ('T32', nslot)
    for oc in range(8):
        P.copy('pool', Rb[:, oc, 0:n], T[:, oc, 0:n], reads=[tk], writes=['Rb'])
        P.act(SQb[:, oc, 0:n], T[:, oc, 0:n], AF.Square, reads=[tk], writes=['SQb'])
    pm = psi_ref[0] % 8
    pq = (psi_ref[0] + 1) % 8
    psi_ref[0] += 2
    for oc in range(8):
        P.mm(G.PS[pm][:, 0:n], ONESB[:], Rb[:, oc, 0:n], start=(oc == 0), stop=(oc == 7), reads=['Rb', 'ONESB'], writes=[('ps', pm)])
    for oc in range(8):
        P.mm(G.PS[pq][:, 0:n], ONESB[:], SQb[:, oc, 0:n], start=(oc == 0), stop=(oc == 7), reads=['SQb', 'ONESB'], writes=[('ps', pq)])
    P.copy('act', MEAN[:, 0:n], G.PS[pm][:, 0:n], reads=[('ps', pm)], writes=['MEAN'])
    P.tt('pool', M2[:, 0:n], MEAN[:, 0:n], MEAN[:, 0:n], ALU.mult, reads=['MEAN'], writes=['M2'])
    P.tt('dve', VAR[:, 0:n], G.PS[pq][:, 0:n], M2[:, 0:n], ALU.subtract, reads=[('ps', pq), 'M2'], writes=['VAR'])
    P.act(VAR[:, 0:n], VAR[:, 0:n], AF.Sqrt, bias=G.EPS[:], reads=['VAR'], writes=['VAR'])
    P.add('dve', lambda e, o=RSTD[:, 0:n], i=VAR[:, 0:n]: e.reciprocal(out=o, in_=i), reads=['VAR'], writes=['RSTD'])
    for oc in range(8):
        P.tt('pool', T[:, oc, 0:n], T[:, oc, 0:n], MEAN[:, 0:n], ALU.subtract, reads=[tk, 'MEAN'], writes=[tk])
        P.tt('dve', T[:, oc, 0:n], T[:, oc, 0:n], RSTD[:, 0:n], ALU.mult, reads=[tk, 'RSTD'], writes=[tk])
        P.ts('dve', T[:, oc, 0:n], T[:, oc, 0:n], vec(G, l, gname, oc), vec(G, l, bname, oc), ALU.mult, ALU.add, reads=[tk], writes=[tk])


def stage_3a(G, l):
    nc, P, I, S = G.nc, G.P, G.I, G.S
    last = (l == DEPTH - 1)
    with ExitStack() as es:
        def sb(name, shape, dt=F32):
            return es.enter_context(G.sbt(name, shape, dt))
        GLW = sb('GLW', [128, 8, D], BF16)
        WO_ = sb('WO_', [128, 16, D], BF16)
        ONESB = sb('ONESB', [128, 128], BF16)
        Y5 = [sb('Y5_%d' % i, [128, 8, 512], BF16) for i in range(2)]
        Y5n = sb('Y5n', [128, 8, CTX], BF16)
        RGt = [sb('RGt%d' % i, [128, 8, 512], BF16) for i in range(2)]
        Xin = [sb('Xin%d' % i, [128, 8, 512]) for i in range(2)]
        S5o = sb('S5o', [128, 8, 512], BF16)
        T32 = [sb('T32_%d' % i, [128, 8, 512]) for i in range(2)]
        Rb = sb('Rb', [128, 8, 512], BF16)
        SQb = sb('SQb', [128, 8, 512], BF16)
        sig = [sb('sig%d' % i, [128, 512]) for i in range(2)]
        MEAN = sb('MEAN', [128, 512])
        M2 = sb('M2', [128, 512])
        VAR = sb('VAR', [128, 512])
        RSTD = sb('RSTD', [128, 512])
        tiles = (Rb, SQb, MEAN, M2, VAR, RSTD, ONESB)
        for k in range(8):
            P.dma('pool', GLW[:, k, :], I['glu_w'][l, k * 128:(k + 1) * 128, :], 'GLWd', writes=['GLW'])
        for k in range(16):
            P.dma('pool', WO_[:, k, :], I['w_out'][l, k * 128:(k + 1) * 128, :], 'WO_d', writes=['WO_'])
        P.memset('dve', ONESB[:], 1.0 / D, writes=['ONESB'])
        psi = [0]
        sgc = [0]
        blocks = []
        for b in range(NB):
            for (kind, t0, n) in seg_list():
                if kind == 'ctx' and last:
                    continue
                blocks.append((b, kind, t0, n))

        def phA(i):
            b, kind, t0, n = blocks[i]
            slot = i % 2
            xsrc = (I['cxT'] if kind == 'ctx' else I['xT']) if l == 0 else S['x', kind]
            off = t0 if kind == 'ctx' else CTX + t0
            P.dma('sp', Y5[slot][:, :, 0:n], S['y5', b][:, off:off + n].rearrange('(k p) t -> p k t', p=128), ('Y5', slot), writes=[('Y5', slot)])
            P.dma('sp', RGt[slot][:, :, 0:n], S['rg', b][:, off:off + n].rearrange('(k p) t -> p k t', p=128), ('RGt', slot), writes=[('RGt', slot)])
            P.dma('sp', Xin[slot][:, :, 0:n], xsrc[b, :, t0:t0 + n].rearrange('(k p) t -> p k t', p=128), ('Xin', slot), writes=[('Xin', slot)])
            if kind == 'ctx':
                for k in range(8):
                    for cc in range(4):
                        P.copy('pool', Y5n[:, k, cc * 64:(cc + 1) * 64].rearrange('p (j q) -> p j q', j=8),
                               Y5[slot][:, k, 0:CTX].rearrange('p (j c q) -> p j c q', j=8, c=4)[:, :, cc, :],
                               reads=[('Y5', slot)], writes=['Y5n'])

        def phB(i):
            b, kind, t0, n = blocks[i]
            slot = i % 2
            if kind == 'ctx':
                ysrc, yk = Y5n, 'Y5n'
            else:
                ysrc, yk = Y5[slot], ('Y5', slot)
            for oc in range(8):
                pb = psi[0] % 8
                psi[0] += 1
                for k in range(8):
                    P.mm(G.PS[pb][:, 0:n], GLW[:, k, oc * 128:(oc + 1) * 128], ysrc[:, k, 0:n], start=(k == 0), stop=(k == 7),
                         reads=['GLW', yk], writes=[('ps', pb)])
                ss = sgc[0] % 2
                sgc[0] += 1
                P.act(sig[ss][:, 0:n], G.PS[pb][:, 0:n], AF.Sigmoid, bias=vec(G, l, 'glu_b', oc), reads=[('ps', pb)], writes=[('sig', ss)])
                P.tt('dve', S5o[:, oc, 0:n], ysrc[:, oc, 0:n], sig[ss][:, 0:n], ALU.mult, reads=[yk, ('sig', ss)], writes=['S5o'])

        def phC(i):
            b, kind, t0, n = blocks[i]
            slot = i % 2
            m = 2 if kind == 'ctx' else b
            T = T32[slot]
            for oc in range(8):
                pb = psi[0] % 8
                psi[0] += 1
                for k in range(16):
                    rhs = RGt[slot][:, k, 0:n] if k < 8 else S5o[:, k - 8, 0:n]
                    P.mm(G.PS[pb][:, 0:n], WO_[:, k, oc * 128:(oc + 1) * 128], rhs, start=(k == 0), stop=(k == 15),
                         reads=['WO_', ('RGt', slot), 'S5o'], writes=[('ps', pb)])
                P.ts('dve', T[:, oc, 0:n], G.PS[pb][:, 0:n], vec(G, l, 'b_out', oc), modv(G, l, 2, oc, m), ALU.add, ALU.mult,
                     reads=[('ps', pb), 'MOD'], writes=[('T32', slot)])
                P.stt(T[:, oc, 0:n], Xin[slot][:, oc, 0:n], ALPHA, T[:, oc, 0:n], ALU.mult, ALU.add,
                      reads=[('Xin', slot), ('T32', slot)], writes=[('T32', slot)])

        def phD(i):
            b, kind, t0, n = blocks[i]
            slot = i % 2
            T = T32[slot]
            ln_block(G, l, P, T, None, n, slot, tiles, 'ln1_g', 'ln1_b', psi, None)
            P.dma('sp', S['x1', kind][b, :, t0:t0 + n].rearrange('(k p) t -> p k t', p=128), T[:, :, 0:n], ('T32', slot),
                  reads=[('T32', slot)])

        nb_ = len(blocks)
        phA(0)
        for i in range(nb_):
            if i + 1 < nb_:
                phA(i + 1)
            phB(i)
            if i > 0 and OPT_3:
                phD(i - 1)
            phC(i)
            if not OPT_3:
                phD(i)
        if OPT_3:
            phD(nb_ - 1)
        P.flush()


def stage_3b(G, l):
    nc, P, I, S = G.nc, G.P, G.I, G.S
    last = (l == DEPTH - 1)
    NBLK = 256
    with ExitStack() as es:
        def sb(name, shape, dt=F32):
            return es.enter_context(G.sbt(name, shape, dt))
        W1b = sb('W1b', [128, 8, 4 * D], BF16)
        W2b = sb('W2b', [128, 32, D], BF16)
        ONESB = sb('ONESB', [128, 128], BF16)
        X1 = [sb('X1_%d' % i, [128, 8, NBLK]) for i in range(2)]
        X1m = sb('X1m', [128, 8, NBLK], BF16)
        Hh = sb('Hh', [128, 32, NBLK], BF16)
        hr = [sb('hr%d' % i, [128, NBLK], BF16) for i in range(2)]
        T32 = [sb('T32_%d' % i, [128, 8, NBLK]) for i in range(2)]
        Rb = sb('Rb', [128, 8, NBLK], BF16)
        SQb = sb('SQb', [128, 8, NBLK], BF16)
        MEAN = sb('MEAN', [128, NBLK])
        M2 = sb('M2', [128, NBLK])
        VAR = sb('VAR', [128, NBLK])
        RSTD = sb('RSTD', [128, NBLK])
        tiles = (Rb, SQb, MEAN, M2, VAR, RSTD, ONESB)
        for k in range(8):
            for h in range(2):
                P.dma('pool', W1b[:, k, h * 2048:(h + 1) * 2048], I['w1'][l, k * 128:(k + 1) * 128, h * 2048:(h + 1) * 2048],
                      'W1bd', writes=['W1b'])
        for k in range(32):
            P.dma('pool', W2b[:, k, :], I['w2'][l, k * 128:(k + 1) * 128, :], 'W2bd', writes=['W2b'])
        P.memset('dve', ONESB[:], 1.0 / D, writes=['ONESB'])
        psi = [0]
        hsc = [0]
        blocks = []
        for b in range(NB):
            segs = [] if last else [('ctx', 0, CTX)]
            segs += [('lat', t * NBLK, NBLK) for t in range(SEQ // NBLK)]
            for (kind, t0, n) in segs:
                blocks.append((b, kind, t0, n))

        def phA(i):
            b, kind, t0, n = blocks[i]
            slot = i % 2
            P.dma('sp', X1[slot][:], S['x1', kind][b, :, t0:t0 + n].rearrange('(k p) t -> p k t', p=128), ('X1', slot), writes=[('X1', slot)])

        def phB(i):
            b, kind, t0, n = blocks[i]
            slot = i % 2
            m = 2 if kind == 'ctx' else b
            for k in range(8):
                P.ts('dve', X1m[:, k, :], X1[slot][:, k, :], G.MP1[:, l, 1, k, m:m + 1], modv(G, l, 3, k, m), ALU.mult, ALU.add,
                     reads=[('X1', slot), 'MP1', 'MOD'], writes=['X1m'])
            for hc in range(32):
                pb = psi[0] % 8
                psi[0] += 1
                for k in range(8):
                    P.mm(G.PS[pb][:, 0:n], W1b[:, k, hc * 128:(hc + 1) * 128], X1m[:, k, :], start=(k == 0), stop=(k == 7),
                         reads=['W1b', 'X1m'], writes=[('ps', pb)])
                h2 = hsc[0] % 2
                hsc[0] += 1
                P.act(hr[h2][:], G.PS[pb][:, 0:n], AF.Relu, bias=vec(G, l, 'b1', hc), reads=[('ps', pb)], writes=[('hr', h2)])
                P.tt('pool', Hh[:, hc, :], hr[h2][:], hr[h2][:], ALU.mult, reads=[('hr', h2)], writes=['Hh'])

        def phC(i):
            b, kind, t0, n = blocks[i]
            slot = i % 2
            m = 2 if kind == 'ctx' else b
            T = T32[slot]
            for oc in range(8):
                pb = psi[0] % 8
                psi[0] += 1
                for k in range(32):
                    P.mm(G.PS[pb][:, 0:n], W2b[:, k, oc * 128:(oc + 1) * 128], Hh[:, k, :], start=(k == 0), stop=(k == 31),
                         reads=['W2b', 'Hh'], writes=[('ps', pb)])
                P.ts('dve', T[:, oc, :], G.PS[pb][:, 0:n], vec(G, l, 'b2', oc), modv(G, l, 5, oc, m), ALU.add, ALU.mult,
                     reads=[('ps', pb), 'MOD'], writes=[('T32', slot)])
                P.stt(T[:, oc, :], X1[slot][:, oc, :], ALPHA, T[:, oc, :], ALU.mult, ALU.add,
                      reads=[('X1', slot), ('T32', slot)], writes=[('T32', slot)])

        def phD(i):
            b, kind, t0, n = blocks[i]
            slot = i % 2
            T = T32[slot]
            ln_block(G, l, P, T, None, n, slot, tiles, 'ln2_g', 'ln2_b', psi, None)
            if last:
                dst = G.yT[b, :, t0:t0 + n]
            else:
                dst = S['x', kind][b, :, t0:t0 + n]
            P.dma('sp', dst.rearrange('(k p) t -> p k t', p=128), T[:, :, 0:n], ('T32', slot), reads=[('T32', slot)])

        nb_ = len(blocks)
        phA(0)
        for i in range(nb_):
            if i + 1 < nb_:
                phA(i + 1)
            phB(i)
            if i > 0 and OPT_3:
                phD(i - 1)
            phC(i)
            if not OPT_3:
                phD(i)
        if OPT_3:
            phD(nb_ - 1)
        P.flush()


def rev(ap):
    return ap[:, ::-1]


def stage_1b(G, l):
    nc, P, I, S = G.nc, G.P, G.I, G.S
    NT = CTX + SEQ
    with ExitStack() as es:
        def sb(name, shape, dt):
            return es.enter_context(G.sbt(name, shape, dt))
        identf = sb('identf', [128, 128], F32)
        DG = sb('DG', [128, 8, 4, 128], BF16)
        GW = sb('GW', [128, 2, 2, 8, 128], BF16)
        cneg = sb('cneg', [128, 2, 8], F32)
        ctmp = sb('ctmp', [128, 2, 8], F32)
        RP = [sb('RP%d' % i, [128, NT + 6], BF16) for i in range(2)]
        GGt = [sb('GGt%d' % i, [128, NT], BF16) for i in range(2)]
        OUT = [sb('OUT%d' % i, [128, NT], BF16) for i in range(1)]
        A = [sb('A%d' % i, [128, NT], F32) for i in range(2)]
        Bt = [sb('B%d' % i, [128, NT], F32) for i in range(2)]
        HF = sb('HF', [128, NT], F32)
        XC32 = [sb('XC32_%d' % i, [128, 512], F32) for i in range(2)]
        XCB = [sb('XCB%d' % i, [128, 512], BF16) for i in range(2)]
        Rt = [sb('Rt%d' % i, [128, 512], F32) for i in range(2)]
        It = [sb('It%d' % i, [128, 512], F32) for i in range(2)]
        Tt = [sb('Tt%d' % i, [128, 512], F32) for i in range(2)]
        Mt = [sb('Mt%d' % i, [128, 512], F32) for i in range(2)]
        Ut = [sb('Ut%d' % i, [128, 512], F32) for i in range(2)]

        P.dma('sp', identf[:], I['ident'], 'identf', writes=['identf'])
        for w, nm in enumerate(('rg_wa', 'rg_wi')):
            for d in range(2):
                P.dma('pool', GW[:, w, d, :, :], I[nm][l, d].rearrange('h i j -> i h j'), 'GWd', writes=['GW'])
        for fc in range(8):
            for tap in range(4):
                P.ts('dve', DG[:, fc, tap, :], identf[:], vec(G, l, 'conv_w%d' % tap, fc), None, ALU.mult,
                     reads=['identf'], writes=['DG'])
        for d in range(2):
            o = VOFF['lam%d' % d]
            P.act(ctmp[:, d, :], G.VEC[:, l, o:o + 8], AF.Exp, scale=-1.0, writes=['ctmp'])
        P.act(cneg[:], ctmp[:], AF.Ln, bias=G.ONE[:], reads=['ctmp'], writes=['cneg0'])
        P.ts('dve', cneg[:], cneg[:], -RG_C, None, ALU.mult, reads=['cneg0'], writes=['cneg'])
        for i in range(2):
            P.memset('pool', RP[i][:], 0.0, writes=[('RP', i)])
        segs = [(0, CTX, 1)] + [(CTX + t * 512, 512, 4 + CTX + t * 512) for t in range(SEQ // 512)]
        it = 0
        bi = 0
        psi = 0
        for b in range(NB):
            for fc in range(8):
                slot = it % 2
                it += 1
                rows = slice(fc * 128, (fc + 1) * 128)
                P.dma('sp', RP[slot][:, 1:1 + CTX], S['rgx', b][rows, 0:CTX], ('RPc', slot), writes=[('RP', slot)])
                P.dma('sp', RP[slot][:, 4 + CTX:4 + CTX + SEQ], S['rgx', b][rows, CTX:NT], ('RPl', slot), writes=[('RP', slot)])
                P.dma('sp', GGt[slot][:], S['gg', b][rows, :], ('GGt', slot), writes=[('GGt', slot)])
                for (off, n, pidx) in segs:
                    bs = bi % 2
                    bi += 1
                    pb = psi % 8
                    psi += 1
                    ps = G.PS[pb]
                    for tap in range(4):
                        P.mm(ps[:, 0:n], DG[:, fc, tap, :], RP[slot][:, pidx + tap - 1:pidx + tap - 1 + n], start=(tap == 0), stop=(tap == 3),
                             reads=['DG', ('RP', slot)], writes=[('ps', pb)])
                    P.act(XC32[bs][:, 0:n], ps[:, 0:n], AF.Identity, bias=vec(G, l, 'conv_b', fc), reads=[('ps', pb)], writes=[('XC32', bs)])
                    P.copy('dve' if OPT_1B else 'pool', XCB[bs][:, 0:n], XC32[bs][:, 0:n], reads=[('XC32', bs)], writes=[('XCB', bs)])
                    for d in range(2):
                        pr = psi % 8
                        pi_ = (psi + 1) % 8
                        psi += 2
                        P.mm(G.PS[pr][:, 0:n], GW[:, 0, d, fc, :], XCB[bs][:, 0:n], start=True, stop=True,
                             reads=['GW', ('XCB', bs)], writes=[('ps', pr)])
                        P.mm(G.PS[pi_][:, 0:n], GW[:, 1, d, fc, :], XCB[bs][:, 0:n], start=True, stop=True,
                             reads=['GW', ('XCB', bs)], writes=[('ps', pi_)])
                        P.act(Rt[d][:, 0:n], G.PS[pr][:, 0:n], AF.Sigmoid, bias=vec(G, l, 'ba%d' % d, fc), reads=[('ps', pr)], writes=[('Rt', d)])
                        P.act(It[d][:, 0:n], G.PS[pi_][:, 0:n], AF.Sigmoid, bias=vec(G, l, 'bi%d' % d, fc), reads=[('ps', pi_)], writes=[('It', d)])
                        P.act(A[d][:, off:off + n], Rt[d][:, 0:n], AF.Exp, scale=cneg[:, d, fc:fc + 1], reads=[('Rt', d), 'cneg'], writes=[('A', d)])
                        P.tt('dve' if OPT_1B else 'pool', Tt[d][:, 0:n], A[d][:, off:off + n], A[d][:, off:off + n], ALU.mult, reads=[('A', d)], writes=[('Tt', d)])
                        P.act(Mt[d][:, 0:n], Tt[d][:, 0:n], AF.Sqrt, bias=G.ONE[:], scale=-1.0, reads=[('Tt', d)], writes=[('Mt', d)])
                        P.tt('dve', Ut[d][:, 0:n], It[d][:, 0:n], XC32[bs][:, 0:n], ALU.mult, reads=[('It', d), ('XC32', bs)], writes=[('Ut', d)])
                        P.tt('dve', Bt[d][:, off:off + n], Ut[d][:, 0:n], Mt[d][:, 0:n], ALU.mult, reads=[('Ut', d), ('Mt', d)], writes=[('B', d)])
                P.scan(HF[:], A[0][:], Bt[0][:], 0.0, reads=[('A', 0), ('B', 0)], writes=['HF'])
                HB = A[0]
                P.scan(rev(HB[:, 0:CTX]), rev(A[1][:, 0:CTX]), rev(Bt[1][:, 0:CTX]), 0.0, reads=[('A', 1), ('B', 1)], writes=[('A', 0)])
                P.scan(rev(HB[:, CTX:NT]), rev(A[1][:, CTX:NT]), rev(Bt[1][:, CTX:NT]), HB[:, 0:1], reads=[('A', 1), ('B', 1), ('A', 0)], writes=[('A', 0)])
                P.tt('dve' if OPT_1B else 'pool', HF[:], HF[:], HB[:], ALU.add, reads=['HF', ('A', 0)], writes=['HF'])
                P.tt('dve', OUT[0][:], HF[:], GGt[slot][:], ALU.mult, reads=['HF', ('GGt', slot)], writes=['OUT'])
                P.dma('sp', S['rg', b][rows, :], OUT[0][:], 'OUTst', reads=['OUT'])
        P.flush()


def pack_inputs(inp):
    f = lambda a: np.ascontiguousarray(np.asarray(a, dtype=np.float32))
    x = np.asarray(inp['x'], np.float32)
    ctx = np.asarray(inp['ctx'], np.float32)
    c = np.asarray(inp['c'], np.float32)
    c_ctx = np.asarray(inp['c_ctx'], np.float32)

    def pk(v):
        v = np.asarray(v, np.float32)
        return v.reshape(-1, 128).T

    vecs = np.zeros((DEPTH, 128, NV), np.float32)
    for l in range(DEPTH):
        cols = {'ln1_g': inp['ln1_g'][l], 'ln1_b': inp['ln1_b'][l], 'ln2_g': inp['ln2_g'][l], 'ln2_b': inp['ln2_b'][l],
                'conv_w0': inp['conv_w'][l][0], 'conv_w1': inp['conv_w'][l][1], 'conv_w2': inp['conv_w'][l][2],
                'conv_w3': inp['conv_w'][l][3], 'conv_b': inp['conv_b'][l], 'lam0': inp['rg_lambda'][l][0],
                'lam1': inp['rg_lambda'][l][1], 'ba0': inp['rg_ba'][l][0], 'ba1': inp['rg_ba'][l][1],
                'bi0': inp['rg_bi'][l][0], 'bi1': inp['rg_bi'][l][1], 'glu_b': inp['s5_glu_b'][l], 'b_out': inp['b_out'][l],
                'b2': inp['mlp_b2'][l], 'b1': inp['mlp_b1'][l], 's5d': inp['s5_d'][l]}
        for n, ncol in VEC_NAMES:
            vecs[l, :, VOFF[n]:VOFF[n] + ncol] = pk(cols[n])
    ada_bP = np.stack([pk(np.asarray(inp['ada_b'])[l]) for l in range(DEPTH)])
    shared = {
        'ada_w': f(inp['ada_w']), 'ada_bP': f(ada_bP), 'vecs': vecs, 'w_in': f(inp['w_in']),
        'rg_wa': f(inp['rg_wa']), 'rg_wi': f(inp['rg_wi']), 'glu_w': f(inp['s5_glu_w']), 'w_out': f(inp['w_out']),
        'w1': f(inp['mlp_w1']), 'w2': f(inp['mlp_w2']), 'ident': np.eye(128, dtype=np.float32),
    }
    A = lambda n: np.asarray(inp[n], np.float32)
    s5a = np.stack([A('s5_a_re'), A('s5_a_im')], axis=2)
    shared['s5a'] = f(s5a.transpose(0, 1, 4, 2, 3).reshape(DEPTH, 128, 2, 64))
    shared['s5ldt'] = f(np.broadcast_to(A('s5_log_dt')[:, :, None, :], (DEPTH, 2, 64, 64)).reshape(DEPTH, 128, 64))
    s5b = np.stack([A('s5_b_re'), A('s5_b_im')], axis=2)
    shared['s5B'] = f(s5b.transpose(0, 1, 4, 2, 3, 5).reshape(DEPTH, 128, 2, 64, 16))
    s5c = np.stack([A('s5_c_re'), A('s5_c_im')], axis=2)
    shared['s5C'] = f(s5c.transpose(0, 1, 5, 2, 3, 4).reshape(DEPTH, 128, 2, 64, 16))
    dgh = A('s5_d').reshape(DEPTH, 64, 16)
    shared['s5DP'] = f(np.broadcast_to(dgh.transpose(0, 2, 1)[:, None, :, :], (DEPTH, 8, 16, 64)).reshape(DEPTH, 128, 64))
    cm = np.zeros((128, 2, 4, 4, 128), np.float32)
    for i in range(8):
        for j in range(8):
            for d1 in range(4):
                for d2 in range(4):
                    lag = 8 * (j - i) + d1 + 4 * (d2 - 2)
                    if lag >= 0:
                        cm[i * 16:(i + 1) * 16, 0, d1, d2, j * 16:(j + 1) * 16] = 1.0
                    if lag <= 0:
                        cm[i * 16:(i + 1) * 16, 1, d1, d2, j * 16:(j + 1) * 16] = 1.0
    shared['cmask'] = cm
    cv = np.zeros((128, 4), np.float32)
    cv[:64, 0] = 1.0
    cv[64:, 0] = -1.0
    cv[64:, 1] = 63.0
    cv[64:, 2] = 65.0
    shared['cvec'] = cv
    shared['mvals'] = f(np.broadcast_to(np.arange(-63, 65, dtype=np.float32)[None, :], (128, 128)))
    maps = []
    for k in range(NCORES):
        bs = slice(k * NB, (k + 1) * NB)
        cT = np.zeros((128, 8, 4), np.float32)
        for m in range(NB):
            cT[:, :, m] = pk(c[k * NB + m])
        cT[:, :, 2] = pk(c_ctx)
        d = dict(shared)
        d['xT'] = np.ascontiguousarray(x[bs].transpose(0, 2, 1))
        d['cxT'] = np.ascontiguousarray(ctx[bs].transpose(0, 2, 1))
        d['cT'] = cT
        maps.append(d)
    return maps


_CACHE = {}


def kernel(**inputs):
    maps = pack_inputs(inputs)
    if 'nc' not in _CACHE:
        _CACHE['nc'] = build_program()[0]
    nc = _CACHE['nc']
    res = run_bass_kernel_spmd(nc, maps, core_ids=list(range(NCORES)))
    outs = [r['yT'] for r in res.results]
    y = np.concatenate(outs, axis=0)
    return np.ascontiguousarray(y.transpose(0, 2, 1)).astype(np.float32)
```

```python
import math
from contextlib import ExitStack
import numpy as np
import concourse.bass as bass
import concourse.mybir as mybir
from concourse.bass_utils import run_bass_kernel_spmd

F32 = mybir.dt.float32
BF16 = mybir.dt.bfloat16
AF = mybir.ActivationFunctionType
ALU = mybir.AluOpType

D = 1024
SEQ = 4096
CTX = 256
DEPTH = 2
NB = 2
ALPHA = (2.0 * DEPTH) ** 0.25
LN_EPS = 1e-5
RG_C = 8.0
NCORES = 8
OPT_PREP = False
OPT_1B = True
OPT_3 = True

VEC_NAMES = [('ln1_g', 8), ('ln1_b', 8), ('ln2_g', 8), ('ln2_b', 8), ('conv_w0', 8), ('conv_w1', 8),
             ('conv_w2', 8), ('conv_w3', 8), ('conv_b', 8), ('lam0', 8), ('lam1', 8), ('ba0', 8),
             ('ba1', 8), ('bi0', 8), ('bi1', 8), ('glu_b', 8), ('b_out', 8), ('b2', 8), ('b1', 32),
             ('s5d', 8)]
VOFF = {}
_o = 0
for _n, _c in VEC_NAMES:
    VOFF[_n] = _o
    _o += _c
NV = _o


class Op:
    __slots__ = ('eng', 'fn', 'deps', 'ddeps', 'signal', 'sem', 'val', 'is_dma')


class Prog:
    def __init__(self, nc, es):
        self.nc = nc
        self.engs = {'pe': nc.tensor, 'act': nc.scalar, 'dve': nc.vector, 'pool': nc.gpsimd, 'sp': nc.sync}
        self.sem = {e: es.enter_context(nc.semaphore('s_' + e)) for e in ('pe', 'act', 'dve', 'pool')}
        self.cnt = {e: 0 for e in self.sem}
        self.es = es
        self.dsem = {}
        self.dcum = {}
        self.nops = 0
        self.free = []
        self.free_sw = []
        self.dkind = {}
        self.nsem = 0
        self.barrier = []
        self._reset()

    def _reset(self):
        self.ops = {e: [] for e in self.engs}
        self.last_w = {}
        self.readers = {}

    def add(self, eng, fn, reads=(), writes=(), dma_key=None, signal=False):
        op = Op()
        op.eng = eng
        op.fn = fn
        op.signal = signal
        op.is_dma = dma_key is not None
        op.sem = None
        op.val = None
        deps = []
        for k in reads:
            w = self.last_w.get(k)
            if w is not None:
                deps.append(w)
        for k in writes:
            w = self.last_w.get(k)
            if w is not None:
                deps.append(w)
            deps.extend(self.readers.get(k, {}).values())
        cdeps = []
        ddeps = {}
        seen = set()
        for d in deps:
            if id(d) in seen:
                continue
            seen.add(id(d))
            if d.is_dma:
                ddeps[d.sem] = self.dcum[d.sem]
            else:
                if d.eng == 'pe' and eng == 'pe':
                    continue
                cdeps.append(d)
        op.deps = cdeps
        op.ddeps = ddeps
        if op.is_dma:
            if dma_key not in self.dsem:
                fl = self.free_sw if eng == 'pool' else self.free
                self.dkind[dma_key] = eng == 'pool'
                if fl:
                    h, c0 = fl.pop()
                else:
                    self.nsem += 1
                    h, c0 = self.es.enter_context(self.nc.semaphore('d_' + str(self.nsem))), 0
                self.dsem[dma_key] = h
                self.dcum[dma_key] = c0
            self.dcum[dma_key] += 16
            op.sem = dma_key
            op.val = self.dcum[dma_key]
            assert op.val < 60000, dma_key
        rk = ('d', dma_key) if op.is_dma else eng
        for k in reads:
            self.readers.setdefault(k, {})[rk] = op
        for k in writes:
            self.last_w[k] = op
            self.readers[k] = {}
        self.ops[eng].append(op)
        self.nops += 1
        return op

    def flush(self):
        nc = self.nc
        barrier = self.barrier
        for e, lst in self.ops.items():
            for op in lst:
                for d in op.deps:
                    d.signal = True
            for op in reversed(lst):
                if not op.is_dma and op.fn is not None:
                    op.signal = True
                    break
        for e, lst in self.ops.items():
            if e not in self.sem:
                continue
            for op in lst:
                if op.signal and not op.is_dma and op.fn is not None:
                    self.cnt[e] += 1
                    op.sem = e
                    op.val = self.cnt[e]
                    assert op.val < 60000
        ops = self.ops
        sem = self.sem
        dsem = self.dsem

        def emit(ename, eobj):
            waited = {}
            for s, v in barrier:
                eobj.wait_ge(s, v)
            for op in ops[ename]:
                for d in op.deps:
                    s = sem[d.sem]
                    if waited.get(d.sem, 0) < d.val:
                        eobj.wait_ge(s, d.val)
                        waited[d.sem] = d.val
                for k, v in op.ddeps.items():
                    if waited.get(k, 0) < v:
                        eobj.wait_ge(dsem[k], v)
                        waited[k] = v
                if op.fn is None:
                    continue
                ins = op.fn(eobj)
                if op.is_dma:
                    ins.then_inc(dsem[op.sem], 16)
                elif op.signal:
                    ins.then_inc(sem[op.sem], 1)

        with nc.Block() as block:
            @block.tensor
            def _(t):
                emit('pe', t)

            @block.scalar
            def _(s):
                emit('act', s)

            @block.vector
            def _(v):
                emit('dve', v)

            @block.gpsimd
            def _(g):
                emit('pool', g)

            @block.sync
            def _(sp):
                emit('sp', sp)
        self._reset()
        self.barrier = [(self.sem[e], self.cnt[e]) for e in self.sem if self.cnt[e] > 0]
        self.barrier += [(self.dsem[k], self.dcum[k]) for k in self.dsem if self.dcum[k] > 0]
        for k in self.dsem:
            (self.free_sw if self.dkind[k] else self.free).append((self.dsem[k], self.dcum[k]))
        self.dsem = {}
        self.dcum = {}
        self.dkind = {}

    def final_wait(self):
        nc = self.nc
        items = list(self.barrier)
        with nc.Block() as block:
            @block.sync
            def _(sp):
                for s, v in items:
                    sp.wait_ge(s, v)

    def dma(self, q, out, in_, key, reads=(), writes=(), **kw):
        return self.add(q, lambda e, o=out, i=in_, kw=kw: e.dma_start(out=o, in_=i, **kw), reads, writes, dma_key=key)

    def tr(self, out, in_, ident, reads=(), writes=()):
        return self.add('pe', lambda e, o=out, i=in_, d=ident: e.transpose(o, i, d), reads, writes)

    def mm(self, out, lhsT, rhs, start, stop, reads=(), writes=(), signal=False):
        return self.add('pe', lambda e, o=out, l=lhsT, r=rhs, s=start, t=stop: e.matmul(o, lhsT=l, rhs=r, start=s, stop=t),
                        reads, writes, signal=signal)

    def act(self, out, in_, func, bias=None, scale=1.0, reads=(), writes=()):
        if bias is None:
            f = lambda e, o=out, i=in_, fu=func, sc=scale: e.activation(out=o, in_=i, func=fu, scale=sc)
        else:
            f = lambda e, o=out, i=in_, fu=func, b=bias, sc=scale: e.activation(out=o, in_=i, func=fu, bias=b, scale=sc)
        return self.add('act', f, reads, writes)

    def ts(self, eng, out, in0, s1, s2, op0, op1=None, reads=(), writes=()):
        if op1 is None:
            f = lambda e, o=out, i=in0, a=s1, p0=op0: e.tensor_scalar(out=o, in0=i, scalar1=a, scalar2=None, op0=p0)
        else:
            f = lambda e, o=out, i=in0, a=s1, b=s2, p0=op0, p1=op1: e.tensor_scalar(out=o, in0=i, scalar1=a, scalar2=b, op0=p0, op1=p1)
        return self.add(eng, f, reads, writes)

    def tt(self, eng, out, in0, in1, op, reads=(), writes=()):
        return self.add(eng, lambda e, o=out, a=in0, b=in1, p=op: e.tensor_tensor(out=o, in0=a, in1=b, op=p), reads, writes)

    def stt(self, out, in0, scalar, in1, op0, op1, reads=(), writes=()):
        return self.add('dve', lambda e, o=out, a=in0, s=scalar, b=in1, p0=op0, p1=op1:
                        e.scalar_tensor_tensor(out=o, in0=a, scalar=s, in1=b, op0=p0, op1=p1), reads, writes)

    def copy(self, eng, out, in_, reads=(), writes=()):
        if eng == 'act':
            return self.add('act', lambda e, o=out, i=in_: e.activation(out=o, in_=i, func=AF.Copy), reads, writes)
        return self.add(eng, lambda e, o=out, i=in_: e.tensor_copy(out=o, in_=i), reads, writes)

    def scan(self, out, d0, d1, init, reads=(), writes=()):
        return self.add('dve', lambda e, o=out, a=d0, b=d1, i=init:
                        e.tensor_tensor_scan(out=o, data0=a, data1=b, initial=i, op0=ALU.mult, op1=ALU.add), reads, writes)

    def memset(self, eng, out, val, reads=(), writes=()):
        return self.add(eng, lambda e, o=out, v=val: e.memset(o, v), reads, writes)


class Ctx:
    pass


def build_program(debug=(), stages=None):
    nc = bass.Bass("TRN2", target_bir_lowering=False)
    G = Ctx()
    G.nc = nc
    G.uid = [0]

    def sbt(name, shape, dt):
        G.uid[0] += 1
        return nc.sbuf_tensor('%s_u%d' % (name, G.uid[0]), shape, dt)
    G.sbt = sbt
    G.debug = set(debug)
    ges = ExitStack()
    G.ges = ges
    P = Prog(nc, ges)
    G.P = P

    def din(name, shape, dt=F32):
        return nc.dram_tensor(name, list(shape), dt, kind="ExternalInput").ap()

    def dscr(name, shape, dt):
        kind = "ExternalOutput" if name in G.debug else "Internal"
        return nc.dram_tensor(name, list(shape), dt, kind=kind).ap()

    G.dscr = dscr
    I = {}
    I['xT'] = din('xT', [NB, D, SEQ])
    I['cxT'] = din('cxT', [NB, D, CTX])
    I['cT'] = din('cT', [128, 8, 4])
    I['ada_w'] = din('ada_w', [DEPTH, D, 6 * D])
    I['ada_bP'] = din('ada_bP', [DEPTH, 128, 48])
    I['vecs'] = din('vecs', [DEPTH, 128, NV])
    I['w_in'] = din('w_in', [DEPTH, D, 3 * D])
    I['rg_wa'] = din('rg_wa', [DEPTH, 2, 8, 128, 128])
    I['rg_wi'] = din('rg_wi', [DEPTH, 2, 8, 128, 128])
    I['glu_w'] = din('glu_w', [DEPTH, D, D])
    I['w_out'] = din('w_out', [DEPTH, 2 * D, D])
    I['w1'] = din('w1', [DEPTH, D, 4 * D])
    I['w2'] = din('w2', [DEPTH, 4 * D, D])
    I['ident'] = din('ident', [128, 128])
    I['s5a'] = din('s5a', [DEPTH, 128, 2, 64])
    I['s5ldt'] = din('s5ldt', [DEPTH, 128, 64])
    I['s5B'] = din('s5B', [DEPTH, 128, 2, 64, 16])
    I['s5C'] = din('s5C', [DEPTH, 128, 2, 64, 16])
    I['s5DP'] = din('s5DP', [DEPTH, 128, 64])
    I['cmask'] = din('cmask', [128, 2, 4, 4, 128])
    I['cvec'] = din('cvec', [128, 4])
    I['mvals'] = din('mvals', [128, 128])
    G.I = I
    G.yT = nc.dram_tensor('yT', [NB, D, SEQ], F32, kind="ExternalOutput").ap()

    G.PS = [ges.enter_context(nc.psum_tensor('ps%d' % i, [128, 512], F32)) for i in range(8)]
    G.MOD = ges.enter_context(G.sbt('MOD', [128, DEPTH, 48, 4], F32))
    G.MP1 = ges.enter_context(G.sbt('MP1', [128, DEPTH, 2, 8, 4], F32))
    G.VEC = ges.enter_context(G.sbt('VEC', [128, DEPTH, NV], F32))
    G.ONE = ges.enter_context(G.sbt('ONE', [128, 1], F32))
    G.EPS = ges.enter_context(G.sbt('EPS', [128, 1], F32))

    S = {}
    for b in range(NB):
        S['rgx', b] = dscr('rgx%d' % b, [D, CTX + SEQ], BF16)
        S['gg', b] = dscr('gg%d' % b, [D, CTX + SEQ], BF16)
        S['s5u', b] = dscr('s5u%d' % b, [D, CTX + SEQ], BF16)
        S['rg', b] = dscr('rg%d' % b, [D, CTX + SEQ], BF16)
        S['s5uc', b] = dscr('s5uc%d' % b, [D, CTX], BF16)
        S['y5', b] = dscr('y5_%d' % b, [D, CTX + SEQ], BF16)
    for kind, n in (('lat', SEQ), ('ctx', CTX)):
        S['x', kind] = dscr('xs_' + kind, [NB, D, n], F32)
        S['x1', kind] = dscr('x1_' + kind, [NB, D, n], F32)
    for l in range(DEPTH):
        S['WT', l] = dscr('WT%d' % l, [64, 128, 16, 128], BF16)
        S['WS', l] = dscr('WS%d' % l, [64, 128, 2, 8, 128], BF16)
        S['WO', l] = dscr('WO%d' % l, [64, 128, 2, 8, 128], BF16)
        S['L64', l] = dscr('L64_%d' % l, [128, 2, 64], F32)
    G.S = S

    G.stages = stages
    run_stages(G)
    P.final_wait()
    return nc, G


def run_stages(G):
    st = G.stages

    def on(name):
        return st is None or name in st
    if on('mod'):
        stage_mod(G)
    if 'moddbg' in G.debug:
        d = G.dscr('moddbg', [128, DEPTH * 48 * 4], F32)
        G.P.dma('sp', d, G.MOD[:].rearrange('p l c m -> p (l c m)'), 'moddbg', reads=['MOD'])
        G.P.flush()
    for l in range(DEPTH):
        if on('prep%d' % l):
            stage_s5prep(G, l)
        if on('1a%d' % l):
            stage_1a(G, l)
        if on('1b%d' % l):
            stage_1b(G, l)
        if on('2_%d' % l):
            stage_2(G, l)
        if on('3a%d' % l):
            stage_3a(G, l)
        if on('3b%d' % l):
            stage_3b(G, l)


def vec(G, l, name, k=0):
    c = VOFF[name] + k
    return G.VEC[:, l, c:c + 1]


def modv(G, l, j, k, m):
    return G.MOD[:, l, j * 8 + k, m:m + 1]


def stage_mod(G):
    nc, P, I = G.nc, G.P, G.I
    with ExitStack() as es:
        cin = es.enter_context(G.sbt('cin', [128, 8, 4], F32))
        sc = es.enter_context(G.sbt('sc', [128, 8, 4], F32))
        abp = es.enter_context(G.sbt('abp', [128, DEPTH, 48], F32))
        wt = [es.enter_context(G.sbt('adaw%d' % i, [128, 8, 512], F32)) for i in range(2)]
        P.dma('sp', cin[:], I['cT'], 'cin', writes=['cin'])
        P.dma('sp', abp[:], I['ada_bP'].rearrange('l p c -> p l c'), 'abp', writes=['abp'])
        P.dma('sp', G.VEC[:], I['vecs'].rearrange('l p c -> p l c'), 'VEC', writes=['VEC'])
        P.memset('dve', G.ONE[:], 1.0, writes=['ONE'])
        P.memset('dve', G.EPS[:], LN_EPS, writes=['EPS'])
        P.act(sc[:], cin[:], AF.Silu, reads=['cin'], writes=['sc'])
        it = 0
        for l in range(DEPTH):
            for c4 in range(12):
                slot = it % 2
                it += 1
                P.dma('sp', wt[slot][:], I['ada_w'][l, :, c4 * 512:(c4 + 1) * 512].rearrange('(k p) c -> p k c', p=128),
                      ('adaw', slot), writes=[('adaw', slot)])
                ps = G.PS[c4 % 2]
                for cj in range(4):
                    cc = c4 * 4 + cj
                    for k in range(8):
                        P.mm(ps[:, cj * 4:cj * 4 + 4], wt[slot][:, k, cj * 128:(cj + 1) * 128], sc[:, k, :],
                             start=(k == 0), stop=(k == 7), reads=[('adaw', slot), 'sc'], writes=[('ps', c4 % 2)])
                for cj in range(4):
                    cc = c4 * 4 + cj
                    P.ts('dve', G.MOD[:, l, cc, :], ps[:, cj * 4:cj * 4 + 4], abp[:, l, cc:cc + 1], None, ALU.add,
                         reads=[('ps', c4 % 2), 'abp'], writes=['MOD'])
            for s, j in ((0, 1), (1, 4)):
                P.ts('dve', G.MP1[:, l, s, :, :], G.MOD[:, l, j * 8:(j + 1) * 8, :], 1.0, None, ALU.add,
                     reads=['MOD'], writes=['MP1'])
        P.flush()


def seg_list():
    segs = [('ctx', 0, CTX)]
    for t in range(SEQ // 512):
        segs.append(('lat', t * 512, 512))
    return segs


def stage_1a(G, l):
    nc, P, I, S = G.nc, G.P, G.I, G.S
    with ExitStack() as es:
        W = es.enter_context(G.sbt('w_in_b', [128, 8, 3 * D], BF16))
        xin = [es.enter_context(G.sbt('xin%d' % i, [128, 8, 512], F32)) for i in range(2)]
        xm = [es.enter_context(G.sbt('xm%d' % i, [128, 8, 512], BF16)) for i in range(2)]
        st = [[es.enter_context(G.sbt('st%d_%d' % (j, i), [128, 8, 512], BF16)) for i in range(2)] for j in range(3)]
        stc = es.enter_context(G.sbt('stc', [128, 8, CTX], BF16))
        for k in range(8):
            for h in range(2):
                P.dma('pool', W[:, k, h * 1536:(h + 1) * 1536], I['w_in'][l, k * 128:(k + 1) * 128, h * 1536:(h + 1) * 1536],
                      'W1a', writes=[('W', k)])
        it = 0
        psi = 0
        for b in range(NB):
            for (kind, t0, n) in seg_list():
                slot = it % 2
                it += 1
                m = 2 if kind == 'ctx' else b
                src = I['cxT'] if kind == 'ctx' else I['xT']
                if l > 0:
                    src = G.S['x', kind]
                off = t0 if kind == 'ctx' else CTX + t0
                P.dma('sp', xin[slot][:, :, 0:n], src[b, :, t0:t0 + n].rearrange('(k p) t -> p k t', p=128),
                      ('xin', slot), writes=[('xin', slot)])
                for k in range(8):
                    P.ts('dve', xm[slot][:, k, 0:n], xin[slot][:, k, 0:n], G.MP1[:, l, 0, k, m:m + 1], modv(G, l, 0, k, m),
                         ALU.mult, ALU.add, reads=[('xin', slot), 'MP1', 'MOD'], writes=[('xm', slot)])
                for oc in range(24):
                    pb = psi % 4
                    psi += 1
                    ps = G.PS[pb]
                    for k in range(8):
                        P.mm(ps[:, 0:n], W[:, k, oc * 128:(oc + 1) * 128], xm[slot][:, k, 0:n], start=(k == 0), stop=(k == 7),
                             reads=[('W', k), ('xm', slot)], writes=[('ps', pb)])
                    j, o8 = oc // 8, oc % 8
                    dst = st[j][slot][:, o8, 0:n]
                    if j == 1:
                        P.act(dst, ps[:, 0:n], AF.Gelu_apprx_tanh, reads=[('ps', pb)], writes=[('st', j, slot)])
                    elif oc % 3 == 0:
                        P.copy('act', dst, ps[:, 0:n], reads=[('ps', pb)], writes=[('st', j, slot)])
                    else:
                        P.copy('dve', dst, ps[:, 0:n], reads=[('ps', pb)], writes=[('st', j, slot)])
                    if j == 2 and kind == 'ctx':
                        for cc in range(4):
                            P.copy('dve', stc[:, o8, :].rearrange('p (i c ip) -> p i c ip', i=8, c=4)[:, :, cc, :],
                                   ps[:, cc * 64:(cc + 1) * 64].rearrange('p (i ip) -> p i ip', i=8), reads=[('ps', pb)], writes=['stc'])
                if kind == 'ctx':
                    P.dma('sp', S['s5uc', b].rearrange('(k p) t -> p k t', p=128), stc[:], 'stc', reads=['stc'])
                for j, nm in enumerate(('rgx', 'gg', 's5u')):
                    P.dma('sp', S[nm, b][:, off:off + n].rearrange('(k p) t -> p k t', p=128), st[j][slot][:, :, 0:n],
                          ('st', j, slot), reads=[('st', j, slot)], writes=[('dram', nm, b)])
        P.flush()


MAGIC = 12582912.0
TWO_PI = 2.0 * math.pi


def stage_s5prep(G, l):
    nc, P, I, S = G.nc, G.P, G.I, G.S
    GB = 4
    with ExitStack() as es:
        def sb(name, shape, dt=F32):
            return es.enter_context(G.sbt(name, shape, dt))
        A_ = sb('s5a_t', [128, 2, 64])
        ldt = sb('ldt', [128, 64])
        Bin = sb('Bin', [128, 2, 64, 16])
        Cin = sb('Cin', [128, 2, 64, 16])
        DP = sb('DP', [128, 64])
        cmask = sb('cmask_t', [128, 2, 4, 4, 128])
        cvec = sb('cvec_t', [128, 4])
        mvals = sb('mvals_t', [128, 128])
        identf = sb('identf', [128, 128])
        Z = sb('Z', [128, 2, 64])
        ZP = sb('ZP', [128, 2, 64])
        L1 = sb('L1', [128, 2, 64])
        L64 = sb('L64t', [128, 2, 64])
        KS = sb('KS', [128, 2, 64])
        KO = sb('KO', [128, 2, 64])
        Q = sb('Q', [128, 2, 64])
        sm = [sb('sm%d' % i, [128, 64]) for i in range(6)]
        Bbar = sb('Bbar', [128, 2, 64, 16])
        BS = sb('BS', [128, 2, 64, 16])
        CO = sb('CO', [128, 2, 64, 16])
        CinN = sb('CinN', [128, 2, 64, 16])
        CON = sb('CON', [128, 2, 64, 16])
        tmpb = [sb("tmpb%d" % i, [128, 1024]) for i in range(8)]
        tab = sb('tab', [128, 2, GB, 128])
        BmR = sb('BmR', [128, 4, GB, 128], BF16)
        BmI = sb('BmI', [128, 4, GB, 128], BF16)
        AR = sb('AR', [128, 4, GB, 128], BF16)
        AIn = sb('AIn', [128, 4, GB, 128], BF16)
        WSP = sb('WSP', [128, 2, GB, 128])
        MZ1 = sb('MZ1', [128, GB, 128])
        Tout = sb('Tout', [128, GB, 16, 128], BF16)
        WSst = sb('WSst', [128, GB, 2, 8, 128], BF16)
        WOst = sb('WOst', [128, GB, 2, 8, 128], BF16)
        t1 = [sb('t1_%d' % i, [128, 512]) for i in range(2)]
        t2 = [sb('t2_%d' % i, [128, 512]) for i in range(2)]

        for t, nm in ((A_, 's5a'), (ldt, 's5ldt'), (Bin, 's5B'), (Cin, 's5C'), (DP, 's5DP')):
            P.dma('sp', t[:], I[nm][l], 'pl_' + nm, writes=[nm])
        for t, nm in ((cmask, 'cmask'), (cvec, 'cvec'), (mvals, 'mvals'), (identf, 'ident')):
            P.dma('sp', t[:], I[nm], 'pl_' + nm, writes=[nm])

        uid = [0]

        def key():
            uid[0] += 1
            return ('k', uid[0])

        def cexp(o_re, o_im, zr, zi, n, rk, wk, neg_im=False):
            tm = [t[:, 0:n] for t in tmpb]
            k0, k1, k2, k3, k4, k5 = [key() for _ in range(6)]
            P.act(tm[0], zr, AF.Exp, reads=rk, writes=[('tm', 0)])
            P.ts('dve', tm[1], zi, 1.0 / TWO_PI, MAGIC, ALU.mult, ALU.add, reads=rk, writes=[('tm', 1)])
            P.ts('dve', tm[2], tm[1], MAGIC, None, ALU.subtract, reads=[('tm', 1)], writes=[('tm', 2)])
            P.stt(tm[3], zi, 1.0 / TWO_PI, tm[2], ALU.mult, ALU.subtract, reads=rk + [('tm', 2)], writes=[('tm', 3)])
            P.act(tm[4], tm[3], AF.Sin, scale=TWO_PI, reads=[('tm', 3)], writes=[('tm', 4)])
            P.ts('pool', tm[5], zi, 1.0 / TWO_PI, 0.25, ALU.mult, ALU.add, reads=rk, writes=[('tm', 5)])
            P.ts('dve', tm[1], tm[5], MAGIC, None, ALU.add, reads=[('tm', 5)], writes=[('tm', 1)])
            P.ts('dve', tm[2], tm[1], MAGIC, None, ALU.subtract, reads=[('tm', 1)], writes=[('tm', 2)])
            P.tt('dve', tm[3], tm[5], tm[2], ALU.subtract, reads=[('tm', 5), ('tm', 2)], writes=[('tm', 3)])
            P.act(tm[6], tm[3], AF.Sin, scale=TWO_PI, reads=[('tm', 3)], writes=[('tm', 6)])
            P.tt('dve', o_re, tm[0], tm[6], ALU.mult, reads=[('tm', 0), ('tm', 6)], writes=wk)
            if neg_im:
                P.stt(o_im, tm[0], -1.0, tm[4], ALU.mult, ALU.mult, reads=[('tm', 0), ('tm', 4)], writes=wk)
            else:
                P.tt('pool', o_im, tm[0], tm[4], ALU.mult, reads=[('tm', 0), ('tm', 4)], writes=wk)

        def cmul(o_re, o_im, ar, ai, br, bi, n_shape, rk, wk, nb=None):
            sh = n_shape
            n = 1
            for v in sh[1:]:
                n *= v

            def tv(i):
                a = tmpb[i][:, 0:n]
                if len(sh) == 3:
                    return a.rearrange('p (a b) -> p a b', a=sh[1])
                if len(sh) == 4:
                    return a.rearrange('p (a b c) -> p a b c', a=sh[1], b=sh[2])
                return a
            ibr, ibi = (br, bi) if nb is None else nb
            P.tt('dve', tv(0), ar, br, ALU.mult, reads=rk, writes=[('tm', 0)])
            P.tt('dve', tv(1), ai, bi, ALU.mult, reads=rk, writes=[('tm', 1)])
            P.tt('dve', o_re, tv(0), tv(1), ALU.subtract, reads=[('tm', 0), ('tm', 1)], writes=wk)
            P.tt('dve' if OPT_PREP else 'pool', tv(2), ar, ibi, ALU.mult, reads=rk, writes=[('tm', 2)])
            P.tt('pool', tv(3), ai, ibr, ALU.mult, reads=rk, writes=[('tm', 3)])
            P.tt('pool', o_im, tv(2), tv(3), ALU.add, reads=[('tm', 2), ('tm', 3)], writes=wk)

        P.act(sm[0][:], ldt[:], AF.Exp, reads=['s5ldt'], writes=['dt'])
        for c in range(2):
            P.tt('dve', Z[:, c, :], A_[:, c, :], sm[0][:], ALU.mult, reads=['s5a', 'dt'], writes=['Z'])
            P.ts('dve', ZP[:, c, :], Z[:, c, :], cvec[:, 0:1], None, ALU.mult, reads=['Z', 'cvec'], writes=['ZP'])
        cexp(L1[:, 0, :], L1[:, 1, :], Z[:, 0, :], Z[:, 1, :], 64, ['Z'], ['L1'])
        for (dst, col, nm) in ((L64, None, 'L64'), (KS, 1, 'KS'), (KO, 2, 'KO')):
            for c in range(2):
                if col is None:
                    P.ts('dve', sm[1 + c][:], Z[:, c, :], 64.0, None, ALU.mult, reads=['Z'], writes=[('zs', c)])
                else:
                    P.ts('dve', sm[1 + c][:], Z[:, c, :], cvec[:, col:col + 1], None, ALU.mult, reads=['Z', 'cvec'], writes=[('zs', c)])
            cexp(dst[:, 0, :], dst[:, 1, :], sm[1][:], sm[2][:], 64, [('zs', 0), ('zs', 1)], [nm])
        P.dma('sp', S['L64', l], L64[:], 'L64st', reads=['L64'])
        P.ts('dve', sm[1][:], L1[:, 0, :], -1.0, None, ALU.add, reads=['L1'], writes=['numr'])
        P.tt('dve', sm[2][:], A_[:, 0, :], A_[:, 0, :], ALU.mult, reads=['s5a'], writes=['d1'])
        P.tt('dve', sm[3][:], A_[:, 1, :], A_[:, 1, :], ALU.mult, reads=['s5a'], writes=['d2'])
        P.tt('dve', sm[2][:], sm[2][:], sm[3][:], ALU.add, reads=['d1', 'd2'], writes=['den'])
        P.add('dve', lambda e: e.reciprocal(out=sm[3][:], in_=sm[2][:]), reads=['den'], writes=['rden'])
        P.tt('dve', sm[4][:], sm[1][:], A_[:, 0, :], ALU.mult, reads=['numr', 's5a'], writes=['q1'])
        P.tt('dve', sm[5][:], L1[:, 1, :], A_[:, 1, :], ALU.mult, reads=['L1', 's5a'], writes=['q2'])
        P.tt('dve', sm[4][:], sm[4][:], sm[5][:], ALU.add, reads=['q1', 'q2'], writes=['q3'])
        P.tt('dve', Q[:, 0, :], sm[4][:], sm[3][:], ALU.mult, reads=['q3', 'rden'], writes=['Qr'])
        P.tt('dve', sm[4][:], L1[:, 1, :], A_[:, 0, :], ALU.mult, reads=['L1', 's5a', 'Qr'], writes=['q4'])
        P.tt('dve', sm[5][:], sm[1][:], A_[:, 1, :], ALU.mult, reads=['numr', 's5a', 'Qr'], writes=['q5'])
        P.tt('dve', sm[4][:], sm[4][:], sm[5][:], ALU.subtract, reads=['q4', 'q5'], writes=['q6'])
        P.tt('dve', Q[:, 1, :], sm[4][:], sm[3][:], ALU.mult, reads=['q6', 'rden'], writes=['Qi'])

        def bc(ap2, n):
            return ap2.unsqueeze(2).to_broadcast([128, 64, n])
        cmul(Bbar[:, 0], Bbar[:, 1], bc(Q[:, 0, :], 16), bc(Q[:, 1, :], 16), Bin[:, 0], Bin[:, 1], [128, 64, 16],
             ['Qr', 'Qi', 's5B'], ['Bbar'])
        cmul(BS[:, 0], BS[:, 1], bc(KS[:, 0, :], 16), bc(KS[:, 1, :], 16), Bbar[:, 0], Bbar[:, 1], [128, 64, 16],
             ['KS', 'Bbar'], ['BS'])
        cmul(CO[:, 0], CO[:, 1], bc(KO[:, 0, :], 16), bc(KO[:, 1, :], 16), Cin[:, 0], Cin[:, 1], [128, 64, 16],
             ['KO', 's5C'], ['CO'])
        P.ts('dve', CinN[:].rearrange('p c g h -> p (c g h)'), Cin[:].rearrange('p c g h -> p (c g h)'), -1.0, None, ALU.mult,
             reads=['s5C'], writes=['CinN'])
        P.ts('dve', CON[:].rearrange('p c g h -> p (c g h)'), CO[:].rearrange('p c g h -> p (c g h)'), -1.0, None, ALU.mult,
             reads=['CO'], writes=['CON'])

        psi = 0
        for gb in range(64 // GB):
            g0 = gb * GB
            gs = slice(g0, g0 + GB)
            for c in range(2):
                P.tt('dve', tmpb[7][:, 0:GB * 128].rearrange('p (a b) -> p a b', a=GB) if c == 0 else
                     MZ1[:],
                     ZP[:, c, gs].unsqueeze(2).to_broadcast([128, GB, 128]),
                     mvals[:].unsqueeze(1).to_broadcast([128, GB, 128]), ALU.mult,
                     reads=['ZP', 'mvals'], writes=[('mz', c)])
            cexp(tab[:, 0].rearrange('p a b -> p (a b)'), tab[:, 1].rearrange('p a b -> p (a b)'),
                 tmpb[7][:, 0:GB * 128], MZ1[:].rearrange('p a b -> p (a b)'), GB * 128,
                 [('mz', 0), ('mz', 1)], ['tab'])

            def tabv(c, start, step, n_inner):
                if step > 0:
                    v = tab[:, c, :, start:start + 7 * step + 1:step]
                else:
                    stop = start + 7 * step - 1
                    v = tab[:, c, :, start:(stop if stop >= 0 else None):step]
                return v.unsqueeze(3).to_broadcast([128, GB, 8, n_inner])

            def vecv(t, c):
                return t[:, c, gs, :].unsqueeze(2).to_broadcast([128, GB, 8, 16])

            def o4(t):
                return t.rearrange('p g (a b) -> p g a b', a=8)
            for d1 in range(4):
                cmul(o4(BmR[:, d1]), o4(BmI[:, d1]), tabv(0, 63 + d1, -8, 16), tabv(1, 63 + d1, -8, 16), vecv(Bbar, 0), vecv(Bbar, 1),
                     [128, GB, 8, 16], ['tab', 'Bbar'], ['Bm'])
            for d2 in range(4):
                cmul(o4(AR[:, d2]), o4(AIn[:, d2]), tabv(0, 63 + 4 * (d2 - 2), 8, 16), tabv(1, 63 + 4 * (d2 - 2), 8, 16),
                     vecv(Cin, 0), vecv(Cin, 1), [128, GB, 8, 16], ['tab', 's5C', 'CinN'], ['Am'], nb=(vecv(CinN, 0), vecv(CinN, 1)))
            for gl in range(GB):
                for d1 in range(4):
                    pa, pb = psi % 8, (psi + 1) % 8
                    psi += 2
                    for (pp, h0) in ((pa, 0), (pb, 64)):
                        hs = slice(h0, h0 + 64)
                        out = G.PS[pp][:].rearrange('p (a b) -> p a b', a=4)
                        P.mm(out, BmR[hs, d1, gl, :], AR[hs, :, gl, :], start=True, stop=False, reads=['Bm', 'Am'], writes=[('ps', pp)])
                        P.mm(out, BmI[hs, d1, gl, :], AIn[hs, :, gl, :], start=False, stop=True, reads=['Bm', 'Am'], writes=[('ps', pp)])
                    ts_ = (gl * 4 + d1) % 2
                    P.tt('dve', t1[ts_][:], G.PS[pa][:], cmask[:, 0, d1].rearrange('p a b -> p (a b)'), ALU.mult,
                         reads=[('ps', pa), 'cmask'], writes=[('t1', ts_)])
                    if d1 == 0:
                        P.stt(t1[ts_][:, 256:384], identf[:], DP[:, g0 + gl:g0 + gl + 1], t1[ts_][:, 256:384], ALU.mult, ALU.add,
                              reads=[('t1', ts_), 'ident', 's5DP'], writes=[('t1', ts_)])
                    P.tt('dve', t2[ts_][:], G.PS[pb][:], cmask[:, 1, d1].rearrange('p a b -> p (a b)'), ALU.mult,
                         reads=[('ps', pb), 'cmask'], writes=[('t2', ts_)])
                    P.tt('pool', Tout[:, gl, d1 * 4:(d1 + 1) * 4, :].rearrange('p a b -> p (a b)'), t1[ts_][:], t2[ts_][:], ALU.add,
                         reads=[('t1', ts_), ('t2', ts_)], writes=['Tout'])
            P.dma('sp', S['WT', l][gs].rearrange('g p a b -> p g a b'), Tout[:], 'Toutst', reads=['Tout'])
            for Ip in range(8):
                cmul(o4(WSP[:, 0]), o4(WSP[:, 1]), tabv(0, 126 - Ip, -8, 16), tabv(1, 126 - Ip, -8, 16), vecv(BS, 0), vecv(BS, 1),
                     [128, GB, 8, 16], ['tab', 'BS'], ['WSP'])
                for c in range(2):
                    for g4 in range(GB // 4):
                        pp = psi % 8
                        psi += 1
                        for gq in range(4):
                            gl = g4 * 4 + gq
                            P.tr(G.PS[pp][:, gq * 128:(gq + 1) * 128], WSP[:, c, gl, :], identf[:], reads=['WSP', 'ident'], writes=[('ps', pp)])
                        P.copy('act', WSst[:, g4 * 4:(g4 + 1) * 4, c, Ip, :], G.PS[pp][:].rearrange('p (a b) -> p a b', a=4),
                               reads=[('ps', pp)], writes=['WSst'])
            P.dma('sp', S['WS', l][gs].rearrange('g p c i q -> p g c i q'), WSst[:], 'WSstst', reads=['WSst'])
            for Jp in range(8):
                cmul(o4(WOst[:, :, 0, Jp, :]), o4(WOst[:, :, 1, Jp, :]), tabv(0, 64 + Jp, 8, 16), tabv(1, 64 + Jp, 8, 16),
                     vecv(CO, 0), vecv(CO, 1), [128, GB, 8, 16], ['tab', 'CO', 'CON'], ['WOst'], nb=(vecv(CON, 0), vecv(CON, 1)))
            P.dma('sp', S['WO', l][gs].rearrange('g p c j q -> p g c j q'), WOst[:], 'WOstst', reads=['WOst'])
        P.flush()


def tslot(delta):
    d1 = delta % 4
    d2 = (delta - d1) // 4
    return d1 * 4 + d2 + 2


def stage_2(G, l):
    nc, P, I, S = G.nc, G.P, G.I, G.S
    last = (l == DEPTH - 1)
    NS = 68
    import os
    CUT = int(os.environ.get('S2CUT', '9'))
    with ExitStack() as es:
        def sb(name, shape, dt=F32):
            return es.enter_context(G.sbt(name, shape, dt))
        UTl = sb('UTl', [128, 64, 512], BF16)
        UTc = sb('UTc', [128, 64, 4, 8], BF16)
        X2 = sb('X2', [128, 2, 64, NS])
        Hb = sb('Hb', [128, 2, 64, NS], BF16)
        Lt = sb('Lt', [128, 2, 64])
        L1s = sb('L1s', [128, 2, 64])
        L2s = sb('L2s', [128, 2, 64])
        tA = sb('tA', [128, 2, 64])
        tB = sb('tB', [128, 2, 64])
        WSg = [sb('WSg%d' % i, [128, 2, 8, 128], BF16) for i in range(2)]
        WTg = [sb('WTg%d' % i, [128, 16, 128], BF16) for i in range(2)]
        WOg = [sb('WOg%d' % i, [128, 2, 8, 128], BF16) for i in range(2)]
        Yst = [sb('Yst%d' % i, [128, 8, 512], BF16) for i in range(2)]
        Yc = sb('Yc', [128, 64, 4, 8], BF16)

        P.dma('sp', Lt[:], S['L64', l], 'Lt', writes=['Lt'])
        P.copy('dve', L1s[:, 0, :], Lt[:, 0, :], reads=['Lt'], writes=['L1s'])
        P.copy('dve', L1s[:, 1, :], Lt[:, 0, :], reads=['Lt'], writes=['L1s'])
        P.ts('dve', L2s[:, 0, :], Lt[:, 1, :], -1.0, None, ALU.mult, reads=['Lt'], writes=['L2s'])
        P.copy('dve', L2s[:, 1, :], Lt[:, 1, :], reads=['Lt'], writes=['L2s'])
        P.memset('pool', Hb[0:64, :, :, 64:65], 0.0, writes=['Hb0'])
        P.memset('pool', Hb[64:128, :, :, 67:68], 0.0, writes=['Hb0'])
        psi = 0
        wi = 0
        for b in range(NB if CUT >= 9 else 1):
            for i in range(8):
                P.dma('sp', UTl[i * 16:(i + 1) * 16, :, :],
                      S['s5u', b][:, CTX + 512 * i:CTX + 512 * (i + 1)].rearrange('(g h) t -> h g t', h=16),
                      ('UTl', i), writes=['UTl'])
                P.dma('sp', UTc[i * 16:(i + 1) * 16, :, :, :].rearrange('p g c j -> p g (c j)'),
                      S['s5uc', b][:, 32 * i:32 * (i + 1)].rearrange('(g h) t -> h g t', h=16),
                      ('UTc', i), writes=['UTc'])
            if CUT < 2:
                continue
            G7 = 7
            for g0 in range(0, 64, G7):
                gn = min(G7, 64 - g0)
                pbank = []
                for c in range(2):
                    pbank.append(psi % 8)
                    psi += 1
                for gl in range(gn):
                    g = g0 + gl
                    ws = wi % 2
                    wi += 1
                    P.dma('sp', WSg[ws][:], S['WS', l][g], ('WSg', ws), writes=[('WSg', ws)])
                    for c in range(2):
                        ps = G.PS[pbank[c]]
                        for Ip in range(8):
                            P.mm(ps[:, gl * NS:gl * NS + 64], WSg[ws][:, c, Ip, :], UTl[:, g, Ip * 64:(Ip + 1) * 64], start=(Ip == 0), stop=(Ip == 7),
                                 reads=[('WSg', ws), 'UTl'], writes=[('ps', pbank[c])])
                        for Ip in range(8):
                            P.mm(ps[:, gl * NS + 64:gl * NS + 68], WSg[ws][:, c, Ip, :], UTc[:, g, :, Ip], start=(Ip == 0), stop=(Ip == 7),
                                 reads=[('WSg', ws), 'UTc'], writes=[('ps', pbank[c])])
                for c in range(2):
                    ps = G.PS[pbank[c]]
                    pv = ps[:, 0:gn * NS].rearrange('p (g s) -> p g s', s=NS)
                    eng = 'dve'
                    P.copy(eng, X2[0:64, c, g0:g0 + gn, 0:4], pv[0:64, :, 64:68], reads=[('ps', pbank[c])], writes=['X2'])
                    P.copy(eng, X2[0:64, c, g0:g0 + gn, 4:68], pv[0:64, :, 0:64], reads=[('ps', pbank[c])], writes=['X2'])
                    P.copy(eng, X2[64:128, c, g0:g0 + gn, 0:4], pv[64:128, :, 67:63:-1], reads=[('ps', pbank[c])], writes=['X2'])
                    P.copy(eng, X2[64:128, c, g0:g0 + gn, 4:68], pv[64:128, :, 63::-1], reads=[('ps', pbank[c])], writes=['X2'])
            if CUT < 3:
                continue
            for s_ in range(1, NS):
                prev = X2[:, :, :, s_ - 1]
                prev_sw = X2[:, ::-1, :, s_ - 1]
                cur = X2[:, :, :, s_]
                P.tt('dve', tA[:], L1s[:], prev, ALU.mult, reads=['L1s', 'X2'], writes=['tA'])
                P.tt('dve', tB[:], L2s[:], prev_sw, ALU.mult, reads=['L2s', 'X2'], writes=['tB'])
                P.tt('dve', cur, cur, tA[:], ALU.add, reads=['X2', 'tA'], writes=['X2'])
                P.tt('dve', cur, cur, tB[:], ALU.add, reads=['X2', 'tB'], writes=['X2'])
            for c in range(2):
                P.copy('dve', Hb[0:64, c, :, 0:64], X2[0:64, c, :, 3:67], reads=['X2'], writes=['Hb'])
                P.copy('dve', Hb[0:64, c, :, 65:68], X2[0:64, c, :, 0:3], reads=['X2'], writes=['Hb'])
                P.copy('dve', Hb[64:128, c, :, 0:64], X2[64:128, c, :, 66:2:-1], reads=['X2'], writes=['Hb'])
                P.copy('dve', Hb[64:128, c, :, 64:67], X2[64:128, c, :, 2::-1], reads=['X2'], writes=['Hb'])
            if CUT < 4:
                continue
            for g in range(64 if CUT >= 5 else 8):
                ws = wi % 2
                wi += 1
                P.dma('sp', WTg[ws][:], S['WT', l][g], ('WTg', ws), writes=[('WTg', ws)])
                P.dma('sp', WOg[ws][:], S['WO', l][g], ('WOg', ws), writes=[('WOg', ws)])
                pb = psi % 8
                psi += 1
                ps = G.PS[pb]
                rk = [('WTg', ws), ('WOg', ws), 'UTl', 'UTc', 'Hb', 'Hb0']
                for Jp in range(8):
                    out = ps[:, Jp * 64:(Jp + 1) * 64]
                    for Ip in range(8):
                        P.mm(out, WTg[ws][:, tslot(Jp - Ip), :], UTl[:, g, Ip * 64:(Ip + 1) * 64], start=(Ip == 0), stop=False,
                             reads=rk, writes=[('ps', pb)])
                    P.mm(out, WOg[ws][:, 0, Jp, :], Hb[:, 0, g, 0:64], start=False, stop=False, reads=rk, writes=[('ps', pb)])
                    P.mm(out, WOg[ws][:, 1, Jp, :], Hb[:, 1, g, 0:64], start=False, stop=True, reads=rk, writes=[('ps', pb)])
                ys = (g // 8) % 2
                P.act(Yst[ys][:, g % 8, :], ps[:], AF.Gelu_apprx_tanh, reads=[('ps', pb)], writes=[('Yst', ys)])
                if not last:
                    pc = psi % 8
                    psi += 1
                    psc = G.PS[pc]
                    for Jp in range(8):
                        out = psc[:, Jp * 4:(Jp + 1) * 4]
                        for Ip in range(8):
                            P.mm(out, WTg[ws][:, tslot(Jp - Ip), :], UTc[:, g, :, Ip], start=(Ip == 0), stop=False, reads=rk, writes=[('ps', pc)])
                        P.mm(out, WOg[ws][:, 0, Jp, :], Hb[:, 0, g, 64:68], start=False, stop=False, reads=rk, writes=[('ps', pc)])
                        P.mm(out, WOg[ws][:, 1, Jp, :], Hb[:, 1, g, 64:68], start=False, stop=True, reads=rk, writes=[('ps', pc)])
                    P.act(Yc[:, g, :, :], psc[:, 0:32].rearrange('p (j c) -> p c j', c=4), AF.Gelu_apprx_tanh, reads=[('ps', pc)], writes=['Yc'])
                if g % 8 == 7:
                    g8 = g - 7
                    for j in range(8):
                        P.dma('sp', S['y5', b][g8 * 16:(g8 + 8) * 16, CTX + 512 * j:CTX + 512 * (j + 1)].rearrange('(g h) t -> h g t', h=16),
                              Yst[ys][j * 16:(j + 1) * 16, :, :], ('Yst', ys), reads=[('Yst', ys)])
            if not last:
                for j in range(8):
                    P.dma('sp', S['y5', b][:, 32 * j:32 * (j + 1)].rearrange('(g h) t -> h g t', h=16),
                          Yc[j * 16:(j + 1) * 16, :, :, :].rearrange('p g c j -> p g (c j)'), 'Ycst', reads=['Yc'])
        P.flush()


def ln_block(G, l, P, T, Xres, n, nslot, tiles, gname, bname, psi_ref, rk_extra):
    Rb, SQb, MEAN, M2, VAR, RSTD, ONESB = tiles
    tk = ('T32', nslot)
    for oc in range(8):
        P.copy('pool', Rb[:, oc, 0:n], T[:, oc, 0:n], reads=[tk], writes=['Rb'])
        P.act(SQb[:, oc, 0:n], T[:, oc, 0:n], AF.Square, reads=[tk], writes=['SQb'])
    pm = psi_ref[0] % 8
    pq = (psi_ref[0] + 1) % 8
    psi_ref[0] += 2
    for oc in range(8):
        P.mm(G.PS[pm][:, 0:n], ONESB[:], Rb[:, oc, 0:n], start=(oc == 0), stop=(oc == 7), reads=['Rb', 'ONESB'], writes=[('ps', pm)])
    for oc in range(8):
        P.mm(G.PS[pq][:, 0:n], ONESB[:], SQb[:, oc, 0:n], start=(oc == 0), stop=(oc == 7), reads=['SQb', 'ONESB'], writes=[('ps', pq)])
    P.copy('act', MEAN[:, 0:n], G.PS[pm][:, 0:n], reads=[('ps', pm)], writes=['MEAN'])
    P.tt('pool', M2[:, 0:n], MEAN[:, 0:n], MEAN[:, 0:n], ALU.mult, reads=['MEAN'], writes=['M2'])
    P.tt('dve', VAR[:, 0:n], G.PS[pq][:, 0:n], M2[:, 0:n], ALU.subtract, reads=[('ps', pq), 'M2'], writes=['VAR'])
    P.act(VAR[:, 0:n], VAR[:, 0:n], AF.Sqrt, bias=G.EPS[:], reads=['VAR'], writes=['VAR'])
    P.add('dve', lambda e, o=RSTD[:, 0:n], i=VAR[:, 0:n]: e.reciprocal(out=o, in_=i), reads=['VAR'], writes=['RSTD'])
    for oc in range(8):
        P.tt('pool', T[:, oc, 0:n], T[:, oc, 0:n], MEAN[:, 0:n], ALU.subtract, reads=[tk, 'MEAN'], writes=[tk])
        P.tt('dve', T[:, oc, 0:n], T[:, oc, 0:n], RSTD[:, 0:n], ALU.mult, reads=[tk, 'RSTD'], writes=[tk])
        P.ts('dve', T[:, oc, 0:n], T[:, oc, 0:n], vec(G, l, gname, oc), vec(G, l, bname, oc), ALU.mult, ALU.add, reads=[tk], writes=[tk])


def stage_3a(G, l):
    nc, P, I, S = G.nc, G.P, G.I, G.S
    last = (l == DEPTH - 1)
    with ExitStack() as es:
        def sb(name, shape, dt=F32):
            return es.enter_context(G.sbt(name, shape, dt))
        GLW = sb('GLW', [128, 8, D], BF16)
        WO_ = sb('WO_', [128, 16, D], BF16)
        ONESB = sb('ONESB', [128, 128], BF16)
        Y5 = [sb('Y5_%d' % i, [128, 8, 512], BF16) for i in range(2)]
        Y5n = sb('Y5n', [128, 8, CTX], BF16)
        RGt = [sb('RGt%d' % i, [128, 8, 512], BF16) for i in range(2)]
        Xin = [sb('Xin%d' % i, [128, 8, 512]) for i in range(2)]
        S5o = sb('S5o', [128, 8, 512], BF16)
        T32 = [sb('T32_%d' % i, [128, 8, 512]) for i in range(2)]
        Rb = sb('Rb', [128, 8, 512], BF16)
        SQb = sb('SQb', [128, 8, 512], BF16)
        sig = [sb('sig%d' % i, [128, 512]) for i in range(2)]
        MEAN = sb('MEAN', [128, 512])
        M2 = sb('M2', [128, 512])
        VAR = sb('VAR', [128, 512])
        RSTD = sb('RSTD', [128, 512])
        tiles = (Rb, SQb, MEAN, M2, VAR, RSTD, ONESB)
        for k in range(8):
            P.dma('pool', GLW[:, k, :], I['glu_w'][l, k * 128:(k + 1) * 128, :], 'GLWd', writes=['GLW'])
        for k in range(16):
            P.dma('pool', WO_[:, k, :], I['w_out'][l, k * 128:(k + 1) * 128, :], 'WO_d', writes=['WO_'])
        P.memset('dve', ONESB[:], 1.0 / D, writes=['ONESB'])
        psi = [0]
        sgc = [0]
        blocks = []
        for b in range(NB):
            for (kind, t0, n) in seg_list():
                if kind == 'ctx' and last:
                    continue
                blocks.append((b, kind, t0, n))

        def phA(i):
            b, kind, t0, n = blocks[i]
            slot = i % 2
            xsrc = (I['cxT'] if kind == 'ctx' else I['xT']) if l == 0 else S['x', kind]
            off = t0 if kind == 'ctx' else CTX + t0
            P.dma('sp', Y5[slot][:, :, 0:n], S['y5', b][:, off:off + n].rearrange('(k p) t -> p k t', p=128), ('Y5', slot), writes=[('Y5', slot)])
            P.dma('sp', RGt[slot][:, :, 0:n], S['rg', b][:, off:off + n].rearrange('(k p) t -> p k t', p=128), ('RGt', slot), writes=[('RGt', slot)])
            P.dma('sp', Xin[slot][:, :, 0:n], xsrc[b, :, t0:t0 + n].rearrange('(k p) t -> p k t', p=128), ('Xin', slot), writes=[('Xin', slot)])
            if kind == 'ctx':
                for k in range(8):
                    for cc in range(4):
                        P.copy('pool', Y5n[:, k, cc * 64:(cc + 1) * 64].rearrange('p (j q) -> p j q', j=8),
                               Y5[slot][:, k, 0:CTX].rearrange('p (j c q) -> p j c q', j=8, c=4)[:, :, cc, :],
                               reads=[('Y5', slot)], writes=['Y5n'])

        def phB(i):
            b, kind, t0, n = blocks[i]
            slot = i % 2
            if kind == 'ctx':
                ysrc, yk = Y5n, 'Y5n'
            else:
                ysrc, yk = Y5[slot], ('Y5', slot)
            for oc in range(8):
                pb = psi[0] % 8
                psi[0] += 1
                for k in range(8):
                    P.mm(G.PS[pb][:, 0:n], GLW[:, k, oc * 128:(oc + 1) * 128], ysrc[:, k, 0:n], start=(k == 0), stop=(k == 7),
                         reads=['GLW', yk], writes=[('ps', pb)])
                ss = sgc[0] % 2
                sgc[0] += 1
                P.act(sig[ss][:, 0:n], G.PS[pb][:, 0:n], AF.Sigmoid, bias=vec(G, l, 'glu_b', oc), reads=[('ps', pb)], writes=[('sig', ss)])
                P.tt('dve', S5o[:, oc, 0:n], ysrc[:, oc, 0:n], sig[ss][:, 0:n], ALU.mult, reads=[yk, ('sig', ss)], writes=['S5o'])

        def phC(i):
            b, kind, t0, n = blocks[i]
            slot = i % 2
            m = 2 if kind == 'ctx' else b
            T = T32[slot]
            for oc in range(8):
                pb = psi[0] % 8
                psi[0] += 1
                for k in range(16):
                    rhs = RGt[slot][:, k, 0:n] if k < 8 else S5o[:, k - 8, 0:n]
                    P.mm(G.PS[pb][:, 0:n], WO_[:, k, oc * 128:(oc + 1) * 128], rhs, start=(k == 0), stop=(k == 15),
                         reads=['WO_', ('RGt', slot), 'S5o'], writes=[('ps', pb)])
                P.ts('dve', T[:, oc, 0:n], G.PS[pb][:, 0:n], vec(G, l, 'b_out', oc), modv(G, l, 2, oc, m), ALU.add, ALU.mult,
                     reads=[('ps', pb), 'MOD'], writes=[('T32', slot)])
                P.stt(T[:, oc, 0:n], Xin[slot][:, oc, 0:n], ALPHA, T[:, oc, 0:n], ALU.mult, ALU.add,
                      reads=[('Xin', slot), ('T32', slot)], writes=[('T32', slot)])

        def phD(i):
            b, kind, t0, n = blocks[i]
            slot = i % 2
            T = T32[slot]
            ln_block(G, l, P, T, None, n, slot, tiles, 'ln1_g', 'ln1_b', psi, None)
            P.dma('sp', S['x1', kind][b, :, t0:t0 + n].rearrange('(k p) t -> p k t', p=128), T[:, :, 0:n], ('T32', slot),
                  reads=[('T32', slot)])

        nb_ = len(blocks)
        phA(0)
        for i in range(nb_):
            if i + 1 < nb_:
                phA(i + 1)
            phB(i)
            if i > 0 and OPT_3:
                phD(i - 1)
            phC(i)
            if not OPT_3:
                phD(i)
        if OPT_3:
            phD(nb_ - 1)
        P.flush()


def stage_3b(G, l):
    nc, P, I, S = G.nc, G.P, G.I, G.S
    last = (l == DEPTH - 1)
    NBLK = 256
    with ExitStack() as es:
        def sb(name, shape, dt=F32):
            return es.enter_context(G.sbt(name, shape, dt))
        W1b = sb('W1b', [128, 8, 4 * D], BF16)
        W2b = sb('W2b', [128, 32, D], BF16)
        ONESB = sb('ONESB', [128, 128], BF16)
        X1 = [sb('X1_%d' % i, [128, 8, NBLK]) for i in range(2)]
        X1m = sb('X1m', [128, 8, NBLK], BF16)
        Hh = sb('Hh', [128, 32, NBLK], BF16)
        hr = [sb('hr%d' % i, [128, NBLK], BF16) for i in range(2)]
        T32 = [sb('T32_%d' % i, [128, 8, NBLK]) for i in range(2)]
        Rb = sb('Rb', [128, 8, NBLK], BF16)
        SQb = sb('SQb', [128, 8, NBLK], BF16)
        MEAN = sb('MEAN', [128, NBLK])
        M2 = sb('M2', [128, NBLK])
        VAR = sb('VAR', [128, NBLK])
        RSTD = sb('RSTD', [128, NBLK])
        tiles = (Rb, SQb, MEAN, M2, VAR, RSTD, ONESB)
        for k in range(8):
            for h in range(2):
                P.dma('pool', W1b[:, k, h * 2048:(h + 1) * 2048], I['w1'][l, k * 128:(k + 1) * 128, h * 2048:(h + 1) * 2048],
                      'W1bd', writes=['W1b'])
        for k in range(32):
            P.dma('pool', W2b[:, k, :], I['w2'][l, k * 128:(k + 1) * 128, :], 'W2bd', writes=['W2b'])
        P.memset('dve', ONESB[:], 1.0 / D, writes=['ONESB'])
        psi = [0]
        hsc = [0]
        blocks = []
        for b in range(NB):
            segs = [] if last else [('ctx', 0, CTX)]
            segs += [('lat', t * NBLK, NBLK) for t in range(SEQ // NBLK)]
            for (kind, t0, n) in segs:
                blocks.append((b, kind, t0, n))

        def phA(i):
            b, kind, t0, n = blocks[i]
            slot = i % 2
            P.dma('sp', X1[slot][:], S['x1', kind][b, :, t0:t0 + n].rearrange('(k p) t -> p k t', p=128), ('X1', slot), writes=[('X1', slot)])

        def phB(i):
            b, kind, t0, n = blocks[i]
            slot = i % 2
            m = 2 if kind == 'ctx' else b
            for k in range(8):
                P.ts('dve', X1m[:, k, :], X1[slot][:, k, :], G.MP1[:, l, 1, k, m:m + 1], modv(G, l, 3, k, m), ALU.mult, ALU.add,
                     reads=[('X1', slot), 'MP1', 'MOD'], writes=['X1m'])
            for hc in range(32):
                pb = psi[0] % 8
                psi[0] += 1
                for k in range(8):
                    P.mm(G.PS[pb][:, 0:n], W1b[:, k, hc * 128:(hc + 1) * 128], X1m[:, k, :], start=(k == 0), stop=(k == 7),
                         reads=['W1b', 'X1m'], writes=[('ps', pb)])
                h2 = hsc[0] % 2
                hsc[0] += 1
                P.act(hr[h2][:], G.PS[pb][:, 0:n], AF.Relu, bias=vec(G, l, 'b1', hc), reads=[('ps', pb)], writes=[('hr', h2)])
                P.tt('pool', Hh[:, hc, :], hr[h2][:], hr[h2][:], ALU.mult, reads=[('hr', h2)], writes=['Hh'])

        def phC(i):
            b, kind, t0, n = blocks[i]
            slot = i % 2
            m = 2 if kind == 'ctx' else b
            T = T32[slot]
            for oc in range(8):
                pb = psi[0] % 8
                psi[0] += 1
                for k in range(32):
                    P.mm(G.PS[pb][:, 0:n], W2b[:, k, oc * 128:(oc + 1) * 128], Hh[:, k, :], start=(k == 0), stop=(k == 31),
                         reads=['W2b', 'Hh'], writes=[('ps', pb)])
                P.ts('dve', T[:, oc, :], G.PS[pb][:, 0:n], vec(G, l, 'b2', oc), modv(G, l, 5, oc, m), ALU.add, ALU.mult,
                     reads=[('ps', pb), 'MOD'], writes=[('T32', slot)])
                P.stt(T[:, oc, :], X1[slot][:, oc, :], ALPHA, T[:, oc, :], ALU.mult, ALU.add,
                      reads=[('X1', slot), ('T32', slot)], writes=[('T32', slot)])

        def phD(i):
            b, kind, t0, n = blocks[i]
            slot = i % 2
            T = T32[slot]
            ln_block(G, l, P, T, None, n, slot, tiles, 'ln2_g', 'ln2_b', psi, None)
            if last:
                dst = G.yT[b, :, t0:t0 + n]
            else:
                dst = S['x', kind][b, :, t0:t0 + n]
            P.dma('sp', dst.rearrange('(k p) t -> p k t', p=128), T[:, :, 0:n], ('T32', slot), reads=[('T32', slot)])

        nb_ = len(blocks)
        phA(0)
        for i in range(nb_):
            if i + 1 < nb_:
                phA(i + 1)
            phB(i)
            if i > 0 and OPT_3:
                phD(i - 1)
            phC(i)
            if not OPT_3:
                phD(i)
        if OPT_3:
            phD(nb_ - 1)
        P.flush()


def rev(ap):
    return ap[:, ::-1]


def stage_1b(G, l):
    nc, P, I, S = G.nc, G.P, G.I, G.S
    NT = CTX + SEQ
    with ExitStack() as es:
        def sb(name, shape, dt):
            return es.enter_context(G.sbt(name, shape, dt))
        identf = sb('identf', [128, 128], F32)
        DG = sb('DG', [128, 8, 4, 128], BF16)
        GW = sb('GW', [128, 2, 2, 8, 128], BF16)
        cneg = sb('cneg', [128, 2, 8], F32)
        ctmp = sb('ctmp', [128, 2, 8], F32)
        RP = [sb('RP%d' % i, [128, NT + 6], BF16) for i in range(2)]
        GGt = [sb('GGt%d' % i, [128, NT], BF16) for i in range(2)]
        OUT = [sb('OUT%d' % i, [128, NT], BF16) for i in range(1)]
        A = [sb('A%d' % i, [128, NT], F32) for i in range(2)]
        Bt = [sb('B%d' % i, [128, NT], F32) for i in range(2)]
        HF = sb('HF', [128, NT], F32)
        XC32 = [sb('XC32_%d' % i, [128, 512], F32) for i in range(3)]
        XCB = [sb('XCB%d' % i, [128, 512], BF16) for i in range(3)]
        Rt = [[sb('Rt%d_%d' % (d, i), [128, 512], F32) for i in range(2)] for d in range(2)]
        It = [[sb('It%d_%d' % (d, i), [128, 512], F32) for i in range(2)] for d in range(2)]
        Tt = [[sb('Tt%d_%d' % (d, i), [128, 512], F32) for i in range(3)] for d in range(2)]
        Mt = [[sb('Mt%d_%d' % (d, i), [128, 512], F32) for i in range(2)] for d in range(2)]
        Ut = [[sb('Ut%d_%d' % (d, i), [128, 512], F32) for i in range(3)] for d in range(2)]

        P.dma('sp', identf[:], I['ident'], 'identf', writes=['identf'])
        for w, nm in enumerate(('rg_wa', 'rg_wi')):
            for d in range(2):
                P.dma('pool', GW[:, w, d, :, :], I[nm][l, d].rearrange('h i j -> i h j'), 'GWd', writes=['GW'])
        for fc in range(8):
            for tap in range(4):
                P.ts('dve', DG[:, fc, tap, :], identf[:], vec(G, l, 'conv_w%d' % tap, fc), None, ALU.mult,
                     reads=['identf'], writes=['DG'])
        for d in range(2):
            o = VOFF['lam%d' % d]
            P.act(ctmp[:, d, :], G.VEC[:, l, o:o + 8], AF.Exp, scale=-1.0, writes=['ctmp'])
        P.act(cneg[:], ctmp[:], AF.Ln, bias=G.ONE[:], reads=['ctmp'], writes=['cneg0'])
        P.ts('dve', cneg[:], cneg[:], -RG_C, None, ALU.mult, reads=['cneg0'], writes=['cneg'])
        for i in range(2):
            P.memset('pool', RP[i][:], 0.0, writes=[('RP', i)])
        segs = [(0, CTX, 1)] + [(CTX + t * 512, 512, 4 + CTX + t * 512) for t in range(SEQ // 512)]
        it = 0
        psi = 0
        gseg = [0]
        for b in range(NB):
            for fc in range(8):
                slot = it % 2
                it += 1
                rows = slice(fc * 128, (fc + 1) * 128)
                P.dma('sp', RP[slot][:, 1:1 + CTX], S['rgx', b][rows, 0:CTX], ('RPc', slot), writes=[('RP', slot)])
                P.dma('sp', RP[slot][:, 4 + CTX:4 + CTX + SEQ], S['rgx', b][rows, CTX:NT], ('RPl', slot), writes=[('RP', slot)])
                P.dma('sp', GGt[slot][:], S['gg', b][rows, :], ('GGt', slot), writes=[('GGt', slot)])
                base = gseg[0]
                gseg[0] += len(segs)

                def ph1(si):
                    nonlocal psi
                    off, n, pidx = segs[si]
                    q = (base + si) % 3
                    pb = psi % 8
                    psi += 1
                    ps = G.PS[pb]
                    for tap in range(4):
                        P.mm(ps[:, 0:n], DG[:, fc, tap, :], RP[slot][:, pidx + tap - 1:pidx + tap - 1 + n], start=(tap == 0), stop=(tap == 3),
                             reads=['DG', ('RP', slot)], writes=[('ps', pb)])
                    P.act(XC32[q][:, 0:n], ps[:, 0:n], AF.Identity, bias=vec(G, l, 'conv_b', fc), reads=[('ps', pb)], writes=[('XC32', q)])
                    P.copy('dve', XCB[q][:, 0:n], XC32[q][:, 0:n], reads=[('XC32', q)], writes=[('XCB', q)])

                def ph2(si):
                    nonlocal psi
                    off, n, pidx = segs[si]
                    q = (base + si) % 3
                    r2 = (base + si) % 2
                    for d in range(2):
                        pr = psi % 8
                        pi_ = (psi + 1) % 8
                        psi += 2
                        P.mm(G.PS[pr][:, 0:n], GW[:, 0, d, fc, :], XCB[q][:, 0:n], start=True, stop=True,
                             reads=['GW', ('XCB', q)], writes=[('ps', pr)])
                        P.mm(G.PS[pi_][:, 0:n], GW[:, 1, d, fc, :], XCB[q][:, 0:n], start=True, stop=True,
                             reads=['GW', ('XCB', q)], writes=[('ps', pi_)])
                        P.act(Rt[d][r2][:, 0:n], G.PS[pr][:, 0:n], AF.Sigmoid, bias=vec(G, l, 'ba%d' % d, fc), reads=[('ps', pr)], writes=[('Rt', d, r2)])
                        P.act(It[d][r2][:, 0:n], G.PS[pi_][:, 0:n], AF.Sigmoid, bias=vec(G, l, 'bi%d' % d, fc), reads=[('ps', pi_)], writes=[('It', d, r2)])
                    for d in range(2):
                        P.act(A[d][:, off:off + n], Rt[d][r2][:, 0:n], AF.Exp, scale=cneg[:, d, fc:fc + 1], reads=[('Rt', d, r2), 'cneg'], writes=[('A', d, si)])
                    for d in range(2):
                        P.tt('dve', Tt[d][q][:, 0:n], A[d][:, off:off + n], A[d][:, off:off + n], ALU.mult, reads=[('A', d, si)], writes=[('Tt', d, q)])
                        P.tt('dve', Ut[d][q][:, 0:n], It[d][r2][:, 0:n], XC32[q][:, 0:n], ALU.mult, reads=[('It', d, r2), ('XC32', q)], writes=[('Ut', d, q)])

                def ph3(si):
                    off, n, pidx = segs[si]
                    q = (base + si) % 3
                    r2 = (base + si) % 2
                    for d in range(2):
                        P.act(Mt[d][r2][:, 0:n], Tt[d][q][:, 0:n], AF.Sqrt, bias=G.ONE[:], scale=-1.0, reads=[('Tt', d, q)], writes=[('Mt', d, r2)])
                    for d in range(2):
                        P.tt('dve', Bt[d][:, off:off + n], Ut[d][q][:, 0:n], Mt[d][r2][:, 0:n], ALU.mult, reads=[('Ut', d, q), ('Mt', d, r2)], writes=[('B', d, si)])
                    init = 0.0 if si == 0 else HF[:, off - 1:off]
                    P.scan(HF[:, off:off + n], A[0][:, off:off + n], Bt[0][:, off:off + n], init, reads=[('A', 0, si), ('B', 0, si), 'HF'], writes=['HF'])

                ns_ = len(segs)
                for t in range(ns_ + 2):
                    if t < ns_:
                        ph1(t)
                    if 0 <= t - 1 < ns_:
                        ph2(t - 1)
                    if 0 <= t - 2 < ns_:
                        ph3(t - 2)
                HB = A[0]
                kA1 = [('A', 1, i) for i in range(ns_)] + [('B', 1, i) for i in range(ns_)]
                kA0 = [('A', 0, i) for i in range(ns_)]
                P.scan(rev(HB[:, 0:CTX]), rev(A[1][:, 0:CTX]), rev(Bt[1][:, 0:CTX]), 0.0, reads=kA1, writes=kA0)
                P.scan(rev(HB[:, CTX:NT]), rev(A[1][:, CTX:NT]), rev(Bt[1][:, CTX:NT]), HB[:, 0:1], reads=kA1 + kA0, writes=kA0)
                P.tt('dve', HF[:], HF[:], HB[:], ALU.add, reads=['HF'] + kA0, writes=['HF'])
                P.tt('dve', OUT[0][:], HF[:], GGt[slot][:], ALU.mult, reads=['HF', ('GGt', slot)], writes=['OUT'])
                P.dma('sp', S['rg', b][rows, :], OUT[0][:], 'OUTst', reads=['OUT'])
        P.flush()


def pack_inputs(inp):
    f = lambda a: np.ascontiguousarray(np.asarray(a, dtype=np.float32))
    x = np.asarray(inp['x'], np.float32)
    ctx = np.asarray(inp['ctx'], np.float32)
    c = np.asarray(inp['c'], np.float32)
    c_ctx = np.asarray(inp['c_ctx'], np.float32)

    def pk(v):
        v = np.asarray(v, np.float32)
        return v.reshape(-1, 128).T

    vecs = np.zeros((DEPTH, 128, NV), np.float32)
    for l in range(DEPTH):
        cols = {'ln1_g': inp['ln1_g'][l], 'ln1_b': inp['ln1_b'][l], 'ln2_g': inp['ln2_g'][l], 'ln2_b': inp['ln2_b'][l],
                'conv_w0': inp['conv_w'][l][0], 'conv_w1': inp['conv_w'][l][1], 'conv_w2': inp['conv_w'][l][2],
                'conv_w3': inp['conv_w'][l][3], 'conv_b': inp['conv_b'][l], 'lam0': inp['rg_lambda'][l][0],
                'lam1': inp['rg_lambda'][l][1], 'ba0': inp['rg_ba'][l][0], 'ba1': inp['rg_ba'][l][1],
                'bi0': inp['rg_bi'][l][0], 'bi1': inp['rg_bi'][l][1], 'glu_b': inp['s5_glu_b'][l], 'b_out': inp['b_out'][l],
                'b2': inp['mlp_b2'][l], 'b1': inp['mlp_b1'][l], 's5d': inp['s5_d'][l]}
        for n, ncol in VEC_NAMES:
            vecs[l, :, VOFF[n]:VOFF[n] + ncol] = pk(cols[n])
    ada_bP = np.stack([pk(np.asarray(inp['ada_b'])[l]) for l in range(DEPTH)])
    shared = {
        'ada_w': f(inp['ada_w']), 'ada_bP': f(ada_bP), 'vecs': vecs, 'w_in': f(inp['w_in']),
        'rg_wa': f(inp['rg_wa']), 'rg_wi': f(inp['rg_wi']), 'glu_w': f(inp['s5_glu_w']), 'w_out': f(inp['w_out']),
        'w1': f(inp['mlp_w1']), 'w2': f(inp['mlp_w2']), 'ident': np.eye(128, dtype=np.float32),
    }
    A = lambda n: np.asarray(inp[n], np.float32)
    s5a = np.stack([A('s5_a_re'), A('s5_a_im')], axis=2)
    shared['s5a'] = f(s5a.transpose(0, 1, 4, 2, 3).reshape(DEPTH, 128, 2, 64))
    shared['s5ldt'] = f(np.broadcast_to(A('s5_log_dt')[:, :, None, :], (DEPTH, 2, 64, 64)).reshape(DEPTH, 128, 64))
    s5b = np.stack([A('s5_b_re'), A('s5_b_im')], axis=2)
    shared['s5B'] = f(s5b.transpose(0, 1, 4, 2, 3, 5).reshape(DEPTH, 128, 2, 64, 16))
    s5c = np.stack([A('s5_c_re'), A('s5_c_im')], axis=2)
    shared['s5C'] = f(s5c.transpose(0, 1, 5, 2, 3, 4).reshape(DEPTH, 128, 2, 64, 16))
    dgh = A('s5_d').reshape(DEPTH, 64, 16)
    shared['s5DP'] = f(np.broadcast_to(dgh.transpose(0, 2, 1)[:, None, :, :], (DEPTH, 8, 16, 64)).reshape(DEPTH, 128, 64))
    cm = np.zeros((128, 2, 4, 4, 128), np.float32)
    for i in range(8):
        for j in range(8):
            for d1 in range(4):
                for d2 in range(4):
                    lag = 8 * (j - i) + d1 + 4 * (d2 - 2)
                    if lag >= 0:
                        cm[i * 16:(i + 1) * 16, 0, d1, d2, j * 16:(j + 1) * 16] = 1.0
                    if lag <= 0:
                        cm[i * 16:(i + 1) * 16, 1, d1, d2, j * 16:(j + 1) * 16] = 1.0
    shared['cmask'] = cm
    cv = np.zeros((128, 4), np.float32)
    cv[:64, 0] = 1.0
    cv[64:, 0] = -1.0
    cv[64:, 1] = 63.0
    cv[64:, 2] = 65.0
    shared['cvec'] = cv
    shared['mvals'] = f(np.broadcast_to(np.arange(-63, 65, dtype=np.float32)[None, :], (128, 128)))
    maps = []
    for k in range(NCORES):
        bs = slice(k * NB, (k + 1) * NB)
        cT = np.zeros((128, 8, 4), np.float32)
        for m in range(NB):
            cT[:, :, m] = pk(c[k * NB + m])
        cT[:, :, 2] = pk(c_ctx)
        d = dict(shared)
        d['xT'] = np.ascontiguousarray(x[bs].transpose(0, 2, 1))
        d['cxT'] = np.ascontiguousarray(ctx[bs].transpose(0, 2, 1))
        d['cT'] = cT
        maps.append(d)
    return maps


_CACHE = {}


def kernel(**inputs):
    maps = pack_inputs(inputs)
    if 'nc' not in _CACHE:
        _CACHE['nc'] = build_program()[0]
    nc = _CACHE['nc']
    res = run_bass_kernel_spmd(nc, maps, core_ids=list(range(NCORES)))
    outs = [r['yT'] for r in res.results]
    y = np.concatenate(outs, axis=0)
    return np.ascontiguousarray(y.transpose(0, 2, 1)).astype(np.float32)
```

```python
import math
from contextlib import ExitStack
import numpy as np
import concourse.bass as bass
import concourse.mybir as mybir
from concourse.bass_utils import run_bass_kernel_spmd

F32 = mybir.dt.float32
BF16 = mybir.dt.bfloat16
AF = mybir.ActivationFunctionType
ALU = mybir.AluOpType

D = 1024
SEQ = 4096
CTX = 256
DEPTH = 2
NB = 2
ALPHA = (2.0 * DEPTH) ** 0.25
LN_EPS = 1e-5
RG_C = 8.0
NCORES = 8
OPT_PREP = False
OPT_1B = True
OPT_3 = True

VEC_NAMES = [('ln1_g', 8), ('ln1_b', 8), ('ln2_g', 8), ('ln2_b', 8), ('conv_w0', 8), ('conv_w1', 8),
             ('conv_w2', 8), ('conv_w3', 8), ('conv_b', 8), ('lam0', 8), ('lam1', 8), ('ba0', 8),
             ('ba1', 8), ('bi0', 8), ('bi1', 8), ('glu_b', 8), ('b_out', 8), ('b2', 8), ('b1', 32),
             ('s5d', 8)]
VOFF = {}
_o = 0
for _n, _c in VEC_NAMES:
    VOFF[_n] = _o
    _o += _c
NV = _o


class Op:
    __slots__ = ('eng', 'fn', 'deps', 'ddeps', 'signal', 'sem', 'val', 'is_dma')


class Prog:
    def __init__(self, nc, es):
        self.nc = nc
        self.engs = {'pe': nc.tensor, 'act': nc.scalar, 'dve': nc.vector, 'pool': nc.gpsimd, 'sp': nc.sync}
        self.sem = {e: es.enter_context(nc.semaphore('s_' + e)) for e in ('pe', 'act', 'dve', 'pool')}
        self.cnt = {e: 0 for e in self.sem}
        self.es = es
        self.dsem = {}
        self.dcum = {}
        self.nops = 0
        self.free = []
        self.free_sw = []
        self.dkind = {}
        self.nsem = 0
        self.barrier = []
        self._reset()

    def _reset(self):
        self.ops = {e: [] for e in self.engs}
        self.last_w = {}
        self.readers = {}
        self.swq = []

    def add(self, eng, fn, reads=(), writes=(), dma_key=None, signal=False):
        op = Op()
        op.eng = eng
        op.fn = fn
        op.signal = signal
        op.is_dma = dma_key is not None
        op.sem = None
        op.val = None
        deps = []
        for k in reads:
            w = self.last_w.get(k)
            if w is not None:
                deps.append(w)
        for k in writes:
            w = self.last_w.get(k)
            if w is not None:
                deps.append(w)
            deps.extend(self.readers.get(k, {}).values())
        cdeps = []
        ddeps = {}
        seen = set()
        for d in deps:
            if id(d) in seen:
                continue
            seen.add(id(d))
            if d.is_dma:
                ddeps[d.sem] = self.dcum[d.sem]
            else:
                if d.eng == 'pe' and eng == 'pe':
                    continue
                cdeps.append(d)
        op.deps = cdeps
        op.ddeps = ddeps
        if op.is_dma:
            if dma_key not in self.dsem:
                fl = self.free_sw if eng == 'pool' else self.free
                self.dkind[dma_key] = eng == 'pool'
                if fl:
                    h, c0 = fl.pop()
                else:
                    self.nsem += 1
                    h, c0 = self.es.enter_context(self.nc.semaphore('d_' + str(self.nsem))), 0
                self.dsem[dma_key] = h
                self.dcum[dma_key] = c0
            self.dcum[dma_key] += 16
            op.sem = dma_key
            op.val = self.dcum[dma_key]
            assert op.val < 60000, dma_key
        rk = ('d', dma_key) if op.is_dma else eng
        for k in reads:
            self.readers.setdefault(k, {})[rk] = op
        for k in writes:
            self.last_w[k] = op
            self.readers[k] = {}
        self.ops[eng].append(op)
        self.nops += 1
        return op

    def flush(self):
        nc = self.nc
        barrier = self.barrier
        for e, lst in self.ops.items():
            for op in lst:
                for d in op.deps:
                    d.signal = True
            for op in reversed(lst):
                if not op.is_dma and op.fn is not None:
                    op.signal = True
                    break
        for e, lst in self.ops.items():
            if e not in self.sem:
                continue
            for op in lst:
                if op.signal and not op.is_dma and op.fn is not None:
                    self.cnt[e] += 1
                    op.sem = e
                    op.val = self.cnt[e]
                    assert op.val < 60000
        ops = self.ops
        sem = self.sem
        dsem = self.dsem

        def emit(ename, eobj):
            waited = {}
            for s, v in barrier:
                eobj.wait_ge(s, v)
            for op in ops[ename]:
                for d in op.deps:
                    s = sem[d.sem]
                    if waited.get(d.sem, 0) < d.val:
                        eobj.wait_ge(s, d.val)
                        waited[d.sem] = d.val
                for k, v in op.ddeps.items():
                    if waited.get(k, 0) < v:
                        eobj.wait_ge(dsem[k], v)
                        waited[k] = v
                if op.fn is None:
                    continue
                ins = op.fn(eobj)
                if op.is_dma:
                    ins.then_inc(dsem[op.sem], 16)
                elif op.signal:
                    ins.then_inc(sem[op.sem], 1)

        with nc.Block() as block:
            @block.tensor
            def _(t):
                emit('pe', t)

            @block.scalar
            def _(s):
                emit('act', s)

            @block.vector
            def _(v):
                emit('dve', v)

            @block.gpsimd
            def _(g):
                emit('pool', g)

            @block.sync
            def _(sp):
                emit('sp', sp)
        self._reset()
        self.barrier = [(self.sem[e], self.cnt[e]) for e in self.sem if self.cnt[e] > 0]
        self.barrier += [(self.dsem[k], self.dcum[k]) for k in self.dsem if self.dcum[k] > 0]
        for k in self.dsem:
            (self.free_sw if self.dkind[k] else self.free).append((self.dsem[k], self.dcum[k]))
        self.dsem = {}
        self.dcum = {}
        self.dkind = {}

    def final_wait(self):
        nc = self.nc
        items = list(self.barrier)
        with nc.Block() as block:
            @block.sync
            def _(sp):
                for s, v in items:
                    sp.wait_ge(s, v)

    def dma(self, q, out, in_, key, reads=(), writes=(), **kw):
        return self.add(q, lambda e, o=out, i=in_, kw=kw: e.dma_start(out=o, in_=i, **kw), reads, writes, dma_key=key)

    def tr(self, out, in_, ident, reads=(), writes=()):
        return self.add('pe', lambda e, o=out, i=in_, d=ident: e.transpose(o, i, d), reads, writes)

    def mm(self, out, lhsT, rhs, start, stop, reads=(), writes=(), signal=False):
        return self.add('pe', lambda e, o=out, l=lhsT, r=rhs, s=start, t=stop: e.matmul(o, lhsT=l, rhs=r, start=s, stop=t),
                        reads, writes, signal=signal)

    def act(self, out, in_, func, bias=None, scale=1.0, reads=(), writes=()):
        if bias is None:
            f = lambda e, o=out, i=in_, fu=func, sc=scale: e.activation(out=o, in_=i, func=fu, scale=sc)
        else:
            f = lambda e, o=out, i=in_, fu=func, b=bias, sc=scale: e.activation(out=o, in_=i, func=fu, bias=b, scale=sc)
        return self.add('act', f, reads, writes)

    def ts(self, eng, out, in0, s1, s2, op0, op1=None, reads=(), writes=()):
        if op1 is None:
            f = lambda e, o=out, i=in0, a=s1, p0=op0: e.tensor_scalar(out=o, in0=i, scalar1=a, scalar2=None, op0=p0)
        else:
            f = lambda e, o=out, i=in0, a=s1, b=s2, p0=op0, p1=op1: e.tensor_scalar(out=o, in0=i, scalar1=a, scalar2=b, op0=p0, op1=p1)
        return self.add(eng, f, reads, writes)

    def tt(self, eng, out, in0, in1, op, reads=(), writes=()):
        return self.add(eng, lambda e, o=out, a=in0, b=in1, p=op: e.tensor_tensor(out=o, in0=a, in1=b, op=p), reads, writes)

    def stt(self, out, in0, scalar, in1, op0, op1, reads=(), writes=()):
        return self.add('dve', lambda e, o=out, a=in0, s=scalar, b=in1, p0=op0, p1=op1:
                        e.scalar_tensor_tensor(out=o, in0=a, scalar=s, in1=b, op0=p0, op1=p1), reads, writes)

    def copy(self, eng, out, in_, reads=(), writes=()):
        if eng == 'act':
            return self.add('act', lambda e, o=out, i=in_: e.activation(out=o, in_=i, func=AF.Copy), reads, writes)
        return self.add(eng, lambda e, o=out, i=in_: e.tensor_copy(out=o, in_=i), reads, writes)

    def scan(self, out, d0, d1, init, reads=(), writes=()):
        return self.add('dve', lambda e, o=out, a=d0, b=d1, i=init:
                        e.tensor_tensor_scan(out=o, data0=a, data1=b, initial=i, op0=ALU.mult, op1=ALU.add), reads, writes)

    def memset(self, eng, out, val, reads=(), writes=()):
        return self.add(eng, lambda e, o=out, v=val: e.memset(o, v), reads, writes)


class Ctx:
    pass


def build_program(debug=(), stages=None):
    nc = bass.Bass("TRN2", target_bir_lowering=False)
    G = Ctx()
    G.nc = nc
    G.uid = [0]

    def sbt(name, shape, dt):
        G.uid[0] += 1
        return nc.sbuf_tensor('%s_u%d' % (name, G.uid[0]), shape, dt)
    G.sbt = sbt
    G.debug = set(debug)
    ges = ExitStack()
    G.ges = ges
    P = Prog(nc, ges)
    G.P = P

    def din(name, shape, dt=F32):
        return nc.dram_tensor(name, list(shape), dt, kind="ExternalInput").ap()

    def dscr(name, shape, dt):
        kind = "ExternalOutput" if name in G.debug else "Internal"
        return nc.dram_tensor(name, list(shape), dt, kind=kind).ap()

    G.dscr = dscr
    I = {}
    I['xT'] = din('xT', [NB, D, SEQ])
    I['cxT'] = din('cxT', [NB, D, CTX])
    I['cT'] = din('cT', [128, 8, 4])
    I['ada_w'] = din('ada_w', [DEPTH, D, 6 * D])
    I['ada_bP'] = din('ada_bP', [DEPTH, 128, 48])
    I['vecs'] = din('vecs', [DEPTH, 128, NV])
    I['w_in'] = din('w_in', [DEPTH, D, 3 * D])
    I['rg_wa'] = din('rg_wa', [DEPTH, 2, 8, 128, 128])
    I['rg_wi'] = din('rg_wi', [DEPTH, 2, 8, 128, 128])
    I['glu_w'] = din('glu_w', [DEPTH, D, D])
    I['w_out'] = din('w_out', [DEPTH, 2 * D, D])
    I['w1'] = din('w1', [DEPTH, D, 4 * D])
    I['w2'] = din('w2', [DEPTH, 4 * D, D])
    I['ident'] = din('ident', [128, 128])
    I['s5a'] = din('s5a', [DEPTH, 128, 2, 64])
    I['s5ldt'] = din('s5ldt', [DEPTH, 128, 64])
    I['s5B'] = din('s5B', [DEPTH, 128, 2, 64, 16])
    I['s5C'] = din('s5C', [DEPTH, 128, 2, 64, 16])
    I['s5DP'] = din('s5DP', [DEPTH, 128, 64])
    I['cmask'] = din('cmask', [128, 2, 4, 4, 128])
    I['cvec'] = din('cvec', [128, 4])
    I['mvals'] = din('mvals', [128, 128])
    G.I = I
    G.yT = nc.dram_tensor('yT', [NB, D, SEQ], F32, kind="ExternalOutput").ap()

    G.PS = [ges.enter_context(nc.psum_tensor('ps%d' % i, [128, 512], F32)) for i in range(8)]
    G.MOD = ges.enter_context(G.sbt('MOD', [128, DEPTH, 48, 4], F32))
    G.MP1 = ges.enter_context(G.sbt('MP1', [128, DEPTH, 2, 8, 4], F32))
    G.VEC = ges.enter_context(G.sbt('VEC', [128, DEPTH, NV], F32))
    G.ONE = ges.enter_context(G.sbt('ONE', [128, 1], F32))
    G.EPS = ges.enter_context(G.sbt('EPS', [128, 1], F32))

    S = {}
    for b in range(NB):
        S['rgx', b] = dscr('rgx%d' % b, [D, CTX + SEQ], BF16)
        S['gg', b] = dscr('gg%d' % b, [D, CTX + SEQ], BF16)
        S['s5u', b] = dscr('s5u%d' % b, [D, CTX + SEQ], BF16)
        S['rg', b] = dscr('rg%d' % b, [D, CTX + SEQ], BF16)
        S['s5uc', b] = dscr('s5uc%d' % b, [D, CTX], BF16)
        S['y5', b] = dscr('y5_%d' % b, [D, CTX + SEQ], BF16)
    for kind, n in (('lat', SEQ), ('ctx', CTX)):
        S['x', kind] = dscr('xs_' + kind, [NB, D, n], F32)
        S['x1', kind] = dscr('x1_' + kind, [NB, D, n], F32)
    for l in range(DEPTH):
        S['WT', l] = dscr('WT%d' % l, [64, 128, 16, 128], BF16)
        S['WS', l] = dscr('WS%d' % l, [64, 128, 2, 8, 128], BF16)
        S['WO', l] = dscr('WO%d' % l, [64, 128, 2, 8, 128], BF16)
        S['L64', l] = dscr('L64_%d' % l, [128, 2, 64], F32)
    G.S = S

    G.stages = stages
    run_stages(G)
    P.final_wait()
    return nc, G


def run_stages(G):
    st = G.stages

    def on(name):
        return st is None or name in st
    if on('mod'):
        stage_mod(G)
    if 'moddbg' in G.debug:
        d = G.dscr('moddbg', [128, DEPTH * 48 * 4], F32)
        G.P.dma('sp', d, G.MOD[:].rearrange('p l c m -> p (l c m)'), 'moddbg', reads=['MOD'])
        G.P.flush()
    for l in range(DEPTH):
        if on('prep%d' % l):
            stage_s5prep(G, l)
        if on('1a%d' % l):
            stage_1a(G, l)
        if on('1b%d' % l):
            stage_1b(G, l)
        if on('2_%d' % l):
            stage_2(G, l)
        if on('3a%d' % l):
            stage_3a(G, l)
        if on('3b%d' % l):
            stage_3b(G, l)


def vec(G, l, name, k=0):
    c = VOFF[name] + k
    return G.VEC[:, l, c:c + 1]


def modv(G, l, j, k, m):
    return G.MOD[:, l, j * 8 + k, m:m + 1]


def stage_mod(G):
    nc, P, I = G.nc, G.P, G.I
    with ExitStack() as es:
        cin = es.enter_context(G.sbt('cin', [128, 8, 4], F32))
        sc = es.enter_context(G.sbt('sc', [128, 8, 4], F32))
        abp = es.enter_context(G.sbt('abp', [128, DEPTH, 48], F32))
        wt = [es.enter_context(G.sbt('adaw%d' % i, [128, 8, 512], F32)) for i in range(2)]
        P.dma('sp', cin[:], I['cT'], 'cin', writes=['cin'])
        P.dma('sp', abp[:], I['ada_bP'].rearrange('l p c -> p l c'), 'abp', writes=['abp'])
        P.dma('sp', G.VEC[:], I['vecs'].rearrange('l p c -> p l c'), 'VEC', writes=['VEC'])
        P.memset('dve', G.ONE[:], 1.0, writes=['ONE'])
        P.memset('dve', G.EPS[:], LN_EPS, writes=['EPS'])
        P.act(sc[:], cin[:], AF.Silu, reads=['cin'], writes=['sc'])
        it = 0
        for l in range(DEPTH):
            for c4 in range(12):
                slot = it % 2
                it += 1
                P.dma('sp', wt[slot][:], I['ada_w'][l, :, c4 * 512:(c4 + 1) * 512].rearrange('(k p) c -> p k c', p=128),
                      ('adaw', slot), writes=[('adaw', slot)])
                ps = G.PS[c4 % 2]
                for cj in range(4):
                    cc = c4 * 4 + cj
                    for k in range(8):
                        P.mm(ps[:, cj * 4:cj * 4 + 4], wt[slot][:, k, cj * 128:(cj + 1) * 128], sc[:, k, :],
                             start=(k == 0), stop=(k == 7), reads=[('adaw', slot), 'sc'], writes=[('ps', c4 % 2)])
                for cj in range(4):
                    cc = c4 * 4 + cj
                    P.ts('dve', G.MOD[:, l, cc, :], ps[:, cj * 4:cj * 4 + 4], abp[:, l, cc:cc + 1], None, ALU.add,
                         reads=[('ps', c4 % 2), 'abp'], writes=['MOD'])
            for s, j in ((0, 1), (1, 4)):
                P.ts('dve', G.MP1[:, l, s, :, :], G.MOD[:, l, j * 8:(j + 1) * 8, :], 1.0, None, ALU.add,
                     reads=['MOD'], writes=['MP1'])
        P.flush()


def seg_list():
    segs = [('ctx', 0, CTX)]
    for t in range(SEQ // 512):
        segs.append(('lat', t * 512, 512))
    return segs


def stage_1a(G, l):
    nc, P, I, S = G.nc, G.P, G.I, G.S
    with ExitStack() as es:
        W = es.enter_context(G.sbt('w_in_b', [128, 8, 3 * D], BF16))
        xin = [es.enter_context(G.sbt('xin%d' % i, [128, 8, 512], F32)) for i in range(2)]
        xm = [es.enter_context(G.sbt('xm%d' % i, [128, 8, 512], BF16)) for i in range(2)]
        st = [[es.enter_context(G.sbt('st%d_%d' % (j, i), [128, 8, 512], BF16)) for i in range(2)] for j in range(3)]
        stc = es.enter_context(G.sbt('stc', [128, 8, CTX], BF16))
        for k in range(8):
            for h in range(2):
                P.dma('pool', W[:, k, h * 1536:(h + 1) * 1536], I['w_in'][l, k * 128:(k + 1) * 128, h * 1536:(h + 1) * 1536],
                      'W1a', writes=[('W', k)])
        it = 0
        psi = 0
        for b in range(NB):
            for (kind, t0, n) in seg_list():
                slot = it % 2
                it += 1
                m = 2 if kind == 'ctx' else b
                src = I['cxT'] if kind == 'ctx' else I['xT']
                if l > 0:
                    src = G.S['x', kind]
                off = t0 if kind == 'ctx' else CTX + t0
                P.dma('sp', xin[slot][:, :, 0:n], src[b, :, t0:t0 + n].rearrange('(k p) t -> p k t', p=128),
                      ('xin', slot), writes=[('xin', slot)])
                for k in range(8):
                    P.ts('dve', xm[slot][:, k, 0:n], xin[slot][:, k, 0:n], G.MP1[:, l, 0, k, m:m + 1], modv(G, l, 0, k, m),
                         ALU.mult, ALU.add, reads=[('xin', slot), 'MP1', 'MOD'], writes=[('xm', slot)])
                for oc in range(24):
                    pb = psi % 4
                    psi += 1
                    ps = G.PS[pb]
                    for k in range(8):
                        P.mm(ps[:, 0:n], W[:, k, oc * 128:(oc + 1) * 128], xm[slot][:, k, 0:n], start=(k == 0), stop=(k == 7),
                             reads=[('W', k), ('xm', slot)], writes=[('ps', pb)])
                    j, o8 = oc // 8, oc % 8
                    dst = st[j][slot][:, o8, 0:n]
                    if j == 1:
                        P.act(dst, ps[:, 0:n], AF.Gelu_apprx_tanh, reads=[('ps', pb)], writes=[('st', j, slot)])
                    elif oc % 3 == 0:
                        P.copy('act', dst, ps[:, 0:n], reads=[('ps', pb)], writes=[('st', j, slot)])
                    else:
                        P.copy('dve', dst, ps[:, 0:n], reads=[('ps', pb)], writes=[('st', j, slot)])
                    if j == 2 and kind == 'ctx':
                        for cc in range(4):
                            P.copy('dve', stc[:, o8, :].rearrange('p (i c ip) -> p i c ip', i=8, c=4)[:, :, cc, :],
                                   ps[:, cc * 64:(cc + 1) * 64].rearrange('p (i ip) -> p i ip', i=8), reads=[('ps', pb)], writes=['stc'])
                if kind == 'ctx':
                    P.dma('sp', S['s5uc', b].rearrange('(k p) t -> p k t', p=128), stc[:], 'stc', reads=['stc'])
                for j, nm in enumerate(('rgx', 'gg', 's5u')):
                    P.dma('sp', S[nm, b][:, off:off + n].rearrange('(k p) t -> p k t', p=128), st[j][slot][:, :, 0:n],
                          ('st', j, slot), reads=[('st', j, slot)], writes=[('dram', nm, b)])
        P.flush()


MAGIC = 12582912.0
TWO_PI = 2.0 * math.pi


def stage_s5prep(G, l):
    nc, P, I, S = G.nc, G.P, G.I, G.S
    GB = 4
    with ExitStack() as es:
        def sb(name, shape, dt=F32):
            return es.enter_context(G.sbt(name, shape, dt))
        A_ = sb('s5a_t', [128, 2, 64])
        ldt = sb('ldt', [128, 64])
        Bin = sb('Bin', [128, 2, 64, 16])
        Cin = sb('Cin', [128, 2, 64, 16])
        DP = sb('DP', [128, 64])
        cmask = sb('cmask_t', [128, 2, 4, 4, 128])
        cvec = sb('cvec_t', [128, 4])
        mvals = sb('mvals_t', [128, 128])
        identf = sb('identf', [128, 128])
        Z = sb('Z', [128, 2, 64])
        ZP = sb('ZP', [128, 2, 64])
        L1 = sb('L1', [128, 2, 64])
        L64 = sb('L64t', [128, 2, 64])
        KS = sb('KS', [128, 2, 64])
        KO = sb('KO', [128, 2, 64])
        Q = sb('Q', [128, 2, 64])
        sm = [sb('sm%d' % i, [128, 64]) for i in range(6)]
        Bbar = sb('Bbar', [128, 2, 64, 16])
        BS = sb('BS', [128, 2, 64, 16])
        CO = sb('CO', [128, 2, 64, 16])
        CinN = sb('CinN', [128, 2, 64, 16])
        CON = sb('CON', [128, 2, 64, 16])
        tmpb = [sb("tmpb%d" % i, [128, 1024]) for i in range(8)]
        tab = sb('tab', [128, 2, GB, 128])
        BmR = sb('BmR', [128, 4, GB, 128], BF16)
        BmI = sb('BmI', [128, 4, GB, 128], BF16)
        AR = sb('AR', [128, 4, GB, 128], BF16)
        AIn = sb('AIn', [128, 4, GB, 128], BF16)
        WSP = sb('WSP', [128, 2, GB, 128])
        MZ1 = sb('MZ1', [128, GB, 128])
        Tout = sb('Tout', [128, GB, 16, 128], BF16)
        WSst = sb('WSst', [128, GB, 2, 8, 128], BF16)
        WOst = sb('WOst', [128, GB, 2, 8, 128], BF16)
        t1 = [sb('t1_%d' % i, [128, 512]) for i in range(2)]
        t2 = [sb('t2_%d' % i, [128, 512]) for i in range(2)]

        for t, nm in ((A_, 's5a'), (ldt, 's5ldt'), (Bin, 's5B'), (Cin, 's5C'), (DP, 's5DP')):
            P.dma('sp', t[:], I[nm][l], 'pl_' + nm, writes=[nm])
        for t, nm in ((cmask, 'cmask'), (cvec, 'cvec'), (mvals, 'mvals'), (identf, 'ident')):
            P.dma('sp', t[:], I[nm], 'pl_' + nm, writes=[nm])

        uid = [0]

        def key():
            uid[0] += 1
            return ('k', uid[0])

        def cexp(o_re, o_im, zr, zi, n, rk, wk, neg_im=False):
            tm = [t[:, 0:n] for t in tmpb]
            k0, k1, k2, k3, k4, k5 = [key() for _ in range(6)]
            P.act(tm[0], zr, AF.Exp, reads=rk, writes=[('tm', 0)])
            P.ts('dve', tm[1], zi, 1.0 / TWO_PI, MAGIC, ALU.mult, ALU.add, reads=rk, writes=[('tm', 1)])
            P.ts('dve', tm[2], tm[1], MAGIC, None, ALU.subtract, reads=[('tm', 1)], writes=[('tm', 2)])
            P.stt(tm[3], zi, 1.0 / TWO_PI, tm[2], ALU.mult, ALU.subtract, reads=rk + [('tm', 2)], writes=[('tm', 3)])
            P.act(tm[4], tm[3], AF.Sin, scale=TWO_PI, reads=[('tm', 3)], writes=[('tm', 4)])
            P.ts('pool', tm[5], zi, 1.0 / TWO_PI, 0.25, ALU.mult, ALU.add, reads=rk, writes=[('tm', 5)])
            P.ts('dve', tm[1], tm[5], MAGIC, None, ALU.add, reads=[('tm', 5)], writes=[('tm', 1)])
            P.ts('dve', tm[2], tm[1], MAGIC, None, ALU.subtract, reads=[('tm', 1)], writes=[('tm', 2)])
            P.tt('dve', tm[3], tm[5], tm[2], ALU.subtract, reads=[('tm', 5), ('tm', 2)], writes=[('tm', 3)])
            P.act(tm[6], tm[3], AF.Sin, scale=TWO_PI, reads=[('tm', 3)], writes=[('tm', 6)])
            P.tt('dve', o_re, tm[0], tm[6], ALU.mult, reads=[('tm', 0), ('tm', 6)], writes=wk)
            if neg_im:
                P.stt(o_im, tm[0], -1.0, tm[4], ALU.mult, ALU.mult, reads=[('tm', 0), ('tm', 4)], writes=wk)
            else:
                P.tt('pool', o_im, tm[0], tm[4], ALU.mult, reads=[('tm', 0), ('tm', 4)], writes=wk)

        def cmul(o_re, o_im, ar, ai, br, bi, n_shape, rk, wk, nb=None):
            sh = n_shape
            n = 1
            for v in sh[1:]:
                n *= v

            def tv(i):
                a = tmpb[i][:, 0:n]
                if len(sh) == 3:
                    return a.rearrange('p (a b) -> p a b', a=sh[1])
                if len(sh) == 4:
                    return a.rearrange('p (a b c) -> p a b c', a=sh[1], b=sh[2])
                return a
            ibr, ibi = (br, bi) if nb is None else nb
            P.tt('dve', tv(0), ar, br, ALU.mult, reads=rk, writes=[('tm', 0)])
            P.tt('dve', tv(1), ai, bi, ALU.mult, reads=rk, writes=[('tm', 1)])
            P.tt('dve', o_re, tv(0), tv(1), ALU.subtract, reads=[('tm', 0), ('tm', 1)], writes=wk)
            P.tt('dve' if OPT_PREP else 'pool', tv(2), ar, ibi, ALU.mult, reads=rk, writes=[('tm', 2)])
            P.tt('pool', tv(3), ai, ibr, ALU.mult, reads=rk, writes=[('tm', 3)])
            P.tt('pool', o_im, tv(2), tv(3), ALU.add, reads=[('tm', 2), ('tm', 3)], writes=wk)

        P.act(sm[0][:], ldt[:], AF.Exp, reads=['s5ldt'], writes=['dt'])
        for c in range(2):
            P.tt('dve', Z[:, c, :], A_[:, c, :], sm[0][:], ALU.mult, reads=['s5a', 'dt'], writes=['Z'])
            P.ts('dve', ZP[:, c, :], Z[:, c, :], cvec[:, 0:1], None, ALU.mult, reads=['Z', 'cvec'], writes=['ZP'])
        cexp(L1[:, 0, :], L1[:, 1, :], Z[:, 0, :], Z[:, 1, :], 64, ['Z'], ['L1'])
        for (dst, col, nm) in ((L64, None, 'L64'), (KS, 1, 'KS'), (KO, 2, 'KO')):
            for c in range(2):
                if col is None:
                    P.ts('dve', sm[1 + c][:], Z[:, c, :], 64.0, None, ALU.mult, reads=['Z'], writes=[('zs', c)])
                else:
                    P.ts('dve', sm[1 + c][:], Z[:, c, :], cvec[:, col:col + 1], None, ALU.mult, reads=['Z', 'cvec'], writes=[('zs', c)])
            cexp(dst[:, 0, :], dst[:, 1, :], sm[1][:], sm[2][:], 64, [('zs', 0), ('zs', 1)], [nm])
        P.dma('sp', S['L64', l], L64[:], 'L64st', reads=['L64'])
        P.ts('dve', sm[1][:], L1[:, 0, :], -1.0, None, ALU.add, reads=['L1'], writes=['numr'])
        P.tt('dve', sm[2][:], A_[:, 0, :], A_[:, 0, :], ALU.mult, reads=['s5a'], writes=['d1'])
        P.tt('dve', sm[3][:], A_[:, 1, :], A_[:, 1, :], ALU.mult, reads=['s5a'], writes=['d2'])
        P.tt('dve', sm[2][:], sm[2][:], sm[3][:], ALU.add, reads=['d1', 'd2'], writes=['den'])
        P.add('dve', lambda e: e.reciprocal(out=sm[3][:], in_=sm[2][:]), reads=['den'], writes=['rden'])
        P.tt('dve', sm[4][:], sm[1][:], A_[:, 0, :], ALU.mult, reads=['numr', 's5a'], writes=['q1'])
        P.tt('dve', sm[5][:], L1[:, 1, :], A_[:, 1, :], ALU.mult, reads=['L1', 's5a'], writes=['q2'])
        P.tt('dve', sm[4][:], sm[4][:], sm[5][:], ALU.add, reads=['q1', 'q2'], writes=['q3'])
        P.tt('dve', Q[:, 0, :], sm[4][:], sm[3][:], ALU.mult, reads=['q3', 'rden'], writes=['Qr'])
        P.tt('dve', sm[4][:], L1[:, 1, :], A_[:, 0, :], ALU.mult, reads=['L1', 's5a', 'Qr'], writes=['q4'])
        P.tt('dve', sm[5][:], sm[1][:], A_[:, 1, :], ALU.mult, reads=['numr', 's5a', 'Qr'], writes=['q5'])
        P.tt('dve', sm[4][:], sm[4][:], sm[5][:], ALU.subtract, reads=['q4', 'q5'], writes=['q6'])
        P.tt('dve', Q[:, 1, :], sm[4][:], sm[3][:], ALU.mult, reads=['q6', 'rden'], writes=['Qi'])

        def bc(ap2, n):
            return ap2.unsqueeze(2).to_broadcast([128, 64, n])
        cmul(Bbar[:, 0], Bbar[:, 1], bc(Q[:, 0, :], 16), bc(Q[:, 1, :], 16), Bin[:, 0], Bin[:, 1], [128, 64, 16],
             ['Qr', 'Qi', 's5B'], ['Bbar'])
        cmul(BS[:, 0], BS[:, 1], bc(KS[:, 0, :], 16), bc(KS[:, 1, :], 16), Bbar[:, 0], Bbar[:, 1], [128, 64, 16],
             ['KS', 'Bbar'], ['BS'])
        cmul(CO[:, 0], CO[:, 1], bc(KO[:, 0, :], 16), bc(KO[:, 1, :], 16), Cin[:, 0], Cin[:, 1], [128, 64, 16],
             ['KO', 's5C'], ['CO'])
        P.ts('dve', CinN[:].rearrange('p c g h -> p (c g h)'), Cin[:].rearrange('p c g h -> p (c g h)'), -1.0, None, ALU.mult,
             reads=['s5C'], writes=['CinN'])
        P.ts('dve', CON[:].rearrange('p c g h -> p (c g h)'), CO[:].rearrange('p c g h -> p (c g h)'), -1.0, None, ALU.mult,
             reads=['CO'], writes=['CON'])

        psi = 0
        for gb in range(64 // GB):
            g0 = gb * GB
            gs = slice(g0, g0 + GB)
            for c in range(2):
                P.tt('dve', tmpb[7][:, 0:GB * 128].rearrange('p (a b) -> p a b', a=GB) if c == 0 else
                     MZ1[:],
                     ZP[:, c, gs].unsqueeze(2).to_broadcast([128, GB, 128]),
                     mvals[:].unsqueeze(1).to_broadcast([128, GB, 128]), ALU.mult,
                     reads=['ZP', 'mvals'], writes=[('mz', c)])
            cexp(tab[:, 0].rearrange('p a b -> p (a b)'), tab[:, 1].rearrange('p a b -> p (a b)'),
                 tmpb[7][:, 0:GB * 128], MZ1[:].rearrange('p a b -> p (a b)'), GB * 128,
                 [('mz', 0), ('mz', 1)], ['tab'])

            def tabv(c, start, step, n_inner):
                if step > 0:
                    v = tab[:, c, :, start:start + 7 * step + 1:step]
                else:
                    stop = start + 7 * step - 1
                    v = tab[:, c, :, start:(stop if stop >= 0 else None):step]
                return v.unsqueeze(3).to_broadcast([128, GB, 8, n_inner])

            def vecv(t, c):
                return t[:, c, gs, :].unsqueeze(2).to_broadcast([128, GB, 8, 16])

            def o4(t):
                return t.rearrange('p g (a b) -> p g a b', a=8)
            for d1 in range(4):
                cmul(o4(BmR[:, d1]), o4(BmI[:, d1]), tabv(0, 63 + d1, -8, 16), tabv(1, 63 + d1, -8, 16), vecv(Bbar, 0), vecv(Bbar, 1),
                     [128, GB, 8, 16], ['tab', 'Bbar'], ['Bm'])
            for d2 in range(4):
                cmul(o4(AR[:, d2]), o4(AIn[:, d2]), tabv(0, 63 + 4 * (d2 - 2), 8, 16), tabv(1, 63 + 4 * (d2 - 2), 8, 16),
                     vecv(Cin, 0), vecv(Cin, 1), [128, GB, 8, 16], ['tab', 's5C', 'CinN'], ['Am'], nb=(vecv(CinN, 0), vecv(CinN, 1)))
            for gl in range(GB):
                for d1 in range(4):
                    pa, pb = psi % 8, (psi + 1) % 8
                    psi += 2
                    for (pp, h0) in ((pa, 0), (pb, 64)):
                        hs = slice(h0, h0 + 64)
                        out = G.PS[pp][:].rearrange('p (a b) -> p a b', a=4)
                        P.mm(out, BmR[hs, d1, gl, :], AR[hs, :, gl, :], start=True, stop=False, reads=['Bm', 'Am'], writes=[('ps', pp)])
                        P.mm(out, BmI[hs, d1, gl, :], AIn[hs, :, gl, :], start=False, stop=True, reads=['Bm', 'Am'], writes=[('ps', pp)])
                    ts_ = (gl * 4 + d1) % 2
                    P.tt('dve', t1[ts_][:], G.PS[pa][:], cmask[:, 0, d1].rearrange('p a b -> p (a b)'), ALU.mult,
                         reads=[('ps', pa), 'cmask'], writes=[('t1', ts_)])
                    if d1 == 0:
                        P.stt(t1[ts_][:, 256:384], identf[:], DP[:, g0 + gl:g0 + gl + 1], t1[ts_][:, 256:384], ALU.mult, ALU.add,
                              reads=[('t1', ts_), 'ident', 's5DP'], writes=[('t1', ts_)])
                    P.tt('dve', t2[ts_][:], G.PS[pb][:], cmask[:, 1, d1].rearrange('p a b -> p (a b)'), ALU.mult,
                         reads=[('ps', pb), 'cmask'], writes=[('t2', ts_)])
                    P.tt('pool', Tout[:, gl, d1 * 4:(d1 + 1) * 4, :].rearrange('p a b -> p (a b)'), t1[ts_][:], t2[ts_][:], ALU.add,
                         reads=[('t1', ts_), ('t2', ts_)], writes=['Tout'])
            P.dma('sp', S['WT', l][gs].rearrange('g p a b -> p g a b'), Tout[:], 'Toutst', reads=['Tout'])
            for Ip in range(8):
                cmul(o4(WSP[:, 0]), o4(WSP[:, 1]), tabv(0, 126 - Ip, -8, 16), tabv(1, 126 - Ip, -8, 16), vecv(BS, 0), vecv(BS, 1),
                     [128, GB, 8, 16], ['tab', 'BS'], ['WSP'])
                for c in range(2):
                    for g4 in range(GB // 4):
                        pp = psi % 8
                        psi += 1
                        for gq in range(4):
                            gl = g4 * 4 + gq
                            P.tr(G.PS[pp][:, gq * 128:(gq + 1) * 128], WSP[:, c, gl, :], identf[:], reads=['WSP', 'ident'], writes=[('ps', pp)])
                        P.copy('act', WSst[:, g4 * 4:(g4 + 1) * 4, c, Ip, :], G.PS[pp][:].rearrange('p (a b) -> p a b', a=4),
                               reads=[('ps', pp)], writes=['WSst'])
            P.dma('sp', S['WS', l][gs].rearrange('g p c i q -> p g c i q'), WSst[:], 'WSstst', reads=['WSst'])
            for Jp in range(8):
                cmul(o4(WOst[:, :, 0, Jp, :]), o4(WOst[:, :, 1, Jp, :]), tabv(0, 64 + Jp, 8, 16), tabv(1, 64 + Jp, 8, 16),
                     vecv(CO, 0), vecv(CO, 1), [128, GB, 8, 16], ['tab', 'CO', 'CON'], ['WOst'], nb=(vecv(CON, 0), vecv(CON, 1)))
            P.dma('sp', S['WO', l][gs].rearrange('g p c j q -> p g c j q'), WOst[:], 'WOstst', reads=['WOst'])
        P.flush()


def tslot(delta):
    d1 = delta % 4
    d2 = (delta - d1) // 4
    return d1 * 4 + d2 + 2


def stage_2(G, l):
    nc, P, I, S = G.nc, G.P, G.I, G.S
    last = (l == DEPTH - 1)
    NS = 68
    import os
    CUT = int(os.environ.get('S2CUT', '9'))
    with ExitStack() as es:
        def sb(name, shape, dt=F32):
            return es.enter_context(G.sbt(name, shape, dt))
        UTl = sb('UTl', [128, 64, 512], BF16)
        UTc = sb('UTc', [128, 64, 4, 8], BF16)
        X2 = sb('X2', [128, 2, 64, NS])
        Hb = sb('Hb', [128, 2, 64, NS], BF16)
        Lt = sb('Lt', [128, 2, 64])
        L1s = sb('L1s', [128, 2, 64])
        L2s = sb('L2s', [128, 2, 64])
        tA = sb('tA', [128, 2, 64])
        tB = sb('tB', [128, 2, 64])
        WSg = [sb('WSg%d' % i, [128, 2, 8, 128], BF16) for i in range(2)]
        WTg = [sb('WTg%d' % i, [128, 16, 128], BF16) for i in range(2)]
        WOg = [sb('WOg%d' % i, [128, 2, 8, 128], BF16) for i in range(2)]
        Yst = [sb('Yst%d' % i, [128, 8, 512], BF16) for i in range(2)]
        Yc = sb('Yc', [128, 64, 4, 8], BF16)

        P.dma('sp', Lt[:], S['L64', l], 'Lt', writes=['Lt'])
        P.copy('dve', L1s[:, 0, :], Lt[:, 0, :], reads=['Lt'], writes=['L1s'])
        P.copy('dve', L1s[:, 1, :], Lt[:, 0, :], reads=['Lt'], writes=['L1s'])
        P.ts('dve', L2s[:, 0, :], Lt[:, 1, :], -1.0, None, ALU.mult, reads=['Lt'], writes=['L2s'])
        P.copy('dve', L2s[:, 1, :], Lt[:, 1, :], reads=['Lt'], writes=['L2s'])
        P.memset('pool', Hb[0:64, :, :, 64:65], 0.0, writes=['Hb0'])
        P.memset('pool', Hb[64:128, :, :, 67:68], 0.0, writes=['Hb0'])
        psi = 0
        wi = 0
        for b in range(NB if CUT >= 9 else 1):
            for i in range(8):
                P.dma('sp', UTl[i * 16:(i + 1) * 16, :, :],
                      S['s5u', b][:, CTX + 512 * i:CTX + 512 * (i + 1)].rearrange('(g h) t -> h g t', h=16),
                      ('UTl', i), writes=['UTl'])
                P.dma('sp', UTc[i * 16:(i + 1) * 16, :, :, :].rearrange('p g c j -> p g (c j)'),
                      S['s5uc', b][:, 32 * i:32 * (i + 1)].rearrange('(g h) t -> h g t', h=16),
                      ('UTc', i), writes=['UTc'])
            if CUT < 2:
                continue
            G7 = 7
            for g0 in range(0, 64, G7):
                gn = min(G7, 64 - g0)
                pbank = []
                for c in range(2):
                    pbank.append(psi % 8)
                    psi += 1
                for gl in range(gn):
                    g = g0 + gl
                    ws = wi % 2
                    wi += 1
                    P.dma('sp', WSg[ws][:], S['WS', l][g], ('WSg', ws), writes=[('WSg', ws)])
                    for c in range(2):
                        ps = G.PS[pbank[c]]
                        for Ip in range(8):
                            P.mm(ps[:, gl * NS:gl * NS + 64], WSg[ws][:, c, Ip, :], UTl[:, g, Ip * 64:(Ip + 1) * 64], start=(Ip == 0), stop=(Ip == 7),
                                 reads=[('WSg', ws), 'UTl'], writes=[('ps', pbank[c])])
                        for Ip in range(8):
                            P.mm(ps[:, gl * NS + 64:gl * NS + 68], WSg[ws][:, c, Ip, :], UTc[:, g, :, Ip], start=(Ip == 0), stop=(Ip == 7),
                                 reads=[('WSg', ws), 'UTc'], writes=[('ps', pbank[c])])
                for c in range(2):
                    ps = G.PS[pbank[c]]
                    pv = ps[:, 0:gn * NS].rearrange('p (g s) -> p g s', s=NS)
                    eng = 'dve'
                    P.copy(eng, X2[0:64, c, g0:g0 + gn, 0:4], pv[0:64, :, 64:68], reads=[('ps', pbank[c])], writes=['X2'])
                    P.copy(eng, X2[0:64, c, g0:g0 + gn, 4:68], pv[0:64, :, 0:64], reads=[('ps', pbank[c])], writes=['X2'])
                    P.copy(eng, X2[64:128, c, g0:g0 + gn, 0:4], pv[64:128, :, 67:63:-1], reads=[('ps', pbank[c])], writes=['X2'])
                    P.copy(eng, X2[64:128, c, g0:g0 + gn, 4:68], pv[64:128, :, 63::-1], reads=[('ps', pbank[c])], writes=['X2'])
            if CUT < 3:
                continue
            for s_ in range(1, NS):
                prev = X2[:, :, :, s_ - 1]
                prev_sw = X2[:, ::-1, :, s_ - 1]
                cur = X2[:, :, :, s_]
                P.tt('dve', tA[:], L1s[:], prev, ALU.mult, reads=['L1s', 'X2'], writes=['tA'])
                P.tt('dve', tB[:], L2s[:], prev_sw, ALU.mult, reads=['L2s', 'X2'], writes=['tB'])
                P.tt('dve', cur, cur, tA[:], ALU.add, reads=['X2', 'tA'], writes=['X2'])
                P.tt('dve', cur, cur, tB[:], ALU.add, reads=['X2', 'tB'], writes=['X2'])
            for c in range(2):
                P.copy('dve', Hb[0:64, c, :, 0:64], X2[0:64, c, :, 3:67], reads=['X2'], writes=['Hb'])
                P.copy('dve', Hb[0:64, c, :, 65:68], X2[0:64, c, :, 0:3], reads=['X2'], writes=['Hb'])
                P.copy('dve', Hb[64:128, c, :, 0:64], X2[64:128, c, :, 66:2:-1], reads=['X2'], writes=['Hb'])
                P.copy('dve', Hb[64:128, c, :, 64:67], X2[64:128, c, :, 2::-1], reads=['X2'], writes=['Hb'])
            if CUT < 4:
                continue
            for g in range(64 if CUT >= 5 else 8):
                ws = wi % 2
                wi += 1
                P.dma('sp', WTg[ws][:], S['WT', l][g], ('WTg', ws), writes=[('WTg', ws)])
                P.dma('sp', WOg[ws][:], S['WO', l][g], ('WOg', ws), writes=[('WOg', ws)])
                pb = psi % 8
                psi += 1
                ps = G.PS[pb]
                rk = [('WTg', ws), ('WOg', ws), 'UTl', 'UTc', 'Hb', 'Hb0']
                for Jp in range(8):
                    out = ps[:, Jp * 64:(Jp + 1) * 64]
                    for Ip in range(8):
                        P.mm(out, WTg[ws][:, tslot(Jp - Ip), :], UTl[:, g, Ip * 64:(Ip + 1) * 64], start=(Ip == 0), stop=False,
                             reads=rk, writes=[('ps', pb)])
                    P.mm(out, WOg[ws][:, 0, Jp, :], Hb[:, 0, g, 0:64], start=False, stop=False, reads=rk, writes=[('ps', pb)])
                    P.mm(out, WOg[ws][:, 1, Jp, :], Hb[:, 1, g, 0:64], start=False, stop=True, reads=rk, writes=[('ps', pb)])
                ys = (g // 8) % 2
                P.act(Yst[ys][:, g % 8, :], ps[:], AF.Gelu_apprx_tanh, reads=[('ps', pb)], writes=[('Yst', ys)])
                if not last:
                    pc = psi % 8
                    psi += 1
                    psc = G.PS[pc]
                    for Jp in range(8):
                        out = psc[:, Jp * 4:(Jp + 1) * 4]
                        for Ip in range(8):
                            P.mm(out, WTg[ws][:, tslot(Jp - Ip), :], UTc[:, g, :, Ip], start=(Ip == 0), stop=False, reads=rk, writes=[('ps', pc)])
                        P.mm(out, WOg[ws][:, 0, Jp, :], Hb[:, 0, g, 64:68], start=False, stop=False, reads=rk, writes=[('ps', pc)])
                        P.mm(out, WOg[ws][:, 1, Jp, :], Hb[:, 1, g, 64:68], start=False, stop=True, reads=rk, writes=[('ps', pc)])
                    P.act(Yc[:, g, :, :], psc[:, 0:32].rearrange('p (j c) -> p c j', c=4), AF.Gelu_apprx_tanh, reads=[('ps', pc)], writes=['Yc'])
                if g % 8 == 7:
                    g8 = g - 7
                    for j in range(8):
                        P.dma('sp', S['y5', b][g8 * 16:(g8 + 8) * 16, CTX + 512 * j:CTX + 512 * (j + 1)].rearrange('(g h) t -> h g t', h=16),
                              Yst[ys][j * 16:(j + 1) * 16, :, :], ('Yst', ys), reads=[('Yst', ys)])
            if not last:
                for j in range(8):
                    P.dma('sp', S['y5', b][:, 32 * j:32 * (j + 1)].rearrange('(g h) t -> h g t', h=16),
                          Yc[j * 16:(j + 1) * 16, :, :, :].rearrange('p g c j -> p g (c j)'), 'Ycst', reads=['Yc'])
        P.flush()


def ln_pre(P, T, oc, n, nslot, tiles):
    Rb, SQb = tiles[0], tiles[1]
    tk = ('T32', nslot)
    P.copy('pool', Rb[:, oc, 0:n], T[:, oc, 0:n], reads=[tk], writes=['Rb'])
    P.act(SQb[:, oc, 0:n], T[:, oc, 0:n], AF.Square, reads=[tk], writes=['SQb'])


def ln_block(G, l, P, T, Xres, n, nslot, tiles, gname, bname, psi_ref, rk_extra):
    Rb, SQb, MEAN, M2, VAR, RSTD, ONESB = tiles
    tk = ('T32', nslot)
    pm = psi_ref[0] % 8
    pq = (psi_ref[0] + 1) % 8
    psi_ref[0] += 2
    for oc in range(8):
        P.mm(G.PS[pm][:, 0:n], ONESB[:], Rb[:, oc, 0:n], start=(oc == 0), stop=(oc == 7), reads=['Rb', 'ONESB'], writes=[('ps', pm)])
    for oc in range(8):
        P.mm(G.PS[pq][:, 0:n], ONESB[:], SQb[:, oc, 0:n], start=(oc == 0), stop=(oc == 7), reads=['SQb', 'ONESB'], writes=[('ps', pq)])
    P.copy('act', MEAN[:, 0:n], G.PS[pm][:, 0:n], reads=[('ps', pm)], writes=['MEAN'])
    P.tt('pool', M2[:, 0:n], MEAN[:, 0:n], MEAN[:, 0:n], ALU.mult, reads=['MEAN'], writes=['M2'])
    P.tt('dve', VAR[:, 0:n], G.PS[pq][:, 0:n], M2[:, 0:n], ALU.subtract, reads=[('ps', pq), 'M2'], writes=['VAR'])
    P.act(VAR[:, 0:n], VAR[:, 0:n], AF.Sqrt, bias=G.EPS[:], reads=['VAR'], writes=['VAR'])
    P.add('dve', lambda e, o=RSTD[:, 0:n], i=VAR[:, 0:n]: e.reciprocal(out=o, in_=i), reads=['VAR'], writes=['RSTD'])
    for oc in range(8):
        P.tt('pool', T[:, oc, 0:n], T[:, oc, 0:n], MEAN[:, 0:n], ALU.subtract, reads=[tk, 'MEAN'], writes=[tk])
        P.tt('dve', T[:, oc, 0:n], T[:, oc, 0:n], RSTD[:, 0:n], ALU.mult, reads=[tk, 'RSTD'], writes=[tk])
        P.ts('dve', T[:, oc, 0:n], T[:, oc, 0:n], vec(G, l, gname, oc), vec(G, l, bname, oc), ALU.mult, ALU.add, reads=[tk], writes=[tk])


def stage_3a(G, l):
    nc, P, I, S = G.nc, G.P, G.I, G.S
    last = (l == DEPTH - 1)
    with ExitStack() as es:
        def sb(name, shape, dt=F32):
            return es.enter_context(G.sbt(name, shape, dt))
        GLW = sb('GLW', [128, 8, D], BF16)
        WO_ = sb('WO_', [128, 16, D], BF16)
        ONESB = sb('ONESB', [128, 128], BF16)
        Y5 = [sb('Y5_%d' % i, [128, 8, 512], BF16) for i in range(2)]
        Y5n = sb('Y5n', [128, 8, CTX], BF16)
        RGt = [sb('RGt%d' % i, [128, 8, 512], BF16) for i in range(2)]
        Xin = [sb('Xin%d' % i, [128, 8, 512]) for i in range(2)]
        S5o = sb('S5o', [128, 8, 512], BF16)
        T32 = [sb('T32_%d' % i, [128, 8, 512]) for i in range(2)]
        Rb = sb('Rb', [128, 8, 512], BF16)
        SQb = sb('SQb', [128, 8, 512], BF16)
        sig = [sb('sig%d' % i, [128, 512]) for i in range(2)]
        MEAN = sb('MEAN', [128, 512])
        M2 = sb('M2', [128, 512])
        VAR = sb('VAR', [128, 512])
        RSTD = sb('RSTD', [128, 512])
        tiles = (Rb, SQb, MEAN, M2, VAR, RSTD, ONESB)
        for k in range(8):
            P.dma('pool', GLW[:, k, :], I['glu_w'][l, k * 128:(k + 1) * 128, :], 'GLWd', writes=['GLW'])
        for k in range(16):
            P.dma('pool', WO_[:, k, :], I['w_out'][l, k * 128:(k + 1) * 128, :], 'WO_d', writes=['WO_'])
        P.memset('dve', ONESB[:], 1.0 / D, writes=['ONESB'])
        psi = [0]
        sgc = [0]
        blocks = []
        for b in range(NB):
            for (kind, t0, n) in seg_list():
                if kind == 'ctx' and last:
                    continue
                blocks.append((b, kind, t0, n))

        def phA(i):
            b, kind, t0, n = blocks[i]
            slot = i % 2
            xsrc = (I['cxT'] if kind == 'ctx' else I['xT']) if l == 0 else S['x', kind]
            off = t0 if kind == 'ctx' else CTX + t0
            P.dma('sp', Y5[slot][:, :, 0:n], S['y5', b][:, off:off + n].rearrange('(k p) t -> p k t', p=128), ('Y5', slot), writes=[('Y5', slot)])
            P.dma('sp', RGt[slot][:, :, 0:n], S['rg', b][:, off:off + n].rearrange('(k p) t -> p k t', p=128), ('RGt', slot), writes=[('RGt', slot)])
            P.dma('sp', Xin[slot][:, :, 0:n], xsrc[b, :, t0:t0 + n].rearrange('(k p) t -> p k t', p=128), ('Xin', slot), writes=[('Xin', slot)])
            if kind == 'ctx':
                for k in range(8):
                    for cc in range(4):
                        P.copy('pool', Y5n[:, k, cc * 64:(cc + 1) * 64].rearrange('p (j q) -> p j q', j=8),
                               Y5[slot][:, k, 0:CTX].rearrange('p (j c q) -> p j c q', j=8, c=4)[:, :, cc, :],
                               reads=[('Y5', slot)], writes=['Y5n'])

        def phB(i):
            b, kind, t0, n = blocks[i]
            slot = i % 2
            if kind == 'ctx':
                ysrc, yk = Y5n, 'Y5n'
            else:
                ysrc, yk = Y5[slot], ('Y5', slot)
            for oc in range(8):
                pb = psi[0] % 8
                psi[0] += 1
                for k in range(8):
                    P.mm(G.PS[pb][:, 0:n], GLW[:, k, oc * 128:(oc + 1) * 128], ysrc[:, k, 0:n], start=(k == 0), stop=(k == 7),
                         reads=['GLW', yk], writes=[('ps', pb)])
                ss = sgc[0] % 2
                sgc[0] += 1
                P.act(sig[ss][:, 0:n], G.PS[pb][:, 0:n], AF.Sigmoid, bias=vec(G, l, 'glu_b', oc), reads=[('ps', pb)], writes=[('sig', ss)])
                P.tt('dve', S5o[:, oc, 0:n], ysrc[:, oc, 0:n], sig[ss][:, 0:n], ALU.mult, reads=[yk, ('sig', ss)], writes=['S5o'])

        def phC(i):
            b, kind, t0, n = blocks[i]
            slot = i % 2
            m = 2 if kind == 'ctx' else b
            T = T32[slot]
            for oc in range(8):
                pb = psi[0] % 8
                psi[0] += 1
                for k in range(16):
                    rhs = RGt[slot][:, k, 0:n] if k < 8 else S5o[:, k - 8, 0:n]
                    P.mm(G.PS[pb][:, 0:n], WO_[:, k, oc * 128:(oc + 1) * 128], rhs, start=(k == 0), stop=(k == 15),
                         reads=['WO_', ('RGt', slot), 'S5o'], writes=[('ps', pb)])
                P.ts('dve', T[:, oc, 0:n], G.PS[pb][:, 0:n], vec(G, l, 'b_out', oc), modv(G, l, 2, oc, m), ALU.add, ALU.mult,
                     reads=[('ps', pb), 'MOD'], writes=[('T32', slot)])
                P.stt(T[:, oc, 0:n], Xin[slot][:, oc, 0:n], ALPHA, T[:, oc, 0:n], ALU.mult, ALU.add,
                      reads=[('Xin', slot), ('T32', slot)], writes=[('T32', slot)])
                ln_pre(P, T, oc, n, slot, tiles)

        def phD(i):
            b, kind, t0, n = blocks[i]
            slot = i % 2
            T = T32[slot]
            ln_block(G, l, P, T, None, n, slot, tiles, 'ln1_g', 'ln1_b', psi, None)
            P.dma('sp', S['x1', kind][b, :, t0:t0 + n].rearrange('(k p) t -> p k t', p=128), T[:, :, 0:n], ('T32', slot),
                  reads=[('T32', slot)])

        nb_ = len(blocks)
        phA(0)
        for i in range(nb_):
            if i + 1 < nb_:
                phA(i + 1)
            phB(i)
            if i > 0 and OPT_3:
                phD(i - 1)
            phC(i)
            if not OPT_3:
                phD(i)
        if OPT_3:
            phD(nb_ - 1)
        P.flush()


def stage_3b(G, l):
    nc, P, I, S = G.nc, G.P, G.I, G.S
    last = (l == DEPTH - 1)
    NBLK = 256
    with ExitStack() as es:
        def sb(name, shape, dt=F32):
            return es.enter_context(G.sbt(name, shape, dt))
        W1b = sb('W1b', [128, 8, 4 * D], BF16)
        W2b = sb('W2b', [128, 32, D], BF16)
        ONESB = sb('ONESB', [128, 128], BF16)
        X1 = [sb('X1_%d' % i, [128, 8, NBLK]) for i in range(2)]
        X1m = [sb('X1m%d' % i, [128, 8, NBLK], BF16) for i in range(2)]

        def slot3(i):
            return i % 2
        Hh = sb('Hh', [128, 32, NBLK], BF16)
        hr = [sb('hr%d' % i, [128, NBLK], BF16) for i in range(2)]
        T32 = [sb('T32_%d' % i, [128, 8, NBLK]) for i in range(2)]
        Rb = sb('Rb', [128, 8, NBLK], BF16)
        SQb = sb('SQb', [128, 8, NBLK], BF16)
        MEAN = sb('MEAN', [128, NBLK])
        M2 = sb('M2', [128, NBLK])
        VAR = sb('VAR', [128, NBLK])
        RSTD = sb('RSTD', [128, NBLK])
        tiles = (Rb, SQb, MEAN, M2, VAR, RSTD, ONESB)
        for k in range(8):
            for h in range(2):
                P.dma('pool', W1b[:, k, h * 2048:(h + 1) * 2048], I['w1'][l, k * 128:(k + 1) * 128, h * 2048:(h + 1) * 2048],
                      'W1bd', writes=['W1b'])
        for k in range(32):
            P.dma('pool', W2b[:, k, :], I['w2'][l, k * 128:(k + 1) * 128, :], 'W2bd', writes=['W2b'])
        P.memset('dve', ONESB[:], 1.0 / D, writes=['ONESB'])
        psi = [0]
        hsc = [0]
        blocks = []
        for b in range(NB):
            segs = [] if last else [('ctx', 0, CTX)]
            segs += [('lat', t * NBLK, NBLK) for t in range(SEQ // NBLK)]
            for (kind, t0, n) in segs:
                blocks.append((b, kind, t0, n))

        def phA(i):
            b, kind, t0, n = blocks[i]
            s3 = slot3(i)
            P.dma('sp', X1[s3][:], S['x1', kind][b, :, t0:t0 + n].rearrange('(k p) t -> p k t', p=128), ('X1', s3), writes=[('X1', s3)])

        def phM(i):
            b, kind, t0, n = blocks[i]
            s3 = slot3(i)
            m = 2 if kind == 'ctx' else b
            for k in range(8):
                P.ts('dve', X1m[i % 2][:, k, :], X1[s3][:, k, :], G.MP1[:, l, 1, k, m:m + 1], modv(G, l, 3, k, m), ALU.mult, ALU.add,
                     reads=[('X1', s3), 'MP1', 'MOD'], writes=[('X1m', i % 2)])

        def phB(i):
            b, kind, t0, n = blocks[i]
            slot = i % 2
            m = 2 if kind == 'ctx' else b
            for hc in range(32):
                pb = psi[0] % 8
                psi[0] += 1
                for k in range(8):
                    P.mm(G.PS[pb][:, 0:n], W1b[:, k, hc * 128:(hc + 1) * 128], X1m[i % 2][:, k, :], start=(k == 0), stop=(k == 7),
                         reads=['W1b', ('X1m', i % 2)], writes=[('ps', pb)])
                h2 = hsc[0] % 2
                hsc[0] += 1
                P.act(hr[h2][:], G.PS[pb][:, 0:n], AF.Relu, bias=vec(G, l, 'b1', hc), reads=[('ps', pb)], writes=[('hr', h2)])
                P.tt('pool', Hh[:, hc, :], hr[h2][:], hr[h2][:], ALU.mult, reads=[('hr', h2)], writes=['Hh'])

        def phC(i):
            b, kind, t0, n = blocks[i]
            slot = i % 2
            m = 2 if kind == 'ctx' else b
            T = T32[slot]
            for oc in range(8):
                pb = psi[0] % 8
                psi[0] += 1
                for k in range(32):
                    P.mm(G.PS[pb][:, 0:n], W2b[:, k, oc * 128:(oc + 1) * 128], Hh[:, k, :], start=(k == 0), stop=(k == 31),
                         reads=['W2b', 'Hh'], writes=[('ps', pb)])
                P.ts('dve', T[:, oc, :], G.PS[pb][:, 0:n], vec(G, l, 'b2', oc), modv(G, l, 5, oc, m), ALU.add, ALU.mult,
                     reads=[('ps', pb), 'MOD'], writes=[('T32', slot)])
                P.stt(T[:, oc, :], X1[slot3(i)][:, oc, :], ALPHA, T[:, oc, :], ALU.mult, ALU.add,
                      reads=[('X1', slot3(i)), ('T32', slot)], writes=[('T32', slot)])
                ln_pre(P, T, oc, n, slot, tiles)

        def phD(i):
            b, kind, t0, n = blocks[i]
            slot = i % 2
            T = T32[slot]
            ln_block(G, l, P, T, None, n, slot, tiles, 'ln2_g', 'ln2_b', psi, None)
            if last:
                dst = G.yT[b, :, t0:t0 + n]
            else:
                dst = S['x', kind][b, :, t0:t0 + n]
            P.dma('sp', dst.rearrange('(k p) t -> p k t', p=128), T[:, :, 0:n], ('T32', slot), reads=[('T32', slot)])

        nb_ = len(blocks)
        phA(0)
        phM(0)
        for i in range(nb_):
            if i + 1 < nb_:
                phA(i + 1)
            phB(i)
            if i + 1 < nb_:
                phM(i + 1)
            if i > 0:
                phD(i - 1)
            phC(i)
        phD(nb_ - 1)
        P.flush()


def rev(ap):
    return ap[:, ::-1]


def stage_1b(G, l):
    nc, P, I, S = G.nc, G.P, G.I, G.S
    NT = CTX + SEQ
    with ExitStack() as es:
        def sb(name, shape, dt):
            return es.enter_context(G.sbt(name, shape, dt))
        identf = sb('identf', [128, 128], F32)
        DG = sb('DG', [128, 8, 4, 128], BF16)
        GW = sb('GW', [128, 2, 2, 8, 128], BF16)
        cneg = sb('cneg', [128, 2, 8], F32)
        ctmp = sb('ctmp', [128, 2, 8], F32)
        RP = [sb('RP%d' % i, [128, NT + 6], BF16) for i in range(2)]
        GGt = [sb('GGt%d' % i, [128, NT], BF16) for i in range(2)]
        OUT = [sb('OUT%d' % i, [128, NT], BF16) for i in range(1)]
        A = [sb('A%d' % i, [128, NT], F32) for i in range(2)]
        Bt = [sb('B%d' % i, [128, NT], F32) for i in range(2)]
        HF = sb('HF', [128, NT], F32)
        XC32 = [sb('XC32_%d' % i, [128, 512], F32) for i in range(3)]
        XCB = [sb('XCB%d' % i, [128, 512], BF16) for i in range(3)]
        Rt = [[sb('Rt%d_%d' % (d, i), [128, 512], F32) for i in range(2)] for d in range(2)]
        It = [[sb('It%d_%d' % (d, i), [128, 512], F32) for i in range(2)] for d in range(2)]
        Tt = [[sb('Tt%d_%d' % (d, i), [128, 512], F32) for i in range(3)] for d in range(2)]
        Mt = [[sb('Mt%d_%d' % (d, i), [128, 512], F32) for i in range(2)] for d in range(2)]
        Ut = [[sb('Ut%d_%d' % (d, i), [128, 512], F32) for i in range(3)] for d in range(2)]

        P.dma('sp', identf[:], I['ident'], 'identf', writes=['identf'])
        for w, nm in enumerate(('rg_wa', 'rg_wi')):
            for d in range(2):
                P.dma('pool', GW[:, w, d, :, :], I[nm][l, d].rearrange('h i j -> i h j'), 'GWd', writes=['GW'])
        for fc in range(8):
            for tap in range(4):
                P.ts('dve', DG[:, fc, tap, :], identf[:], vec(G, l, 'conv_w%d' % tap, fc), None, ALU.mult,
                     reads=['identf'], writes=['DG'])
        for d in range(2):
            o = VOFF['lam%d' % d]
            P.act(ctmp[:, d, :], G.VEC[:, l, o:o + 8], AF.Exp, scale=-1.0, writes=['ctmp'])
        P.act(cneg[:], ctmp[:], AF.Ln, bias=G.ONE[:], reads=['ctmp'], writes=['cneg0'])
        P.ts('dve', cneg[:], cneg[:], -RG_C, None, ALU.mult, reads=['cneg0'], writes=['cneg'])
        for i in range(2):
            P.memset('pool', RP[i][:], 0.0, writes=[('RP', i)])
        segs = [(0, CTX, 1)] + [(CTX + t * 512, 512, 4 + CTX + t * 512) for t in range(SEQ // 512)]
        it = 0
        psi = 0
        gseg = [0]
        for b in range(NB):
            for fc in range(8):
                slot = it % 2
                it += 1
                rows = slice(fc * 128, (fc + 1) * 128)
                P.dma('sp', RP[slot][:, 1:1 + CTX], S['rgx', b][rows, 0:CTX], ('RPc', slot), writes=[('RP', slot)])
                P.dma('sp', RP[slot][:, 4 + CTX:4 + CTX + SEQ], S['rgx', b][rows, CTX:NT], ('RPl', slot), writes=[('RP', slot)])
                P.dma('sp', GGt[slot][:], S['gg', b][rows, :], ('GGt', slot), writes=[('GGt', slot)])
                base = gseg[0]
                gseg[0] += len(segs)

                def ph1(si):
                    nonlocal psi
                    off, n, pidx = segs[si]
                    q = (base + si) % 3
                    pb = psi % 8
                    psi += 1
                    ps = G.PS[pb]
                    for tap in range(4):
                        P.mm(ps[:, 0:n], DG[:, fc, tap, :], RP[slot][:, pidx + tap - 1:pidx + tap - 1 + n], start=(tap == 0), stop=(tap == 3),
                             reads=['DG', ('RP', slot)], writes=[('ps', pb)])
                    P.act(XC32[q][:, 0:n], ps[:, 0:n], AF.Identity, bias=vec(G, l, 'conv_b', fc), reads=[('ps', pb)], writes=[('XC32', q)])
                    P.copy('dve', XCB[q][:, 0:n], XC32[q][:, 0:n], reads=[('XC32', q)], writes=[('XCB', q)])

                def ph2(si):
                    nonlocal psi
                    off, n, pidx = segs[si]
                    q = (base + si) % 3
                    r2 = (base + si) % 2
                    for d in range(2):
                        pr = psi % 8
                        pi_ = (psi + 1) % 8
                        psi += 2
                        P.mm(G.PS[pr][:, 0:n], GW[:, 0, d, fc, :], XCB[q][:, 0:n], start=True, stop=True,
                             reads=['GW', ('XCB', q)], writes=[('ps', pr)])
                        P.mm(G.PS[pi_][:, 0:n], GW[:, 1, d, fc, :], XCB[q][:, 0:n], start=True, stop=True,
                             reads=['GW', ('XCB', q)], writes=[('ps', pi_)])
                        P.act(Rt[d][r2][:, 0:n], G.PS[pr][:, 0:n], AF.Sigmoid, bias=vec(G, l, 'ba%d' % d, fc), reads=[('ps', pr)], writes=[('Rt', d, r2)])
                        P.act(It[d][r2][:, 0:n], G.PS[pi_][:, 0:n], AF.Sigmoid, bias=vec(G, l, 'bi%d' % d, fc), reads=[('ps', pi_)], writes=[('It', d, r2)])
                    for d in range(2):
                        P.act(A[d][:, off:off + n], Rt[d][r2][:, 0:n], AF.Exp, scale=cneg[:, d, fc:fc + 1], reads=[('Rt', d, r2), 'cneg'], writes=[('A', d, si)])
                    for d in range(2):
                        P.tt('dve', Tt[d][q][:, 0:n], A[d][:, off:off + n], A[d][:, off:off + n], ALU.mult, reads=[('A', d, si)], writes=[('Tt', d, q)])
                        P.tt('dve', Ut[d][q][:, 0:n], It[d][r2][:, 0:n], XC32[q][:, 0:n], ALU.mult, reads=[('It', d, r2), ('XC32', q)], writes=[('Ut', d, q)])

                def ph3(si):
                    off, n, pidx = segs[si]
                    q = (base + si) % 3
                    r2 = (base + si) % 2
                    for d in range(2):
                        P.act(Mt[d][r2][:, 0:n], Tt[d][q][:, 0:n], AF.Sqrt, bias=G.ONE[:], scale=-1.0, reads=[('Tt', d, q)], writes=[('Mt', d, r2)])
                    for d in range(2):
                        P.tt('dve', Bt[d][:, off:off + n], Ut[d][q][:, 0:n], Mt[d][r2][:, 0:n], ALU.mult, reads=[('Ut', d, q), ('Mt', d, r2)], writes=[('B', d, si)])
                    init = 0.0 if si == 0 else HF[:, off - 1:off]
                    P.scan(HF[:, off:off + n], A[0][:, off:off + n], Bt[0][:, off:off + n], init, reads=[('A', 0, si), ('B', 0, si), 'HF'], writes=['HF'])

                ns_ = len(segs)
                for t in range(ns_ + 2):
                    if t < ns_:
                        ph1(t)
                    if 0 <= t - 1 < ns_:
                        ph2(t - 1)
                    if 0 <= t - 2 < ns_:
                        ph3(t - 2)
                HB = A[0]
                kA1 = [('A', 1, i) for i in range(ns_)] + [('B', 1, i) for i in range(ns_)]
                kA0 = [('A', 0, i) for i in range(ns_)]
                P.scan(rev(HB[:, 0:CTX]), rev(A[1][:, 0:CTX]), rev(Bt[1][:, 0:CTX]), 0.0, reads=kA1, writes=kA0)
                P.scan(rev(HB[:, CTX:NT]), rev(A[1][:, CTX:NT]), rev(Bt[1][:, CTX:NT]), HB[:, 0:1], reads=kA1 + kA0, writes=kA0)
                P.tt('dve', HF[:], HF[:], HB[:], ALU.add, reads=['HF'] + kA0, writes=['HF'])
                P.tt('dve', OUT[0][:], HF[:], GGt[slot][:], ALU.mult, reads=['HF', ('GGt', slot)], writes=['OUT'])
                P.dma('sp', S['rg', b][rows, :], OUT[0][:], 'OUTst', reads=['OUT'])
        P.flush()


def pack_inputs(inp):
    f = lambda a: np.ascontiguousarray(np.asarray(a, dtype=np.float32))
    x = np.asarray(inp['x'], np.float32)
    ctx = np.asarray(inp['ctx'], np.float32)
    c = np.asarray(inp['c'], np.float32)
    c_ctx = np.asarray(inp['c_ctx'], np.float32)

    def pk(v):
        v = np.asarray(v, np.float32)
        return v.reshape(-1, 128).T

    vecs = np.zeros((DEPTH, 128, NV), np.float32)
    for l in range(DEPTH):
        cols = {'ln1_g': inp['ln1_g'][l], 'ln1_b': inp['ln1_b'][l], 'ln2_g': inp['ln2_g'][l], 'ln2_b': inp['ln2_b'][l],
                'conv_w0': inp['conv_w'][l][0], 'conv_w1': inp['conv_w'][l][1], 'conv_w2': inp['conv_w'][l][2],
                'conv_w3': inp['conv_w'][l][3], 'conv_b': inp['conv_b'][l], 'lam0': inp['rg_lambda'][l][0],
                'lam1': inp['rg_lambda'][l][1], 'ba0': inp['rg_ba'][l][0], 'ba1': inp['rg_ba'][l][1],
                'bi0': inp['rg_bi'][l][0], 'bi1': inp['rg_bi'][l][1], 'glu_b': inp['s5_glu_b'][l], 'b_out': inp['b_out'][l],
                'b2': inp['mlp_b2'][l], 'b1': inp['mlp_b1'][l], 's5d': inp['s5_d'][l]}
        for n, ncol in VEC_NAMES:
            vecs[l, :, VOFF[n]:VOFF[n] + ncol] = pk(cols[n])
    ada_bP = np.stack([pk(np.asarray(inp['ada_b'])[l]) for l in range(DEPTH)])
    shared = {
        'ada_w': f(inp['ada_w']), 'ada_bP': f(ada_bP), 'vecs': vecs, 'w_in': f(inp['w_in']),
        'rg_wa': f(inp['rg_wa']), 'rg_wi': f(inp['rg_wi']), 'glu_w': f(inp['s5_glu_w']), 'w_out': f(inp['w_out']),
        'w1': f(inp['mlp_w1']), 'w2': f(inp['mlp_w2']), 'ident': np.eye(128, dtype=np.float32),
    }
    A = lambda n: np.asarray(inp[n], np.float32)
    s5a = np.stack([A('s5_a_re'), A('s5_a_im')], axis=2)
    shared['s5a'] = f(s5a.transpose(0, 1, 4, 2, 3).reshape(DEPTH, 128, 2, 64))
    shared['s5ldt'] = f(np.broadcast_to(A('s5_log_dt')[:, :, None, :], (DEPTH, 2, 64, 64)).reshape(DEPTH, 128, 64))
    s5b = np.stack([A('s5_b_re'), A('s5_b_im')], axis=2)
    shared['s5B'] = f(s5b.transpose(0, 1, 4, 2, 3, 5).reshape(DEPTH, 128, 2, 64, 16))
    s5c = np.stack([A('s5_c_re'), A('s5_c_im')], axis=2)
    shared['s5C'] = f(s5c.transpose(0, 1, 5, 2, 3, 4).reshape(DEPTH, 128, 2, 64, 16))
    dgh = A('s5_d').reshape(DEPTH, 64, 16)
    shared['s5DP'] = f(np.broadcast_to(dgh.transpose(0, 2, 1)[:, None, :, :], (DEPTH, 8, 16, 64)).reshape(DEPTH, 128, 64))
    cm = np.zeros((128, 2, 4, 4, 128), np.float32)
    for i in range(8):
        for j in range(8):
            for d1 in range(4):
                for d2 in range(4):
                    lag = 8 * (j - i) + d1 + 4 * (d2 - 2)
                    if lag >= 0:
                        cm[i * 16:(i + 1) * 16, 0, d1, d2, j * 16:(j + 1) * 16] = 1.0
                    if lag <= 0:
                        cm[i * 16:(i + 1) * 16, 1, d1, d2, j * 16:(j + 1) * 16] = 1.0
    shared['cmask'] = cm
    cv = np.zeros((128, 4), np.float32)
    cv[:64, 0] = 1.0
    cv[64:, 0] = -1.0
    cv[64:, 1] = 63.0
    cv[64:, 2] = 65.0
    shared['cvec'] = cv
    shared['mvals'] = f(np.broadcast_to(np.arange(-63, 65, dtype=np.float32)[None, :], (128, 128)))
    maps = []
    for k in range(NCORES):
        bs = slice(k * NB, (k + 1) * NB)
        cT = np.zeros((128, 8, 4), np.float32)
        for m in range(NB):
            cT[:, :, m] = pk(c[k * NB + m])
        cT[:, :, 2] = pk(c_ctx)
        d = dict(shared)
        d['xT'] = np.ascontiguousarray(x[bs].transpose(0, 2, 1))
        d['cxT'] = np.ascontiguousarray(ctx[bs].transpose(0, 2, 1))
        d['cT'] = cT
        maps.append(d)
    return maps


_CACHE = {}


def kernel(**inputs):
    maps = pack_inputs(inputs)
    if 'nc' not in _CACHE:
        _CACHE['nc'] = build_program()[0]
    nc = _CACHE['nc']
    res = run_bass_kernel_spmd(nc, maps, core_ids=list(range(NCORES)))
    outs = [r['yT'] for r in res.results]
    y = np.concatenate(outs, axis=0)
    return np.ascontiguousarray(y.transpose(0, 2, 1)).astype(np.float32)
```

```python
import math
from contextlib import ExitStack
import numpy as np
import concourse.bass as bass
import concourse.mybir as mybir
from concourse.bass_utils import run_bass_kernel_spmd

F32 = mybir.dt.float32
BF16 = mybir.dt.bfloat16
AF = mybir.ActivationFunctionType
ALU = mybir.AluOpType

D = 1024
SEQ = 4096
CTX = 256
DEPTH = 2
NB = 2
ALPHA = (2.0 * DEPTH) ** 0.25
LN_EPS = 1e-5
RG_C = 8.0
NCORES = 8
OPT_PREP = False
OPT_1B = True
OPT_3 = True

VEC_NAMES = [('ln1_g', 8), ('ln1_b', 8), ('ln2_g', 8), ('ln2_b', 8), ('conv_w0', 8), ('conv_w1', 8),
             ('conv_w2', 8), ('conv_w3', 8), ('conv_b', 8), ('lam0', 8), ('lam1', 8), ('ba0', 8),
             ('ba1', 8), ('bi0', 8), ('bi1', 8), ('glu_b', 8), ('b_out', 8), ('b2', 8), ('b1', 32),
             ('s5d', 8)]
VOFF = {}
_o = 0
for _n, _c in VEC_NAMES:
    VOFF[_n] = _o
    _o += _c
NV = _o


class Op:
    __slots__ = ('eng', 'fn', 'deps', 'ddeps', 'signal', 'sem', 'val', 'is_dma')


class Prog:
    def __init__(self, nc, es):
        self.nc = nc
        self.engs = {'pe': nc.tensor, 'act': nc.scalar, 'dve': nc.vector, 'pool': nc.gpsimd, 'sp': nc.sync}
        self.sem = {e: es.enter_context(nc.semaphore('s_' + e)) for e in ('pe', 'act', 'dve', 'pool')}
        self.cnt = {e: 0 for e in self.sem}
        self.es = es
        self.dsem = {}
        self.dcum = {}
        self.nops = 0
        self.free = []
        self.free_sw = []
        self.dkind = {}
        self.nsem = 0
        self.barrier = []
        self._reset()

    def _reset(self):
        self.ops = {e: [] for e in self.engs}
        self.last_w = {}
        self.readers = {}
        self.swq = []

    def add(self, eng, fn, reads=(), writes=(), dma_key=None, signal=False):
        op = Op()
        op.eng = eng
        op.fn = fn
        op.signal = signal
        op.is_dma = dma_key is not None
        op.sem = None
        op.val = None
        deps = []
        for k in reads:
            w = self.last_w.get(k)
            if w is not None:
                deps.append(w)
        for k in writes:
            w = self.last_w.get(k)
            if w is not None:
                deps.append(w)
            deps.extend(self.readers.get(k, {}).values())
        cdeps = []
        ddeps = {}
        seen = set()
        for d in deps:
            if id(d) in seen:
                continue
            seen.add(id(d))
            if d.is_dma:
                ddeps[d.sem] = self.dcum[d.sem]
            else:
                if d.eng == 'pe' and eng == 'pe':
                    continue
                cdeps.append(d)
        op.deps = cdeps
        op.ddeps = ddeps
        if op.is_dma:
            if dma_key not in self.dsem:
                fl = self.free_sw if eng == 'pool' else self.free
                self.dkind[dma_key] = eng == 'pool'
                if fl:
                    h, c0 = fl.pop()
                else:
                    self.nsem += 1
                    h, c0 = self.es.enter_context(self.nc.semaphore('d_' + str(self.nsem))), 0
                self.dsem[dma_key] = h
                self.dcum[dma_key] = c0
            self.dcum[dma_key] += 16
            op.sem = dma_key
            op.val = self.dcum[dma_key]
            assert op.val < 60000, dma_key
        rk = ('d', dma_key) if op.is_dma else eng
        for k in reads:
            self.readers.setdefault(k, {})[rk] = op
        for k in writes:
            self.last_w[k] = op
            self.readers[k] = {}
        self.ops[eng].append(op)
        self.nops += 1
        return op

    def flush(self):
        nc = self.nc
        barrier = self.barrier
        for e, lst in self.ops.items():
            for op in lst:
                for d in op.deps:
                    d.signal = True
            for op in reversed(lst):
                if not op.is_dma and op.fn is not None:
                    op.signal = True
                    break
        for e, lst in self.ops.items():
            if e not in self.sem:
                continue
            for op in lst:
                if op.signal and not op.is_dma and op.fn is not None:
                    self.cnt[e] += 1
                    op.sem = e
                    op.val = self.cnt[e]
                    assert op.val < 60000
        ops = self.ops
        sem = self.sem
        dsem = self.dsem

        def emit(ename, eobj):
            waited = {}
            for s, v in barrier:
                eobj.wait_ge(s, v)
            for op in ops[ename]:
                for d in op.deps:
                    s = sem[d.sem]
                    if waited.get(d.sem, 0) < d.val:
                        eobj.wait_ge(s, d.val)
                        waited[d.sem] = d.val
                for k, v in op.ddeps.items():
                    if waited.get(k, 0) < v:
                        eobj.wait_ge(dsem[k], v)
                        waited[k] = v
                if op.fn is None:
                    continue
                ins = op.fn(eobj)
                if op.is_dma:
                    ins.then_inc(dsem[op.sem], 16)
                elif op.signal:
                    ins.then_inc(sem[op.sem], 1)

        with nc.Block() as block:
            @block.tensor
            def _(t):
                emit('pe', t)

            @block.scalar
            def _(s):
                emit('act', s)

            @block.vector
            def _(v):
                emit('dve', v)

            @block.gpsimd
            def _(g):
                emit('pool', g)

            @block.sync
            def _(sp):
                emit('sp', sp)
        self._reset()
        self.barrier = [(self.sem[e], self.cnt[e]) for e in self.sem if self.cnt[e] > 0]
        self.barrier += [(self.dsem[k], self.dcum[k]) for k in self.dsem if self.dcum[k] > 0]
        for k in self.dsem:
            (self.free_sw if self.dkind[k] else self.free).append((self.dsem[k], self.dcum[k]))
        self.dsem = {}
        self.dcum = {}
        self.dkind = {}

    def final_wait(self):
        nc = self.nc
        items = list(self.barrier)
        with nc.Block() as block:
            @block.sync
            def _(sp):
                for s, v in items:
                    sp.wait_ge(s, v)

    def dma(self, q, out, in_, key, reads=(), writes=(), **kw):
        return self.add(q, lambda e, o=out, i=in_, kw=kw: e.dma_start(out=o, in_=i, **kw), reads, writes, dma_key=key)

    def tr(self, out, in_, ident, reads=(), writes=()):
        return self.add('pe', lambda e, o=out, i=in_, d=ident: e.transpose(o, i, d), reads, writes)

    def mm(self, out, lhsT, rhs, start, stop, reads=(), writes=(), signal=False):
        return self.add('pe', lambda e, o=out, l=lhsT, r=rhs, s=start, t=stop: e.matmul(o, lhsT=l, rhs=r, start=s, stop=t),
                        reads, writes, signal=signal)

    def act(self, out, in_, func, bias=None, scale=1.0, reads=(), writes=()):
        if bias is None:
            f = lambda e, o=out, i=in_, fu=func, sc=scale: e.activation(out=o, in_=i, func=fu, scale=sc)
        else:
            f = lambda e, o=out, i=in_, fu=func, b=bias, sc=scale: e.activation(out=o, in_=i, func=fu, bias=b, scale=sc)
        return self.add('act', f, reads, writes)

    def ts(self, eng, out, in0, s1, s2, op0, op1=None, reads=(), writes=()):
        if op1 is None:
            f = lambda e, o=out, i=in0, a=s1, p0=op0: e.tensor_scalar(out=o, in0=i, scalar1=a, scalar2=None, op0=p0)
        else:
            f = lambda e, o=out, i=in0, a=s1, b=s2, p0=op0, p1=op1: e.tensor_scalar(out=o, in0=i, scalar1=a, scalar2=b, op0=p0, op1=p1)
        return self.add(eng, f, reads, writes)

    def tt(self, eng, out, in0, in1, op, reads=(), writes=()):
        return self.add(eng, lambda e, o=out, a=in0, b=in1, p=op: e.tensor_tensor(out=o, in0=a, in1=b, op=p), reads, writes)

    def stt(self, out, in0, scalar, in1, op0, op1, reads=(), writes=()):
        return self.add('dve', lambda e, o=out, a=in0, s=scalar, b=in1, p0=op0, p1=op1:
                        e.scalar_tensor_tensor(out=o, in0=a, scalar=s, in1=b, op0=p0, op1=p1), reads, writes)

    def copy(self, eng, out, in_, reads=(), writes=()):
        if eng == 'act':
            return self.add('act', lambda e, o=out, i=in_: e.activation(out=o, in_=i, func=AF.Copy), reads, writes)
        return self.add(eng, lambda e, o=out, i=in_: e.tensor_copy(out=o, in_=i), reads, writes)

    def scan(self, out, d0, d1, init, reads=(), writes=()):
        return self.add('dve', lambda e, o=out, a=d0, b=d1, i=init:
                        e.tensor_tensor_scan(out=o, data0=a, data1=b, initial=i, op0=ALU.mult, op1=ALU.add), reads, writes)

    def memset(self, eng, out, val, reads=(), writes=()):
        return self.add(eng, lambda e, o=out, v=val: e.memset(o, v), reads, writes)


class Ctx:
    pass


def build_program(debug=(), stages=None):
    nc = bass.Bass("TRN2", target_bir_lowering=False)
    G = Ctx()
    G.nc = nc
    G.uid = [0]

    def sbt(name, shape, dt):
        G.uid[0] += 1
        return nc.sbuf_tensor('%s_u%d' % (name, G.uid[0]), shape, dt)
    G.sbt = sbt
    G.debug = set(debug)
    ges = ExitStack()
    G.ges = ges
    P = Prog(nc, ges)
    G.P = P

    def din(name, shape, dt=F32):
        return nc.dram_tensor(name, list(shape), dt, kind="ExternalInput").ap()

    def dscr(name, shape, dt):
        kind = "ExternalOutput" if name in G.debug else "Internal"
        return nc.dram_tensor(name, list(shape), dt, kind=kind).ap()

    G.dscr = dscr
    I = {}
    I['xT'] = din('xT', [NB, D, SEQ])
    I['cxT'] = din('cxT', [NB, D, CTX])
    I['cT'] = din('cT', [128, 8, 4])
    I['ada_w'] = din('ada_w', [DEPTH, D, 6 * D])
    I['ada_bP'] = din('ada_bP', [DEPTH, 128, 48])
    I['vecs'] = din('vecs', [DEPTH, 128, NV])
    I['w_in'] = din('w_in', [DEPTH, D, 3 * D])
    I['rg_wa'] = din('rg_wa', [DEPTH, 2, 8, 128, 128])
    I['rg_wi'] = din('rg_wi', [DEPTH, 2, 8, 128, 128])
    I['glu_w'] = din('glu_w', [DEPTH, D, D])
    I['w_out'] = din('w_out', [DEPTH, 2 * D, D])
    I['w1'] = din('w1', [DEPTH, D, 4 * D])
    I['w2'] = din('w2', [DEPTH, 4 * D, D])
    I['ident'] = din('ident', [128, 128])
    I['s5a'] = din('s5a', [DEPTH, 128, 2, 64])
    I['s5ldt'] = din('s5ldt', [DEPTH, 128, 64])
    I['s5B'] = din('s5B', [DEPTH, 128, 2, 64, 16])
    I['s5C'] = din('s5C', [DEPTH, 128, 2, 64, 16])
    I['s5DP'] = din('s5DP', [DEPTH, 128, 64])
    I['cmask'] = din('cmask', [128, 2, 4, 4, 128])
    I['cvec'] = din('cvec', [128, 4])
    I['mvals'] = din('mvals', [128, 128])
    G.I = I
    G.yT = nc.dram_tensor('yT', [NB, D, SEQ], F32, kind="ExternalOutput").ap()

    G.PS = [ges.enter_context(nc.psum_tensor('ps%d' % i, [128, 512], F32)) for i in range(8)]
    G.MOD = ges.enter_context(G.sbt('MOD', [128, DEPTH, 48, 4], F32))
    G.MP1 = ges.enter_context(G.sbt('MP1', [128, DEPTH, 2, 8, 4], F32))
    G.VEC = ges.enter_context(G.sbt('VEC', [128, DEPTH, NV], F32))
    G.ONE = ges.enter_context(G.sbt('ONE', [128, 1], F32))
    G.EPS = ges.enter_context(G.sbt('EPS', [128, 1], F32))

    S = {}
    for b in range(NB):
        S['rgx', b] = dscr('rgx%d' % b, [D, CTX + SEQ], BF16)
        S['gg', b] = dscr('gg%d' % b, [D, CTX + SEQ], BF16)
        S['s5u', b] = dscr('s5u%d' % b, [D, CTX + SEQ], BF16)
        S['rg', b] = dscr('rg%d' % b, [D, CTX + SEQ], BF16)
        S['s5uc', b] = dscr('s5uc%d' % b, [D, CTX], BF16)
        S['y5', b] = dscr('y5_%d' % b, [D, CTX + SEQ], BF16)
    for kind, n in (('lat', SEQ), ('ctx', CTX)):
        S['x', kind] = dscr('xs_' + kind, [NB, D, n], F32)
        S['x1', kind] = dscr('x1_' + kind, [NB, D, n], F32)
    for l in range(DEPTH):
        S['WT', l] = dscr('WT%d' % l, [64, 128, 16, 128], BF16)
        S['WS', l] = dscr('WS%d' % l, [64, 128, 2, 8, 128], BF16)
        S['WO', l] = dscr('WO%d' % l, [64, 128, 2, 8, 128], BF16)
        S['L64', l] = dscr('L64_%d' % l, [128, 2, 64], F32)
    G.S = S

    G.stages = stages
    run_stages(G)
    P.final_wait()
    return nc, G


def run_stages(G):
    st = G.stages

    def on(name):
        return st is None or name in st
    if on('mod'):
        stage_mod(G)
    if 'moddbg' in G.debug:
        d = G.dscr('moddbg', [128, DEPTH * 48 * 4], F32)
        G.P.dma('sp', d, G.MOD[:].rearrange('p l c m -> p (l c m)'), 'moddbg', reads=['MOD'])
        G.P.flush()
    for l in range(DEPTH):
        if on('prep%d' % l):
            stage_s5prep(G, l)
        if on('1a%d' % l):
            stage_1a(G, l)
        if on('1b%d' % l):
            stage_1b(G, l)
        if on('2_%d' % l):
            stage_2(G, l)
        if on('3a%d' % l):
            stage_3a(G, l)
        if on('3b%d' % l):
            stage_3b(G, l)


def vec(G, l, name, k=0):
    c = VOFF[name] + k
    return G.VEC[:, l, c:c + 1]


def modv(G, l, j, k, m):
    return G.MOD[:, l, j * 8 + k, m:m + 1]


def stage_mod(G):
    nc, P, I = G.nc, G.P, G.I
    with ExitStack() as es:
        cin = es.enter_context(G.sbt('cin', [128, 8, 4], F32))
        sc = es.enter_context(G.sbt('sc', [128, 8, 4], F32))
        abp = es.enter_context(G.sbt('abp', [128, DEPTH, 48], F32))
        wt = [es.enter_context(G.sbt('adaw%d' % i, [128, 8, 512], F32)) for i in range(2)]
        P.dma('sp', cin[:], I['cT'], 'cin', writes=['cin'])
        P.dma('sp', abp[:], I['ada_bP'].rearrange('l p c -> p l c'), 'abp', writes=['abp'])
        P.dma('sp', G.VEC[:], I['vecs'].rearrange('l p c -> p l c'), 'VEC', writes=['VEC'])
        P.memset('dve', G.ONE[:], 1.0, writes=['ONE'])
        P.memset('dve', G.EPS[:], LN_EPS, writes=['EPS'])
        P.act(sc[:], cin[:], AF.Silu, reads=['cin'], writes=['sc'])
        it = 0
        for l in range(DEPTH):
            for c4 in range(12):
                slot = it % 2
                it += 1
                P.dma('sp', wt[slot][:], I['ada_w'][l, :, c4 * 512:(c4 + 1) * 512].rearrange('(k p) c -> p k c', p=128),
                      ('adaw', slot), writes=[('adaw', slot)])
                ps = G.PS[c4 % 2]
                for cj in range(4):
                    cc = c4 * 4 + cj
                    for k in range(8):
                        P.mm(ps[:, cj * 4:cj * 4 + 4], wt[slot][:, k, cj * 128:(cj + 1) * 128], sc[:, k, :],
                             start=(k == 0), stop=(k == 7), reads=[('adaw', slot), 'sc'], writes=[('ps', c4 % 2)])
                for cj in range(4):
                    cc = c4 * 4 + cj
                    P.ts('dve', G.MOD[:, l, cc, :], ps[:, cj * 4:cj * 4 + 4], abp[:, l, cc:cc + 1], None, ALU.add,
                         reads=[('ps', c4 % 2), 'abp'], writes=['MOD'])
            for s, j in ((0, 1), (1, 4)):
                P.ts('dve', G.MP1[:, l, s, :, :], G.MOD[:, l, j * 8:(j + 1) * 8, :], 1.0, None, ALU.add,
                     reads=['MOD'], writes=['MP1'])
        P.flush()


def seg_list():
    segs = [('ctx', 0, CTX)]
    for t in range(SEQ // 512):
        segs.append(('lat', t * 512, 512))
    return segs


def stage_1a(G, l):
    nc, P, I, S = G.nc, G.P, G.I, G.S
    with ExitStack() as es:
        W = es.enter_context(G.sbt('w_in_b', [128, 8, 3 * D], BF16))
        xin = [es.enter_context(G.sbt('xin%d' % i, [128, 8, 512], F32)) for i in range(2)]
        xm = [es.enter_context(G.sbt('xm%d' % i, [128, 8, 512], BF16)) for i in range(2)]
        st = [[es.enter_context(G.sbt('st%d_%d' % (j, i), [128, 8, 512], BF16)) for i in range(2)] for j in range(3)]
        stc = es.enter_context(G.sbt('stc', [128, 8, CTX], BF16))
        for k in range(8):
            for h in range(2):
                P.dma('pool', W[:, k, h * 1536:(h + 1) * 1536], I['w_in'][l, k * 128:(k + 1) * 128, h * 1536:(h + 1) * 1536],
                      'W1a', writes=[('W', k)])
        psi = 0
        blocks = []
        for b in range(NB):
            for (kind, t0, n) in seg_list():
                blocks.append((b, kind, t0, n))

        def ld(i):
            b, kind, t0, n = blocks[i]
            slot = i % 2
            src = I['cxT'] if kind == 'ctx' else I['xT']
            if l > 0:
                src = G.S['x', kind]
            P.dma('sp', xin[slot][:, :, 0:n], src[b, :, t0:t0 + n].rearrange('(k p) t -> p k t', p=128),
                  ('xin', slot), writes=[('xin', slot)])

        def modu(i):
            b, kind, t0, n = blocks[i]
            slot = i % 2
            m = 2 if kind == 'ctx' else b
            for k in range(8):
                P.ts('dve', xm[slot][:, k, 0:n], xin[slot][:, k, 0:n], G.MP1[:, l, 0, k, m:m + 1], modv(G, l, 0, k, m),
                     ALU.mult, ALU.add, reads=[('xin', slot), 'MP1', 'MOD'], writes=[('xm', slot)])

        ld(0)
        modu(0)
        if len(blocks) > 1:
            ld(1)
        for i, (b, kind, t0, n) in enumerate(blocks):
            slot = i % 2
            off = t0 if kind == 'ctx' else CTX + t0
            for oc in range(24):
                if oc == 12 and i + 1 < len(blocks):
                    modu(i + 1)
                    if i + 2 < len(blocks):
                        ld(i + 2)
                pb = psi % 4
                psi += 1
                ps = G.PS[pb]
                for k in range(8):
                    P.mm(ps[:, 0:n], W[:, k, oc * 128:(oc + 1) * 128], xm[slot][:, k, 0:n], start=(k == 0), stop=(k == 7),
                         reads=[('W', k), ('xm', slot)], writes=[('ps', pb)])
                j, o8 = oc // 8, oc % 8
                dst = st[j][slot][:, o8, 0:n]
                if j == 1:
                    P.act(dst, ps[:, 0:n], AF.Gelu_apprx_tanh, reads=[('ps', pb)], writes=[('st', j, slot)])
                elif oc % 3 == 0:
                    P.copy('act', dst, ps[:, 0:n], reads=[('ps', pb)], writes=[('st', j, slot)])
                else:
                    P.copy('dve', dst, ps[:, 0:n], reads=[('ps', pb)], writes=[('st', j, slot)])
                if j == 2 and kind == 'ctx':
                    for cc in range(4):
                        P.copy('dve', stc[:, o8, :].rearrange('p (i c ip) -> p i c ip', i=8, c=4)[:, :, cc, :],
                               ps[:, cc * 64:(cc + 1) * 64].rearrange('p (i ip) -> p i ip', i=8), reads=[('ps', pb)], writes=['stc'])
            if kind == 'ctx':
                P.dma('sp', S['s5uc', b].rearrange('(k p) t -> p k t', p=128), stc[:], 'stc', reads=['stc'])
            for j, nm in enumerate(('rgx', 'gg', 's5u')):
                P.dma('sp', S[nm, b][:, off:off + n].rearrange('(k p) t -> p k t', p=128), st[j][slot][:, :, 0:n],
                      ('st', j, slot), reads=[('st', j, slot)], writes=[('dram', nm, b)])
        P.flush()


MAGIC = 12582912.0
TWO_PI = 2.0 * math.pi


def stage_s5prep(G, l):
    nc, P, I, S = G.nc, G.P, G.I, G.S
    GB = 4
    with ExitStack() as es:
        def sb(name, shape, dt=F32):
            return es.enter_context(G.sbt(name, shape, dt))
        A_ = sb('s5a_t', [128, 2, 64])
        ldt = sb('ldt', [128, 64])
        Bin = sb('Bin', [128, 2, 64, 16])
        Cin = sb('Cin', [128, 2, 64, 16])
        DP = sb('DP', [128, 64])
        cmask = sb('cmask_t', [128, 2, 4, 4, 128])
        cvec = sb('cvec_t', [128, 4])
        mvals = sb('mvals_t', [128, 128])
        identf = sb('identf', [128, 128])
        Z = sb('Z', [128, 2, 64])
        ZP = sb('ZP', [128, 2, 64])
        L1 = sb('L1', [128, 2, 64])
        L64 = sb('L64t', [128, 2, 64])
        KS = sb('KS', [128, 2, 64])
        KO = sb('KO', [128, 2, 64])
        Q = sb('Q', [128, 2, 64])
        sm = [sb('sm%d' % i, [128, 64]) for i in range(6)]
        Bbar = sb('Bbar', [128, 2, 64, 16])
        BS = sb('BS', [128, 2, 64, 16])
        CO = sb('CO', [128, 2, 64, 16])
        CinN = sb('CinN', [128, 2, 64, 16])
        CON = sb('CON', [128, 2, 64, 16])
        tmpb = [sb("tmpb%d" % i, [128, 1024]) for i in range(8)]
        tab = sb('tab', [128, 2, GB, 128])
        BmR = sb('BmR', [128, 4, GB, 128], BF16)
        BmI = sb('BmI', [128, 4, GB, 128], BF16)
        AR = sb('AR', [128, 4, GB, 128], BF16)
        AIn = sb('AIn', [128, 4, GB, 128], BF16)
        WSP = sb('WSP', [128, 2, GB, 128])
        MZ1 = sb('MZ1', [128, GB, 128])
        Tout = sb('Tout', [128, GB, 16, 128], BF16)
        WSst = sb('WSst', [128, GB, 2, 8, 128], BF16)
        WOst = sb('WOst', [128, GB, 2, 8, 128], BF16)
        t1 = [sb('t1_%d' % i, [128, 512]) for i in range(2)]
        t2 = [sb('t2_%d' % i, [128, 512]) for i in range(2)]

        for t, nm in ((A_, 's5a'), (ldt, 's5ldt'), (Bin, 's5B'), (Cin, 's5C'), (DP, 's5DP')):
            P.dma('sp', t[:], I[nm][l], 'pl_' + nm, writes=[nm])
        for t, nm in ((cmask, 'cmask'), (cvec, 'cvec'), (mvals, 'mvals'), (identf, 'ident')):
            P.dma('sp', t[:], I[nm], 'pl_' + nm, writes=[nm])

        uid = [0]

        def key():
            uid[0] += 1
            return ('k', uid[0])

        def cexp(o_re, o_im, zr, zi, n, rk, wk, neg_im=False):
            tm = [t[:, 0:n] for t in tmpb]
            k0, k1, k2, k3, k4, k5 = [key() for _ in range(6)]
            P.act(tm[0], zr, AF.Exp, reads=rk, writes=[('tm', 0)])
            P.ts('dve', tm[1], zi, 1.0 / TWO_PI, MAGIC, ALU.mult, ALU.add, reads=rk, writes=[('tm', 1)])
            P.ts('dve', tm[2], tm[1], MAGIC, None, ALU.subtract, reads=[('tm', 1)], writes=[('tm', 2)])
            P.stt(tm[3], zi, 1.0 / TWO_PI, tm[2], ALU.mult, ALU.subtract, reads=rk + [('tm', 2)], writes=[('tm', 3)])
            P.act(tm[4], tm[3], AF.Sin, scale=TWO_PI, reads=[('tm', 3)], writes=[('tm', 4)])
            P.ts('pool', tm[5], zi, 1.0 / TWO_PI, 0.25, ALU.mult, ALU.add, reads=rk, writes=[('tm', 5)])
            P.ts('dve', tm[1], tm[5], MAGIC, None, ALU.add, reads=[('tm', 5)], writes=[('tm', 1)])
            P.ts('dve', tm[2], tm[1], MAGIC, None, ALU.subtract, reads=[('tm', 1)], writes=[('tm', 2)])
            P.tt('dve', tm[3], tm[5], tm[2], ALU.subtract, reads=[('tm', 5), ('tm', 2)], writes=[('tm', 3)])
            P.act(tm[6], tm[3], AF.Sin, scale=TWO_PI, reads=[('tm', 3)], writes=[('tm', 6)])
            P.tt('dve', o_re, tm[0], tm[6], ALU.mult, reads=[('tm', 0), ('tm', 6)], writes=wk)
            if neg_im:
                P.stt(o_im, tm[0], -1.0, tm[4], ALU.mult, ALU.mult, reads=[('tm', 0), ('tm', 4)], writes=wk)
            else:
                P.tt('pool', o_im, tm[0], tm[4], ALU.mult, reads=[('tm', 0), ('tm', 4)], writes=wk)

        def cmul(o_re, o_im, ar, ai, br, bi, n_shape, rk, wk, nb=None):
            sh = n_shape
            n = 1
            for v in sh[1:]:
                n *= v

            def tv(i):
                a = tmpb[i][:, 0:n]
                if len(sh) == 3:
                    return a.rearrange('p (a b) -> p a b', a=sh[1])
                if len(sh) == 4:
                    return a.rearrange('p (a b c) -> p a b c', a=sh[1], b=sh[2])
                return a
            ibr, ibi = (br, bi) if nb is None else nb
            P.tt('dve', tv(0), ar, br, ALU.mult, reads=rk, writes=[('tm', 0)])
            P.tt('dve', tv(1), ai, bi, ALU.mult, reads=rk, writes=[('tm', 1)])
            P.tt('dve', o_re, tv(0), tv(1), ALU.subtract, reads=[('tm', 0), ('tm', 1)], writes=wk)
            P.tt('dve' if OPT_PREP else 'pool', tv(2), ar, ibi, ALU.mult, reads=rk, writes=[('tm', 2)])
            P.tt('pool', tv(3), ai, ibr, ALU.mult, reads=rk, writes=[('tm', 3)])
            P.tt('pool', o_im, tv(2), tv(3), ALU.add, reads=[('tm', 2), ('tm', 3)], writes=wk)

        P.act(sm[0][:], ldt[:], AF.Exp, reads=['s5ldt'], writes=['dt'])
        for c in range(2):
            P.tt('dve', Z[:, c, :], A_[:, c, :], sm[0][:], ALU.mult, reads=['s5a', 'dt'], writes=['Z'])
            P.ts('dve', ZP[:, c, :], Z[:, c, :], cvec[:, 0:1], None, ALU.mult, reads=['Z', 'cvec'], writes=['ZP'])
        cexp(L1[:, 0, :], L1[:, 1, :], Z[:, 0, :], Z[:, 1, :], 64, ['Z'], ['L1'])
        for (dst, col, nm) in ((L64, None, 'L64'), (KS, 1, 'KS'), (KO, 2, 'KO')):
            for c in range(2):
                if col is None:
                    P.ts('dve', sm[1 + c][:], Z[:, c, :], 64.0, None, ALU.mult, reads=['Z'], writes=[('zs', c)])
                else:
                    P.ts('dve', sm[1 + c][:], Z[:, c, :], cvec[:, col:col + 1], None, ALU.mult, reads=['Z', 'cvec'], writes=[('zs', c)])
            cexp(dst[:, 0, :], dst[:, 1, :], sm[1][:], sm[2][:], 64, [('zs', 0), ('zs', 1)], [nm])
        P.dma('sp', S['L64', l], L64[:], 'L64st', reads=['L64'])
        P.ts('dve', sm[1][:], L1[:, 0, :], -1.0, None, ALU.add, reads=['L1'], writes=['numr'])
        P.tt('dve', sm[2][:], A_[:, 0, :], A_[:, 0, :], ALU.mult, reads=['s5a'], writes=['d1'])
        P.tt('dve', sm[3][:], A_[:, 1, :], A_[:, 1, :], ALU.mult, reads=['s5a'], writes=['d2'])
        P.tt('dve', sm[2][:], sm[2][:], sm[3][:], ALU.add, reads=['d1', 'd2'], writes=['den'])
        P.add('dve', lambda e: e.reciprocal(out=sm[3][:], in_=sm[2][:]), reads=['den'], writes=['rden'])
        P.tt('dve', sm[4][:], sm[1][:], A_[:, 0, :], ALU.mult, reads=['numr', 's5a'], writes=['q1'])
        P.tt('dve', sm[5][:], L1[:, 1, :], A_[:, 1, :], ALU.mult, reads=['L1', 's5a'], writes=['q2'])
        P.tt('dve', sm[4][:], sm[4][:], sm[5][:], ALU.add, reads=['q1', 'q2'], writes=['q3'])
        P.tt('dve', Q[:, 0, :], sm[4][:], sm[3][:], ALU.mult, reads=['q3', 'rden'], writes=['Qr'])
        P.tt('dve', sm[4][:], L1[:, 1, :], A_[:, 0, :], ALU.mult, reads=['L1', 's5a', 'Qr'], writes=['q4'])
        P.tt('dve', sm[5][:], sm[1][:], A_[:, 1, :], ALU.mult, reads=['numr', 's5a', 'Qr'], writes=['q5'])
        P.tt('dve', sm[4][:], sm[4][:], sm[5][:], ALU.subtract, reads=['q4', 'q5'], writes=['q6'])
        P.tt('dve', Q[:, 1, :], sm[4][:], sm[3][:], ALU.mult, reads=['q6', 'rden'], writes=['Qi'])

        def bc(ap2, n):
            return ap2.unsqueeze(2).to_broadcast([128, 64, n])
        cmul(Bbar[:, 0], Bbar[:, 1], bc(Q[:, 0, :], 16), bc(Q[:, 1, :], 16), Bin[:, 0], Bin[:, 1], [128, 64, 16],
             ['Qr', 'Qi', 's5B'], ['Bbar'])
        cmul(BS[:, 0], BS[:, 1], bc(KS[:, 0, :], 16), bc(KS[:, 1, :], 16), Bbar[:, 0], Bbar[:, 1], [128, 64, 16],
             ['KS', 'Bbar'], ['BS'])
        cmul(CO[:, 0], CO[:, 1], bc(KO[:, 0, :], 16), bc(KO[:, 1, :], 16), Cin[:, 0], Cin[:, 1], [128, 64, 16],
             ['KO', 's5C'], ['CO'])
        P.ts('dve', CinN[:].rearrange('p c g h -> p (c g h)'), Cin[:].rearrange('p c g h -> p (c g h)'), -1.0, None, ALU.mult,
             reads=['s5C'], writes=['CinN'])
        P.ts('dve', CON[:].rearrange('p c g h -> p (c g h)'), CO[:].rearrange('p c g h -> p (c g h)'), -1.0, None, ALU.mult,
             reads=['CO'], writes=['CON'])

        psi = 0
        for gb in range(64 // GB):
            g0 = gb * GB
            gs = slice(g0, g0 + GB)
            for c in range(2):
                P.tt('dve', tmpb[7][:, 0:GB * 128].rearrange('p (a b) -> p a b', a=GB) if c == 0 else
                     MZ1[:],
                     ZP[:, c, gs].unsqueeze(2).to_broadcast([128, GB, 128]),
                     mvals[:].unsqueeze(1).to_broadcast([128, GB, 128]), ALU.mult,
                     reads=['ZP', 'mvals'], writes=[('mz', c)])
            cexp(tab[:, 0].rearrange('p a b -> p (a b)'), tab[:, 1].rearrange('p a b -> p (a b)'),
                 tmpb[7][:, 0:GB * 128], MZ1[:].rearrange('p a b -> p (a b)'), GB * 128,
                 [('mz', 0), ('mz', 1)], ['tab'])

            def tabv(c, start, step, n_inner):
                if step > 0:
                    v = tab[:, c, :, start:start + 7 * step + 1:step]
                else:
                    stop = start + 7 * step - 1
                    v = tab[:, c, :, start:(stop if stop >= 0 else None):step]
                return v.unsqueeze(3).to_broadcast([128, GB, 8, n_inner])

            def vecv(t, c):
                return t[:, c, gs, :].unsqueeze(2).to_broadcast([128, GB, 8, 16])

            def o4(t):
                return t.rearrange('p g (a b) -> p g a b', a=8)
            for d1 in range(4):
                cmul(o4(BmR[:, d1]), o4(BmI[:, d1]), tabv(0, 63 + d1, -8, 16), tabv(1, 63 + d1, -8, 16), vecv(Bbar, 0), vecv(Bbar, 1),
                     [128, GB, 8, 16], ['tab', 'Bbar'], ['Bm'])
            for d2 in range(4):
                cmul(o4(AR[:, d2]), o4(AIn[:, d2]), tabv(0, 63 + 4 * (d2 - 2), 8, 16), tabv(1, 63 + 4 * (d2 - 2), 8, 16),
                     vecv(Cin, 0), vecv(Cin, 1), [128, GB, 8, 16], ['tab', 's5C', 'CinN'], ['Am'], nb=(vecv(CinN, 0), vecv(CinN, 1)))
            for gl in range(GB):
                for d1 in range(4):
                    pa, pb = psi % 8, (psi + 1) % 8
                    psi += 2
                    for (pp, h0) in ((pa, 0), (pb, 64)):
                        hs = slice(h0, h0 + 64)
                        out = G.PS[pp][:].rearrange('p (a b) -> p a b', a=4)
                        P.mm(out, BmR[hs, d1, gl, :], AR[hs, :, gl, :], start=True, stop=False, reads=['Bm', 'Am'], writes=[('ps', pp)])
                        P.mm(out, BmI[hs, d1, gl, :], AIn[hs, :, gl, :], start=False, stop=True, reads=['Bm', 'Am'], writes=[('ps', pp)])
                    ts_ = (gl * 4 + d1) % 2
                    P.tt('dve', t1[ts_][:], G.PS[pa][:], cmask[:, 0, d1].rearrange('p a b -> p (a b)'), ALU.mult,
                         reads=[('ps', pa), 'cmask'], writes=[('t1', ts_)])
                    if d1 == 0:
                        P.stt(t1[ts_][:, 256:384], identf[:], DP[:, g0 + gl:g0 + gl + 1], t1[ts_][:, 256:384], ALU.mult, ALU.add,
                              reads=[('t1', ts_), 'ident', 's5DP'], writes=[('t1', ts_)])
                    P.tt('dve', t2[ts_][:], G.PS[pb][:], cmask[:, 1, d1].rearrange('p a b -> p (a b)'), ALU.mult,
                         reads=[('ps', pb), 'cmask'], writes=[('t2', ts_)])
                    P.tt('pool', Tout[:, gl, d1 * 4:(d1 + 1) * 4, :].rearrange('p a b -> p (a b)'), t1[ts_][:], t2[ts_][:], ALU.add,
                         reads=[('t1', ts_), ('t2', ts_)], writes=['Tout'])
            P.dma('sp', S['WT', l][gs].rearrange('g p a b -> p g a b'), Tout[:], 'Toutst', reads=['Tout'])
            for Ip in range(8):
                cmul(o4(WSP[:, 0]), o4(WSP[:, 1]), tabv(0, 126 - Ip, -8, 16), tabv(1, 126 - Ip, -8, 16), vecv(BS, 0), vecv(BS, 1),
                     [128, GB, 8, 16], ['tab', 'BS'], ['WSP'])
                for c in range(2):
                    for g4 in range(GB // 4):
                        pp = psi % 8
                        psi += 1
                        for gq in range(4):
                            gl = g4 * 4 + gq
                            P.tr(G.PS[pp][:, gq * 128:(gq + 1) * 128], WSP[:, c, gl, :], identf[:], reads=['WSP', 'ident'], writes=[('ps', pp)])
                        P.copy('act', WSst[:, g4 * 4:(g4 + 1) * 4, c, Ip, :], G.PS[pp][:].rearrange('p (a b) -> p a b', a=4),
                               reads=[('ps', pp)], writes=['WSst'])
            P.dma('sp', S['WS', l][gs].rearrange('g p c i q -> p g c i q'), WSst[:], 'WSstst', reads=['WSst'])
            for Jp in range(8):
                cmul(o4(WOst[:, :, 0, Jp, :]), o4(WOst[:, :, 1, Jp, :]), tabv(0, 64 + Jp, 8, 16), tabv(1, 64 + Jp, 8, 16),
                     vecv(CO, 0), vecv(CO, 1), [128, GB, 8, 16], ['tab', 'CO', 'CON'], ['WOst'], nb=(vecv(CON, 0), vecv(CON, 1)))
            P.dma('sp', S['WO', l][gs].rearrange('g p c j q -> p g c j q'), WOst[:], 'WOstst', reads=['WOst'])
        P.flush()


def tslot(delta):
    d1 = delta % 4
    d2 = (delta - d1) // 4
    return d1 * 4 + d2 + 2


def stage_2(G, l):
    nc, P, I, S = G.nc, G.P, G.I, G.S
    last = (l == DEPTH - 1)
    NS = 68
    import os
    CUT = int(os.environ.get('S2CUT', '9'))
    with ExitStack() as es:
        def sb(name, shape, dt=F32):
            return es.enter_context(G.sbt(name, shape, dt))
        UTl = sb('UTl', [128, 64, 512], BF16)
        UTc = sb('UTc', [128, 64, 4, 8], BF16)
        X2 = sb('X2', [128, 2, 64, NS])
        Hb = sb('Hb', [128, 2, 64, NS], BF16)
        Lt = sb('Lt', [128, 2, 64])
        L1s = sb('L1s', [128, 2, 64])
        L2s = sb('L2s', [128, 2, 64])
        tA = sb('tA', [128, 2, 64])
        tB = sb('tB', [128, 2, 64])
        WSg = [sb('WSg%d' % i, [128, 2, 8, 128], BF16) for i in range(2)]
        WTg = [sb('WTg%d' % i, [128, 16, 128], BF16) for i in range(2)]
        WOg = [sb('WOg%d' % i, [128, 2, 8, 128], BF16) for i in range(2)]
        Yst = [sb('Yst%d' % i, [128, 8, 512], BF16) for i in range(2)]
        Yc = sb('Yc', [128, 64, 4, 8], BF16)

        P.dma('sp', Lt[:], S['L64', l], 'Lt', writes=['Lt'])
        P.copy('dve', L1s[:, 0, :], Lt[:, 0, :], reads=['Lt'], writes=['L1s'])
        P.copy('dve', L1s[:, 1, :], Lt[:, 0, :], reads=['Lt'], writes=['L1s'])
        P.ts('dve', L2s[:, 0, :], Lt[:, 1, :], -1.0, None, ALU.mult, reads=['Lt'], writes=['L2s'])
        P.copy('dve', L2s[:, 1, :], Lt[:, 1, :], reads=['Lt'], writes=['L2s'])
        P.memset('pool', Hb[0:64, :, :, 64:65], 0.0, writes=['Hb0'])
        P.memset('pool', Hb[64:128, :, :, 67:68], 0.0, writes=['Hb0'])
        psi = 0
        wi = 0
        for b in range(NB if CUT >= 9 else 1):
            for i in range(8):
                P.dma('sp', UTl[i * 16:(i + 1) * 16, :, :],
                      S['s5u', b][:, CTX + 512 * i:CTX + 512 * (i + 1)].rearrange('(g h) t -> h g t', h=16),
                      ('UTl', i), writes=['UTl'])
                P.dma('sp', UTc[i * 16:(i + 1) * 16, :, :, :].rearrange('p g c j -> p g (c j)'),
                      S['s5uc', b][:, 32 * i:32 * (i + 1)].rearrange('(g h) t -> h g t', h=16),
                      ('UTc', i), writes=['UTc'])
            if CUT < 2:
                continue
            G7 = 7
            for g0 in range(0, 64, G7):
                gn = min(G7, 64 - g0)
                pbank = []
                for c in range(2):
                    pbank.append(psi % 8)
                    psi += 1
                for gl in range(gn):
                    g = g0 + gl
                    ws = wi % 2
                    wi += 1
                    P.dma('sp', WSg[ws][:], S['WS', l][g], ('WSg', ws), writes=[('WSg', ws)])
                    for c in range(2):
                        ps = G.PS[pbank[c]]
                        for Ip in range(8):
                            P.mm(ps[:, gl * NS:gl * NS + 64], WSg[ws][:, c, Ip, :], UTl[:, g, Ip * 64:(Ip + 1) * 64], start=(Ip == 0), stop=(Ip == 7),
                                 reads=[('WSg', ws), 'UTl'], writes=[('ps', pbank[c])])
                        for Ip in range(8):
                            P.mm(ps[:, gl * NS + 64:gl * NS + 68], WSg[ws][:, c, Ip, :], UTc[:, g, :, Ip], start=(Ip == 0), stop=(Ip == 7),
                                 reads=[('WSg', ws), 'UTc'], writes=[('ps', pbank[c])])
                for c in range(2):
                    ps = G.PS[pbank[c]]
                    pv = ps[:, 0:gn * NS].rearrange('p (g s) -> p g s', s=NS)
                    eng = 'dve'
                    P.copy(eng, X2[0:64, c, g0:g0 + gn, 0:4], pv[0:64, :, 64:68], reads=[('ps', pbank[c])], writes=['X2'])
                    P.copy(eng, X2[0:64, c, g0:g0 + gn, 4:68], pv[0:64, :, 0:64], reads=[('ps', pbank[c])], writes=['X2'])
                    P.copy(eng, X2[64:128, c, g0:g0 + gn, 0:4], pv[64:128, :, 67:63:-1], reads=[('ps', pbank[c])], writes=['X2'])
                    P.copy(eng, X2[64:128, c, g0:g0 + gn, 4:68], pv[64:128, :, 63::-1], reads=[('ps', pbank[c])], writes=['X2'])
            if CUT < 3:
                continue
            for s_ in range(1, NS):
                prev = X2[:, :, :, s_ - 1]
                prev_sw = X2[:, ::-1, :, s_ - 1]
                cur = X2[:, :, :, s_]
                P.tt('dve', tA[:], L1s[:], prev, ALU.mult, reads=['L1s', 'X2'], writes=['tA'])
                P.tt('dve', tB[:], L2s[:], prev_sw, ALU.mult, reads=['L2s', 'X2'], writes=['tB'])
                P.tt('dve', cur, cur, tA[:], ALU.add, reads=['X2', 'tA'], writes=['X2'])
                P.tt('dve', cur, cur, tB[:], ALU.add, reads=['X2', 'tB'], writes=['X2'])
            for c in range(2):
                P.copy('dve', Hb[0:64, c, :, 0:64], X2[0:64, c, :, 3:67], reads=['X2'], writes=['Hb'])
                P.copy('dve', Hb[0:64, c, :, 65:68], X2[0:64, c, :, 0:3], reads=['X2'], writes=['Hb'])
                P.copy('dve', Hb[64:128, c, :, 0:64], X2[64:128, c, :, 66:2:-1], reads=['X2'], writes=['Hb'])
                P.copy('dve', Hb[64:128, c, :, 64:67], X2[64:128, c, :, 2::-1], reads=['X2'], writes=['Hb'])
            if CUT < 4:
                continue
            for g in range(64 if CUT >= 5 else 8):
                ws = wi % 2
                wi += 1
                P.dma('sp', WTg[ws][:], S['WT', l][g], ('WTg', ws), writes=[('WTg', ws)])
                P.dma('sp', WOg[ws][:], S['WO', l][g], ('WOg', ws), writes=[('WOg', ws)])
                pb = psi % 8
                psi += 1
                ps = G.PS[pb]
                rk = [('WTg', ws), ('WOg', ws), 'UTl', 'UTc', 'Hb', 'Hb0']
                for Jp in range(8):
                    out = ps[:, Jp * 64:(Jp + 1) * 64]
                    for Ip in range(8):
                        P.mm(out, WTg[ws][:, tslot(Jp - Ip), :], UTl[:, g, Ip * 64:(Ip + 1) * 64], start=(Ip == 0), stop=False,
                             reads=rk, writes=[('ps', pb)])
                    P.mm(out, WOg[ws][:, 0, Jp, :], Hb[:, 0, g, 0:64], start=False, stop=False, reads=rk, writes=[('ps', pb)])
                    P.mm(out, WOg[ws][:, 1, Jp, :], Hb[:, 1, g, 0:64], start=False, stop=True, reads=rk, writes=[('ps', pb)])
                ys = (g // 8) % 2
                P.act(Yst[ys][:, g % 8, :], ps[:], AF.Gelu_apprx_tanh, reads=[('ps', pb)], writes=[('Yst', ys)])
                if not last:
                    pc = psi % 8
                    psi += 1
                    psc = G.PS[pc]
                    for Jp in range(8):
                        out = psc[:, Jp * 4:(Jp + 1) * 4]
                        for Ip in range(8):
                            P.mm(out, WTg[ws][:, tslot(Jp - Ip), :], UTc[:, g, :, Ip], start=(Ip == 0), stop=False, reads=rk, writes=[('ps', pc)])
                        P.mm(out, WOg[ws][:, 0, Jp, :], Hb[:, 0, g, 64:68], start=False, stop=False, reads=rk, writes=[('ps', pc)])
                        P.mm(out, WOg[ws][:, 1, Jp, :], Hb[:, 1, g, 64:68], start=False, stop=True, reads=rk, writes=[('ps', pc)])
                    P.act(Yc[:, g, :, :], psc[:, 0:32].rearrange('p (j c) -> p c j', c=4), AF.Gelu_apprx_tanh, reads=[('ps', pc)], writes=['Yc'])
                if g % 8 == 7:
                    g8 = g - 7
                    for j in range(8):
                        P.dma('sp', S['y5', b][g8 * 16:(g8 + 8) * 16, CTX + 512 * j:CTX + 512 * (j + 1)].rearrange('(g h) t -> h g t', h=16),
                              Yst[ys][j * 16:(j + 1) * 16, :, :], ('Yst', ys), reads=[('Yst', ys)])
            if not last:
                for j in range(8):
                    P.dma('sp', S['y5', b][:, 32 * j:32 * (j + 1)].rearrange('(g h) t -> h g t', h=16),
                          Yc[j * 16:(j + 1) * 16, :, :, :].rearrange('p g c j -> p g (c j)'), 'Ycst', reads=['Yc'])
        P.flush()


def ln_pre(P, T, oc, n, nslot, tiles):
    Rb, SQb = tiles[0], tiles[1]
    tk = ('T32', nslot)
    P.copy('pool', Rb[:, oc, 0:n], T[:, oc, 0:n], reads=[tk], writes=['Rb'])
    P.act(SQb[:, oc, 0:n], T[:, oc, 0:n], AF.Square, reads=[tk], writes=['SQb'])


def ln_block(G, l, P, T, Xres, n, nslot, tiles, gname, bname, psi_ref, rk_extra):
    Rb, SQb, MEAN, M2, VAR, RSTD, ONESB = tiles
    tk = ('T32', nslot)
    pm = psi_ref[0] % 8
    pq = (psi_ref[0] + 1) % 8
    psi_ref[0] += 2
    for oc in range(8):
        P.mm(G.PS[pm][:, 0:n], ONESB[:], Rb[:, oc, 0:n], start=(oc == 0), stop=(oc == 7), reads=['Rb', 'ONESB'], writes=[('ps', pm)])
    for oc in range(8):
        P.mm(G.PS[pq][:, 0:n], ONESB[:], SQb[:, oc, 0:n], start=(oc == 0), stop=(oc == 7), reads=['SQb', 'ONESB'], writes=[('ps', pq)])
    P.copy('act', MEAN[:, 0:n], G.PS[pm][:, 0:n], reads=[('ps', pm)], writes=['MEAN'])
    P.tt('pool', M2[:, 0:n], MEAN[:, 0:n], MEAN[:, 0:n], ALU.mult, reads=['MEAN'], writes=['M2'])
    P.tt('dve', VAR[:, 0:n], G.PS[pq][:, 0:n], M2[:, 0:n], ALU.subtract, reads=[('ps', pq), 'M2'], writes=['VAR'])
    P.act(VAR[:, 0:n], VAR[:, 0:n], AF.Sqrt, bias=G.EPS[:], reads=['VAR'], writes=['VAR'])
    P.add('dve', lambda e, o=RSTD[:, 0:n], i=VAR[:, 0:n]: e.reciprocal(out=o, in_=i), reads=['VAR'], writes=['RSTD'])
    for oc in range(8):
        P.tt('pool', T[:, oc, 0:n], T[:, oc, 0:n], MEAN[:, 0:n], ALU.subtract, reads=[tk, 'MEAN'], writes=[tk])
        P.tt('dve', T[:, oc, 0:n], T[:, oc, 0:n], RSTD[:, 0:n], ALU.mult, reads=[tk, 'RSTD'], writes=[tk])
        P.ts('dve', T[:, oc, 0:n], T[:, oc, 0:n], vec(G, l, gname, oc), vec(G, l, bname, oc), ALU.mult, ALU.add, reads=[tk], writes=[tk])


def stage_3a(G, l):
    nc, P, I, S = G.nc, G.P, G.I, G.S
    last = (l == DEPTH - 1)
    with ExitStack() as es:
        def sb(name, shape, dt=F32):
            return es.enter_context(G.sbt(name, shape, dt))
        GLW = sb('GLW', [128, 8, D], BF16)
        WO_ = sb('WO_', [128, 16, D], BF16)
        ONESB = sb('ONESB', [128, 128], BF16)
        Y5 = [sb('Y5_%d' % i, [128, 8, 512], BF16) for i in range(2)]
        Y5n = sb('Y5n', [128, 8, CTX], BF16)
        RGt = [sb('RGt%d' % i, [128, 8, 512], BF16) for i in range(2)]
        Xin = [sb('Xin%d' % i, [128, 8, 512]) for i in range(2)]
        S5o = sb('S5o', [128, 8, 512], BF16)
        T32 = [sb('T32_%d' % i, [128, 8, 512]) for i in range(2)]
        Rb = sb('Rb', [128, 8, 512], BF16)
        SQb = sb('SQb', [128, 8, 512], BF16)
        sig = [sb('sig%d' % i, [128, 512]) for i in range(2)]
        MEAN = sb('MEAN', [128, 512])
        M2 = sb('M2', [128, 512])
        VAR = sb('VAR', [128, 512])
        RSTD = sb('RSTD', [128, 512])
        tiles = (Rb, SQb, MEAN, M2, VAR, RSTD, ONESB)
        for k in range(8):
            P.dma('pool', GLW[:, k, :], I['glu_w'][l, k * 128:(k + 1) * 128, :], 'GLWd', writes=['GLW'])
        for k in range(16):
            P.dma('pool', WO_[:, k, :], I['w_out'][l, k * 128:(k + 1) * 128, :], 'WO_d', writes=['WO_'])
        P.memset('dve', ONESB[:], 1.0 / D, writes=['ONESB'])
        psi = [0]
        sgc = [0]
        blocks = []
        for b in range(NB):
            for (kind, t0, n) in seg_list():
                if kind == 'ctx' and last:
                    continue
                blocks.append((b, kind, t0, n))

        def phA(i):
            b, kind, t0, n = blocks[i]
            slot = i % 2
            xsrc = (I['cxT'] if kind == 'ctx' else I['xT']) if l == 0 else S['x', kind]
            off = t0 if kind == 'ctx' else CTX + t0
            P.dma('sp', Y5[slot][:, :, 0:n], S['y5', b][:, off:off + n].rearrange('(k p) t -> p k t', p=128), ('Y5', slot), writes=[('Y5', slot)])
            P.dma('sp', RGt[slot][:, :, 0:n], S['rg', b][:, off:off + n].rearrange('(k p) t -> p k t', p=128), ('RGt', slot), writes=[('RGt', slot)])
            P.dma('sp', Xin[slot][:, :, 0:n], xsrc[b, :, t0:t0 + n].rearrange('(k p) t -> p k t', p=128), ('Xin', slot), writes=[('Xin', slot)])
            if kind == 'ctx':
                for k in range(8):
                    for cc in range(4):
                        P.copy('pool', Y5n[:, k, cc * 64:(cc + 1) * 64].rearrange('p (j q) -> p j q', j=8),
                               Y5[slot][:, k, 0:CTX].rearrange('p (j c q) -> p j c q', j=8, c=4)[:, :, cc, :],
                               reads=[('Y5', slot)], writes=['Y5n'])

        def phB(i):
            b, kind, t0, n = blocks[i]
            slot = i % 2
            if kind == 'ctx':
                ysrc, yk = Y5n, 'Y5n'
            else:
                ysrc, yk = Y5[slot], ('Y5', slot)
            for oc in range(8):
                pb = psi[0] % 8
                psi[0] += 1
                for k in range(8):
                    P.mm(G.PS[pb][:, 0:n], GLW[:, k, oc * 128:(oc + 1) * 128], ysrc[:, k, 0:n], start=(k == 0), stop=(k == 7),
                         reads=['GLW', yk], writes=[('ps', pb)])
                ss = sgc[0] % 2
                sgc[0] += 1
                P.act(sig[ss][:, 0:n], G.PS[pb][:, 0:n], AF.Sigmoid, bias=vec(G, l, 'glu_b', oc), reads=[('ps', pb)], writes=[('sig', ss)])
                P.tt('dve', S5o[:, oc, 0:n], ysrc[:, oc, 0:n], sig[ss][:, 0:n], ALU.mult, reads=[yk, ('sig', ss)], writes=['S5o'])

        def phC(i):
            b, kind, t0, n = blocks[i]
            slot = i % 2
            m = 2 if kind == 'ctx' else b
            T = T32[slot]
            for oc in range(8):
                pb = psi[0] % 8
                psi[0] += 1
                for k in range(16):
                    rhs = RGt[slot][:, k, 0:n] if k < 8 else S5o[:, k - 8, 0:n]
                    P.mm(G.PS[pb][:, 0:n], WO_[:, k, oc * 128:(oc + 1) * 128], rhs, start=(k == 0), stop=(k == 15),
                         reads=['WO_', ('RGt', slot), 'S5o'], writes=[('ps', pb)])
                P.ts('dve', T[:, oc, 0:n], G.PS[pb][:, 0:n], vec(G, l, 'b_out', oc), modv(G, l, 2, oc, m), ALU.add, ALU.mult,
                     reads=[('ps', pb), 'MOD'], writes=[('T32', slot)])
                P.stt(T[:, oc, 0:n], Xin[slot][:, oc, 0:n], ALPHA, T[:, oc, 0:n], ALU.mult, ALU.add,
                      reads=[('Xin', slot), ('T32', slot)], writes=[('T32', slot)])
                ln_pre(P, T, oc, n, slot, tiles)

        def phD(i):
            b, kind, t0, n = blocks[i]
            slot = i % 2
            T = T32[slot]
            ln_block(G, l, P, T, None, n, slot, tiles, 'ln1_g', 'ln1_b', psi, None)
            P.dma('sp', S['x1', kind][b, :, t0:t0 + n].rearrange('(k p) t -> p k t', p=128), T[:, :, 0:n], ('T32', slot),
                  reads=[('T32', slot)])

        nb_ = len(blocks)
        phA(0)
        for i in range(nb_):
            if i + 1 < nb_:
                phA(i + 1)
            phB(i)
            if i > 0 and OPT_3:
                phD(i - 1)
            phC(i)
            if not OPT_3:
                phD(i)
        if OPT_3:
            phD(nb_ - 1)
        P.flush()


def stage_3b(G, l):
    nc, P, I, S = G.nc, G.P, G.I, G.S
    last = (l == DEPTH - 1)
    NBLK = 256
    with ExitStack() as es:
        def sb(name, shape, dt=F32):
            return es.enter_context(G.sbt(name, shape, dt))
        W1b = sb('W1b', [128, 8, 4 * D], BF16)
        W2b = sb('W2b', [128, 32, D], BF16)
        ONESB = sb('ONESB', [128, 128], BF16)
        X1 = [sb('X1_%d' % i, [128, 8, NBLK]) for i in range(2)]
        X1m = [sb('X1m%d' % i, [128, 8, NBLK], BF16) for i in range(2)]

        def slot3(i):
            return i % 2
        Hh = sb('Hh', [128, 32, NBLK], BF16)
        hr = [sb('hr%d' % i, [128, NBLK], BF16) for i in range(2)]
        T32 = [sb('T32_%d' % i, [128, 8, NBLK]) for i in range(2)]
        Rb = sb('Rb', [128, 8, NBLK], BF16)
        SQb = sb('SQb', [128, 8, NBLK], BF16)
        MEAN = sb('MEAN', [128, NBLK])
        M2 = sb('M2', [128, NBLK])
        VAR = sb('VAR', [128, NBLK])
        RSTD = sb('RSTD', [128, NBLK])
        tiles = (Rb, SQb, MEAN, M2, VAR, RSTD, ONESB)
        for k in range(8):
            for h in range(2):
                P.dma('pool', W1b[:, k, h * 2048:(h + 1) * 2048], I['w1'][l, k * 128:(k + 1) * 128, h * 2048:(h + 1) * 2048],
                      'W1bd', writes=['W1b'])
        for k in range(32):
            P.dma('pool', W2b[:, k, :], I['w2'][l, k * 128:(k + 1) * 128, :], 'W2bd', writes=['W2b'])
        P.memset('dve', ONESB[:], 1.0 / D, writes=['ONESB'])
        psi = [0]
        hsc = [0]
        blocks = []
        for b in range(NB):
            segs = [] if last else [('ctx', 0, CTX)]
            segs += [('lat', t * NBLK, NBLK) for t in range(SEQ // NBLK)]
            for (kind, t0, n) in segs:
                blocks.append((b, kind, t0, n))

        def phA(i):
            b, kind, t0, n = blocks[i]
            s3 = slot3(i)
            P.dma('sp', X1[s3][:], S['x1', kind][b, :, t0:t0 + n].rearrange('(k p) t -> p k t', p=128), ('X1', s3), writes=[('X1', s3)])

        def phM(i):
            b, kind, t0, n = blocks[i]
            s3 = slot3(i)
            m = 2 if kind == 'ctx' else b
            for k in range(8):
                P.ts('dve', X1m[i % 2][:, k, :], X1[s3][:, k, :], G.MP1[:, l, 1, k, m:m + 1], modv(G, l, 3, k, m), ALU.mult, ALU.add,
                     reads=[('X1', s3), 'MP1', 'MOD'], writes=[('X1m', i % 2)])

        def phB(i):
            b, kind, t0, n = blocks[i]
            slot = i % 2
            m = 2 if kind == 'ctx' else b
            for hc in range(32):
                pb = psi[0] % 8
                psi[0] += 1
                for k in range(8):
                    P.mm(G.PS[pb][:, 0:n], W1b[:, k, hc * 128:(hc + 1) * 128], X1m[i % 2][:, k, :], start=(k == 0), stop=(k == 7),
                         reads=['W1b', ('X1m', i % 2)], writes=[('ps', pb)])
                h2 = hsc[0] % 2
                hsc[0] += 1
                P.act(hr[h2][:], G.PS[pb][:, 0:n], AF.Relu, bias=vec(G, l, 'b1', hc), reads=[('ps', pb)], writes=[('hr', h2)])
                P.tt('pool', Hh[:, hc, :], hr[h2][:], hr[h2][:], ALU.mult, reads=[('hr', h2)], writes=['Hh'])

        def phC(i):
            b, kind, t0, n = blocks[i]
            slot = i % 2
            m = 2 if kind == 'ctx' else b
            T = T32[slot]
            for oc in range(8):
                pb = psi[0] % 8
                psi[0] += 1
                for k in range(32):
                    P.mm(G.PS[pb][:, 0:n], W2b[:, k, oc * 128:(oc + 1) * 128], Hh[:, k, :], start=(k == 0), stop=(k == 31),
                         reads=['W2b', 'Hh'], writes=[('ps', pb)])
                P.ts('dve', T[:, oc, :], G.PS[pb][:, 0:n], vec(G, l, 'b2', oc), modv(G, l, 5, oc, m), ALU.add, ALU.mult,
                     reads=[('ps', pb), 'MOD'], writes=[('T32', slot)])
                P.stt(T[:, oc, :], X1[slot3(i)][:, oc, :], ALPHA, T[:, oc, :], ALU.mult, ALU.add,
                      reads=[('X1', slot3(i)), ('T32', slot)], writes=[('T32', slot)])
                ln_pre(P, T, oc, n, slot, tiles)

        def phD(i):
            b, kind, t0, n = blocks[i]
            slot = i % 2
            T = T32[slot]
            ln_block(G, l, P, T, None, n, slot, tiles, 'ln2_g', 'ln2_b', psi, None)
            if last:
                dst = G.yT[b, :, t0:t0 + n]
            else:
                dst = S['x', kind][b, :, t0:t0 + n]
            P.dma('sp', dst.rearrange('(k p) t -> p k t', p=128), T[:, :, 0:n], ('T32', slot), reads=[('T32', slot)])

        nb_ = len(blocks)
        phA(0)
        phM(0)
        for i in range(nb_):
            if i + 1 < nb_:
                phA(i + 1)
            phB(i)
            if i + 1 < nb_:
                phM(i + 1)
            if i > 0:
                phD(i - 1)
            phC(i)
        phD(nb_ - 1)
        P.flush()


def rev(ap):
    return ap[:, ::-1]


def stage_1b(G, l):
    nc, P, I, S = G.nc, G.P, G.I, G.S
    NT = CTX + SEQ
    with ExitStack() as es:
        def sb(name, shape, dt):
            return es.enter_context(G.sbt(name, shape, dt))
        identf = sb('identf', [128, 128], F32)
        DG = sb('DG', [128, 8, 4, 128], BF16)
        GW = sb('GW', [128, 2, 2, 8, 128], BF16)
        cneg = sb('cneg', [128, 2, 8], F32)
        ctmp = sb('ctmp', [128, 2, 8], F32)
        RP = [sb('RP%d' % i, [128, NT + 6], BF16) for i in range(2)]
        GGt = [sb('GGt%d' % i, [128, NT], BF16) for i in range(2)]
        OUT = [sb('OUT%d' % i, [128, NT], BF16) for i in range(1)]
        A = [sb('A%d' % i, [128, NT], F32) for i in range(2)]
        Bt = [sb('B%d' % i, [128, NT], F32) for i in range(2)]
        HF = sb('HF', [128, NT], F32)
        XC32 = [sb('XC32_%d' % i, [128, 512], F32) for i in range(3)]
        XCB = [sb('XCB%d' % i, [128, 512], BF16) for i in range(3)]
        Rt = [[sb('Rt%d_%d' % (d, i), [128, 512], F32) for i in range(2)] for d in range(2)]
        It = [[sb('It%d_%d' % (d, i), [128, 512], F32) for i in range(2)] for d in range(2)]
        Tt = [[sb('Tt%d_%d' % (d, i), [128, 512], F32) for i in range(3)] for d in range(2)]
        Mt = [[sb('Mt%d_%d' % (d, i), [128, 512], F32) for i in range(2)] for d in range(2)]
        Ut = [[sb('Ut%d_%d' % (d, i), [128, 512], F32) for i in range(3)] for d in range(2)]

        P.dma('sp', identf[:], I['ident'], 'identf', writes=['identf'])
        for w, nm in enumerate(('rg_wa', 'rg_wi')):
            for d in range(2):
                P.dma('pool', GW[:, w, d, :, :], I[nm][l, d].rearrange('h i j -> i h j'), 'GWd', writes=['GW'])
        for fc in range(8):
            for tap in range(4):
                P.ts('dve', DG[:, fc, tap, :], identf[:], vec(G, l, 'conv_w%d' % tap, fc), None, ALU.mult,
                     reads=['identf'], writes=['DG'])
        for d in range(2):
            o = VOFF['lam%d' % d]
            P.act(ctmp[:, d, :], G.VEC[:, l, o:o + 8], AF.Exp, scale=-1.0, writes=['ctmp'])
        P.act(cneg[:], ctmp[:], AF.Ln, bias=G.ONE[:], reads=['ctmp'], writes=['cneg0'])
        P.ts('dve', cneg[:], cneg[:], -RG_C, None, ALU.mult, reads=['cneg0'], writes=['cneg'])
        for i in range(2):
            P.memset('pool', RP[i][:], 0.0, writes=[('RP', i)])
        segs = [(0, CTX, 1)] + [(CTX + t * 512, 512, 4 + CTX + t * 512) for t in range(SEQ // 512)]
        it = 0
        psi = 0
        gseg = [0]
        for b in range(NB):
            for fc in range(8):
                slot = it % 2
                rows = slice(fc * 128, (fc + 1) * 128)

                def ld1b(j):
                    bb, ff = j // 8, j % 8
                    sl = j % 2
                    rr = slice(ff * 128, (ff + 1) * 128)
                    P.dma('sp', RP[sl][:, 1:1 + CTX], S['rgx', bb][rr, 0:CTX], ('RPc', sl), writes=[('RP', sl)])
                    P.dma('sp', RP[sl][:, 4 + CTX:4 + CTX + SEQ], S['rgx', bb][rr, CTX:NT], ('RPl', sl), writes=[('RP', sl)])
                    P.dma('sp', GGt[sl][:], S['gg', bb][rr, :], ('GGt', sl), writes=[('GGt', sl)])
                if it == 0:
                    ld1b(0)
                if it + 1 < NB * 8:
                    ld1b(it + 1)
                it += 1
                base = gseg[0]
                gseg[0] += len(segs)

                def ph1(si):
                    nonlocal psi
                    off, n, pidx = segs[si]
                    q = (base + si) % 3
                    pb = psi % 8
                    psi += 1
                    ps = G.PS[pb]
                    for tap in range(4):
                        P.mm(ps[:, 0:n], DG[:, fc, tap, :], RP[slot][:, pidx + tap - 1:pidx + tap - 1 + n], start=(tap == 0), stop=(tap == 3),
                             reads=['DG', ('RP', slot)], writes=[('ps', pb)])
                    P.act(XC32[q][:, 0:n], ps[:, 0:n], AF.Identity, bias=vec(G, l, 'conv_b', fc), reads=[('ps', pb)], writes=[('XC32', q)])
                    P.copy('dve', XCB[q][:, 0:n], XC32[q][:, 0:n], reads=[('XC32', q)], writes=[('XCB', q)])

                def ph2(si):
                    nonlocal psi
                    off, n, pidx = segs[si]
                    q = (base + si) % 3
                    r2 = (base + si) % 2
                    for d in range(2):
                        pr = psi % 8
                        pi_ = (psi + 1) % 8
                        psi += 2
                        P.mm(G.PS[pr][:, 0:n], GW[:, 0, d, fc, :], XCB[q][:, 0:n], start=True, stop=True,
                             reads=['GW', ('XCB', q)], writes=[('ps', pr)])
                        P.mm(G.PS[pi_][:, 0:n], GW[:, 1, d, fc, :], XCB[q][:, 0:n], start=True, stop=True,
                             reads=['GW', ('XCB', q)], writes=[('ps', pi_)])
                        P.act(Rt[d][r2][:, 0:n], G.PS[pr][:, 0:n], AF.Sigmoid, bias=vec(G, l, 'ba%d' % d, fc), reads=[('ps', pr)], writes=[('Rt', d, r2)])
                        P.act(It[d][r2][:, 0:n], G.PS[pi_][:, 0:n], AF.Sigmoid, bias=vec(G, l, 'bi%d' % d, fc), reads=[('ps', pi_)], writes=[('It', d, r2)])
                    for d in range(2):
                        P.act(A[d][:, off:off + n], Rt[d][r2][:, 0:n], AF.Exp, scale=cneg[:, d, fc:fc + 1], reads=[('Rt', d, r2), 'cneg'], writes=[('A', d, si)])
                    for d in range(2):
                        P.tt('dve', Tt[d][q][:, 0:n], A[d][:, off:off + n], A[d][:, off:off + n], ALU.mult, reads=[('A', d, si)], writes=[('Tt', d, q)])
                        P.tt('dve', Ut[d][q][:, 0:n], It[d][r2][:, 0:n], XC32[q][:, 0:n], ALU.mult, reads=[('It', d, r2), ('XC32', q)], writes=[('Ut', d, q)])

                def ph3(si):
                    off, n, pidx = segs[si]
                    q = (base + si) % 3
                    r2 = (base + si) % 2
                    for d in range(2):
                        P.act(Mt[d][r2][:, 0:n], Tt[d][q][:, 0:n], AF.Sqrt, bias=G.ONE[:], scale=-1.0, reads=[('Tt', d, q)], writes=[('Mt', d, r2)])
                    for d in range(2):
                        P.tt('dve', Bt[d][:, off:off + n], Ut[d][q][:, 0:n], Mt[d][r2][:, 0:n], ALU.mult, reads=[('Ut', d, q), ('Mt', d, r2)], writes=[('B', d, si)])
                    init = 0.0 if si == 0 else HF[:, off - 1:off]
                    P.scan(HF[:, off:off + n], A[0][:, off:off + n], Bt[0][:, off:off + n], init, reads=[('A', 0, si), ('B', 0, si), 'HF'], writes=['HF'])

                ns_ = len(segs)
                for t in range(ns_ + 2):
                    if t < ns_:
                        ph1(t)
                    if 0 <= t - 1 < ns_:
                        ph2(t - 1)
                    if 0 <= t - 2 < ns_:
                        ph3(t - 2)
                HB = A[0]
                kA1 = [('A', 1, i) for i in range(ns_)] + [('B', 1, i) for i in range(ns_)]
                kA0 = [('A', 0, i) for i in range(ns_)]
                P.scan(rev(HB[:, 0:CTX]), rev(A[1][:, 0:CTX]), rev(Bt[1][:, 0:CTX]), 0.0, reads=kA1, writes=kA0)
                P.scan(rev(HB[:, CTX:NT]), rev(A[1][:, CTX:NT]), rev(Bt[1][:, CTX:NT]), HB[:, 0:1], reads=kA1 + kA0, writes=kA0)
                P.tt('dve', HF[:], HF[:], HB[:], ALU.add, reads=['HF'] + kA0, writes=['HF'])
                P.tt('dve', OUT[0][:], HF[:], GGt[slot][:], ALU.mult, reads=['HF', ('GGt', slot)], writes=['OUT'])
                P.dma('sp', S['rg', b][rows, :], OUT[0][:], 'OUTst', reads=['OUT'])
        P.flush()


def pack_inputs(inp):
    f = lambda a: np.ascontiguousarray(np.asarray(a, dtype=np.float32))
    x = np.asarray(inp['x'], np.float32)
    ctx = np.asarray(inp['ctx'], np.float32)
    c = np.asarray(inp['c'], np.float32)
    c_ctx = np.asarray(inp['c_ctx'], np.float32)

    def pk(v):
        v = np.asarray(v, np.float32)
        return v.reshape(-1, 128).T

    vecs = np.zeros((DEPTH, 128, NV), np.float32)
    for l in range(DEPTH):
        cols = {'ln1_g': inp['ln1_g'][l], 'ln1_b': inp['ln1_b'][l], 'ln2_g': inp['ln2_g'][l], 'ln2_b': inp['ln2_b'][l],
                'conv_w0': inp['conv_w'][l][0], 'conv_w1': inp['conv_w'][l][1], 'conv_w2': inp['conv_w'][l][2],
                'conv_w3': inp['conv_w'][l][3], 'conv_b': inp['conv_b'][l], 'lam0': inp['rg_lambda'][l][0],
                'lam1': inp['rg_lambda'][l][1], 'ba0': inp['rg_ba'][l][0], 'ba1': inp['rg_ba'][l][1],
                'bi0': inp['rg_bi'][l][0], 'bi1': inp['rg_bi'][l][1], 'glu_b': inp['s5_glu_b'][l], 'b_out': inp['b_out'][l],
                'b2': inp['mlp_b2'][l], 'b1': inp['mlp_b1'][l], 's5d': inp['s5_d'][l]}
        for n, ncol in VEC_NAMES:
            vecs[l, :, VOFF[n]:VOFF[n] + ncol] = pk(cols[n])
    ada_bP = np.stack([pk(np.asarray(inp['ada_b'])[l]) for l in range(DEPTH)])
    shared = {
        'ada_w': f(inp['ada_w']), 'ada_bP': f(ada_bP), 'vecs': vecs, 'w_in': f(inp['w_in']),
        'rg_wa': f(inp['rg_wa']), 'rg_wi': f(inp['rg_wi']), 'glu_w': f(inp['s5_glu_w']), 'w_out': f(inp['w_out']),
        'w1': f(inp['mlp_w1']), 'w2': f(inp['mlp_w2']), 'ident': np.eye(128, dtype=np.float32),
    }
    A = lambda n: np.asarray(inp[n], np.float32)
    s5a = np.stack([A('s5_a_re'), A('s5_a_im')], axis=2)
    shared['s5a'] = f(s5a.transpose(0, 1, 4, 2, 3).reshape(DEPTH, 128, 2, 64))
    shared['s5ldt'] = f(np.broadcast_to(A('s5_log_dt')[:, :, None, :], (DEPTH, 2, 64, 64)).reshape(DEPTH, 128, 64))
    s5b = np.stack([A('s5_b_re'), A('s5_b_im')], axis=2)
    shared['s5B'] = f(s5b.transpose(0, 1, 4, 2, 3, 5).reshape(DEPTH, 128, 2, 64, 16))
    s5c = np.stack([A('s5_c_re'), A('s5_c_im')], axis=2)
    shared['s5C'] = f(s5c.transpose(0, 1, 5, 2, 3, 4).reshape(DEPTH, 128, 2, 64, 16))
    dgh = A('s5_d').reshape(DEPTH, 64, 16)
    shared['s5DP'] = f(np.broadcast_to(dgh.transpose(0, 2, 1)[:, None, :, :], (DEPTH, 8, 16, 64)).reshape(DEPTH, 128, 64))
    cm = np.zeros((128, 2, 4, 4, 128), np.float32)
    for i in range(8):
        for j in range(8):
            for d1 in range(4):
                for d2 in range(4):
                    lag = 8 * (j - i) + d1 + 4 * (d2 - 2)
                    if lag >= 0:
                        cm[i * 16:(i + 1) * 16, 0, d1, d2, j * 16:(j + 1) * 16] = 1.0
                    if lag <= 0:
                        cm[i * 16:(i + 1) * 16, 1, d1, d2, j * 16:(j + 1) * 16] = 1.0
    shared['cmask'] = cm
    cv = np.zeros((128, 4), np.float32)
    cv[:64, 0] = 1.0
    cv[64:, 0] = -1.0
    cv[64:, 1] = 63.0
    cv[64:, 2] = 65.0
    shared['cvec'] = cv
    shared['mvals'] = f(np.broadcast_to(np.arange(-63, 65, dtype=np.float32)[None, :], (128, 128)))
    maps = []
    for k in range(NCORES):
        bs = slice(k * NB, (k + 1) * NB)
        cT = np.zeros((128, 8, 4), np.float32)
        for m in range(NB):
            cT[:, :, m] = pk(c[k * NB + m])
        cT[:, :, 2] = pk(c_ctx)
        d = dict(shared)
        d['xT'] = np.ascontiguousarray(x[bs].transpose(0, 2, 1))
        d['cxT'] = np.ascontiguousarray(ctx[bs].transpose(0, 2, 1))
        d['cT'] = cT
        maps.append(d)
    return maps


_CACHE = {}


def kernel(**inputs):
    maps = pack_inputs(inputs)
    if 'nc' not in _CACHE:
        _CACHE['nc'] = build_program()[0]
    nc = _CACHE['nc']
    res = run_bass_kernel_spmd(nc, maps, core_ids=list(range(NCORES)))
    outs = [r['yT'] for r in res.results]
    y = np.concatenate(outs, axis=0)
    return np.ascontiguousarray(y.transpose(0, 2, 1)).astype(np.float32)
```

```python
import math
from contextlib import ExitStack
import numpy as np
import concourse.bass as bass
import concourse.mybir as mybir
from concourse.bass_utils import run_bass_kernel_spmd

F32 = mybir.dt.float32
BF16 = mybir.dt.bfloat16
AF = mybir.ActivationFunctionType
ALU = mybir.AluOpType

D = 1024
SEQ = 4096
CTX = 256
DEPTH = 2
NB = 2
ALPHA = (2.0 * DEPTH) ** 0.25
LN_EPS = 1e-5
RG_C = 8.0
NCORES = 8
OPT_PREP = False
OPT_1B = True
OPT_3 = True

VEC_NAMES = [('ln1_g', 8), ('ln1_b', 8), ('ln2_g', 8), ('ln2_b', 8), ('conv_w0', 8), ('conv_w1', 8),
             ('conv_w2', 8), ('conv_w3', 8), ('conv_b', 8), ('lam0', 8), ('lam1', 8), ('ba0', 8),
             ('ba1', 8), ('bi0', 8), ('bi1', 8), ('glu_b', 8), ('b_out', 8), ('b2', 8), ('b1', 32),
             ('s5d', 8)]
VOFF = {}
_o = 0
for _n, _c in VEC_NAMES:
    VOFF[_n] = _o
    _o += _c
NV = _o


class Op:
    __slots__ = ('eng', 'fn', 'deps', 'ddeps', 'signal', 'sem', 'val', 'is_dma')


class Prog:
    def __init__(self, nc, es):
        self.nc = nc
        self.engs = {'pe': nc.tensor, 'act': nc.scalar, 'dve': nc.vector, 'pool': nc.gpsimd, 'sp': nc.sync}
        self.sem = {e: es.enter_context(nc.semaphore('s_' + e)) for e in ('pe', 'act', 'dve', 'pool')}
        self.cnt = {e: 0 for e in self.sem}
        self.es = es
        self.dsem = {}
        self.dcum = {}
        self.nops = 0
        self.free = []
        self.free_sw = []
        self.dkind = {}
        self.nsem = 0
        self.barrier = []
        self._reset()

    def _reset(self):
        self.ops = {e: [] for e in self.engs}
        self.last_w = {}
        self.readers = {}
        self.swq = []

    def add(self, eng, fn, reads=(), writes=(), dma_key=None, signal=False):
        op = Op()
        op.eng = eng
        op.fn = fn
        op.signal = signal
        op.is_dma = dma_key is not None
        op.sem = None
        op.val = None
        deps = []
        for k in reads:
            w = self.last_w.get(k)
            if w is not None:
                deps.append(w)
        for k in writes:
            w = self.last_w.get(k)
            if w is not None:
                deps.append(w)
            deps.extend(self.readers.get(k, {}).values())
        cdeps = []
        ddeps = {}
        seen = set()
        for d in deps:
            if id(d) in seen:
                continue
            seen.add(id(d))
            if d.is_dma:
                ddeps[d.sem] = self.dcum[d.sem]
            else:
                if d.eng == 'pe' and eng == 'pe':
                    continue
                cdeps.append(d)
        op.deps = cdeps
        op.ddeps = ddeps
        if op.is_dma:
            if dma_key not in self.dsem:
                fl = self.free_sw if eng == 'pool' else self.free
                self.dkind[dma_key] = eng == 'pool'
                if fl:
                    h, c0 = fl.pop()
                else:
                    self.nsem += 1
                    h, c0 = self.es.enter_context(self.nc.semaphore('d_' + str(self.nsem))), 0
                self.dsem[dma_key] = h
                self.dcum[dma_key] = c0
            self.dcum[dma_key] += 16
            op.sem = dma_key
            op.val = self.dcum[dma_key]
            assert op.val < 60000, dma_key
        rk = ('d', dma_key) if op.is_dma else eng
        for k in reads:
            self.readers.setdefault(k, {})[rk] = op
        for k in writes:
            self.last_w[k] = op
            self.readers[k] = {}
        self.ops[eng].append(op)
        self.nops += 1
        return op

    def flush(self):
        nc = self.nc
        barrier = self.barrier
        for e, lst in self.ops.items():
            for op in lst:
                for d in op.deps:
                    d.signal = True
            for op in reversed(lst):
                if not op.is_dma and op.fn is not None:
                    op.signal = True
                    break
        for e, lst in self.ops.items():
            if e not in self.sem:
                continue
            for op in lst:
                if op.signal and not op.is_dma and op.fn is not None:
                    self.cnt[e] += 1
                    op.sem = e
                    op.val = self.cnt[e]
                    assert op.val < 60000
        ops = self.ops
        sem = self.sem
        dsem = self.dsem

        def emit(ename, eobj):
            waited = {}
            for s, v in barrier:
                eobj.wait_ge(s, v)
            for op in ops[ename]:
                for d in op.deps:
                    s = sem[d.sem]
                    if waited.get(d.sem, 0) < d.val:
                        eobj.wait_ge(s, d.val)
                        waited[d.sem] = d.val
                for k, v in op.ddeps.items():
                    if waited.get(k, 0) < v:
                        eobj.wait_ge(dsem[k], v)
                        waited[k] = v
                if op.fn is None:
                    continue
                ins = op.fn(eobj)
                if op.is_dma:
                    ins.then_inc(dsem[op.sem], 16)
                elif op.signal:
                    ins.then_inc(sem[op.sem], 1)

        with nc.Block() as block:
            @block.tensor
            def _(t):
                emit('pe', t)

            @block.scalar
            def _(s):
                emit('act', s)

            @block.vector
            def _(v):
                emit('dve', v)

            @block.gpsimd
            def _(g):
                emit('pool', g)

            @block.sync
            def _(sp):
                emit('sp', sp)
        self._reset()
        self.barrier = [(self.sem[e], self.cnt[e]) for e in self.sem if self.cnt[e] > 0]
        self.barrier += [(self.dsem[k], self.dcum[k]) for k in self.dsem if self.dcum[k] > 0]
        for k in self.dsem:
            (self.free_sw if self.dkind[k] else self.free).append((self.dsem[k], self.dcum[k]))
        self.dsem = {}
        self.dcum = {}
        self.dkind = {}

    def final_wait(self):
        nc = self.nc
        items = list(self.barrier)
        with nc.Block() as block:
            @block.sync
            def _(sp):
                for s, v in items:
                    sp.wait_ge(s, v)

    def dma(self, q, out, in_, key, reads=(), writes=(), **kw):
        return self.add(q, lambda e, o=out, i=in_, kw=kw: e.dma_start(out=o, in_=i, **kw), reads, writes, dma_key=key)

    def tr(self, out, in_, ident, reads=(), writes=()):
        return self.add('pe', lambda e, o=out, i=in_, d=ident: e.transpose(o, i, d), reads, writes)

    def mm(self, out, lhsT, rhs, start, stop, reads=(), writes=(), signal=False):
        return self.add('pe', lambda e, o=out, l=lhsT, r=rhs, s=start, t=stop: e.matmul(o, lhsT=l, rhs=r, start=s, stop=t),
                        reads, writes, signal=signal)

    def act(self, out, in_, func, bias=None, scale=1.0, reads=(), writes=()):
        if bias is None:
            f = lambda e, o=out, i=in_, fu=func, sc=scale: e.activation(out=o, in_=i, func=fu, scale=sc)
        else:
            f = lambda e, o=out, i=in_, fu=func, b=bias, sc=scale: e.activation(out=o, in_=i, func=fu, bias=b, scale=sc)
        return self.add('act', f, reads, writes)

    def ts(self, eng, out, in0, s1, s2, op0, op1=None, reads=(), writes=()):
        if op1 is None:
            f = lambda e, o=out, i=in0, a=s1, p0=op0: e.tensor_scalar(out=o, in0=i, scalar1=a, scalar2=None, op0=p0)
        else:
            f = lambda e, o=out, i=in0, a=s1, b=s2, p0=op0, p1=op1: e.tensor_scalar(out=o, in0=i, scalar1=a, scalar2=b, op0=p0, op1=p1)
        return self.add(eng, f, reads, writes)

    def tt(self, eng, out, in0, in1, op, reads=(), writes=()):
        return self.add(eng, lambda e, o=out, a=in0, b=in1, p=op: e.tensor_tensor(out=o, in0=a, in1=b, op=p), reads, writes)

    def stt(self, out, in0, scalar, in1, op0, op1, reads=(), writes=()):
        return self.add('dve', lambda e, o=out, a=in0, s=scalar, b=in1, p0=op0, p1=op1:
                        e.scalar_tensor_tensor(out=o, in0=a, scalar=s, in1=b, op0=p0, op1=p1), reads, writes)

    def copy(self, eng, out, in_, reads=(), writes=()):
        if eng == 'act':
            return self.add('act', lambda e, o=out, i=in_: e.activation(out=o, in_=i, func=AF.Copy), reads, writes)
        return self.add(eng, lambda e, o=out, i=in_: e.tensor_copy(out=o, in_=i), reads, writes)

    def scan(self, out, d0, d1, init, reads=(), writes=()):
        return self.add('dve', lambda e, o=out, a=d0, b=d1, i=init:
                        e.tensor_tensor_scan(out=o, data0=a, data1=b, initial=i, op0=ALU.mult, op1=ALU.add), reads, writes)

    def memset(self, eng, out, val, reads=(), writes=()):
        return self.add(eng, lambda e, o=out, v=val: e.memset(o, v), reads, writes)


class Ctx:
    pass


def build_program(debug=(), stages=None):
    nc = bass.Bass("TRN2", target_bir_lowering=False)
    G = Ctx()
    G.nc = nc
    G.uid = [0]

    def sbt(name, shape, dt):
        G.uid[0] += 1
        return nc.sbuf_tensor('%s_u%d' % (name, G.uid[0]), shape, dt)
    G.sbt = sbt
    G.debug = set(debug)
    ges = ExitStack()
    G.ges = ges
    P = Prog(nc, ges)
    G.P = P

    def din(name, shape, dt=F32):
        return nc.dram_tensor(name, list(shape), dt, kind="ExternalInput").ap()

    def dscr(name, shape, dt):
        kind = "ExternalOutput" if name in G.debug else "Internal"
        return nc.dram_tensor(name, list(shape), dt, kind=kind).ap()

    G.dscr = dscr
    I = {}
    I['xT'] = din('xT', [NB, D, SEQ])
    I['cxT'] = din('cxT', [NB, D, CTX])
    I['cT'] = din('cT', [128, 8, 4])
    I['ada_w'] = din('ada_w', [DEPTH, D, 6 * D])
    I['ada_bP'] = din('ada_bP', [DEPTH, 128, 48])
    I['vecs'] = din('vecs', [DEPTH, 128, NV])
    I['w_in'] = din('w_in', [DEPTH, D, 3 * D])
    I['rg_wa'] = din('rg_wa', [DEPTH, 2, 8, 128, 128])
    I['rg_wi'] = din('rg_wi', [DEPTH, 2, 8, 128, 128])
    I['glu_w'] = din('glu_w', [DEPTH, D, D])
    I['w_out'] = din('w_out', [DEPTH, 2 * D, D])
    I['w1'] = din('w1', [DEPTH, D, 4 * D])
    I['w2'] = din('w2', [DEPTH, 4 * D, D])
    I['ident'] = din('ident', [128, 128])
    I['s5a'] = din('s5a', [DEPTH, 128, 2, 64])
    I['s5ldt'] = din('s5ldt', [DEPTH, 128, 64])
    I['s5B'] = din('s5B', [DEPTH, 128, 2, 64, 16])
    I['s5C'] = din('s5C', [DEPTH, 128, 2, 64, 16])
    I['s5DP'] = din('s5DP', [DEPTH, 128, 64])
    I['cmask'] = din('cmask', [128, 2, 4, 4, 128])
    I['cvec'] = din('cvec', [128, 4])
    I['mvals'] = din('mvals', [128, 128])
    G.I = I
    G.yT = nc.dram_tensor('yT', [NB, D, SEQ], F32, kind="ExternalOutput").ap()

    G.PS = [ges.enter_context(nc.psum_tensor('ps%d' % i, [128, 512], F32)) for i in range(8)]
    G.MOD = ges.enter_context(G.sbt('MOD', [128, DEPTH, 48, 4], F32))
    G.MP1 = ges.enter_context(G.sbt('MP1', [128, DEPTH, 2, 8, 4], F32))
    G.VEC = ges.enter_context(G.sbt('VEC', [128, DEPTH, NV], F32))
    G.ONE = ges.enter_context(G.sbt('ONE', [128, 1], F32))
    G.EPS = ges.enter_context(G.sbt('EPS', [128, 1], F32))

    S = {}
    for b in range(NB):
        S['rgx', b] = dscr('rgx%d' % b, [D, CTX + SEQ], BF16)
        S['gg', b] = dscr('gg%d' % b, [D, CTX + SEQ], BF16)
        S['s5u', b] = dscr('s5u%d' % b, [D, CTX + SEQ], BF16)
        S['rg', b] = dscr('rg%d' % b, [D, CTX + SEQ], BF16)
        S['s5uc', b] = dscr('s5uc%d' % b, [D, CTX], BF16)
        S['y5', b] = dscr('y5_%d' % b, [D, CTX + SEQ], BF16)
    for kind, n in (('lat', SEQ), ('ctx', CTX)):
        S['x', kind] = dscr('xs_' + kind, [NB, D, n], F32)
        S['x1', kind] = dscr('x1_' + kind, [NB, D, n], F32)
    for l in range(DEPTH):
        S['WT', l] = dscr('WT%d' % l, [64, 128, 16, 128], BF16)
        S['WS', l] = dscr('WS%d' % l, [64, 128, 2, 8, 128], BF16)
        S['WO', l] = dscr('WO%d' % l, [64, 128, 2, 8, 128], BF16)
        S['L64', l] = dscr('L64_%d' % l, [128, 2, 64], F32)
    G.S = S

    G.stages = stages
    run_stages(G)
    P.final_wait()
    return nc, G


def run_stages(G):
    st = G.stages

    def on(name):
        return st is None or name in st
    if on('mod'):
        stage_mod(G)
    if 'moddbg' in G.debug:
        d = G.dscr('moddbg', [128, DEPTH * 48 * 4], F32)
        G.P.dma('sp', d, G.MOD[:].rearrange('p l c m -> p (l c m)'), 'moddbg', reads=['MOD'])
        G.P.flush()
    for l in range(DEPTH):
        if on('prep%d' % l):
            stage_s5prep(G, l)
        if on('1a%d' % l):
            stage_1a(G, l)
        if on('1b%d' % l):
            stage_1b(G, l)
        if on('2_%d' % l):
            stage_2(G, l)
        if on('3a%d' % l):
            stage_3a(G, l)
        if on('3b%d' % l):
            stage_3b(G, l)


def vec(G, l, name, k=0):
    c = VOFF[name] + k
    return G.VEC[:, l, c:c + 1]


def modv(G, l, j, k, m):
    return G.MOD[:, l, j * 8 + k, m:m + 1]


def stage_mod(G):
    nc, P, I = G.nc, G.P, G.I
    with ExitStack() as es:
        cin = es.enter_context(G.sbt('cin', [128, 8, 4], F32))
        sc = es.enter_context(G.sbt('sc', [128, 8, 4], F32))
        abp = es.enter_context(G.sbt('abp', [128, DEPTH, 48], F32))
        wt = [es.enter_context(G.sbt('adaw%d' % i, [128, 8, 512], F32)) for i in range(2)]
        P.dma('sp', cin[:], I['cT'], 'cin', writes=['cin'])
        P.dma('sp', abp[:], I['ada_bP'].rearrange('l p c -> p l c'), 'abp', writes=['abp'])
        P.dma('sp', G.VEC[:], I['vecs'].rearrange('l p c -> p l c'), 'VEC', writes=['VEC'])
        P.memset('dve', G.ONE[:], 1.0, writes=['ONE'])
        P.memset('dve', G.EPS[:], LN_EPS, writes=['EPS'])
        P.act(sc[:], cin[:], AF.Silu, reads=['cin'], writes=['sc'])
        it = 0
        for l in range(DEPTH):
            for c4 in range(12):
                slot = it % 2
                it += 1
                P.dma('sp', wt[slot][:], I['ada_w'][l, :, c4 * 512:(c4 + 1) * 512].rearrange('(k p) c -> p k c', p=128),
                      ('adaw', slot), writes=[('adaw', slot)])
                ps = G.PS[c4 % 2]
                for cj in range(4):
                    cc = c4 * 4 + cj
                    for k in range(8):
                        P.mm(ps[:, cj * 4:cj * 4 + 4], wt[slot][:, k, cj * 128:(cj + 1) * 128], sc[:, k, :],
                             start=(k == 0), stop=(k == 7), reads=[('adaw', slot), 'sc'], writes=[('ps', c4 % 2)])
                for cj in range(4):
                    cc = c4 * 4 + cj
                    P.ts('dve', G.MOD[:, l, cc, :], ps[:, cj * 4:cj * 4 + 4], abp[:, l, cc:cc + 1], None, ALU.add,
                         reads=[('ps', c4 % 2), 'abp'], writes=['MOD'])
            for s, j in ((0, 1), (1, 4)):
                P.ts('dve', G.MP1[:, l, s, :, :], G.MOD[:, l, j * 8:(j + 1) * 8, :], 1.0, None, ALU.add,
                     reads=['MOD'], writes=['MP1'])
        P.flush()


def seg_list():
    segs = [('ctx', 0, CTX)]
    for t in range(SEQ // 512):
        segs.append(('lat', t * 512, 512))
    return segs


def stage_1a(G, l):
    nc, P, I, S = G.nc, G.P, G.I, G.S
    with ExitStack() as es:
        W = es.enter_context(G.sbt('w_in_b', [128, 8, 3 * D], BF16))
        xin = [es.enter_context(G.sbt('xin%d' % i, [128, 8, 512], F32)) for i in range(2)]
        xm = [es.enter_context(G.sbt('xm%d' % i, [128, 8, 512], BF16)) for i in range(2)]
        st = [[es.enter_context(G.sbt('st%d_%d' % (j, i), [128, 8, 512], BF16)) for i in range(2)] for j in range(3)]
        stc = es.enter_context(G.sbt('stc', [128, 8, CTX], BF16))
        for k in range(8):
            for h in range(2):
                P.dma('pool', W[:, k, h * 1536:(h + 1) * 1536], I['w_in'][l, k * 128:(k + 1) * 128, h * 1536:(h + 1) * 1536],
                      'W1a', writes=[('W', k)])
        psi = 0
        blocks = []
        for b in range(NB):
            for (kind, t0, n) in seg_list():
                blocks.append((b, kind, t0, n))

        def ld(i):
            b, kind, t0, n = blocks[i]
            slot = i % 2
            src = I['cxT'] if kind == 'ctx' else I['xT']
            if l > 0:
                src = G.S['x', kind]
            P.dma('sp', xin[slot][:, :, 0:n], src[b, :, t0:t0 + n].rearrange('(k p) t -> p k t', p=128),
                  ('xin', slot), writes=[('xin', slot)])

        def modu(i):
            b, kind, t0, n = blocks[i]
            slot = i % 2
            m = 2 if kind == 'ctx' else b
            for k in range(8):
                P.ts('dve', xm[slot][:, k, 0:n], xin[slot][:, k, 0:n], G.MP1[:, l, 0, k, m:m + 1], modv(G, l, 0, k, m),
                     ALU.mult, ALU.add, reads=[('xin', slot), 'MP1', 'MOD'], writes=[('xm', slot)])

        ld(0)
        modu(0)
        if len(blocks) > 1:
            ld(1)
        for i, (b, kind, t0, n) in enumerate(blocks):
            slot = i % 2
            off = t0 if kind == 'ctx' else CTX + t0
            for oc in range(24):
                if oc == 12 and i + 1 < len(blocks):
                    modu(i + 1)
                    if i + 2 < len(blocks):
                        ld(i + 2)
                pb = psi % 4
                psi += 1
                ps = G.PS[pb]
                for k in range(8):
                    P.mm(ps[:, 0:n], W[:, k, oc * 128:(oc + 1) * 128], xm[slot][:, k, 0:n], start=(k == 0), stop=(k == 7),
                         reads=[('W', k), ('xm', slot)], writes=[('ps', pb)])
                j, o8 = oc // 8, oc % 8
                dst = st[j][slot][:, o8, 0:n]
                if j == 1:
                    P.act(dst, ps[:, 0:n], AF.Gelu_apprx_tanh, reads=[('ps', pb)], writes=[('st', j, slot)])
                elif oc % 3 == 0:
                    P.copy('act', dst, ps[:, 0:n], reads=[('ps', pb)], writes=[('st', j, slot)])
                else:
                    P.copy('dve', dst, ps[:, 0:n], reads=[('ps', pb)], writes=[('st', j, slot)])
                if j == 2 and kind == 'ctx':
                    for cc in range(4):
                        P.copy('dve', stc[:, o8, :].rearrange('p (i c ip) -> p i c ip', i=8, c=4)[:, :, cc, :],
                               ps[:, cc * 64:(cc + 1) * 64].rearrange('p (i ip) -> p i ip', i=8), reads=[('ps', pb)], writes=['stc'])
            if kind == 'ctx':
                P.dma('sp', S['s5uc', b].rearrange('(k p) t -> p k t', p=128), stc[:], 'stc', reads=['stc'])
            for j, nm in enumerate(('rgx', 'gg', 's5u')):
                P.dma('sp', S[nm, b][:, off:off + n].rearrange('(k p) t -> p k t', p=128), st[j][slot][:, :, 0:n],
                      ('st', j, slot), reads=[('st', j, slot)], writes=[('dram', nm, b)])
        P.flush()


MAGIC = 12582912.0
TWO_PI = 2.0 * math.pi


def stage_s5prep(G, l):
    nc, P, I, S = G.nc, G.P, G.I, G.S
    GB = 4
    with ExitStack() as es:
        def sb(name, shape, dt=F32):
            return es.enter_context(G.sbt(name, shape, dt))
        A_ = sb('s5a_t', [128, 2, 64])
        ldt = sb('ldt', [128, 64])
        Bin = sb('Bin', [128, 2, 64, 16])
        Cin = sb('Cin', [128, 2, 64, 16])
        DP = sb('DP', [128, 64])
        cmask = sb('cmask_t', [128, 2, 4, 4, 128])
        cvec = sb('cvec_t', [128, 4])
        mvals = sb('mvals_t', [128, 128])
        identf = sb('identf', [128, 128])
        Z = sb('Z', [128, 2, 64])
        ZP = sb('ZP', [128, 2, 64])
        L1 = sb('L1', [128, 2, 64])
        L64 = sb('L64t', [128, 2, 64])
        KS = sb('KS', [128, 2, 64])
        KO = sb('KO', [128, 2, 64])
        Q = sb('Q', [128, 2, 64])
        sm = [sb('sm%d' % i, [128, 64]) for i in range(6)]
        Bbar = sb('Bbar', [128, 2, 64, 16])
        BS = sb('BS', [128, 2, 64, 16])
        CO = sb('CO', [128, 2, 64, 16])
        CinN = sb('CinN', [128, 2, 64, 16])
        CON = sb('CON', [128, 2, 64, 16])
        tmpb = [sb("tmpb%d" % i, [128, 1024]) for i in range(8)]
        tab = sb('tab', [128, 2, GB, 128])
        BmR = sb('BmR', [128, 4, GB, 128], BF16)
        BmI = sb('BmI', [128, 4, GB, 128], BF16)
        AR = sb('AR', [128, 4, GB, 128], BF16)
        AIn = sb('AIn', [128, 4, GB, 128], BF16)
        WSP = sb('WSP', [128, 2, GB, 128])
        MZ1 = sb('MZ1', [128, GB, 128])
        Tout = sb('Tout', [128, GB, 16, 128], BF16)
        WSst = sb('WSst', [128, GB, 2, 8, 128], BF16)
        WOst = sb('WOst', [128, GB, 2, 8, 128], BF16)
        t1 = [sb('t1_%d' % i, [128, 512]) for i in range(2)]
        t2 = [sb('t2_%d' % i, [128, 512]) for i in range(2)]

        for t, nm in ((A_, 's5a'), (ldt, 's5ldt'), (Bin, 's5B'), (Cin, 's5C'), (DP, 's5DP')):
            P.dma('sp', t[:], I[nm][l], 'pl_' + nm, writes=[nm])
        for t, nm in ((cmask, 'cmask'), (cvec, 'cvec'), (mvals, 'mvals'), (identf, 'ident')):
            P.dma('sp', t[:], I[nm], 'pl_' + nm, writes=[nm])

        uid = [0]

        def key():
            uid[0] += 1
            return ('k', uid[0])

        def cexp(o_re, o_im, zr, zi, n, rk, wk, neg_im=False):
            tm = [t[:, 0:n] for t in tmpb]
            k0, k1, k2, k3, k4, k5 = [key() for _ in range(6)]
            P.act(tm[0], zr, AF.Exp, reads=rk, writes=[('tm', 0)])
            P.ts('dve', tm[1], zi, 1.0 / TWO_PI, MAGIC, ALU.mult, ALU.add, reads=rk, writes=[('tm', 1)])
            P.ts('dve', tm[2], tm[1], MAGIC, None, ALU.subtract, reads=[('tm', 1)], writes=[('tm', 2)])
            P.stt(tm[3], zi, 1.0 / TWO_PI, tm[2], ALU.mult, ALU.subtract, reads=rk + [('tm', 2)], writes=[('tm', 3)])
            P.act(tm[4], tm[3], AF.Sin, scale=TWO_PI, reads=[('tm', 3)], writes=[('tm', 4)])
            P.ts('pool', tm[5], zi, 1.0 / TWO_PI, 0.25, ALU.mult, ALU.add, reads=rk, writes=[('tm', 5)])
            P.ts('dve', tm[1], tm[5], MAGIC, None, ALU.add, reads=[('tm', 5)], writes=[('tm', 1)])
            P.ts('dve', tm[2], tm[1], MAGIC, None, ALU.subtract, reads=[('tm', 1)], writes=[('tm', 2)])
            P.tt('dve', tm[3], tm[5], tm[2], ALU.subtract, reads=[('tm', 5), ('tm', 2)], writes=[('tm', 3)])
            P.act(tm[6], tm[3], AF.Sin, scale=TWO_PI, reads=[('tm', 3)], writes=[('tm', 6)])
            P.tt('dve', o_re, tm[0], tm[6], ALU.mult, reads=[('tm', 0), ('tm', 6)], writes=wk)
            if neg_im:
                P.stt(o_im, tm[0], -1.0, tm[4], ALU.mult, ALU.mult, reads=[('tm', 0), ('tm', 4)], writes=wk)
            else:
                P.tt('pool', o_im, tm[0], tm[4], ALU.mult, reads=[('tm', 0), ('tm', 4)], writes=wk)

        def cmul(o_re, o_im, ar, ai, br, bi, n_shape, rk, wk, nb=None):
            sh = n_shape
            n = 1
            for v in sh[1:]:
                n *= v

            def tv(i):
                a = tmpb[i][:, 0:n]
                if len(sh) == 3:
                    return a.rearrange('p (a b) -> p a b', a=sh[1])
                if len(sh) == 4:
                    return a.rearrange('p (a b c) -> p a b c', a=sh[1], b=sh[2])
                return a
            ibr, ibi = (br, bi) if nb is None else nb
            P.tt('dve', tv(0), ar, br, ALU.mult, reads=rk, writes=[('tm', 0)])
            P.tt('dve', tv(1), ai, bi, ALU.mult, reads=rk, writes=[('tm', 1)])
            P.tt('dve', o_re, tv(0), tv(1), ALU.subtract, reads=[('tm', 0), ('tm', 1)], writes=wk)
            P.tt('dve' if OPT_PREP else 'pool', tv(2), ar, ibi, ALU.mult, reads=rk, writes=[('tm', 2)])
            P.tt('pool', tv(3), ai, ibr, ALU.mult, reads=rk, writes=[('tm', 3)])
            P.tt('pool', o_im, tv(2), tv(3), ALU.add, reads=[('tm', 2), ('tm', 3)], writes=wk)

        P.act(sm[0][:], ldt[:], AF.Exp, reads=['s5ldt'], writes=['dt'])
        for c in range(2):
            P.tt('dve', Z[:, c, :], A_[:, c, :], sm[0][:], ALU.mult, reads=['s5a', 'dt'], writes=['Z'])
            P.ts('dve', ZP[:, c, :], Z[:, c, :], cvec[:, 0:1], None, ALU.mult, reads=['Z', 'cvec'], writes=['ZP'])
        cexp(L1[:, 0, :], L1[:, 1, :], Z[:, 0, :], Z[:, 1, :], 64, ['Z'], ['L1'])
        for (dst, col, nm) in ((L64, None, 'L64'), (KS, 1, 'KS'), (KO, 2, 'KO')):
            for c in range(2):
                if col is None:
                    P.ts('dve', sm[1 + c][:], Z[:, c, :], 64.0, None, ALU.mult, reads=['Z'], writes=[('zs', c)])
                else:
                    P.ts('dve', sm[1 + c][:], Z[:, c, :], cvec[:, col:col + 1], None, ALU.mult, reads=['Z', 'cvec'], writes=[('zs', c)])
            cexp(dst[:, 0, :], dst[:, 1, :], sm[1][:], sm[2][:], 64, [('zs', 0), ('zs', 1)], [nm])
        P.dma('sp', S['L64', l], L64[:], 'L64st', reads=['L64'])
        P.ts('dve', sm[1][:], L1[:, 0, :], -1.0, None, ALU.add, reads=['L1'], writes=['numr'])
        P.tt('dve', sm[2][:], A_[:, 0, :], A_[:, 0, :], ALU.mult, reads=['s5a'], writes=['d1'])
        P.tt('dve', sm[3][:], A_[:, 1, :], A_[:, 1, :], ALU.mult, reads=['s5a'], writes=['d2'])
        P.tt('dve', sm[2][:], sm[2][:], sm[3][:], ALU.add, reads=['d1', 'd2'], writes=['den'])
        P.add('dve', lambda e: e.reciprocal(out=sm[3][:], in_=sm[2][:]), reads=['den'], writes=['rden'])
        P.tt('dve', sm[4][:], sm[1][:], A_[:, 0, :], ALU.mult, reads=['numr', 's5a'], writes=['q1'])
        P.tt('dve', sm[5][:], L1[:, 1, :], A_[:, 1, :], ALU.mult, reads=['L1', 's5a'], writes=['q2'])
        P.tt('dve', sm[4][:], sm[4][:], sm[5][:], ALU.add, reads=['q1', 'q2'], writes=['q3'])
        P.tt('dve', Q[:, 0, :], sm[4][:], sm[3][:], ALU.mult, reads=['q3', 'rden'], writes=['Qr'])
        P.tt('dve', sm[4][:], L1[:, 1, :], A_[:, 0, :], ALU.mult, reads=['L1', 's5a', 'Qr'], writes=['q4'])
        P.tt('dve', sm[5][:], sm[1][:], A_[:, 1, :], ALU.mult, reads=['numr', 's5a', 'Qr'], writes=['q5'])
        P.tt('dve', sm[4][:], sm[4][:], sm[5][:], ALU.subtract, reads=['q4', 'q5'], writes=['q6'])
        P.tt('dve', Q[:, 1, :], sm[4][:], sm[3][:], ALU.mult, reads=['q6', 'rden'], writes=['Qi'])

        def bc(ap2, n):
            return ap2.unsqueeze(2).to_broadcast([128, 64, n])
        cmul(Bbar[:, 0], Bbar[:, 1], bc(Q[:, 0, :], 16), bc(Q[:, 1, :], 16), Bin[:, 0], Bin[:, 1], [128, 64, 16],
             ['Qr', 'Qi', 's5B'], ['Bbar'])
        cmul(BS[:, 0], BS[:, 1], bc(KS[:, 0, :], 16), bc(KS[:, 1, :], 16), Bbar[:, 0], Bbar[:, 1], [128, 64, 16],
             ['KS', 'Bbar'], ['BS'])
        cmul(CO[:, 0], CO[:, 1], bc(KO[:, 0, :], 16), bc(KO[:, 1, :], 16), Cin[:, 0], Cin[:, 1], [128, 64, 16],
             ['KO', 's5C'], ['CO'])
        P.ts('dve', CinN[:].rearrange('p c g h -> p (c g h)'), Cin[:].rearrange('p c g h -> p (c g h)'), -1.0, None, ALU.mult,
             reads=['s5C'], writes=['CinN'])
        P.ts('dve', CON[:].rearrange('p c g h -> p (c g h)'), CO[:].rearrange('p c g h -> p (c g h)'), -1.0, None, ALU.mult,
             reads=['CO'], writes=['CON'])

        psi = 0
        for gb in range(64 // GB):
            g0 = gb * GB
            gs = slice(g0, g0 + GB)
            for c in range(2):
                P.tt('dve', tmpb[7][:, 0:GB * 128].rearrange('p (a b) -> p a b', a=GB) if c == 0 else
                     MZ1[:],
                     ZP[:, c, gs].unsqueeze(2).to_broadcast([128, GB, 128]),
                     mvals[:].unsqueeze(1).to_broadcast([128, GB, 128]), ALU.mult,
                     reads=['ZP', 'mvals'], writes=[('mz', c)])
            cexp(tab[:, 0].rearrange('p a b -> p (a b)'), tab[:, 1].rearrange('p a b -> p (a b)'),
                 tmpb[7][:, 0:GB * 128], MZ1[:].rearrange('p a b -> p (a b)'), GB * 128,
                 [('mz', 0), ('mz', 1)], ['tab'])

            def tabv(c, start, step, n_inner):
                if step > 0:
                    v = tab[:, c, :, start:start + 7 * step + 1:step]
                else:
                    stop = start + 7 * step - 1
                    v = tab[:, c, :, start:(stop if stop >= 0 else None):step]
                return v.unsqueeze(3).to_broadcast([128, GB, 8, n_inner])

            def vecv(t, c):
                return t[:, c, gs, :].unsqueeze(2).to_broadcast([128, GB, 8, 16])

            def o4(t):
                return t.rearrange('p g (a b) -> p g a b', a=8)
            for d1 in range(4):
                cmul(o4(BmR[:, d1]), o4(BmI[:, d1]), tabv(0, 63 + d1, -8, 16), tabv(1, 63 + d1, -8, 16), vecv(Bbar, 0), vecv(Bbar, 1),
                     [128, GB, 8, 16], ['tab', 'Bbar'], ['Bm'])
            for d2 in range(4):
                cmul(o4(AR[:, d2]), o4(AIn[:, d2]), tabv(0, 63 + 4 * (d2 - 2), 8, 16), tabv(1, 63 + 4 * (d2 - 2), 8, 16),
                     vecv(Cin, 0), vecv(Cin, 1), [128, GB, 8, 16], ['tab', 's5C', 'CinN'], ['Am'], nb=(vecv(CinN, 0), vecv(CinN, 1)))
            for gl in range(GB):
                for d1 in range(4):
                    pa, pb = psi % 8, (psi + 1) % 8
                    psi += 2
                    for (pp, h0) in ((pa, 0), (pb, 64)):
                        hs = slice(h0, h0 + 64)
                        out = G.PS[pp][:].rearrange('p (a b) -> p a b', a=4)
                        P.mm(out, BmR[hs, d1, gl, :], AR[hs, :, gl, :], start=True, stop=False, reads=['Bm', 'Am'], writes=[('ps', pp)])
                        P.mm(out, BmI[hs, d1, gl, :], AIn[hs, :, gl, :], start=False, stop=True, reads=['Bm', 'Am'], writes=[('ps', pp)])
                    ts_ = (gl * 4 + d1) % 2
                    P.tt('dve', t1[ts_][:], G.PS[pa][:], cmask[:, 0, d1].rearrange('p a b -> p (a b)'), ALU.mult,
                         reads=[('ps', pa), 'cmask'], writes=[('t1', ts_)])
                    if d1 == 0:
                        P.stt(t1[ts_][:, 256:384], identf[:], DP[:, g0 + gl:g0 + gl + 1], t1[ts_][:, 256:384], ALU.mult, ALU.add,
                              reads=[('t1', ts_), 'ident', 's5DP'], writes=[('t1', ts_)])
                    P.tt('dve', t2[ts_][:], G.PS[pb][:], cmask[:, 1, d1].rearrange('p a b -> p (a b)'), ALU.mult,
                         reads=[('ps', pb), 'cmask'], writes=[('t2', ts_)])
                    P.tt('pool', Tout[:, gl, d1 * 4:(d1 + 1) * 4, :].rearrange('p a b -> p (a b)'), t1[ts_][:], t2[ts_][:], ALU.add,
                         reads=[('t1', ts_), ('t2', ts_)], writes=['Tout'])
            P.dma('sp', S['WT', l][gs].rearrange('g p a b -> p g a b'), Tout[:], 'Toutst', reads=['Tout'])
            for Ip in range(8):
                cmul(o4(WSP[:, 0]), o4(WSP[:, 1]), tabv(0, 126 - Ip, -8, 16), tabv(1, 126 - Ip, -8, 16), vecv(BS, 0), vecv(BS, 1),
                     [128, GB, 8, 16], ['tab', 'BS'], ['WSP'])
                for c in range(2):
                    for g4 in range(GB // 4):
                        pp = psi % 8
                        psi += 1
                        for gq in range(4):
                            gl = g4 * 4 + gq
                            P.tr(G.PS[pp][:, gq * 128:(gq + 1) * 128], WSP[:, c, gl, :], identf[:], reads=['WSP', 'ident'], writes=[('ps', pp)])
                        P.copy('act', WSst[:, g4 * 4:(g4 + 1) * 4, c, Ip, :], G.PS[pp][:].rearrange('p (a b) -> p a b', a=4),
                               reads=[('ps', pp)], writes=['WSst'])
            P.dma('sp', S['WS', l][gs].rearrange('g p c i q -> p g c i q'), WSst[:], 'WSstst', reads=['WSst'])
            for Jp in range(8):
                cmul(o4(WOst[:, :, 0, Jp, :]), o4(WOst[:, :, 1, Jp, :]), tabv(0, 64 + Jp, 8, 16), tabv(1, 64 + Jp, 8, 16),
                     vecv(CO, 0), vecv(CO, 1), [128, GB, 8, 16], ['tab', 'CO', 'CON'], ['WOst'], nb=(vecv(CON, 0), vecv(CON, 1)))
            P.dma('sp', S['WO', l][gs].rearrange('g p c j q -> p g c j q'), WOst[:], 'WOstst', reads=['WOst'])
        P.flush()


def tslot(delta):
    d1 = delta % 4
    d2 = (delta - d1) // 4
    return d1 * 4 + d2 + 2


def stage_2(G, l):
    nc, P, I, S = G.nc, G.P, G.I, G.S
    last = (l == DEPTH - 1)
    NS = 68
    import os
    CUT = int(os.environ.get('S2CUT', '9'))
    with ExitStack() as es:
        def sb(name, shape, dt=F32):
            return es.enter_context(G.sbt(name, shape, dt))
        UTl = sb('UTl', [128, 64, 512], BF16)
        UTc = sb('UTc', [128, 64, 4, 8], BF16)
        X2 = sb('X2', [128, 2, 64, NS])
        Hb = sb('Hb', [128, 2, 64, NS], BF16)
        Lt = sb('Lt', [128, 2, 64])
        L1s = sb('L1s', [128, 2, 64])
        L2s = sb('L2s', [128, 2, 64])
        tA = sb('tA', [128, 2, 64])
        tB = sb('tB', [128, 2, 64])
        NWS = 5
        WSg = [sb('WSg%d' % i, [128, 2, 8, 128], BF16) for i in range(NWS)]
        WTg = [sb('WTg%d' % i, [128, 16, 128], BF16) for i in range(NWS)]
        WOg = [sb('WOg%d' % i, [128, 2, 8, 128], BF16) for i in range(NWS)]
        Yst = [sb('Yst%d' % i, [128, 8, 512], BF16) for i in range(2)]
        Yc = sb('Yc', [128, 64, 4, 8], BF16)

        P.dma('sp', Lt[:], S['L64', l], 'Lt', writes=['Lt'])
        P.copy('dve', L1s[:, 0, :], Lt[:, 0, :], reads=['Lt'], writes=['L1s'])
        P.copy('dve', L1s[:, 1, :], Lt[:, 0, :], reads=['Lt'], writes=['L1s'])
        P.ts('dve', L2s[:, 0, :], Lt[:, 1, :], -1.0, None, ALU.mult, reads=['Lt'], writes=['L2s'])
        P.copy('dve', L2s[:, 1, :], Lt[:, 1, :], reads=['Lt'], writes=['L2s'])
        P.memset('pool', Hb[0:64, :, :, 64:65], 0.0, writes=['Hb0'])
        P.memset('pool', Hb[64:128, :, :, 67:68], 0.0, writes=['Hb0'])
        psi = 0
        wi = 0
        for b in range(NB if CUT >= 9 else 1):
            for i in range(8):
                P.dma('sp', UTl[i * 16:(i + 1) * 16, :, :],
                      S['s5u', b][:, CTX + 512 * i:CTX + 512 * (i + 1)].rearrange('(g h) t -> h g t', h=16),
                      ('UTl', i), writes=['UTl'])
                P.dma('sp', UTc[i * 16:(i + 1) * 16, :, :, :].rearrange('p g c j -> p g (c j)'),
                      S['s5uc', b][:, 32 * i:32 * (i + 1)].rearrange('(g h) t -> h g t', h=16),
                      ('UTc', i), writes=['UTc'])
            if CUT < 2:
                continue
            G7 = 7
            for g0 in range(0, 64, G7):
                gn = min(G7, 64 - g0)
                pbank = []
                for c in range(2):
                    pbank.append(psi % 8)
                    psi += 1
                for gl in range(gn):
                    g = g0 + gl
                    ws = wi % NWS
                    wi += 1
                    P.dma('sp', WSg[ws][:], S['WS', l][g], ('WSg', ws), writes=[('WSg', ws)])
                    for c in range(2):
                        ps = G.PS[pbank[c]]
                        for Ip in range(8):
                            P.mm(ps[:, gl * NS:gl * NS + 64], WSg[ws][:, c, Ip, :], UTl[:, g, Ip * 64:(Ip + 1) * 64], start=(Ip == 0), stop=(Ip == 7),
                                 reads=[('WSg', ws), 'UTl'], writes=[('ps', pbank[c])])
                        for Ip in range(8):
                            P.mm(ps[:, gl * NS + 64:gl * NS + 68], WSg[ws][:, c, Ip, :], UTc[:, g, :, Ip], start=(Ip == 0), stop=(Ip == 7),
                                 reads=[('WSg', ws), 'UTc'], writes=[('ps', pbank[c])])
                for c in range(2):
                    ps = G.PS[pbank[c]]
                    pv = ps[:, 0:gn * NS].rearrange('p (g s) -> p g s', s=NS)
                    eng = 'dve'
                    P.copy(eng, X2[0:64, c, g0:g0 + gn, 0:4], pv[0:64, :, 64:68], reads=[('ps', pbank[c])], writes=['X2'])
                    P.copy(eng, X2[0:64, c, g0:g0 + gn, 4:68], pv[0:64, :, 0:64], reads=[('ps', pbank[c])], writes=['X2'])
                    P.copy(eng, X2[64:128, c, g0:g0 + gn, 0:4], pv[64:128, :, 67:63:-1], reads=[('ps', pbank[c])], writes=['X2'])
                    P.copy(eng, X2[64:128, c, g0:g0 + gn, 4:68], pv[64:128, :, 63::-1], reads=[('ps', pbank[c])], writes=['X2'])
            if CUT < 3:
                continue
            for s_ in range(1, NS):
                prev = X2[:, :, :, s_ - 1]
                prev_sw = X2[:, ::-1, :, s_ - 1]
                cur = X2[:, :, :, s_]
                P.tt('dve', tA[:], L1s[:], prev, ALU.mult, reads=['L1s', 'X2'], writes=['tA'])
                P.tt('dve', tB[:], L2s[:], prev_sw, ALU.mult, reads=['L2s', 'X2'], writes=['tB'])
                P.tt('dve', cur, cur, tA[:], ALU.add, reads=['X2', 'tA'], writes=['X2'])
                P.tt('dve', cur, cur, tB[:], ALU.add, reads=['X2', 'tB'], writes=['X2'])
            for c in range(2):
                P.copy('dve', Hb[0:64, c, :, 0:64], X2[0:64, c, :, 3:67], reads=['X2'], writes=['Hb'])
                P.copy('dve', Hb[0:64, c, :, 65:68], X2[0:64, c, :, 0:3], reads=['X2'], writes=['Hb'])
                P.copy('dve', Hb[64:128, c, :, 0:64], X2[64:128, c, :, 66:2:-1], reads=['X2'], writes=['Hb'])
                P.copy('dve', Hb[64:128, c, :, 64:67], X2[64:128, c, :, 2::-1], reads=['X2'], writes=['Hb'])
            if CUT < 4:
                continue
            for g in range(64 if CUT >= 5 else 8):
                ws = wi % NWS
                wi += 1
                P.dma('sp', WTg[ws][:], S['WT', l][g], ('WTg', ws), writes=[('WTg', ws)])
                P.dma('sp', WOg[ws][:], S['WO', l][g], ('WOg', ws), writes=[('WOg', ws)])
                pb = psi % 8
                psi += 1
                ps = G.PS[pb]
                rk = [('WTg', ws), ('WOg', ws), 'UTl', 'UTc', 'Hb', 'Hb0']
                for Jp in range(8):
                    out = ps[:, Jp * 64:(Jp + 1) * 64]
                    for Ip in range(8):
                        P.mm(out, WTg[ws][:, tslot(Jp - Ip), :], UTl[:, g, Ip * 64:(Ip + 1) * 64], start=(Ip == 0), stop=False,
                             reads=rk, writes=[('ps', pb)])
                    P.mm(out, WOg[ws][:, 0, Jp, :], Hb[:, 0, g, 0:64], start=False, stop=False, reads=rk, writes=[('ps', pb)])
                    P.mm(out, WOg[ws][:, 1, Jp, :], Hb[:, 1, g, 0:64], start=False, stop=True, reads=rk, writes=[('ps', pb)])
                ys = (g // 8) % 2
                P.act(Yst[ys][:, g % 8, :], ps[:], AF.Gelu_apprx_tanh, reads=[('ps', pb)], writes=[('Yst', ys)])
                if not last:
                    pc = psi % 8
                    psi += 1
                    psc = G.PS[pc]
                    for Jp in range(8):
                        out = psc[:, Jp * 4:(Jp + 1) * 4]
                        for Ip in range(8):
                            P.mm(out, WTg[ws][:, tslot(Jp - Ip), :], UTc[:, g, :, Ip], start=(Ip == 0), stop=False, reads=rk, writes=[('ps', pc)])
                        P.mm(out, WOg[ws][:, 0, Jp, :], Hb[:, 0, g, 64:68], start=False, stop=False, reads=rk, writes=[('ps', pc)])
                        P.mm(out, WOg[ws][:, 1, Jp, :], Hb[:, 1, g, 64:68], start=False, stop=True, reads=rk, writes=[('ps', pc)])
                    P.act(Yc[:, g, :, :], psc[:, 0:32].rearrange('p (j c) -> p c j', c=4), AF.Gelu_apprx_tanh, reads=[('ps', pc)], writes=['Yc'])
                if g % 8 == 7:
                    g8 = g - 7
                    for j in range(8):
                        P.dma('sp', S['y5', b][g8 * 16:(g8 + 8) * 16, CTX + 512 * j:CTX + 512 * (j + 1)].rearrange('(g h) t -> h g t', h=16),
                              Yst[ys][j * 16:(j + 1) * 16, :, :], ('Yst', ys), reads=[('Yst', ys)])
            if not last:
                for j in range(8):
                    P.dma('sp', S['y5', b][:, 32 * j:32 * (j + 1)].rearrange('(g h) t -> h g t', h=16),
                          Yc[j * 16:(j + 1) * 16, :, :, :].rearrange('p g c j -> p g (c j)'), 'Ycst', reads=['Yc'])
        P.flush()


def ln_pre(P, T, oc, n, nslot, tiles):
    Rb, SQb = tiles[0], tiles[1]
    tk = ('T32', nslot)
    P.copy('pool', Rb[:, oc, 0:n], T[:, oc, 0:n], reads=[tk], writes=['Rb'])
    P.act(SQb[:, oc, 0:n], T[:, oc, 0:n], AF.Square, reads=[tk], writes=['SQb'])


def ln_block(G, l, P, T, Xres, n, nslot, tiles, gname, bname, psi_ref, rk_extra):
    Rb, SQb, MEAN, M2, VAR, RSTD, ONESB = tiles
    tk = ('T32', nslot)
    pm = psi_ref[0] % 8
    pq = (psi_ref[0] + 1) % 8
    psi_ref[0] += 2
    for oc in range(8):
        P.mm(G.PS[pm][:, 0:n], ONESB[:], Rb[:, oc, 0:n], start=(oc == 0), stop=(oc == 7), reads=['Rb', 'ONESB'], writes=[('ps', pm)])
    for oc in range(8):
        P.mm(G.PS[pq][:, 0:n], ONESB[:], SQb[:, oc, 0:n], start=(oc == 0), stop=(oc == 7), reads=['SQb', 'ONESB'], writes=[('ps', pq)])
    P.copy('act', MEAN[:, 0:n], G.PS[pm][:, 0:n], reads=[('ps', pm)], writes=['MEAN'])
    P.tt('pool', M2[:, 0:n], MEAN[:, 0:n], MEAN[:, 0:n], ALU.mult, reads=['MEAN'], writes=['M2'])
    P.tt('dve', VAR[:, 0:n], G.PS[pq][:, 0:n], M2[:, 0:n], ALU.subtract, reads=[('ps', pq), 'M2'], writes=['VAR'])
    P.act(VAR[:, 0:n], VAR[:, 0:n], AF.Sqrt, bias=G.EPS[:], reads=['VAR'], writes=['VAR'])
    P.add('dve', lambda e, o=RSTD[:, 0:n], i=VAR[:, 0:n]: e.reciprocal(out=o, in_=i), reads=['VAR'], writes=['RSTD'])
    for oc in range(8):
        P.tt('pool', T[:, oc, 0:n], T[:, oc, 0:n], MEAN[:, 0:n], ALU.subtract, reads=[tk, 'MEAN'], writes=[tk])
        P.tt('dve', T[:, oc, 0:n], T[:, oc, 0:n], RSTD[:, 0:n], ALU.mult, reads=[tk, 'RSTD'], writes=[tk])
        P.ts('dve', T[:, oc, 0:n], T[:, oc, 0:n], vec(G, l, gname, oc), vec(G, l, bname, oc), ALU.mult, ALU.add, reads=[tk], writes=[tk])


def stage_3a(G, l):
    nc, P, I, S = G.nc, G.P, G.I, G.S
    last = (l == DEPTH - 1)
    with ExitStack() as es:
        def sb(name, shape, dt=F32):
            return es.enter_context(G.sbt(name, shape, dt))
        GLW = sb('GLW', [128, 8, D], BF16)
        WO_ = sb('WO_', [128, 16, D], BF16)
        ONESB = sb('ONESB', [128, 128], BF16)
        Y5 = [sb('Y5_%d' % i, [128, 8, 512], BF16) for i in range(2)]
        Y5n = sb('Y5n', [128, 8, CTX], BF16)
        RGt = [sb('RGt%d' % i, [128, 8, 512], BF16) for i in range(2)]
        Xin = [sb('Xin%d' % i, [128, 8, 512]) for i in range(2)]
        S5o = sb('S5o', [128, 8, 512], BF16)
        T32 = [sb('T32_%d' % i, [128, 8, 512]) for i in range(2)]
        Rb = sb('Rb', [128, 8, 512], BF16)
        SQb = sb('SQb', [128, 8, 512], BF16)
        sig = [sb('sig%d' % i, [128, 512]) for i in range(2)]
        MEAN = sb('MEAN', [128, 512])
        M2 = sb('M2', [128, 512])
        VAR = sb('VAR', [128, 512])
        RSTD = sb('RSTD', [128, 512])
        tiles = (Rb, SQb, MEAN, M2, VAR, RSTD, ONESB)
        for k in range(8):
            P.dma('pool', GLW[:, k, :], I['glu_w'][l, k * 128:(k + 1) * 128, :], 'GLWd', writes=['GLW'])
        for k in range(16):
            P.dma('pool', WO_[:, k, :], I['w_out'][l, k * 128:(k + 1) * 128, :], 'WO_d', writes=['WO_'])
        P.memset('dve', ONESB[:], 1.0 / D, writes=['ONESB'])
        psi = [0]
        sgc = [0]
        blocks = []
        for b in range(NB):
            for (kind, t0, n) in seg_list():
                if kind == 'ctx' and last:
                    continue
                blocks.append((b, kind, t0, n))

        def phA(i):
            b, kind, t0, n = blocks[i]
            slot = i % 2
            xsrc = (I['cxT'] if kind == 'ctx' else I['xT']) if l == 0 else S['x', kind]
            off = t0 if kind == 'ctx' else CTX + t0
            P.dma('sp', Y5[slot][:, :, 0:n], S['y5', b][:, off:off + n].rearrange('(k p) t -> p k t', p=128), ('Y5', slot), writes=[('Y5', slot)])
            P.dma('sp', RGt[slot][:, :, 0:n], S['rg', b][:, off:off + n].rearrange('(k p) t -> p k t', p=128), ('RGt', slot), writes=[('RGt', slot)])
            P.dma('sp', Xin[slot][:, :, 0:n], xsrc[b, :, t0:t0 + n].rearrange('(k p) t -> p k t', p=128), ('Xin', slot), writes=[('Xin', slot)])
            if kind == 'ctx':
                for k in range(8):
                    for cc in range(4):
                        P.copy('pool', Y5n[:, k, cc * 64:(cc + 1) * 64].rearrange('p (j q) -> p j q', j=8),
                               Y5[slot][:, k, 0:CTX].rearrange('p (j c q) -> p j c q', j=8, c=4)[:, :, cc, :],
                               reads=[('Y5', slot)], writes=['Y5n'])

        def phB(i):
            b, kind, t0, n = blocks[i]
            slot = i % 2
            if kind == 'ctx':
                ysrc, yk = Y5n, 'Y5n'
            else:
                ysrc, yk = Y5[slot], ('Y5', slot)
            for oc in range(8):
                pb = psi[0] % 8
                psi[0] += 1
                for k in range(8):
                    P.mm(G.PS[pb][:, 0:n], GLW[:, k, oc * 128:(oc + 1) * 128], ysrc[:, k, 0:n], start=(k == 0), stop=(k == 7),
                         reads=['GLW', yk], writes=[('ps', pb)])
                ss = sgc[0] % 2
                sgc[0] += 1
                P.act(sig[ss][:, 0:n], G.PS[pb][:, 0:n], AF.Sigmoid, bias=vec(G, l, 'glu_b', oc), reads=[('ps', pb)], writes=[('sig', ss)])
                P.tt('dve', S5o[:, oc, 0:n], ysrc[:, oc, 0:n], sig[ss][:, 0:n], ALU.mult, reads=[yk, ('sig', ss)], writes=['S5o'])

        def phC(i):
            b, kind, t0, n = blocks[i]
            slot = i % 2
            m = 2 if kind == 'ctx' else b
            T = T32[slot]
            for oc in range(8):
                pb = psi[0] % 8
                psi[0] += 1
                for k in range(16):
                    rhs = RGt[slot][:, k, 0:n] if k < 8 else S5o[:, k - 8, 0:n]
                    P.mm(G.PS[pb][:, 0:n], WO_[:, k, oc * 128:(oc + 1) * 128], rhs, start=(k == 0), stop=(k == 15),
                         reads=['WO_', ('RGt', slot), 'S5o'], writes=[('ps', pb)])
                P.ts('dve', T[:, oc, 0:n], G.PS[pb][:, 0:n], vec(G, l, 'b_out', oc), modv(G, l, 2, oc, m), ALU.add, ALU.mult,
                     reads=[('ps', pb), 'MOD'], writes=[('T32', slot)])
                P.stt(T[:, oc, 0:n], Xin[slot][:, oc, 0:n], ALPHA, T[:, oc, 0:n], ALU.mult, ALU.add,
                      reads=[('Xin', slot), ('T32', slot)], writes=[('T32', slot)])
                ln_pre(P, T, oc, n, slot, tiles)

        def phD(i):
            b, kind, t0, n = blocks[i]
            slot = i % 2
            T = T32[slot]
            ln_block(G, l, P, T, None, n, slot, tiles, 'ln1_g', 'ln1_b', psi, None)
            P.dma('sp', S['x1', kind][b, :, t0:t0 + n].rearrange('(k p) t -> p k t', p=128), T[:, :, 0:n], ('T32', slot),
                  reads=[('T32', slot)])

        nb_ = len(blocks)
        phA(0)
        for i in range(nb_):
            if i + 1 < nb_:
                phA(i + 1)
            phB(i)
            if i > 0 and OPT_3:
                phD(i - 1)
            phC(i)
            if not OPT_3:
                phD(i)
        if OPT_3:
            phD(nb_ - 1)
        P.flush()


def stage_3b(G, l):
    nc, P, I, S = G.nc, G.P, G.I, G.S
    last = (l == DEPTH - 1)
    NBLK = 256
    with ExitStack() as es:
        def sb(name, shape, dt=F32):
            return es.enter_context(G.sbt(name, shape, dt))
        W1b = sb('W1b', [128, 8, 4 * D], BF16)
        W2b = sb('W2b', [128, 32, D], BF16)
        ONESB = sb('ONESB', [128, 128], BF16)
        X1 = [sb('X1_%d' % i, [128, 8, NBLK]) for i in range(2)]
        X1m = [sb('X1m%d' % i, [128, 8, NBLK], BF16) for i in range(2)]

        def slot3(i):
            return i % 2
        Hh = sb('Hh', [128, 32, NBLK], BF16)
        hr = [sb('hr%d' % i, [128, NBLK], BF16) for i in range(2)]
        T32 = [sb('T32_%d' % i, [128, 8, NBLK]) for i in range(2)]
        Rb = sb('Rb', [128, 8, NBLK], BF16)
        SQb = sb('SQb', [128, 8, NBLK], BF16)
        MEAN = sb('MEAN', [128, NBLK])
        M2 = sb('M2', [128, NBLK])
        VAR = sb('VAR', [128, NBLK])
        RSTD = sb('RSTD', [128, NBLK])
        tiles = (Rb, SQb, MEAN, M2, VAR, RSTD, ONESB)
        for k in range(8):
            for h in range(2):
                P.dma('pool', W1b[:, k, h * 2048:(h + 1) * 2048], I['w1'][l, k * 128:(k + 1) * 128, h * 2048:(h + 1) * 2048],
                      'W1bd', writes=['W1b'])
        for k in range(32):
            P.dma('pool', W2b[:, k, :], I['w2'][l, k * 128:(k + 1) * 128, :], 'W2bd', writes=['W2b'])
        P.memset('dve', ONESB[:], 1.0 / D, writes=['ONESB'])
        psi = [0]
        hsc = [0]
        blocks = []
        for b in range(NB):
            segs = [] if last else [('ctx', 0, CTX)]
            segs += [('lat', t * NBLK, NBLK) for t in range(SEQ // NBLK)]
            for (kind, t0, n) in segs:
                blocks.append((b, kind, t0, n))

        def phA(i):
            b, kind, t0, n = blocks[i]
            s3 = slot3(i)
            P.dma('sp', X1[s3][:], S['x1', kind][b, :, t0:t0 + n].rearrange('(k p) t -> p k t', p=128), ('X1', s3), writes=[('X1', s3)])

        def phM(i):
            b, kind, t0, n = blocks[i]
            s3 = slot3(i)
            m = 2 if kind == 'ctx' else b
            for k in range(8):
                P.ts('dve', X1m[i % 2][:, k, :], X1[s3][:, k, :], G.MP1[:, l, 1, k, m:m + 1], modv(G, l, 3, k, m), ALU.mult, ALU.add,
                     reads=[('X1', s3), 'MP1', 'MOD'], writes=[('X1m', i % 2)])

        def phB(i):
            b, kind, t0, n = blocks[i]
            slot = i % 2
            m = 2 if kind == 'ctx' else b
            for hc in range(32):
                pb = psi[0] % 8
                psi[0] += 1
                for k in range(8):
                    P.mm(G.PS[pb][:, 0:n], W1b[:, k, hc * 128:(hc + 1) * 128], X1m[i % 2][:, k, :], start=(k == 0), stop=(k == 7),
                         reads=['W1b', ('X1m', i % 2)], writes=[('ps', pb)])
                h2 = hsc[0] % 2
                hsc[0] += 1
                P.act(hr[h2][:], G.PS[pb][:, 0:n], AF.Relu, bias=vec(G, l, 'b1', hc), reads=[('ps', pb)], writes=[('hr', h2)])
                P.tt('pool', Hh[:, hc, :], hr[h2][:], hr[h2][:], ALU.mult, reads=[('hr', h2)], writes=['Hh'])

        def phC(i):
            b, kind, t0, n = blocks[i]
            slot = i % 2
            m = 2 if kind == 'ctx' else b
            T = T32[slot]
            for oc in range(8):
                pb = psi[0] % 8
                psi[0] += 1
                for k in range(32):
                    P.mm(G.PS[pb][:, 0:n], W2b[:, k, oc * 128:(oc + 1) * 128], Hh[:, k, :], start=(k == 0), stop=(k == 31),
                         reads=['W2b', 'Hh'], writes=[('ps', pb)])
                P.ts('dve', T[:, oc, :], G.PS[pb][:, 0:n], vec(G, l, 'b2', oc), modv(G, l, 5, oc, m), ALU.add, ALU.mult,
                     reads=[('ps', pb), 'MOD'], writes=[('T32', slot)])
                P.stt(T[:, oc, :], X1[slot3(i)][:, oc, :], ALPHA, T[:, oc, :], ALU.mult, ALU.add,
                      reads=[('X1', slot3(i)), ('T32', slot)], writes=[('T32', slot)])
                ln_pre(P, T, oc, n, slot, tiles)

        def phD(i):
            b, kind, t0, n = blocks[i]
            slot = i % 2
            T = T32[slot]
            ln_block(G, l, P, T, None, n, slot, tiles, 'ln2_g', 'ln2_b', psi, None)
            if last:
                dst = G.yT[b, :, t0:t0 + n]
            else:
                dst = S['x', kind][b, :, t0:t0 + n]
            P.dma('sp', dst.rearrange('(k p) t -> p k t', p=128), T[:, :, 0:n], ('T32', slot), reads=[('T32', slot)])

        nb_ = len(blocks)
        phA(0)
        phM(0)
        for i in range(nb_):
            if i + 1 < nb_:
                phA(i + 1)
            phB(i)
            if i + 1 < nb_:
                phM(i + 1)
            if i > 0:
                phD(i - 1)
            phC(i)
        phD(nb_ - 1)
        P.flush()


def rev(ap):
    return ap[:, ::-1]


def stage_1b(G, l):
    nc, P, I, S = G.nc, G.P, G.I, G.S
    NT = CTX + SEQ
    with ExitStack() as es:
        def sb(name, shape, dt):
            return es.enter_context(G.sbt(name, shape, dt))
        identf = sb('identf', [128, 128], F32)
        DG = sb('DG', [128, 8, 4, 128], BF16)
        GW = sb('GW', [128, 2, 2, 8, 128], BF16)
        cneg = sb('cneg', [128, 2, 8], F32)
        ctmp = sb('ctmp', [128, 2, 8], F32)
        RP = [sb('RP%d' % i, [128, NT + 6], BF16) for i in range(2)]
        GGt = [sb('GGt%d' % i, [128, NT], BF16) for i in range(2)]
        OUT = [sb('OUT%d' % i, [128, NT], BF16) for i in range(1)]
        A = [sb('A%d' % i, [128, NT], F32) for i in range(2)]
        Bt = [sb('B%d' % i, [128, NT], F32) for i in range(2)]
        HF = sb('HF', [128, NT], F32)
        XC32 = [sb('XC32_%d' % i, [128, 512], F32) for i in range(3)]
        XCB = [sb('XCB%d' % i, [128, 512], BF16) for i in range(3)]
        Rt = [[sb('Rt%d_%d' % (d, i), [128, 512], F32) for i in range(2)] for d in range(2)]
        It = [[sb('It%d_%d' % (d, i), [128, 512], F32) for i in range(2)] for d in range(2)]
        Tt = [[sb('Tt%d_%d' % (d, i), [128, 512], F32) for i in range(3)] for d in range(2)]
        Mt = [[sb('Mt%d_%d' % (d, i), [128, 512], F32) for i in range(2)] for d in range(2)]
        Ut = [[sb('Ut%d_%d' % (d, i), [128, 512], F32) for i in range(3)] for d in range(2)]

        P.dma('sp', identf[:], I['ident'], 'identf', writes=['identf'])
        for w, nm in enumerate(('rg_wa', 'rg_wi')):
            for d in range(2):
                P.dma('pool', GW[:, w, d, :, :], I[nm][l, d].rearrange('h i j -> i h j'), 'GWd', writes=['GW'])
        for fc in range(8):
            for tap in range(4):
                P.ts('dve', DG[:, fc, tap, :], identf[:], vec(G, l, 'conv_w%d' % tap, fc), None, ALU.mult,
                     reads=['identf'], writes=['DG'])
        for d in range(2):
            o = VOFF['lam%d' % d]
            P.act(ctmp[:, d, :], G.VEC[:, l, o:o + 8], AF.Exp, scale=-1.0, writes=['ctmp'])
        P.act(cneg[:], ctmp[:], AF.Ln, bias=G.ONE[:], reads=['ctmp'], writes=['cneg0'])
        P.ts('dve', cneg[:], cneg[:], -RG_C, None, ALU.mult, reads=['cneg0'], writes=['cneg'])
        for i in range(2):
            P.memset('pool', RP[i][:], 0.0, writes=[('RP', i)])
        segs = [(0, CTX, 1)] + [(CTX + t * 512, 512, 4 + CTX + t * 512) for t in range(SEQ // 512)]
        it = 0
        psi = 0
        gseg = [0]
        for b in range(NB):
            for fc in range(8):
                slot = it % 2
                rows = slice(fc * 128, (fc + 1) * 128)

                def ld1b(j):
                    bb, ff = j // 8, j % 8
                    sl = j % 2
                    rr = slice(ff * 128, (ff + 1) * 128)
                    P.dma('sp', RP[sl][:, 1:1 + CTX], S['rgx', bb][rr, 0:CTX], ('RPc', sl), writes=[('RP', sl)])
                    P.dma('sp', RP[sl][:, 4 + CTX:4 + CTX + SEQ], S['rgx', bb][rr, CTX:NT], ('RPl', sl), writes=[('RP', sl)])
                    P.dma('sp', GGt[sl][:], S['gg', bb][rr, :], ('GGt', sl), writes=[('GGt', sl)])
                if it == 0:
                    ld1b(0)
                if it + 1 < NB * 8:
                    ld1b(it + 1)
                it += 1
                base = gseg[0]
                gseg[0] += len(segs)

                def ph1(si):
                    nonlocal psi
                    off, n, pidx = segs[si]
                    q = (base + si) % 3
                    pb = psi % 8
                    psi += 1
                    ps = G.PS[pb]
                    for tap in range(4):
                        P.mm(ps[:, 0:n], DG[:, fc, tap, :], RP[slot][:, pidx + tap - 1:pidx + tap - 1 + n], start=(tap == 0), stop=(tap == 3),
                             reads=['DG', ('RP', slot)], writes=[('ps', pb)])
                    P.act(XC32[q][:, 0:n], ps[:, 0:n], AF.Identity, bias=vec(G, l, 'conv_b', fc), reads=[('ps', pb)], writes=[('XC32', q)])
                    P.copy('dve', XCB[q][:, 0:n], XC32[q][:, 0:n], reads=[('XC32', q)], writes=[('XCB', q)])

                def ph2(si):
                    nonlocal psi
                    off, n, pidx = segs[si]
                    q = (base + si) % 3
                    r2 = (base + si) % 2
                    for d in range(2):
                        pr = psi % 8
                        pi_ = (psi + 1) % 8
                        psi += 2
                        P.mm(G.PS[pr][:, 0:n], GW[:, 0, d, fc, :], XCB[q][:, 0:n], start=True, stop=True,
                             reads=['GW', ('XCB', q)], writes=[('ps', pr)])
                        P.mm(G.PS[pi_][:, 0:n], GW[:, 1, d, fc, :], XCB[q][:, 0:n], start=True, stop=True,
                             reads=['GW', ('XCB', q)], writes=[('ps', pi_)])
                        P.act(Rt[d][r2][:, 0:n], G.PS[pr][:, 0:n], AF.Sigmoid, bias=vec(G, l, 'ba%d' % d, fc), reads=[('ps', pr)], writes=[('Rt', d, r2)])
                        P.act(It[d][r2][:, 0:n], G.PS[pi_][:, 0:n], AF.Sigmoid, bias=vec(G, l, 'bi%d' % d, fc), reads=[('ps', pi_)], writes=[('It', d, r2)])
                    for d in range(2):
                        P.act(A[d][:, off:off + n], Rt[d][r2][:, 0:n], AF.Exp, scale=cneg[:, d, fc:fc + 1], reads=[('Rt', d, r2), 'cneg'], writes=[('A', d, si)])
                    for d in range(2):
                        P.tt('dve', Tt[d][q][:, 0:n], A[d][:, off:off + n], A[d][:, off:off + n], ALU.mult, reads=[('A', d, si)], writes=[('Tt', d, q)])
                        P.tt('dve', Ut[d][q][:, 0:n], It[d][r2][:, 0:n], XC32[q][:, 0:n], ALU.mult, reads=[('It', d, r2), ('XC32', q)], writes=[('Ut', d, q)])

                def ph3(si):
                    off, n, pidx = segs[si]
                    q = (base + si) % 3
                    r2 = (base + si) % 2
                    for d in range(2):
                        P.act(Mt[d][r2][:, 0:n], Tt[d][q][:, 0:n], AF.Sqrt, bias=G.ONE[:], scale=-1.0, reads=[('Tt', d, q)], writes=[('Mt', d, r2)])
                    for d in range(2):
                        P.tt('dve', Bt[d][:, off:off + n], Ut[d][q][:, 0:n], Mt[d][r2][:, 0:n], ALU.mult, reads=[('Ut', d, q), ('Mt', d, r2)], writes=[('B', d, si)])
                    init = 0.0 if si == 0 else HF[:, off - 1:off]
                    P.scan(HF[:, off:off + n], A[0][:, off:off + n], Bt[0][:, off:off + n], init, reads=[('A', 0, si), ('B', 0, si), 'HF'], writes=['HF'])

                ns_ = len(segs)
                for t in range(ns_ + 2):
                    if t < ns_:
                        ph1(t)
                    if 0 <= t - 1 < ns_:
                        ph2(t - 1)
                    if 0 <= t - 2 < ns_:
                        ph3(t - 2)
                HB = A[0]
                kA1 = [('A', 1, i) for i in range(ns_)] + [('B', 1, i) for i in range(ns_)]
                kA0 = [('A', 0, i) for i in range(ns_)]
                P.scan(rev(HB[:, 0:CTX]), rev(A[1][:, 0:CTX]), rev(Bt[1][:, 0:CTX]), 0.0, reads=kA1, writes=kA0)
                P.scan(rev(HB[:, CTX:NT]), rev(A[1][:, CTX:NT]), rev(Bt[1][:, CTX:NT]), HB[:, 0:1], reads=kA1 + kA0, writes=kA0)
                P.tt('dve', HF[:], HF[:], HB[:], ALU.add, reads=['HF'] + kA0, writes=['HF'])
                P.tt('dve', OUT[0][:], HF[:], GGt[slot][:], ALU.mult, reads=['HF', ('GGt', slot)], writes=['OUT'])
                P.dma('sp', S['rg', b][rows, :], OUT[0][:], 'OUTst', reads=['OUT'])
        P.flush()


def pack_inputs(inp):
    f = lambda a: np.ascontiguousarray(np.asarray(a, dtype=np.float32))
    x = np.asarray(inp['x'], np.float32)
    ctx = np.asarray(inp['ctx'], np.float32)
    c = np.asarray(inp['c'], np.float32)
    c_ctx = np.asarray(inp['c_ctx'], np.float32)

    def pk(v):
        v = np.asarray(v, np.float32)
        return v.reshape(-1, 128).T

    vecs = np.zeros((DEPTH, 128, NV), np.float32)
    for l in range(DEPTH):
        cols = {'ln1_g': inp['ln1_g'][l], 'ln1_b': inp['ln1_b'][l], 'ln2_g': inp['ln2_g'][l], 'ln2_b': inp['ln2_b'][l],
                'conv_w0': inp['conv_w'][l][0], 'conv_w1': inp['conv_w'][l][1], 'conv_w2': inp['conv_w'][l][2],
                'conv_w3': inp['conv_w'][l][3], 'conv_b': inp['conv_b'][l], 'lam0': inp['rg_lambda'][l][0],
                'lam1': inp['rg_lambda'][l][1], 'ba0': inp['rg_ba'][l][0], 'ba1': inp['rg_ba'][l][1],
                'bi0': inp['rg_bi'][l][0], 'bi1': inp['rg_bi'][l][1], 'glu_b': inp['s5_glu_b'][l], 'b_out': inp['b_out'][l],
                'b2': inp['mlp_b2'][l], 'b1': inp['mlp_b1'][l], 's5d': inp['s5_d'][l]}
        for n, ncol in VEC_NAMES:
            vecs[l, :, VOFF[n]:VOFF[n] + ncol] = pk(cols[n])
    ada_bP = np.stack([pk(np.asarray(inp['ada_b'])[l]) for l in range(DEPTH)])
    shared = {
        'ada_w': f(inp['ada_w']), 'ada_bP': f(ada_bP), 'vecs': vecs, 'w_in': f(inp['w_in']),
        'rg_wa': f(inp['rg_wa']), 'rg_wi': f(inp['rg_wi']), 'glu_w': f(inp['s5_glu_w']), 'w_out': f(inp['w_out']),
        'w1': f(inp['mlp_w1']), 'w2': f(inp['mlp_w2']), 'ident': np.eye(128, dtype=np.float32),
    }
    A = lambda n: np.asarray(inp[n], np.float32)
    s5a = np.stack([A('s5_a_re'), A('s5_a_im')], axis=2)
    shared['s5a'] = f(s5a.transpose(0, 1, 4, 2, 3).reshape(DEPTH, 128, 2, 64))
    shared['s5ldt'] = f(np.broadcast_to(A('s5_log_dt')[:, :, None, :], (DEPTH, 2, 64, 64)).reshape(DEPTH, 128, 64))
    s5b = np.stack([A('s5_b_re'), A('s5_b_im')], axis=2)
    shared['s5B'] = f(s5b.transpose(0, 1, 4, 2, 3, 5).reshape(DEPTH, 128, 2, 64, 16))
    s5c = np.stack([A('s5_c_re'), A('s5_c_im')], axis=2)
    shared['s5C'] = f(s5c.transpose(0, 1, 5, 2, 3, 4).reshape(DEPTH, 128, 2, 64, 16))
    dgh = A('s5_d').reshape(DEPTH, 64, 16)
    shared['s5DP'] = f(np.broadcast_to(dgh.transpose(0, 2, 1)[:, None, :, :], (DEPTH, 8, 16, 64)).reshape(DEPTH, 128, 64))
    cm = np.zeros((128, 2, 4, 4, 128), np.float32)
    for i in range(8):
        for j in range(8):
            for d1 in range(4):
                for d2 in range(4):
                    lag = 8 * (j - i) + d1 + 4 * (d2 - 2)
                    if lag >= 0:
                        cm[i * 16:(i + 1) * 16, 0, d1, d2, j * 16:(j + 1) * 16] = 1.0
                    if lag <= 0:
                        cm[i * 16:(i + 1) * 16, 1, d1, d2, j * 16:(j + 1) * 16] = 1.0
    shared['cmask'] = cm
    cv = np.zeros((128, 4), np.float32)
    cv[:64, 0] = 1.0
    cv[64:, 0] = -1.0
    cv[64:, 1] = 63.0
    cv[64:, 2] = 65.0
    shared['cvec'] = cv
    shared['mvals'] = f(np.broadcast_to(np.arange(-63, 65, dtype=np.float32)[None, :], (128, 128)))
    maps = []
    for k in range(NCORES):
        bs = slice(k * NB, (k + 1) * NB)
        cT = np.zeros((128, 8, 4), np.float32)
        for m in range(NB):
            cT[:, :, m] = pk(c[k * NB + m])
        cT[:, :, 2] = pk(c_ctx)
        d = dict(shared)
        d['xT'] = np.ascontiguousarray(x[bs].transpose(0, 2, 1))
        d['cxT'] = np.ascontiguousarray(ctx[bs].transpose(0, 2, 1))
        d['cT'] = cT
        maps.append(d)
    return maps


_CACHE = {}


def kernel(**inputs):
    maps = pack_inputs(inputs)
    if 'nc' not in _CACHE:
        _CACHE['nc'] = build_program()[0]
    nc = _CACHE['nc']
    res = run_bass_kernel_spmd(nc, maps, core_ids=list(range(NCORES)))
    outs = [r['yT'] for r in res.results]
    y = np.concatenate(outs, axis=0)
    return np.ascontiguousarray(y.transpose(0, 2, 1)).astype(np.float32)
```

```python
import math
from contextlib import ExitStack
import numpy as np
import concourse.bass as bass
import concourse.mybir as mybir
from concourse.bass_utils import run_bass_kernel_spmd

F32 = mybir.dt.float32
BF16 = mybir.dt.bfloat16
AF = mybir.ActivationFunctionType
ALU = mybir.AluOpType

D = 1024
SEQ = 4096
CTX = 256
DEPTH = 2
NB = 2
ALPHA = (2.0 * DEPTH) ** 0.25
LN_EPS = 1e-5
RG_C = 8.0
NCORES = 8
OPT_PREP = False
OPT_1B = True
OPT_3 = True

VEC_NAMES = [('ln1_g', 8), ('ln1_b', 8), ('ln2_g', 8), ('ln2_b', 8), ('conv_w0', 8), ('conv_w1', 8),
             ('conv_w2', 8), ('conv_w3', 8), ('conv_b', 8), ('lam0', 8), ('lam1', 8), ('ba0', 8),
             ('ba1', 8), ('bi0', 8), ('bi1', 8), ('glu_b', 8), ('b_out', 8), ('b2', 8), ('b1', 32),
             ('s5d', 8)]
VOFF = {}
_o = 0
for _n, _c in VEC_NAMES:
    VOFF[_n] = _o
    _o += _c
NV = _o


class Op:
    __slots__ = ('eng', 'fn', 'deps', 'ddeps', 'signal', 'sem', 'val', 'is_dma')


class Prog:
    def __init__(self, nc, es):
        self.nc = nc
        self.engs = {'pe': nc.tensor, 'act': nc.scalar, 'dve': nc.vector, 'pool': nc.gpsimd, 'sp': nc.sync}
        self.sem = {e: es.enter_context(nc.semaphore('s_' + e)) for e in ('pe', 'act', 'dve', 'pool')}
        self.cnt = {e: 0 for e in self.sem}
        self.es = es
        self.dsem = {}
        self.dcum = {}
        self.nops = 0
        self.free = []
        self.free_sw = []
        self.dkind = {}
        self.nsem = 0
        self.barrier = []
        self._reset()

    def _reset(self):
        self.ops = {e: [] for e in self.engs}
        self.last_w = {}
        self.readers = {}
        self.swq = []

    def add(self, eng, fn, reads=(), writes=(), dma_key=None, signal=False):
        op = Op()
        op.eng = eng
        op.fn = fn
        op.signal = signal
        op.is_dma = dma_key is not None
        op.sem = None
        op.val = None
        deps = []
        for k in reads:
            w = self.last_w.get(k)
            if w is not None:
                deps.append(w)
        for k in writes:
            w = self.last_w.get(k)
            if w is not None:
                deps.append(w)
            deps.extend(self.readers.get(k, {}).values())
        cdeps = []
        ddeps = {}
        seen = set()
        for d in deps:
            if id(d) in seen:
                continue
            seen.add(id(d))
            if d.is_dma:
                ddeps[d.sem] = self.dcum[d.sem]
            else:
                if d.eng == 'pe' and eng == 'pe':
                    continue
                cdeps.append(d)
        op.deps = cdeps
        op.ddeps = ddeps
        if op.is_dma:
            if dma_key not in self.dsem:
                fl = self.free_sw if eng == 'pool' else self.free
                self.dkind[dma_key] = eng == 'pool'
                if fl:
                    h, c0 = fl.pop()
                else:
                    self.nsem += 1
                    h, c0 = self.es.enter_context(self.nc.semaphore('d_' + str(self.nsem))), 0
                self.dsem[dma_key] = h
                self.dcum[dma_key] = c0
            self.dcum[dma_key] += 16
            op.sem = dma_key
            op.val = self.dcum[dma_key]
            assert op.val < 60000, dma_key
        rk = ('d', dma_key) if op.is_dma else eng
        for k in reads:
            self.readers.setdefault(k, {})[rk] = op
        for k in writes:
            self.last_w[k] = op
            self.readers[k] = {}
        self.ops[eng].append(op)
        self.nops += 1
        return op

    def flush(self):
        nc = self.nc
        barrier = self.barrier
        for e, lst in self.ops.items():
            for op in lst:
                for d in op.deps:
                    d.signal = True
            for op in reversed(lst):
                if not op.is_dma and op.fn is not None:
                    op.signal = True
                    break
        for e, lst in self.ops.items():
            if e not in self.sem:
                continue
            for op in lst:
                if op.signal and not op.is_dma and op.fn is not None:
                    self.cnt[e] += 1
                    op.sem = e
                    op.val = self.cnt[e]
                    assert op.val < 60000
        ops = self.ops
        sem = self.sem
        dsem = self.dsem

        def emit(ename, eobj):
            waited = {}
            for s, v in barrier:
                eobj.wait_ge(s, v)
            for op in ops[ename]:
                for d in op.deps:
                    s = sem[d.sem]
                    if waited.get(d.sem, 0) < d.val:
                        eobj.wait_ge(s, d.val)
                        waited[d.sem] = d.val
                for k, v in op.ddeps.items():
                    if waited.get(k, 0) < v:
                        eobj.wait_ge(dsem[k], v)
                        waited[k] = v
                if op.fn is None:
                    continue
                ins = op.fn(eobj)
                if op.is_dma:
                    ins.then_inc(dsem[op.sem], 16)
                elif op.signal:
                    ins.then_inc(sem[op.sem], 1)

        with nc.Block() as block:
            @block.tensor
            def _(t):
                emit('pe', t)

            @block.scalar
            def _(s):
                emit('act', s)

            @block.vector
            def _(v):
                emit('dve', v)

            @block.gpsimd
            def _(g):
                emit('pool', g)

            @block.sync
            def _(sp):
                emit('sp', sp)
        self._reset()
        self.barrier = [(self.sem[e], self.cnt[e]) for e in self.sem if self.cnt[e] > 0]
        self.barrier += [(self.dsem[k], self.dcum[k]) for k in self.dsem if self.dcum[k] > 0]
        for k in self.dsem:
            (self.free_sw if self.dkind[k] else self.free).append((self.dsem[k], self.dcum[k]))
        self.dsem = {}
        self.dcum = {}
        self.dkind = {}

    def final_wait(self):
        nc = self.nc
        items = list(self.barrier)
        with nc.Block() as block:
            @block.sync
            def _(sp):
                for s, v in items:
                    sp.wait_ge(s, v)

    def dma(self, q, out, in_, key, reads=(), writes=(), **kw):
        return self.add(q, lambda e, o=out, i=in_, kw=kw: e.dma_start(out=o, in_=i, **kw), reads, writes, dma_key=key)

    def tr(self, out, in_, ident, reads=(), writes=()):
        return self.add('pe', lambda e, o=out, i=in_, d=ident: e.transpose(o, i, d), reads, writes)

    def mm(self, out, lhsT, rhs, start, stop, reads=(), writes=(), signal=False):
        return self.add('pe', lambda e, o=out, l=lhsT, r=rhs, s=start, t=stop: e.matmul(o, lhsT=l, rhs=r, start=s, stop=t),
                        reads, writes, signal=signal)

    def act(self, out, in_, func, bias=None, scale=1.0, reads=(), writes=()):
        if bias is None:
            f = lambda e, o=out, i=in_, fu=func, sc=scale: e.activation(out=o, in_=i, func=fu, scale=sc)
        else:
            f = lambda e, o=out, i=in_, fu=func, b=bias, sc=scale: e.activation(out=o, in_=i, func=fu, bias=b, scale=sc)
        return self.add('act', f, reads, writes)

    def ts(self, eng, out, in0, s1, s2, op0, op1=None, reads=(), writes=()):
        if op1 is None:
            f = lambda e, o=out, i=in0, a=s1, p0=op0: e.tensor_scalar(out=o, in0=i, scalar1=a, scalar2=None, op0=p0)
        else:
            f = lambda e, o=out, i=in0, a=s1, b=s2, p0=op0, p1=op1: e.tensor_scalar(out=o, in0=i, scalar1=a, scalar2=b, op0=p0, op1=p1)
        return self.add(eng, f, reads, writes)

    def tt(self, eng, out, in0, in1, op, reads=(), writes=()):
        return self.add(eng, lambda e, o=out, a=in0, b=in1, p=op: e.tensor_tensor(out=o, in0=a, in1=b, op=p), reads, writes)

    def stt(self, out, in0, scalar, in1, op0, op1, reads=(), writes=()):
        return self.add('dve', lambda e, o=out, a=in0, s=scalar, b=in1, p0=op0, p1=op1:
                        e.scalar_tensor_tensor(out=o, in0=a, scalar=s, in1=b, op0=p0, op1=p1), reads, writes)

    def copy(self, eng, out, in_, reads=(), writes=()):
        if eng == 'act':
            return self.add('act', lambda e, o=out, i=in_: e.activation(out=o, in_=i, func=AF.Copy), reads, writes)
        return self.add(eng, lambda e, o=out, i=in_: e.tensor_copy(out=o, in_=i), reads, writes)

    def scan(self, out, d0, d1, init, reads=(), writes=()):
        return self.add('dve', lambda e, o=out, a=d0, b=d1, i=init:
                        e.tensor_tensor_scan(out=o, data0=a, data1=b, initial=i, op0=ALU.mult, op1=ALU.add), reads, writes)

    def memset(self, eng, out, val, reads=(), writes=()):
        return self.add(eng, lambda e, o=out, v=val: e.memset(o, v), reads, writes)


class Ctx:
    pass


def build_program(debug=(), stages=None):
    nc = bass.Bass("TRN2", target_bir_lowering=False)
    G = Ctx()
    G.nc = nc
    G.uid = [0]

    def sbt(name, shape, dt):
        G.uid[0] += 1
        return nc.sbuf_tensor('%s_u%d' % (name, G.uid[0]), shape, dt)
    G.sbt = sbt
    G.debug = set(debug)
    ges = ExitStack()
    G.ges = ges
    P = Prog(nc, ges)
    G.P = P

    def din(name, shape, dt=F32):
        return nc.dram_tensor(name, list(shape), dt, kind="ExternalInput").ap()

    def dscr(name, shape, dt):
        kind = "ExternalOutput" if name in G.debug else "Internal"
        return nc.dram_tensor(name, list(shape), dt, kind=kind).ap()

    G.dscr = dscr
    I = {}
    I['xT'] = din('xT', [NB, D, SEQ])
    I['cxT'] = din('cxT', [NB, D, CTX])
    I['cT'] = din('cT', [128, 8, 4])
    I['ada_w'] = din('ada_w', [DEPTH, D, 6 * D])
    I['ada_bP'] = din('ada_bP', [DEPTH, 128, 48])
    I['vecs'] = din('vecs', [DEPTH, 128, NV])
    I['w_in'] = din('w_in', [DEPTH, D, 3 * D])
    I['rg_wa'] = din('rg_wa', [DEPTH, 2, 8, 128, 128])
    I['rg_wi'] = din('rg_wi', [DEPTH, 2, 8, 128, 128])
    I['glu_w'] = din('glu_w', [DEPTH, D, D])
    I['w_out'] = din('w_out', [DEPTH, 2 * D, D])
    I['w1'] = din('w1', [DEPTH, D, 4 * D])
    I['w2'] = din('w2', [DEPTH, 4 * D, D])
    I['ident'] = din('ident', [128, 128])
    I['s5a'] = din('s5a', [DEPTH, 128, 2, 64])
    I['s5ldt'] = din('s5ldt', [DEPTH, 128, 64])
    I['s5B'] = din('s5B', [DEPTH, 128, 2, 64, 16])
    I['s5C'] = din('s5C', [DEPTH, 128, 2, 64, 16])
    I['s5DP'] = din('s5DP', [DEPTH, 128, 64])
    I['cmask'] = din('cmask', [128, 2, 4, 4, 128])
    I['cvec'] = din('cvec', [128, 4])
    I['mvals'] = din('mvals', [128, 128])
    G.I = I
    G.yT = nc.dram_tensor('yT', [NB, D, SEQ], F32, kind="ExternalOutput").ap()

    G.PS = [ges.enter_context(nc.psum_tensor('ps%d' % i, [128, 512], F32)) for i in range(8)]
    G.MOD = ges.enter_context(G.sbt('MOD', [128, DEPTH, 48, 4], F32))
    G.MP1 = ges.enter_context(G.sbt('MP1', [128, DEPTH, 2, 8, 4], F32))
    G.VEC = ges.enter_context(G.sbt('VEC', [128, DEPTH, NV], F32))
    G.ONE = ges.enter_context(G.sbt('ONE', [128, 1], F32))
    G.EPS = ges.enter_context(G.sbt('EPS', [128, 1], F32))

    S = {}
    for b in range(NB):
        S['rgx', b] = dscr('rgx%d' % b, [D, CTX + SEQ], BF16)
        S['gg', b] = dscr('gg%d' % b, [D, CTX + SEQ], BF16)
        S['s5u', b] = dscr('s5u%d' % b, [D, CTX + SEQ], BF16)
        S['rg', b] = dscr('rg%d' % b, [D, CTX + SEQ], BF16)
        S['s5uc', b] = dscr('s5uc%d' % b, [D, CTX], BF16)
        S['y5', b] = dscr('y5_%d' % b, [D, CTX + SEQ], BF16)
    for kind, n in (('lat', SEQ), ('ctx', CTX)):
        S['x', kind] = dscr('xs_' + kind, [NB, D, n], F32)
        S['x1', kind] = dscr('x1_' + kind, [NB, D, n], F32)
    for l in range(DEPTH):
        S['WT', l] = dscr('WT%d' % l, [64, 128, 16, 128], BF16)
        S['WS', l] = dscr('WS%d' % l, [64, 128, 2, 8, 128], BF16)
        S['WO', l] = dscr('WO%d' % l, [64, 128, 2, 8, 128], BF16)
        S['L64', l] = dscr('L64_%d' % l, [128, 2, 64], F32)
    G.S = S

    G.stages = stages
    run_stages(G)
    P.final_wait()
    return nc, G


def run_stages(G):
    st = G.stages

    def on(name):
        return st is None or name in st
    if on('mod'):
        stage_mod(G)
    if 'moddbg' in G.debug:
        d = G.dscr('moddbg', [128, DEPTH * 48 * 4], F32)
        G.P.dma('sp', d, G.MOD[:].rearrange('p l c m -> p (l c m)'), 'moddbg', reads=['MOD'])
        G.P.flush()
    for l in range(DEPTH):
        if on('prep%d' % l):
            stage_s5prep(G, l)
        if on('1a%d' % l):
            stage_1a(G, l)
        if on('1b%d' % l):
            stage_1b(G, l)
        if on('2_%d' % l):
            stage_2(G, l)
        if on('3a%d' % l):
            stage_3a(G, l)
        if on('3b%d' % l):
            stage_3b(G, l)


def vec(G, l, name, k=0):
    c = VOFF[name] + k
    return G.VEC[:, l, c:c + 1]


def modv(G, l, j, k, m):
    return G.MOD[:, l, j * 8 + k, m:m + 1]


def stage_mod(G):
    nc, P, I = G.nc, G.P, G.I
    with ExitStack() as es:
        cin = es.enter_context(G.sbt('cin', [128, 8, 4], F32))
        sc = es.enter_context(G.sbt('sc', [128, 8, 4], F32))
        abp = es.enter_context(G.sbt('abp', [128, DEPTH, 48], F32))
        wt = [es.enter_context(G.sbt('adaw%d' % i, [128, 8, 512], F32)) for i in range(2)]
        P.dma('sp', cin[:], I['cT'], 'cin', writes=['cin'])
        P.dma('sp', abp[:], I['ada_bP'].rearrange('l p c -> p l c'), 'abp', writes=['abp'])
        P.dma('sp', G.VEC[:], I['vecs'].rearrange('l p c -> p l c'), 'VEC', writes=['VEC'])
        P.memset('dve', G.ONE[:], 1.0, writes=['ONE'])
        P.memset('dve', G.EPS[:], LN_EPS, writes=['EPS'])
        P.act(sc[:], cin[:], AF.Silu, reads=['cin'], writes=['sc'])
        it = 0
        for l in range(DEPTH):
            for c4 in range(12):
                slot = it % 2
                it += 1
                P.dma('sp', wt[slot][:], I['ada_w'][l, :, c4 * 512:(c4 + 1) * 512].rearrange('(k p) c -> p k c', p=128),
                      ('adaw', slot), writes=[('adaw', slot)])
                ps = G.PS[c4 % 2]
                for cj in range(4):
                    cc = c4 * 4 + cj
                    for k in range(8):
                        P.mm(ps[:, cj * 4:cj * 4 + 4], wt[slot][:, k, cj * 128:(cj + 1) * 128], sc[:, k, :],
                             start=(k == 0), stop=(k == 7), reads=[('adaw', slot), 'sc'], writes=[('ps', c4 % 2)])
                for cj in range(4):
                    cc = c4 * 4 + cj
                    P.ts('dve', G.MOD[:, l, cc, :], ps[:, cj * 4:cj * 4 + 4], abp[:, l, cc:cc + 1], None, ALU.add,
                         reads=[('ps', c4 % 2), 'abp'], writes=['MOD'])
            for s, j in ((0, 1), (1, 4)):
                P.ts('dve', G.MP1[:, l, s, :, :], G.MOD[:, l, j * 8:(j + 1) * 8, :], 1.0, None, ALU.add,
                     reads=['MOD'], writes=['MP1'])
        P.flush()


def seg_list():
    segs = [('ctx', 0, CTX)]
    for t in range(SEQ // 512):
        segs.append(('lat', t * 512, 512))
    return segs


def stage_1a(G, l):
    nc, P, I, S = G.nc, G.P, G.I, G.S
    with ExitStack() as es:
        W = es.enter_context(G.sbt('w_in_b', [128, 8, 3 * D], BF16))
        xin = [es.enter_context(G.sbt('xin%d' % i, [128, 8, 512], F32)) for i in range(2)]
        xm = [es.enter_context(G.sbt('xm%d' % i, [128, 8, 512], BF16)) for i in range(2)]
        st = [[es.enter_context(G.sbt('st%d_%d' % (j, i), [128, 8, 512], BF16)) for i in range(2)] for j in range(3)]
        stc = es.enter_context(G.sbt('stc', [128, 8, CTX], BF16))
        for k in range(8):
            for h in range(2):
                P.dma('pool', W[:, k, h * 1536:(h + 1) * 1536], I['w_in'][l, k * 128:(k + 1) * 128, h * 1536:(h + 1) * 1536],
                      'W1a', writes=[('W', k)])
        psi = 0
        blocks = []
        for b in range(NB):
            for (kind, t0, n) in seg_list():
                blocks.append((b, kind, t0, n))

        def ld(i):
            b, kind, t0, n = blocks[i]
            slot = i % 2
            src = I['cxT'] if kind == 'ctx' else I['xT']
            if l > 0:
                src = G.S['x', kind]
            P.dma('sp', xin[slot][:, :, 0:n], src[b, :, t0:t0 + n].rearrange('(k p) t -> p k t', p=128),
                  ('xin', slot), writes=[('xin', slot)])

        def modu(i):
            b, kind, t0, n = blocks[i]
            slot = i % 2
            m = 2 if kind == 'ctx' else b
            for k in range(8):
                P.ts('dve', xm[slot][:, k, 0:n], xin[slot][:, k, 0:n], G.MP1[:, l, 0, k, m:m + 1], modv(G, l, 0, k, m),
                     ALU.mult, ALU.add, reads=[('xin', slot), 'MP1', 'MOD'], writes=[('xm', slot)])

        ld(0)
        modu(0)
        if len(blocks) > 1:
            ld(1)
        for i, (b, kind, t0, n) in enumerate(blocks):
            slot = i % 2
            off = t0 if kind == 'ctx' else CTX + t0
            for oc in range(24):
                if oc == 12 and i + 1 < len(blocks):
                    modu(i + 1)
                    if i + 2 < len(blocks):
                        ld(i + 2)
                pb = psi % 8
                psi += 1
                ps = G.PS[pb]
                for k in range(8):
                    P.mm(ps[:, 0:n], W[:, k, oc * 128:(oc + 1) * 128], xm[slot][:, k, 0:n], start=(k == 0), stop=(k == 7),
                         reads=[('W', k), ('xm', slot)], writes=[('ps', pb)])
                j, o8 = oc // 8, oc % 8
                dst = st[j][slot][:, o8, 0:n]
                if j == 1:
                    P.act(dst, ps[:, 0:n], AF.Gelu_apprx_tanh, reads=[('ps', pb)], writes=[('st', j, slot)])
                elif oc % 3 == 0:
                    P.copy('act', dst, ps[:, 0:n], reads=[('ps', pb)], writes=[('st', j, slot)])
                else:
                    P.copy('dve', dst, ps[:, 0:n], reads=[('ps', pb)], writes=[('st', j, slot)])
                if j == 2 and kind == 'ctx':
                    for cc in range(4):
                        P.copy('dve', stc[:, o8, :].rearrange('p (i c ip) -> p i c ip', i=8, c=4)[:, :, cc, :],
                               ps[:, cc * 64:(cc + 1) * 64].rearrange('p (i ip) -> p i ip', i=8), reads=[('ps', pb)], writes=['stc'])
            if kind == 'ctx':
                P.dma('sp', S['s5uc', b].rearrange('(k p) t -> p k t', p=128), stc[:], 'stc', reads=['stc'])
            for j, nm in enumerate(('rgx', 'gg', 's5u')):
                P.dma('sp', S[nm, b][:, off:off + n].rearrange('(k p) t -> p k t', p=128), st[j][slot][:, :, 0:n],
                      ('st', j, slot), reads=[('st', j, slot)], writes=[('dram', nm, b)])
        P.flush()


MAGIC = 12582912.0
TWO_PI = 2.0 * math.pi


def stage_s5prep(G, l):
    nc, P, I, S = G.nc, G.P, G.I, G.S
    GB = 4
    with ExitStack() as es:
        def sb(name, shape, dt=F32):
            return es.enter_context(G.sbt(name, shape, dt))
        A_ = sb('s5a_t', [128, 2, 64])
        ldt = sb('ldt', [128, 64])
        Bin = sb('Bin', [128, 2, 64, 16])
        Cin = sb('Cin', [128, 2, 64, 16])
        DP = sb('DP', [128, 64])
        cmask = sb('cmask_t', [128, 2, 4, 4, 128])
        cvec = sb('cvec_t', [128, 4])
        mvals = sb('mvals_t', [128, 128])
        identf = sb('identf', [128, 128])
        Z = sb('Z', [128, 2, 64])
        ZP = sb('ZP', [128, 2, 64])
        L1 = sb('L1', [128, 2, 64])
        L64 = sb('L64t', [128, 2, 64])
        KS = sb('KS', [128, 2, 64])
        KO = sb('KO', [128, 2, 64])
        Q = sb('Q', [128, 2, 64])
        sm = [sb('sm%d' % i, [128, 64]) for i in range(6)]
        Bbar = sb('Bbar', [128, 2, 64, 16])
        BS = sb('BS', [128, 2, 64, 16])
        CO = sb('CO', [128, 2, 64, 16])
        CinN = sb('CinN', [128, 2, 64, 16])
        CON = sb('CON', [128, 2, 64, 16])
        tmpb = [sb("tmpb%d" % i, [128, 1024]) for i in range(8)]
        tab = sb('tab', [128, 2, GB, 128])
        BmR = sb('BmR', [128, 4, GB, 128], BF16)
        BmI = sb('BmI', [128, 4, GB, 128], BF16)
        AR = sb('AR', [128, 4, GB, 128], BF16)
        AIn = sb('AIn', [128, 4, GB, 128], BF16)
        WSP = sb('WSP', [128, 2, GB, 128])
        MZ1 = sb('MZ1', [128, GB, 128])
        Tout = sb('Tout', [128, GB, 16, 128], BF16)
        WSst = sb('WSst', [128, GB, 2, 8, 128], BF16)
        WOst = sb('WOst', [128, GB, 2, 8, 128], BF16)
        t1 = [sb('t1_%d' % i, [128, 512]) for i in range(2)]
        t2 = [sb('t2_%d' % i, [128, 512]) for i in range(2)]

        for t, nm in ((A_, 's5a'), (ldt, 's5ldt'), (Bin, 's5B'), (Cin, 's5C'), (DP, 's5DP')):
            P.dma('sp', t[:], I[nm][l], 'pl_' + nm, writes=[nm])
        for t, nm in ((cmask, 'cmask'), (cvec, 'cvec'), (mvals, 'mvals'), (identf, 'ident')):
            P.dma('sp', t[:], I[nm], 'pl_' + nm, writes=[nm])

        uid = [0]

        def key():
            uid[0] += 1
            return ('k', uid[0])

        def cexp(o_re, o_im, zr, zi, n, rk, wk, neg_im=False):
            tm = [t[:, 0:n] for t in tmpb]
            k0, k1, k2, k3, k4, k5 = [key() for _ in range(6)]
            P.act(tm[0], zr, AF.Exp, reads=rk, writes=[('tm', 0)])
            P.ts('dve', tm[1], zi, 1.0 / TWO_PI, MAGIC, ALU.mult, ALU.add, reads=rk, writes=[('tm', 1)])
            P.ts('dve', tm[2], tm[1], MAGIC, None, ALU.subtract, reads=[('tm', 1)], writes=[('tm', 2)])
            P.stt(tm[3], zi, 1.0 / TWO_PI, tm[2], ALU.mult, ALU.subtract, reads=rk + [('tm', 2)], writes=[('tm', 3)])
            P.act(tm[4], tm[3], AF.Sin, scale=TWO_PI, reads=[('tm', 3)], writes=[('tm', 4)])
            P.ts('pool', tm[5], zi, 1.0 / TWO_PI, 0.25, ALU.mult, ALU.add, reads=rk, writes=[('tm', 5)])
            P.ts('dve', tm[1], tm[5], MAGIC, None, ALU.add, reads=[('tm', 5)], writes=[('tm', 1)])
            P.ts('dve', tm[2], tm[1], MAGIC, None, ALU.subtract, reads=[('tm', 1)], writes=[('tm', 2)])
            P.tt('dve', tm[3], tm[5], tm[2], ALU.subtract, reads=[('tm', 5), ('tm', 2)], writes=[('tm', 3)])
            P.act(tm[6], tm[3], AF.Sin, scale=TWO_PI, reads=[('tm', 3)], writes=[('tm', 6)])
            P.tt('dve', o_re, tm[0], tm[6], ALU.mult, reads=[('tm', 0), ('tm', 6)], writes=wk)
            if neg_im:
                P.stt(o_im, tm[0], -1.0, tm[4], ALU.mult, ALU.mult, reads=[('tm', 0), ('tm', 4)], writes=wk)
            else:
                P.tt('pool', o_im, tm[0], tm[4], ALU.mult, reads=[('tm', 0), ('tm', 4)], writes=wk)

        def cmul(o_re, o_im, ar, ai, br, bi, n_shape, rk, wk, nb=None):
            sh = n_shape
            n = 1
            for v in sh[1:]:
                n *= v

            def tv(i):
                a = tmpb[i][:, 0:n]
                if len(sh) == 3:
                    return a.rearrange('p (a b) -> p a b', a=sh[1])
                if len(sh) == 4:
                    return a.rearrange('p (a b c) -> p a b c', a=sh[1], b=sh[2])
                return a
            ibr, ibi = (br, bi) if nb is None else nb
            P.tt('dve', tv(0), ar, br, ALU.mult, reads=rk, writes=[('tm', 0)])
            P.tt('dve', tv(1), ai, bi, ALU.mult, reads=rk, writes=[('tm', 1)])
            P.tt('dve', o_re, tv(0), tv(1), ALU.subtract, reads=[('tm', 0), ('tm', 1)], writes=wk)
            P.tt('dve' if OPT_PREP else 'pool', tv(2), ar, ibi, ALU.mult, reads=rk, writes=[('tm', 2)])
            P.tt('pool', tv(3), ai, ibr, ALU.mult, reads=rk, writes=[('tm', 3)])
            P.tt('pool', o_im, tv(2), tv(3), ALU.add, reads=[('tm', 2), ('tm', 3)], writes=wk)

        P.act(sm[0][:], ldt[:], AF.Exp, reads=['s5ldt'], writes=['dt'])
        for c in range(2):
            P.tt('dve', Z[:, c, :], A_[:, c, :], sm[0][:], ALU.mult, reads=['s5a', 'dt'], writes=['Z'])
            P.ts('dve', ZP[:, c, :], Z[:, c, :], cvec[:, 0:1], None, ALU.mult, reads=['Z', 'cvec'], writes=['ZP'])
        cexp(L1[:, 0, :], L1[:, 1, :], Z[:, 0, :], Z[:, 1, :], 64, ['Z'], ['L1'])
        for (dst, col, nm) in ((L64, None, 'L64'), (KS, 1, 'KS'), (KO, 2, 'KO')):
            for c in range(2):
                if col is None:
                    P.ts('dve', sm[1 + c][:], Z[:, c, :], 64.0, None, ALU.mult, reads=['Z'], writes=[('zs', c)])
                else:
                    P.ts('dve', sm[1 + c][:], Z[:, c, :], cvec[:, col:col + 1], None, ALU.mult, reads=['Z', 'cvec'], writes=[('zs', c)])
            cexp(dst[:, 0, :], dst[:, 1, :], sm[1][:], sm[2][:], 64, [('zs', 0), ('zs', 1)], [nm])
        P.dma('sp', S['L64', l], L64[:], 'L64st', reads=['L64'])
        P.ts('dve', sm[1][:], L1[:, 0, :], -1.0, None, ALU.add, reads=['L1'], writes=['numr'])
        P.tt('dve', sm[2][:], A_[:, 0, :], A_[:, 0, :], ALU.mult, reads=['s5a'], writes=['d1'])
        P.tt('dve', sm[3][:], A_[:, 1, :], A_[:, 1, :], ALU.mult, reads=['s5a'], writes=['d2'])
        P.tt('dve', sm[2][:], sm[2][:], sm[3][:], ALU.add, reads=['d1', 'd2'], writes=['den'])
        P.add('dve', lambda e: e.reciprocal(out=sm[3][:], in_=sm[2][:]), reads=['den'], writes=['rden'])
        P.tt('dve', sm[4][:], sm[1][:], A_[:, 0, :], ALU.mult, reads=['numr', 's5a'], writes=['q1'])
        P.tt('dve', sm[5][:], L1[:, 1, :], A_[:, 1, :], ALU.mult, reads=['L1', 's5a'], writes=['q2'])
        P.tt('dve', sm[4][:], sm[4][:], sm[5][:], ALU.add, reads=['q1', 'q2'], writes=['q3'])
        P.tt('dve', Q[:, 0, :], sm[4][:], sm[3][:], ALU.mult, reads=['q3', 'rden'], writes=['Qr'])
        P.tt('dve', sm[4][:], L1[:, 1, :], A_[:, 0, :], ALU.mult, reads=['L1', 's5a', 'Qr'], writes=['q4'])
        P.tt('dve', sm[5][:], sm[1][:], A_[:, 1, :], ALU.mult, reads=['numr', 's5a', 'Qr'], writes=['q5'])
        P.tt('dve', sm[4][:], sm[4][:], sm[5][:], ALU.subtract, reads=['q4', 'q5'], writes=['q6'])
        P.tt('dve', Q[:, 1, :], sm[4][:], sm[3][:], ALU.mult, reads=['q6', 'rden'], writes=['Qi'])

        def bc(ap2, n):
            return ap2.unsqueeze(2).to_broadcast([128, 64, n])
        cmul(Bbar[:, 0], Bbar[:, 1], bc(Q[:, 0, :], 16), bc(Q[:, 1, :], 16), Bin[:, 0], Bin[:, 1], [128, 64, 16],
             ['Qr', 'Qi', 's5B'], ['Bbar'])
        cmul(BS[:, 0], BS[:, 1], bc(KS[:, 0, :], 16), bc(KS[:, 1, :], 16), Bbar[:, 0], Bbar[:, 1], [128, 64, 16],
             ['KS', 'Bbar'], ['BS'])
        cmul(CO[:, 0], CO[:, 1], bc(KO[:, 0, :], 16), bc(KO[:, 1, :], 16), Cin[:, 0], Cin[:, 1], [128, 64, 16],
             ['KO', 's5C'], ['CO'])
        P.ts('dve', CinN[:].rearrange('p c g h -> p (c g h)'), Cin[:].rearrange('p c g h -> p (c g h)'), -1.0, None, ALU.mult,
             reads=['s5C'], writes=['CinN'])
        P.ts('dve', CON[:].rearrange('p c g h -> p (c g h)'), CO[:].rearrange('p c g h -> p (c g h)'), -1.0, None, ALU.mult,
             reads=['CO'], writes=['CON'])

        psi = 0
        for gb in range(64 // GB):
            g0 = gb * GB
            gs = slice(g0, g0 + GB)
            for c in range(2):
                P.tt('dve', tmpb[7][:, 0:GB * 128].rearrange('p (a b) -> p a b', a=GB) if c == 0 else
                     MZ1[:],
                     ZP[:, c, gs].unsqueeze(2).to_broadcast([128, GB, 128]),
                     mvals[:].unsqueeze(1).to_broadcast([128, GB, 128]), ALU.mult,
                     reads=['ZP', 'mvals'], writes=[('mz', c)])
            cexp(tab[:, 0].rearrange('p a b -> p (a b)'), tab[:, 1].rearrange('p a b -> p (a b)'),
                 tmpb[7][:, 0:GB * 128], MZ1[:].rearrange('p a b -> p (a b)'), GB * 128,
                 [('mz', 0), ('mz', 1)], ['tab'])

            def tabv(c, start, step, n_inner):
                if step > 0:
                    v = tab[:, c, :, start:start + 7 * step + 1:step]
                else:
                    stop = start + 7 * step - 1
                    v = tab[:, c, :, start:(stop if stop >= 0 else None):step]
                return v.unsqueeze(3).to_broadcast([128, GB, 8, n_inner])

            def vecv(t, c):
                return t[:, c, gs, :].unsqueeze(2).to_broadcast([128, GB, 8, 16])

            def o4(t):
                return t.rearrange('p g (a b) -> p g a b', a=8)
            for d1 in range(4):
                cmul(o4(BmR[:, d1]), o4(BmI[:, d1]), tabv(0, 63 + d1, -8, 16), tabv(1, 63 + d1, -8, 16), vecv(Bbar, 0), vecv(Bbar, 1),
                     [128, GB, 8, 16], ['tab', 'Bbar'], ['Bm'])
            for d2 in range(4):
                cmul(o4(AR[:, d2]), o4(AIn[:, d2]), tabv(0, 63 + 4 * (d2 - 2), 8, 16), tabv(1, 63 + 4 * (d2 - 2), 8, 16),
                     vecv(Cin, 0), vecv(Cin, 1), [128, GB, 8, 16], ['tab', 's5C', 'CinN'], ['Am'], nb=(vecv(CinN, 0), vecv(CinN, 1)))
            for gl in range(GB):
                for d1 in range(4):
                    pa, pb = psi % 8, (psi + 1) % 8
                    psi += 2
                    for (pp, h0) in ((pa, 0), (pb, 64)):
                        hs = slice(h0, h0 + 64)
                        out = G.PS[pp][:].rearrange('p (a b) -> p a b', a=4)
                        P.mm(out, BmR[hs, d1, gl, :], AR[hs, :, gl, :], start=True, stop=False, reads=['Bm', 'Am'], writes=[('ps', pp)])
                        P.mm(out, BmI[hs, d1, gl, :], AIn[hs, :, gl, :], start=False, stop=True, reads=['Bm', 'Am'], writes=[('ps', pp)])
                    ts_ = (gl * 4 + d1) % 2
                    P.tt('dve', t1[ts_][:], G.PS[pa][:], cmask[:, 0, d1].rearrange('p a b -> p (a b)'), ALU.mult,
                         reads=[('ps', pa), 'cmask'], writes=[('t1', ts_)])
                    if d1 == 0:
                        P.stt(t1[ts_][:, 256:384], identf[:], DP[:, g0 + gl:g0 + gl + 1], t1[ts_][:, 256:384], ALU.mult, ALU.add,
                              reads=[('t1', ts_), 'ident', 's5DP'], writes=[('t1', ts_)])
                    P.tt('dve', t2[ts_][:], G.PS[pb][:], cmask[:, 1, d1].rearrange('p a b -> p (a b)'), ALU.mult,
                         reads=[('ps', pb), 'cmask'], writes=[('t2', ts_)])
                    P.tt('pool', Tout[:, gl, d1 * 4:(d1 + 1) * 4, :].rearrange('p a b -> p (a b)'), t1[ts_][:], t2[ts_][:], ALU.add,
                         reads=[('t1', ts_), ('t2', ts_)], writes=['Tout'])
            P.dma('sp', S['WT', l][gs].rearrange('g p a b -> p g a b'), Tout[:], 'Toutst', reads=['Tout'])
            for Ip in range(8):
                cmul(o4(WSP[:, 0]), o4(WSP[:, 1]), tabv(0, 126 - Ip, -8, 16), tabv(1, 126 - Ip, -8, 16), vecv(BS, 0), vecv(BS, 1),
                     [128, GB, 8, 16], ['tab', 'BS'], ['WSP'])
                for c in range(2):
                    for g4 in range(GB // 4):
                        pp = psi % 8
                        psi += 1
                        for gq in range(4):
                            gl = g4 * 4 + gq
                            P.tr(G.PS[pp][:, gq * 128:(gq + 1) * 128], WSP[:, c, gl, :], identf[:], reads=['WSP', 'ident'], writes=[('ps', pp)])
                        P.copy('act', WSst[:, g4 * 4:(g4 + 1) * 4, c, Ip, :], G.PS[pp][:].rearrange('p (a b) -> p a b', a=4),
                               reads=[('ps', pp)], writes=['WSst'])
            P.dma('sp', S['WS', l][gs].rearrange('g p c i q -> p g c i q'), WSst[:], 'WSstst', reads=['WSst'])
            for Jp in range(8):
                cmul(o4(WOst[:, :, 0, Jp, :]), o4(WOst[:, :, 1, Jp, :]), tabv(0, 64 + Jp, 8, 16), tabv(1, 64 + Jp, 8, 16),
                     vecv(CO, 0), vecv(CO, 1), [128, GB, 8, 16], ['tab', 'CO', 'CON'], ['WOst'], nb=(vecv(CON, 0), vecv(CON, 1)))
            P.dma('sp', S['WO', l][gs].rearrange('g p c j q -> p g c j q'), WOst[:], 'WOstst', reads=['WOst'])
        P.flush()


def tslot(delta):
    d1 = delta % 4
    d2 = (delta - d1) // 4
    return d1 * 4 + d2 + 2


def stage_2(G, l):
    nc, P, I, S = G.nc, G.P, G.I, G.S
    last = (l == DEPTH - 1)
    NS = 68
    import os
    CUT = int(os.environ.get('S2CUT', '9'))
    with ExitStack() as es:
        def sb(name, shape, dt=F32):
            return es.enter_context(G.sbt(name, shape, dt))
        UTl = sb('UTl', [128, 64, 512], BF16)
        UTc = sb('UTc', [128, 64, 4, 8], BF16)
        X2 = sb('X2', [128, 2, 64, NS])
        Hb = sb('Hb', [128, 2, 64, NS], BF16)
        Lt = sb('Lt', [128, 2, 64])
        L1s = sb('L1s', [128, 2, 64])
        L2s = sb('L2s', [128, 2, 64])
        tA = sb('tA', [128, 2, 64])
        tB = sb('tB', [128, 2, 64])
        NWS = 4
        WSg = [sb('WSg%d' % i, [128, 2, 8, 128], BF16) for i in range(NWS)]
        WTg = [sb('WTg%d' % i, [128, 16, 128], BF16) for i in range(NWS)]
        WOg = [sb('WOg%d' % i, [128, 2, 8, 128], BF16) for i in range(NWS)]
        Yst = [sb('Yst%d' % i, [128, 8, 512], BF16) for i in range(2)]
        Yc = sb('Yc', [128, 64, 4, 8], BF16)

        P.dma('sp', Lt[:], S['L64', l], 'Lt', writes=['Lt'])
        P.copy('dve', L1s[:, 0, :], Lt[:, 0, :], reads=['Lt'], writes=['L1s'])
        P.copy('dve', L1s[:, 1, :], Lt[:, 0, :], reads=['Lt'], writes=['L1s'])
        P.ts('dve', L2s[:, 0, :], Lt[:, 1, :], -1.0, None, ALU.mult, reads=['Lt'], writes=['L2s'])
        P.copy('dve', L2s[:, 1, :], Lt[:, 1, :], reads=['Lt'], writes=['L2s'])
        P.memset('pool', Hb[0:64, :, :, 64:65], 0.0, writes=['Hb0'])
        P.memset('pool', Hb[64:128, :, :, 67:68], 0.0, writes=['Hb0'])
        psi = 0
        wi = 0
        for b in range(NB if CUT >= 9 else 1):
            for i in range(8):
                P.dma('sp', UTl[i * 16:(i + 1) * 16, :, :],
                      S['s5u', b][:, CTX + 512 * i:CTX + 512 * (i + 1)].rearrange('(g h) t -> h g t', h=16),
                      ('UTl', i), writes=['UTl'])
                P.dma('sp', UTc[i * 16:(i + 1) * 16, :, :, :].rearrange('p g c j -> p g (c j)'),
                      S['s5uc', b][:, 32 * i:32 * (i + 1)].rearrange('(g h) t -> h g t', h=16),
                      ('UTc', i), writes=['UTc'])
            if CUT < 2:
                continue
            G7 = 7
            for g0 in range(0, 64, G7):
                gn = min(G7, 64 - g0)
                pbank = []
                for c in range(2):
                    pbank.append(psi % 8)
                    psi += 1
                for gl in range(gn):
                    g = g0 + gl
                    ws = wi % NWS
                    wi += 1
                    P.dma('sp', WSg[ws][:], S['WS', l][g], ('WSg', ws), writes=[('WSg', ws)])
                    for c in range(2):
                        ps = G.PS[pbank[c]]
                        for Ip in range(8):
                            P.mm(ps[:, gl * NS:gl * NS + 64], WSg[ws][:, c, Ip, :], UTl[:, g, Ip * 64:(Ip + 1) * 64], start=(Ip == 0), stop=(Ip == 7),
                                 reads=[('WSg', ws), 'UTl'], writes=[('ps', pbank[c])])
                        for Ip in range(8):
                            P.mm(ps[:, gl * NS + 64:gl * NS + 68], WSg[ws][:, c, Ip, :], UTc[:, g, :, Ip], start=(Ip == 0), stop=(Ip == 7),
                                 reads=[('WSg', ws), 'UTc'], writes=[('ps', pbank[c])])
                for c in range(2):
                    ps = G.PS[pbank[c]]
                    pv = ps[:, 0:gn * NS].rearrange('p (g s) -> p g s', s=NS)
                    eng = 'dve'
                    P.copy(eng, X2[0:64, c, g0:g0 + gn, 0:4], pv[0:64, :, 64:68], reads=[('ps', pbank[c])], writes=['X2'])
                    P.copy(eng, X2[0:64, c, g0:g0 + gn, 4:68], pv[0:64, :, 0:64], reads=[('ps', pbank[c])], writes=['X2'])
                    P.copy(eng, X2[64:128, c, g0:g0 + gn, 0:4], pv[64:128, :, 67:63:-1], reads=[('ps', pbank[c])], writes=['X2'])
                    P.copy(eng, X2[64:128, c, g0:g0 + gn, 4:68], pv[64:128, :, 63::-1], reads=[('ps', pbank[c])], writes=['X2'])
            if CUT < 3:
                continue
            for s_ in range(1, NS):
                prev = X2[:, :, :, s_ - 1]
                prev_sw = X2[:, ::-1, :, s_ - 1]
                cur = X2[:, :, :, s_]
                P.tt('dve', tA[:], L1s[:], prev, ALU.mult, reads=['L1s', 'X2'], writes=['tA'])
                P.tt('dve', tB[:], L2s[:], prev_sw, ALU.mult, reads=['L2s', 'X2'], writes=['tB'])
                P.tt('dve', cur, cur, tA[:], ALU.add, reads=['X2', 'tA'], writes=['X2'])
                P.tt('dve', cur, cur, tB[:], ALU.add, reads=['X2', 'tB'], writes=['X2'])
            for c in range(2):
                P.copy('dve', Hb[0:64, c, :, 0:64], X2[0:64, c, :, 3:67], reads=['X2'], writes=['Hb'])
                P.copy('dve', Hb[0:64, c, :, 65:68], X2[0:64, c, :, 0:3], reads=['X2'], writes=['Hb'])
                P.copy('dve', Hb[64:128, c, :, 0:64], X2[64:128, c, :, 66:2:-1], reads=['X2'], writes=['Hb'])
                P.copy('dve', Hb[64:128, c, :, 64:67], X2[64:128, c, :, 2::-1], reads=['X2'], writes=['Hb'])
            if CUT < 4:
                continue
            for g in range(64 if CUT >= 5 else 8):
                ws = wi % NWS
                wi += 1
                P.dma('sp', WTg[ws][:], S['WT', l][g], ('WTg', ws), writes=[('WTg', ws)])
                P.dma('sp', WOg[ws][:], S['WO', l][g], ('WOg', ws), writes=[('WOg', ws)])
                pb = psi % 8
                psi += 1
                ps = G.PS[pb]
                rk = [('WTg', ws), ('WOg', ws), 'UTl', 'UTc', 'Hb', 'Hb0']
                for Jp in range(8):
                    out = ps[:, Jp * 64:(Jp + 1) * 64]
                    for Ip in range(8):
                        P.mm(out, WTg[ws][:, tslot(Jp - Ip), :], UTl[:, g, Ip * 64:(Ip + 1) * 64], start=(Ip == 0), stop=False,
                             reads=rk, writes=[('ps', pb)])
                    P.mm(out, WOg[ws][:, 0, Jp, :], Hb[:, 0, g, 0:64], start=False, stop=False, reads=rk, writes=[('ps', pb)])
                    P.mm(out, WOg[ws][:, 1, Jp, :], Hb[:, 1, g, 0:64], start=False, stop=True, reads=rk, writes=[('ps', pb)])
                ys = (g // 8) % 2
                P.act(Yst[ys][:, g % 8, :], ps[:], AF.Gelu_apprx_tanh, reads=[('ps', pb)], writes=[('Yst', ys)])
                if not last:
                    pc = psi % 8
                    psi += 1
                    psc = G.PS[pc]
                    for Jp in range(8):
                        out = psc[:, Jp * 4:(Jp + 1) * 4]
                        for Ip in range(8):
                            P.mm(out, WTg[ws][:, tslot(Jp - Ip), :], UTc[:, g, :, Ip], start=(Ip == 0), stop=False, reads=rk, writes=[('ps', pc)])
                        P.mm(out, WOg[ws][:, 0, Jp, :], Hb[:, 0, g, 64:68], start=False, stop=False, reads=rk, writes=[('ps', pc)])
                        P.mm(out, WOg[ws][:, 1, Jp, :], Hb[:, 1, g, 64:68], start=False, stop=True, reads=rk, writes=[('ps', pc)])
                    P.act(Yc[:, g, :, :], psc[:, 0:32].rearrange('p (j c) -> p c j', c=4), AF.Gelu_apprx_tanh, reads=[('ps', pc)], writes=['Yc'])
                if g % 8 == 7:
                    g8 = g - 7
                    for j in range(8):
                        P.dma('sp', S['y5', b][g8 * 16:(g8 + 8) * 16, CTX + 512 * j:CTX + 512 * (j + 1)].rearrange('(g h) t -> h g t', h=16),
                              Yst[ys][j * 16:(j + 1) * 16, :, :], ('Yst', ys), reads=[('Yst', ys)])
            if not last:
                for j in range(8):
                    P.dma('sp', S['y5', b][:, 32 * j:32 * (j + 1)].rearrange('(g h) t -> h g t', h=16),
                          Yc[j * 16:(j + 1) * 16, :, :, :].rearrange('p g c j -> p g (c j)'), 'Ycst', reads=['Yc'])
        P.flush()


def ln_pre(P, T, oc, n, nslot, tiles):
    Rb, SQb = tiles[0], tiles[1]
    tk = ('T32', nslot)
    P.copy('pool', Rb[:, oc, 0:n], T[:, oc, 0:n], reads=[tk], writes=['Rb'])
    P.act(SQb[:, oc, 0:n], T[:, oc, 0:n], AF.Square, reads=[tk], writes=['SQb'])


def ln_block(G, l, P, T, Xres, n, nslot, tiles, gname, bname, psi_ref, rk_extra):
    Rb, SQb, MEAN, M2, VAR, RSTD, ONESB = tiles
    tk = ('T32', nslot)
    pm = psi_ref[0] % 8
    pq = (psi_ref[0] + 1) % 8
    psi_ref[0] += 2
    for oc in range(8):
        P.mm(G.PS[pm][:, 0:n], ONESB[:], Rb[:, oc, 0:n], start=(oc == 0), stop=(oc == 7), reads=['Rb', 'ONESB'], writes=[('ps', pm)])
    for oc in range(8):
        P.mm(G.PS[pq][:, 0:n], ONESB[:], SQb[:, oc, 0:n], start=(oc == 0), stop=(oc == 7), reads=['SQb', 'ONESB'], writes=[('ps', pq)])
    P.copy('act', MEAN[:, 0:n], G.PS[pm][:, 0:n], reads=[('ps', pm)], writes=['MEAN'])
    P.tt('pool', M2[:, 0:n], MEAN[:, 0:n], MEAN[:, 0:n], ALU.mult, reads=['MEAN'], writes=['M2'])
    P.tt('dve', VAR[:, 0:n], G.PS[pq][:, 0:n], M2[:, 0:n], ALU.subtract, reads=[('ps', pq), 'M2'], writes=['VAR'])
    P.act(VAR[:, 0:n], VAR[:, 0:n], AF.Sqrt, bias=G.EPS[:], reads=['VAR'], writes=['VAR'])
    P.add('dve', lambda e, o=RSTD[:, 0:n], i=VAR[:, 0:n]: e.reciprocal(out=o, in_=i), reads=['VAR'], writes=['RSTD'])
    for oc in range(8):
        P.tt('pool', T[:, oc, 0:n], T[:, oc, 0:n], MEAN[:, 0:n], ALU.subtract, reads=[tk, 'MEAN'], writes=[tk])
        P.tt('dve', T[:, oc, 0:n], T[:, oc, 0:n], RSTD[:, 0:n], ALU.mult, reads=[tk, 'RSTD'], writes=[tk])
        P.ts('dve', T[:, oc, 0:n], T[:, oc, 0:n], vec(G, l, gname, oc), vec(G, l, bname, oc), ALU.mult, ALU.add, reads=[tk], writes=[tk])


def stage_3a(G, l):
    nc, P, I, S = G.nc, G.P, G.I, G.S
    last = (l == DEPTH - 1)
    with ExitStack() as es:
        def sb(name, shape, dt=F32):
            return es.enter_context(G.sbt(name, shape, dt))
        GLW = sb('GLW', [128, 8, D], BF16)
        WO_ = sb('WO_', [128, 16, D], BF16)
        ONESB = sb('ONESB', [128, 128], BF16)
        Y5 = [sb('Y5_%d' % i, [128, 8, 512], BF16) for i in range(2)]
        Y5n = sb('Y5n', [128, 8, CTX], BF16)
        RGt = [sb('RGt%d' % i, [128, 8, 512], BF16) for i in range(2)]
        Xin = [sb('Xin%d' % i, [128, 8, 512]) for i in range(2)]
        S5o = sb('S5o', [128, 8, 512], BF16)
        T32 = [sb('T32_%d' % i, [128, 8, 512]) for i in range(2)]
        Rb = sb('Rb', [128, 8, 512], BF16)
        SQb = sb('SQb', [128, 8, 512], BF16)
        sig = [sb('sig%d' % i, [128, 512]) for i in range(2)]
        MEAN = sb('MEAN', [128, 512])
        M2 = sb('M2', [128, 512])
        VAR = sb('VAR', [128, 512])
        RSTD = sb('RSTD', [128, 512])
        tiles = (Rb, SQb, MEAN, M2, VAR, RSTD, ONESB)
        for k in range(8):
            P.dma('pool', GLW[:, k, :], I['glu_w'][l, k * 128:(k + 1) * 128, :], 'GLWd', writes=['GLW'])
        for k in range(16):
            P.dma('pool', WO_[:, k, :], I['w_out'][l, k * 128:(k + 1) * 128, :], 'WO_d', writes=['WO_'])
        P.memset('dve', ONESB[:], 1.0 / D, writes=['ONESB'])
        psi = [0]
        sgc = [0]
        blocks = []
        for b in range(NB):
            for (kind, t0, n) in seg_list():
                if kind == 'ctx' and last:
                    continue
                blocks.append((b, kind, t0, n))

        def phA(i):
            b, kind, t0, n = blocks[i]
            slot = i % 2
            xsrc = (I['cxT'] if kind == 'ctx' else I['xT']) if l == 0 else S['x', kind]
            off = t0 if kind == 'ctx' else CTX + t0
            P.dma('sp', Y5[slot][:, :, 0:n], S['y5', b][:, off:off + n].rearrange('(k p) t -> p k t', p=128), ('Y5', slot), writes=[('Y5', slot)])
            P.dma('sp', RGt[slot][:, :, 0:n], S['rg', b][:, off:off + n].rearrange('(k p) t -> p k t', p=128), ('RGt', slot), writes=[('RGt', slot)])
            P.dma('sp', Xin[slot][:, :, 0:n], xsrc[b, :, t0:t0 + n].rearrange('(k p) t -> p k t', p=128), ('Xin', slot), writes=[('Xin', slot)])
            if kind == 'ctx':
                for k in range(8):
                    for cc in range(4):
                        P.copy('pool', Y5n[:, k, cc * 64:(cc + 1) * 64].rearrange('p (j q) -> p j q', j=8),
                               Y5[slot][:, k, 0:CTX].rearrange('p (j c q) -> p j c q', j=8, c=4)[:, :, cc, :],
                               reads=[('Y5', slot)], writes=['Y5n'])

        def phB(i):
            b, kind, t0, n = blocks[i]
            slot = i % 2
            if kind == 'ctx':
                ysrc, yk = Y5n, 'Y5n'
            else:
                ysrc, yk = Y5[slot], ('Y5', slot)
            for oc in range(8):
                pb = psi[0] % 8
                psi[0] += 1
                for k in range(8):
                    P.mm(G.PS[pb][:, 0:n], GLW[:, k, oc * 128:(oc + 1) * 128], ysrc[:, k, 0:n], start=(k == 0), stop=(k == 7),
                         reads=['GLW', yk], writes=[('ps', pb)])
                ss = sgc[0] % 2
                sgc[0] += 1
                P.act(sig[ss][:, 0:n], G.PS[pb][:, 0:n], AF.Sigmoid, bias=vec(G, l, 'glu_b', oc), reads=[('ps', pb)], writes=[('sig', ss)])
                P.tt('dve', S5o[:, oc, 0:n], ysrc[:, oc, 0:n], sig[ss][:, 0:n], ALU.mult, reads=[yk, ('sig', ss)], writes=['S5o'])

        def phC(i):
            b, kind, t0, n = blocks[i]
            slot = i % 2
            m = 2 if kind == 'ctx' else b
            T = T32[slot]
            for oc in range(8):
                pb = psi[0] % 8
                psi[0] += 1
                for k in range(16):
                    rhs = RGt[slot][:, k, 0:n] if k < 8 else S5o[:, k - 8, 0:n]
                    P.mm(G.PS[pb][:, 0:n], WO_[:, k, oc * 128:(oc + 1) * 128], rhs, start=(k == 0), stop=(k == 15),
                         reads=['WO_', ('RGt', slot), 'S5o'], writes=[('ps', pb)])
                P.ts('dve', T[:, oc, 0:n], G.PS[pb][:, 0:n], vec(G, l, 'b_out', oc), modv(G, l, 2, oc, m), ALU.add, ALU.mult,
                     reads=[('ps', pb), 'MOD'], writes=[('T32', slot)])
                P.stt(T[:, oc, 0:n], Xin[slot][:, oc, 0:n], ALPHA, T[:, oc, 0:n], ALU.mult, ALU.add,
                      reads=[('Xin', slot), ('T32', slot)], writes=[('T32', slot)])
                ln_pre(P, T, oc, n, slot, tiles)

        def phD(i):
            b, kind, t0, n = blocks[i]
            slot = i % 2
            T = T32[slot]
            ln_block(G, l, P, T, None, n, slot, tiles, 'ln1_g', 'ln1_b', psi, None)
            P.dma('sp', S['x1', kind][b, :, t0:t0 + n].rearrange('(k p) t -> p k t', p=128), T[:, :, 0:n], ('T32', slot),
                  reads=[('T32', slot)])

        nb_ = len(blocks)
        phA(0)
        for i in range(nb_):
            if i + 1 < nb_:
                phA(i + 1)
            phB(i)
            if i > 0 and OPT_3:
                phD(i - 1)
            phC(i)
            if not OPT_3:
                phD(i)
        if OPT_3:
            phD(nb_ - 1)
        P.flush()


def stage_3b(G, l):
    nc, P, I, S = G.nc, G.P, G.I, G.S
    last = (l == DEPTH - 1)
    NBLK = 256
    with ExitStack() as es:
        def sb(name, shape, dt=F32):
            return es.enter_context(G.sbt(name, shape, dt))
        W1b = sb('W1b', [128, 8, 4 * D], BF16)
        W2b = sb('W2b', [128, 32, D], BF16)
        ONESB = sb('ONESB', [128, 128], BF16)
        X1 = [sb('X1_%d' % i, [128, 8, NBLK]) for i in range(2)]
        X1m = [sb('X1m%d' % i, [128, 8, NBLK], BF16) for i in range(2)]

        def slot3(i):
            return i % 2
        Hh = sb('Hh', [128, 32, NBLK], BF16)
        hr = [sb('hr%d' % i, [128, NBLK], BF16) for i in range(2)]
        T32 = [sb('T32_%d' % i, [128, 8, NBLK]) for i in range(2)]
        Rb = sb('Rb', [128, 8, NBLK], BF16)
        SQb = sb('SQb', [128, 8, NBLK], BF16)
        MEAN = sb('MEAN', [128, NBLK])
        M2 = sb('M2', [128, NBLK])
        VAR = sb('VAR', [128, NBLK])
        RSTD = sb('RSTD', [128, NBLK])
        tiles = (Rb, SQb, MEAN, M2, VAR, RSTD, ONESB)
        for k in range(8):
            for h in range(2):
                P.dma('pool', W1b[:, k, h * 2048:(h + 1) * 2048], I['w1'][l, k * 128:(k + 1) * 128, h * 2048:(h + 1) * 2048],
                      'W1bd', writes=['W1b'])
        for k in range(32):
            P.dma('pool', W2b[:, k, :], I['w2'][l, k * 128:(k + 1) * 128, :], 'W2bd', writes=['W2b'])
        P.memset('dve', ONESB[:], 1.0 / D, writes=['ONESB'])
        psi = [0]
        hsc = [0]
        blocks = []
        for b in range(NB):
            segs = [] if last else [('ctx', 0, CTX)]
            segs += [('lat', t * NBLK, NBLK) for t in range(SEQ // NBLK)]
            for (kind, t0, n) in segs:
                blocks.append((b, kind, t0, n))

        def phA(i):
            b, kind, t0, n = blocks[i]
            s3 = slot3(i)
            P.dma('sp', X1[s3][:], S['x1', kind][b, :, t0:t0 + n].rearrange('(k p) t -> p k t', p=128), ('X1', s3), writes=[('X1', s3)])

        def phM(i):
            b, kind, t0, n = blocks[i]
            s3 = slot3(i)
            m = 2 if kind == 'ctx' else b
            for k in range(8):
                P.ts('dve', X1m[i % 2][:, k, :], X1[s3][:, k, :], G.MP1[:, l, 1, k, m:m + 1], modv(G, l, 3, k, m), ALU.mult, ALU.add,
                     reads=[('X1', s3), 'MP1', 'MOD'], writes=[('X1m', i % 2)])

        def phB(i):
            b, kind, t0, n = blocks[i]
            slot = i % 2
            m = 2 if kind == 'ctx' else b
            for hc in range(32):
                pb = psi[0] % 8
                psi[0] += 1
                for k in range(8):
                    P.mm(G.PS[pb][:, 0:n], W1b[:, k, hc * 128:(hc + 1) * 128], X1m[i % 2][:, k, :], start=(k == 0), stop=(k == 7),
                         reads=['W1b', ('X1m', i % 2)], writes=[('ps', pb)])
                h2 = hsc[0] % 2
                hsc[0] += 1
                P.act(hr[h2][:], G.PS[pb][:, 0:n], AF.Relu, bias=vec(G, l, 'b1', hc), reads=[('ps', pb)], writes=[('hr', h2)])
                P.tt('pool', Hh[:, hc, :], hr[h2][:], hr[h2][:], ALU.mult, reads=[('hr', h2)], writes=['Hh'])

        def phC(i):
            b, kind, t0, n = blocks[i]
            slot = i % 2
            m = 2 if kind == 'ctx' else b
            T = T32[slot]
            for oc in range(8):
                pb = psi[0] % 8
                psi[0] += 1
                for k in range(32):
                    P.mm(G.PS[pb][:, 0:n], W2b[:, k, oc * 128:(oc + 1) * 128], Hh[:, k, :], start=(k == 0), stop=(k == 31),
                         reads=['W2b', 'Hh'], writes=[('ps', pb)])
                P.ts('dve', T[:, oc, :], G.PS[pb][:, 0:n], vec(G, l, 'b2', oc), modv(G, l, 5, oc, m), ALU.add, ALU.mult,
                     reads=[('ps', pb), 'MOD'], writes=[('T32', slot)])
                P.stt(T[:, oc, :], X1[slot3(i)][:, oc, :], ALPHA, T[:, oc, :], ALU.mult, ALU.add,
                      reads=[('X1', slot3(i)), ('T32', slot)], writes=[('T32', slot)])
                ln_pre(P, T, oc, n, slot, tiles)

        def phD(i):
            b, kind, t0, n = blocks[i]
            slot = i % 2
            T = T32[slot]
            ln_block(G, l, P, T, None, n, slot, tiles, 'ln2_g', 'ln2_b', psi, None)
            if last:
                dst = G.yT[b, :, t0:t0 + n]
            else:
                dst = S['x', kind][b, :, t0:t0 + n]
            P.dma('sp', dst.rearrange('(k p) t -> p k t', p=128), T[:, :, 0:n], ('T32', slot), reads=[('T32', slot)])

        nb_ = len(blocks)
        phA(0)
        phM(0)
        for i in range(nb_):
            if i + 1 < nb_:
                phA(i + 1)
            phB(i)
            if i + 1 < nb_:
                phM(i + 1)
            if i > 0:
                phD(i - 1)
            phC(i)
        phD(nb_ - 1)
        P.flush()


def rev(ap):
    return ap[:, ::-1]


def stage_1b(G, l):
    nc, P, I, S = G.nc, G.P, G.I, G.S
    NT = CTX + SEQ
    with ExitStack() as es:
        def sb(name, shape, dt):
            return es.enter_context(G.sbt(name, shape, dt))
        identf = sb('identf', [128, 128], F32)
        DG = sb('DG', [128, 8, 4, 128], BF16)
        GW = sb('GW', [128, 2, 2, 8, 128], BF16)
        cneg = sb('cneg', [128, 2, 8], F32)
        ctmp = sb('ctmp', [128, 2, 8], F32)
        RP = [sb('RP%d' % i, [128, NT + 6], BF16) for i in range(2)]
        GGt = [sb('GGt%d' % i, [128, NT], BF16) for i in range(2)]
        OUT = [sb('OUT%d' % i, [128, NT], BF16) for i in range(1)]
        A = [sb('A%d' % i, [128, NT], F32) for i in range(2)]
        Bt = [sb('B%d' % i, [128, NT], F32) for i in range(2)]
        HF = sb('HF', [128, NT], F32)
        XC32 = [sb('XC32_%d' % i, [128, 512], F32) for i in range(3)]
        XCB = [sb('XCB%d' % i, [128, 512], BF16) for i in range(3)]
        Rt = [[sb('Rt%d_%d' % (d, i), [128, 512], F32) for i in range(2)] for d in range(2)]
        It = [[sb('It%d_%d' % (d, i), [128, 512], F32) for i in range(2)] for d in range(2)]
        Tt = [[sb('Tt%d_%d' % (d, i), [128, 512], F32) for i in range(3)] for d in range(2)]
        Mt = [[sb('Mt%d_%d' % (d, i), [128, 512], F32) for i in range(2)] for d in range(2)]
        Ut = [[sb('Ut%d_%d' % (d, i), [128, 512], F32) for i in range(3)] for d in range(2)]

        P.dma('sp', identf[:], I['ident'], 'identf', writes=['identf'])
        for w, nm in enumerate(('rg_wa', 'rg_wi')):
            for d in range(2):
                P.dma('pool', GW[:, w, d, :, :], I[nm][l, d].rearrange('h i j -> i h j'), 'GWd', writes=['GW'])
        for fc in range(8):
            for tap in range(4):
                P.ts('dve', DG[:, fc, tap, :], identf[:], vec(G, l, 'conv_w%d' % tap, fc), None, ALU.mult,
                     reads=['identf'], writes=['DG'])
        for d in range(2):
            o = VOFF['lam%d' % d]
            P.act(ctmp[:, d, :], G.VEC[:, l, o:o + 8], AF.Exp, scale=-1.0, writes=['ctmp'])
        P.act(cneg[:], ctmp[:], AF.Ln, bias=G.ONE[:], reads=['ctmp'], writes=['cneg0'])
        P.ts('dve', cneg[:], cneg[:], -RG_C, None, ALU.mult, reads=['cneg0'], writes=['cneg'])
        for i in range(2):
            P.memset('pool', RP[i][:], 0.0, writes=[('RP', i)])
        segs = [(0, CTX, 1)] + [(CTX + t * 512, 512, 4 + CTX + t * 512) for t in range(SEQ // 512)]
        it = 0
        psi = 0
        gseg = [0]
        for b in range(NB):
            for fc in range(8):
                slot = it % 2
                rows = slice(fc * 128, (fc + 1) * 128)

                def ld1b(j):
                    bb, ff = j // 8, j % 8
                    sl = j % 2
                    rr = slice(ff * 128, (ff + 1) * 128)
                    P.dma('sp', RP[sl][:, 1:1 + CTX], S['rgx', bb][rr, 0:CTX], ('RPc', sl), writes=[('RP', sl)])
                    P.dma('sp', RP[sl][:, 4 + CTX:4 + CTX + SEQ], S['rgx', bb][rr, CTX:NT], ('RPl', sl), writes=[('RP', sl)])
                    P.dma('sp', GGt[sl][:], S['gg', bb][rr, :], ('GGt', sl), writes=[('GGt', sl)])
                if it == 0:
                    ld1b(0)
                if it + 1 < NB * 8:
                    ld1b(it + 1)
                it += 1
                base = gseg[0]
                gseg[0] += len(segs)

                def ph1(si):
                    nonlocal psi
                    off, n, pidx = segs[si]
                    q = (base + si) % 3
                    pb = psi % 8
                    psi += 1
                    ps = G.PS[pb]
                    for tap in range(4):
                        P.mm(ps[:, 0:n], DG[:, fc, tap, :], RP[slot][:, pidx + tap - 1:pidx + tap - 1 + n], start=(tap == 0), stop=(tap == 3),
                             reads=['DG', ('RP', slot)], writes=[('ps', pb)])
                    P.act(XC32[q][:, 0:n], ps[:, 0:n], AF.Identity, bias=vec(G, l, 'conv_b', fc), reads=[('ps', pb)], writes=[('XC32', q)])
                    P.copy('dve', XCB[q][:, 0:n], XC32[q][:, 0:n], reads=[('XC32', q)], writes=[('XCB', q)])

                def ph2(si):
                    nonlocal psi
                    off, n, pidx = segs[si]
                    q = (base + si) % 3
                    r2 = (base + si) % 2
                    for d in range(2):
                        pr = psi % 8
                        pi_ = (psi + 1) % 8
                        psi += 2
                        P.mm(G.PS[pr][:, 0:n], GW[:, 0, d, fc, :], XCB[q][:, 0:n], start=True, stop=True,
                             reads=['GW', ('XCB', q)], writes=[('ps', pr)])
                        P.mm(G.PS[pi_][:, 0:n], GW[:, 1, d, fc, :], XCB[q][:, 0:n], start=True, stop=True,
                             reads=['GW', ('XCB', q)], writes=[('ps', pi_)])
                        P.act(Rt[d][r2][:, 0:n], G.PS[pr][:, 0:n], AF.Sigmoid, bias=vec(G, l, 'ba%d' % d, fc), reads=[('ps', pr)], writes=[('Rt', d, r2)])
                        P.act(It[d][r2][:, 0:n], G.PS[pi_][:, 0:n], AF.Sigmoid, bias=vec(G, l, 'bi%d' % d, fc), reads=[('ps', pi_)], writes=[('It', d, r2)])
                    for d in range(2):
                        P.act(A[d][:, off:off + n], Rt[d][r2][:, 0:n], AF.Exp, scale=cneg[:, d, fc:fc + 1], reads=[('Rt', d, r2), 'cneg'], writes=[('A', d, si)])
                    for d in range(2):
                        P.tt('dve', Tt[d][q][:, 0:n], A[d][:, off:off + n], A[d][:, off:off + n], ALU.mult, reads=[('A', d, si)], writes=[('Tt', d, q)])
                        P.tt('dve', Ut[d][q][:, 0:n], It[d][r2][:, 0:n], XC32[q][:, 0:n], ALU.mult, reads=[('It', d, r2), ('XC32', q)], writes=[('Ut', d, q)])

                def ph3(si):
                    off, n, pidx = segs[si]
                    q = (base + si) % 3
                    r2 = (base + si) % 2
                    for d in range(2):
                        P.act(Mt[d][r2][:, 0:n], Tt[d][q][:, 0:n], AF.Sqrt, bias=G.ONE[:], scale=-1.0, reads=[('Tt', d, q)], writes=[('Mt', d, r2)])
                    for d in range(2):
                        P.tt('dve', Bt[d][:, off:off + n], Ut[d][q][:, 0:n], Mt[d][r2][:, 0:n], ALU.mult, reads=[('Ut', d, q), ('Mt', d, r2)], writes=[('B', d, si)])
                    init = 0.0 if si == 0 else HF[:, off - 1:off]
                    P.scan(HF[:, off:off + n], A[0][:, off:off + n], Bt[0][:, off:off + n], init, reads=[('A', 0, si), ('B', 0, si), 'HF'], writes=['HF'])

                ns_ = len(segs)
                for t in range(ns_ + 2):
                    if t < ns_:
                        ph1(t)
                    if 0 <= t - 1 < ns_:
                        ph2(t - 1)
                    if 0 <= t - 2 < ns_:
                        ph3(t - 2)
                HB = A[0]
                kA1 = [('A', 1, i) for i in range(ns_)] + [('B', 1, i) for i in range(ns_)]
                kA0 = [('A', 0, i) for i in range(ns_)]
                P.scan(rev(HB[:, 0:CTX]), rev(A[1][:, 0:CTX]), rev(Bt[1][:, 0:CTX]), 0.0, reads=kA1, writes=kA0)
                P.scan(rev(HB[:, CTX:NT]), rev(A[1][:, CTX:NT]), rev(Bt[1][:, CTX:NT]), HB[:, 0:1], reads=kA1 + kA0, writes=kA0)
                P.tt('dve', HF[:], HF[:], HB[:], ALU.add, reads=['HF'] + kA0, writes=['HF'])
                P.tt('dve', OUT[0][:], HF[:], GGt[slot][:], ALU.mult, reads=['HF', ('GGt', slot)], writes=['OUT'])
                P.dma('sp', S['rg', b][rows, :], OUT[0][:], 'OUTst', reads=['OUT'])
        P.flush()


def pack_inputs(inp):
    f = lambda a: np.ascontiguousarray(np.asarray(a, dtype=np.float32))
    x = np.asarray(inp['x'], np.float32)
    ctx = np.asarray(inp['ctx'], np.float32)
    c = np.asarray(inp['c'], np.float32)
    c_ctx = np.asarray(inp['c_ctx'], np.float32)

    def pk(v):
        v = np.asarray(v, np.float32)
        return v.reshape(-1, 128).T

    vecs = np.zeros((DEPTH, 128, NV), np.float32)
    for l in range(DEPTH):
        cols = {'ln1_g': inp['ln1_g'][l], 'ln1_b': inp['ln1_b'][l], 'ln2_g': inp['ln2_g'][l], 'ln2_b': inp['ln2_b'][l],
                'conv_w0': inp['conv_w'][l][0], 'conv_w1': inp['conv_w'][l][1], 'conv_w2': inp['conv_w'][l][2],
                'conv_w3': inp['conv_w'][l][3], 'conv_b': inp['conv_b'][l], 'lam0': inp['rg_lambda'][l][0],
                'lam1': inp['rg_lambda'][l][1], 'ba0': inp['rg_ba'][l][0], 'ba1': inp['rg_ba'][l][1],
                'bi0': inp['rg_bi'][l][0], 'bi1': inp['rg_bi'][l][1], 'glu_b': inp['s5_glu_b'][l], 'b_out': inp['b_out'][l],
                'b2': inp['mlp_b2'][l], 'b1': inp['mlp_b1'][l], 's5d': inp['s5_d'][l]}
        for n, ncol in VEC_NAMES:
            vecs[l, :, VOFF[n]:VOFF[n] + ncol] = pk(cols[n])
    ada_bP = np.stack([pk(np.asarray(inp['ada_b'])[l]) for l in range(DEPTH)])
    shared = {
        'ada_w': f(inp['ada_w']), 'ada_bP': f(ada_bP), 'vecs': vecs, 'w_in': f(inp['w_in']),
        'rg_wa': f(inp['rg_wa']), 'rg_wi': f(inp['rg_wi']), 'glu_w': f(inp['s5_glu_w']), 'w_out': f(inp['w_out']),
        'w1': f(inp['mlp_w1']), 'w2': f(inp['mlp_w2']), 'ident': np.eye(128, dtype=np.float32),
    }
    A = lambda n: np.asarray(inp[n], np.float32)
    s5a = np.stack([A('s5_a_re'), A('s5_a_im')], axis=2)
    shared['s5a'] = f(s5a.transpose(0, 1, 4, 2, 3).reshape(DEPTH, 128, 2, 64))
    shared['s5ldt'] = f(np.broadcast_to(A('s5_log_dt')[:, :, None, :], (DEPTH, 2, 64, 64)).reshape(DEPTH, 128, 64))
    s5b = np.stack([A('s5_b_re'), A('s5_b_im')], axis=2)
    shared['s5B'] = f(s5b.transpose(0, 1, 4, 2, 3, 5).reshape(DEPTH, 128, 2, 64, 16))
    s5c = np.stack([A('s5_c_re'), A('s5_c_im')], axis=2)
    shared['s5C'] = f(s5c.transpose(0, 1, 5, 2, 3, 4).reshape(DEPTH, 128, 2, 64, 16))
    dgh = A('s5_d').reshape(DEPTH, 64, 16)
    shared['s5DP'] = f(np.broadcast_to(dgh.transpose(0, 2, 1)[:, None, :, :], (DEPTH, 8, 16, 64)).reshape(DEPTH, 128, 64))
    cm = np.zeros((128, 2, 4, 4, 128), np.float32)
    for i in range(8):
        for j in range(8):
            for d1 in range(4):
                for d2 in range(4):
                    lag = 8 * (j - i) + d1 + 4 * (d2 - 2)
                    if lag >= 0:
                        cm[i * 16:(i + 1) * 16, 0, d1, d2, j * 16:(j + 1) * 16] = 1.0
                    if lag <= 0:
                        cm[i * 16:(i + 1) * 16, 1, d1, d2, j * 16:(j + 1) * 16] = 1.0
    shared['cmask'] = cm
    cv = np.zeros((128, 4), np.float32)
    cv[:64, 0] = 1.0
    cv[64:, 0] = -1.0
    cv[64:, 1] = 63.0
    cv[64:, 2] = 65.0
    shared['cvec'] = cv
    shared['mvals'] = f(np.broadcast_to(np.arange(-63, 65, dtype=np.float32)[None, :], (128, 128)))
    maps = []
    for k in range(NCORES):
        bs = slice(k * NB, (k + 1) * NB)
        cT = np.zeros((128, 8, 4), np.float32)
        for m in range(NB):
            cT[:, :, m] = pk(c[k * NB + m])
        cT[:, :, 2] = pk(c_ctx)
        d = dict(shared)
        d['xT'] = np.ascontiguousarray(x[bs].transpose(0, 2, 1))
        d['cxT'] = np.ascontiguousarray(ctx[bs].transpose(0, 2, 1))
        d['cT'] = cT
        maps.append(d)
    return maps


_CACHE = {}


def kernel(**inputs):
    maps = pack_inputs(inputs)
    if 'nc' not in _CACHE:
        _CACHE['nc'] = build_program()[0]
    nc = _CACHE['nc']
    res = run_bass_kernel_spmd(nc, maps, core_ids=list(range(NCORES)))
    outs = [r['yT'] for r in res.results]
    y = np.concatenate(outs, axis=0)
    return np.ascontiguousarray(y.transpose(0, 2, 1)).astype(np.float32)
```
